# Optimizing a Trainium2 kernel written in Bass

```python
import math
import jax, jax.numpy as jnp
from jax import lax
import numpy as np


D_MODEL = 2048
BATCH = 4
SEQ = 4096
DEPTH = 4

RWKV_HEADS = 16
RWKV_HEAD_DIM = 64
RWKV_DIM = RWKV_HEADS * RWKV_HEAD_DIM
DECAY_LORA = 64
AAA_LORA = 64
GATE_LORA = 160
RWKV_GN_EPS = 64e-5

SG_GROUPS = 4
SG_GROUP_DIM = 128
SG_DIM = SG_GROUPS * SG_GROUP_DIM
SG_CHUNK = 128

ATT_HEADS = 8
ATT_HEAD_DIM = 64
ATT_DIM = ATT_HEADS * ATT_HEAD_DIM
IDX_HEADS = 16
IDX_HEAD_DIM = 64
TOPK_MAX = 256
Q_BLOCK = 128

REL_BUCKETS = 32
REL_MAX_DIST = 128

N_BRANCH = 3
D_MIX = RWKV_DIM + SG_DIM + ATT_DIM
D_FF = 5632
LN_EPS = 1e-5
DEEPNORM_ALPHA = (2 * DEPTH) ** 0.25
DEEPNORM_BETA = (8 * DEPTH) ** -0.25

RWKV_SPLIT_POINTS = (RWKV_DIM, 2 * RWKV_DIM, 3 * RWKV_DIM,
                     3 * RWKV_DIM + DECAY_LORA, 3 * RWKV_DIM + DECAY_LORA + AAA_LORA)
RWKV_COLS = 3 * RWKV_DIM + DECAY_LORA + AAA_LORA + GATE_LORA
SG_COLS = 2 * SG_DIM
ATT_SPLIT_POINTS = (ATT_DIM, 2 * ATT_DIM, 3 * ATT_DIM,
                    3 * ATT_DIM + IDX_HEADS * IDX_HEAD_DIM,
                    3 * ATT_DIM + IDX_HEADS * IDX_HEAD_DIM + IDX_HEAD_DIM)
ATT_COLS = 3 * ATT_DIM + IDX_HEADS * IDX_HEAD_DIM + IDX_HEAD_DIM + IDX_HEADS
GATE_COLS = N_BRANCH * D_MODEL
GROUP_SPLIT_POINTS = (RWKV_COLS, RWKV_COLS + SG_COLS, RWKV_COLS + SG_COLS + ATT_COLS)
D_IN = RWKV_COLS + SG_COLS + ATT_COLS + GATE_COLS

kernel_name = "hybrid_rwkv7_sgu_dsa_macaron_deepnorm"


def layer_norm(x, g, b, eps=LN_EPS):
    xf = x.astype(jnp.float32)
    mu = jnp.mean(xf, axis=-1, keepdims=True)
    var = jnp.mean(jnp.square(xf - mu), axis=-1, keepdims=True)
    return ((xf - mu) * lax.rsqrt(var + eps) * g + b).astype(x.dtype)


def swiglu_ffn(x, w_in, w_out):
    gate, up = jnp.split(x @ w_in, 2, axis=-1)
    return (jax.nn.silu(gate) * up) @ w_out


def rel_bucket(dist):
    max_exact = REL_BUCKETS // 2
    d_f = jnp.maximum(dist, 1).astype(jnp.float32)
    large = max_exact + (jnp.log(d_f / max_exact) / math.log(REL_MAX_DIST / max_exact)
                         * (REL_BUCKETS - max_exact)).astype(jnp.int32)
    large = jnp.minimum(large, REL_BUCKETS - 1)
    return jnp.where(dist < max_exact, dist, large)


def rwkv7_time_mix(p, mu, w0, w2, a0, a2, g2, k_k, k_a, r_k, gn_g, gn_b):
    B, S, _ = p.shape
    H, N = RWKV_HEADS, RWKV_HEAD_DIM
    p_prev = jnp.pad(p, ((0, 0), (1, 0), (0, 0)))[:, :-1]
    p = p + (p_prev - p) * mu
    r, k, v, wd, ad, gd = jnp.split(p, RWKV_SPLIT_POINTS, axis=-1)
    wlog = -jax.nn.softplus(-(w0 + jnp.tanh(wd) @ w2)) - 0.5
    decay = jnp.exp(-jnp.exp(wlog.astype(jnp.float32)))
    a = jax.nn.sigmoid(a0 + ad @ a2)
    g = jax.nn.sigmoid(gd) @ g2

    def heads(t):
        return t.astype(jnp.float32).reshape(B, S, H, N)

    kk = heads(k * k_k)
    kk = kk / jnp.maximum(jnp.sqrt(jnp.sum(jnp.square(kk), axis=-1, keepdims=True)), 1e-12)
    k = k * (1 + (a - 1) * k_a)
    r_h, k_h, v_h, a_h, w_h = heads(r), heads(k), heads(v), heads(a), decay.reshape(B, S, H, N)
    xs = tuple(jnp.moveaxis(t, 1, 0) for t in (r_h, w_h, k_h, v_h, -kk, kk * a_h))

    def step(state, inp):
        r_t, w_t, k_t, v_t, aa_t, bb_t = inp
        sa = jnp.einsum('bhij,bhj->bhi', state, aa_t)
        state = (state * w_t[:, :, None, :] + sa[..., None] * bb_t[:, :, None, :]
                 + v_t[..., None] * k_t[:, :, None, :])
        return state, jnp.einsum('bhij,bhj->bhi', state, r_t)

    _, y = lax.scan(step, jnp.zeros((B, H, N, N), jnp.float32), xs)
    y = jnp.moveaxis(y, 0, 1)
    y_mu = jnp.mean(y, axis=-1, keepdims=True)
    y_var = jnp.mean(jnp.square(y - y_mu), axis=-1, keepdims=True)
    y = ((y - y_mu) * lax.rsqrt(y_var + RWKV_GN_EPS)).reshape(B, S, RWKV_DIM) * gn_g + gn_b
    bonus = jnp.sum(r_h * k_h * r_k, axis=-1, keepdims=True) * v_h
    y = y + bonus.reshape(B, S, RWKV_DIM)
    return (y * g).astype(p.dtype)


def spatial_gating(p, ln_g, ln_b, w_s, b_s):
    B, S, _ = p.shape
    z = jax.nn.gelu(p, approximate=False)
    u, v = z[..., :SG_DIM], z[..., SG_DIM:]
    v = layer_norm(v, ln_g, ln_b)
    v = v.reshape(B, S // SG_CHUNK, SG_CHUNK, SG_GROUPS, SG_GROUP_DIM)
    causal = jnp.tril(jnp.ones((SG_CHUNK, SG_CHUNK), dtype=bool))
    w = jnp.where(causal[None], w_s, jnp.zeros_like(w_s))
    mixed = jnp.einsum('gij,bcjgd->bcigd', w, v) + jnp.transpose(b_s)[:, :, None]
    return u * mixed.reshape(B, S, SG_DIM)


def dsa_attention(p, idx_ln_g, idx_ln_b, rel_bias):
    B, S, _ = p.shape
    top_k = min(TOPK_MAX, S // 4)
    nb = S // Q_BLOCK
    q, k, v, q_idx, k_idx, w_idx = jnp.split(p, ATT_SPLIT_POINTS, axis=-1)
    q = q.reshape(B, S, ATT_HEADS, ATT_HEAD_DIM)
    k = k.reshape(B, S, ATT_HEADS, ATT_HEAD_DIM)
    v = v.reshape(B, S, ATT_HEADS, ATT_HEAD_DIM)
    q_idx = q_idx.reshape(B, S, IDX_HEADS, IDX_HEAD_DIM)
    k_idx = layer_norm(k_idx, idx_ln_g, idx_ln_b)
    w_idx = w_idx * IDX_HEADS ** -0.5
    key_pos = jnp.arange(S, dtype=jnp.int32)

    def to_blocks(t):
        return jnp.moveaxis(t.reshape(B, nb, Q_BLOCK, *t.shape[2:]), 1, 0)

    def block(args):
        qb, qib, wb, start = args
        q_pos = start + jnp.arange(Q_BLOCK, dtype=jnp.int32)
        causal = key_pos[None, :] <= q_pos[:, None]
        dots = jnp.einsum('bqhd,bsd->bqhs', qib, k_idx) * IDX_HEAD_DIM ** -0.5
        score = jnp.einsum('bqh,bqhs->bqs', wb, jax.nn.relu(dots)).astype(jnp.float32)
        score = jnp.where(causal[None], score, -jnp.inf)
        _, sel = lax.top_k(score, top_k)
        k_sel = jax.vmap(lambda kb, ib: kb[ib])(k, sel)
        v_sel = jax.vmap(lambda vb, ib: vb[ib])(v, sel)
        dist = q_pos[None, :, None] - sel
        bias = rel_bias[rel_bucket(jnp.maximum(dist, 0))]
        logits = (jnp.einsum('bqhd,bqkhd->bqhk', qb, k_sel).astype(jnp.float32)
                  * ATT_HEAD_DIM ** -0.5 + jnp.swapaxes(bias, -1, -2))
        logits = jnp.where((dist >= 0)[:, :, None, :], logits, -jnp.inf)
        probs = jax.nn.softmax(logits, axis=-1).astype(v.dtype)
        return jnp.einsum('bqhk,bqkhd->bqhd', probs, v_sel)

    starts = jnp.arange(nb, dtype=jnp.int32) * Q_BLOCK
    out = lax.map(block, (to_blocks(q), to_blocks(q_idx), to_blocks(w_idx), starts))
    return jnp.moveaxis(out, 0, 1).reshape(B, S, ATT_DIM)


def hybrid_token_mixer(x, w_in, b_gate, rwkv_mu, rwkv_w0, rwkv_w2, rwkv_a0, rwkv_a2, rwkv_g2,
                       rwkv_k_k, rwkv_k_a, rwkv_r_k, rwkv_gn_g, rwkv_gn_b,
                       sg_ln_g, sg_ln_b, sg_w, sg_b, idx_ln_g, idx_ln_b, rel_bias,
                       w_branch, w_o):
    B, S, _ = x.shape
    p_rwkv, p_sg, p_att, p_gate = jnp.split(x @ w_in, GROUP_SPLIT_POINTS, axis=-1)
    y_rwkv = rwkv7_time_mix(p_rwkv, rwkv_mu, rwkv_w0, rwkv_w2, rwkv_a0, rwkv_a2, rwkv_g2,
                            rwkv_k_k, rwkv_k_a, rwkv_r_k, rwkv_gn_g, rwkv_gn_b)
    y_sg = spatial_gating(p_sg, sg_ln_g, sg_ln_b, sg_w, sg_b)
    y_att = dsa_attention(p_att, idx_ln_g, idx_ln_b, rel_bias)
    gates = jax.nn.sigmoid(p_gate + b_gate).reshape(B, S, N_BRANCH, D_MODEL)
    z_rwkv = y_rwkv @ w_branch[:RWKV_DIM]
    z_sg = y_sg @ w_branch[RWKV_DIM:RWKV_DIM + SG_DIM]
    z_att = y_att @ w_branch[RWKV_DIM + SG_DIM:]
    merged = gates[:, :, 0] * z_rwkv + gates[:, :, 1] * z_sg + gates[:, :, 2] * z_att
    return merged @ w_o


def setup_inputs(seed: int = 0) -> dict:
    key = jax.random.key(seed)
    keys = jax.random.split(key, 40)
    counter = [0]

    def nxt():
        counter[0] += 1
        return keys[counter[0] - 1]

    def normal(shape, scale):
        return scale * jax.random.normal(nxt(), shape, jnp.float32)

    def uniform(shape, lo, hi):
        return jax.random.uniform(nxt(), shape, jnp.float32, lo, hi)

    L = DEPTH
    return {
        'x': normal((BATCH, SEQ, D_MODEL), 1.0),
        'ffn1_w_in': normal((L, D_MODEL, 2 * D_FF), D_MODEL ** -0.5),
        'ffn1_w_out': normal((L, D_FF, D_MODEL), DEEPNORM_BETA * D_FF ** -0.5),
        'ln1_g': 1.0 + normal((L, D_MODEL), 0.02),
        'ln1_b': normal((L, D_MODEL), 0.02),
        'w_in': normal((L, D_MODEL, D_IN), D_MODEL ** -0.5),
        'b_gate': normal((L, GATE_COLS), 0.1),
        'rwkv_mu': uniform((L, RWKV_COLS), 0.2, 0.8),
        'rwkv_w0': uniform((L, RWKV_DIM), -6.0, 0.0),
        'rwkv_w2': normal((L, DECAY_LORA, RWKV_DIM), 0.1),
        'rwkv_a0': normal((L, RWKV_DIM), 0.1),
        'rwkv_a2': normal((L, AAA_LORA, RWKV_DIM), AAA_LORA ** -0.5),
        'rwkv_g2': normal((L, GATE_LORA, RWKV_DIM), GATE_LORA ** -0.5),
        'rwkv_k_k': 0.85 + normal((L, RWKV_DIM), 0.02),
        'rwkv_k_a': 1.0 + normal((L, RWKV_DIM), 0.02),
        'rwkv_r_k': normal((L, RWKV_HEADS, RWKV_HEAD_DIM), 0.1),
        'rwkv_gn_g': 1.0 + normal((L, RWKV_DIM), 0.02),
        'rwkv_gn_b': normal((L, RWKV_DIM), 0.02),
        'sg_ln_g': 1.0 + normal((L, SG_DIM), 0.02),
        'sg_ln_b': normal((L, SG_DIM), 0.02),
        'sg_w': normal((L, SG_GROUPS, SG_CHUNK, SG_CHUNK), SG_CHUNK ** -0.5),
        'sg_b': 1.0 + normal((L, SG_GROUPS, SG_CHUNK), 0.1),
        'idx_ln_g': 1.0 + normal((L, IDX_HEAD_DIM), 0.02),
        'idx_ln_b': normal((L, IDX_HEAD_DIM), 0.02),
        'rel_bias': normal((REL_BUCKETS, ATT_HEADS), 0.5),
        'w_branch': normal((L, D_MIX, D_MODEL), D_MIX ** -0.5),
        'w_o': normal((L, D_MODEL, D_MODEL), DEEPNORM_BETA * D_MODEL ** -0.5),
        'ln2_g': 1.0 + normal((L, D_MODEL), 0.02),
        'ln2_b': normal((L, D_MODEL), 0.02),
        'ffn2_w_in': normal((L, D_MODEL, 2 * D_FF), D_MODEL ** -0.5),
        'ffn2_w_out': normal((L, D_FF, D_MODEL), DEEPNORM_BETA * D_FF ** -0.5),
        'ln3_g': 1.0 + normal((L, D_MODEL), 0.02),
        'ln3_b': normal((L, D_MODEL), 0.02),
    }


def reference(x, ffn1_w_in, ffn1_w_out, ln1_g, ln1_b, w_in, b_gate, rwkv_mu, rwkv_w0, rwkv_w2,
              rwkv_a0, rwkv_a2, rwkv_g2, rwkv_k_k, rwkv_k_a, rwkv_r_k, rwkv_gn_g, rwkv_gn_b,
              sg_ln_g, sg_ln_b, sg_w, sg_b, idx_ln_g, idx_ln_b, rel_bias, w_branch, w_o,
              ln2_g, ln2_b, ffn2_w_in, ffn2_w_out, ln3_g, ln3_b):
    for l in range(DEPTH):
        x = layer_norm(DEEPNORM_ALPHA * x + 0.5 * swiglu_ffn(x, ffn1_w_in[l], ffn1_w_out[l]),
                       ln1_g[l], ln1_b[l])
        mix = hybrid_token_mixer(x, w_in[l], b_gate[l], rwkv_mu[l], rwkv_w0[l], rwkv_w2[l],
                                 rwkv_a0[l], rwkv_a2[l], rwkv_g2[l], rwkv_k_k[l], rwkv_k_a[l],
                                 rwkv_r_k[l], rwkv_gn_g[l], rwkv_gn_b[l], sg_ln_g[l], sg_ln_b[l],
                                 sg_w[l], sg_b[l], idx_ln_g[l], idx_ln_b[l], rel_bias,
                                 w_branch[l], w_o[l])
        x = layer_norm(DEEPNORM_ALPHA * x + mix, ln2_g[l], ln2_b[l])
        x = layer_norm(DEEPNORM_ALPHA * x + 0.5 * swiglu_ffn(x, ffn2_w_in[l], ffn2_w_out[l]),
                       ln3_g[l], ln3_b[l])
    return x
```

```python
import math


import numpy as np
from contextlib import ExitStack
import concourse.bass as bass
import concourse.mybir as mybir
from concourse.bass_utils import run_bass_kernel_spmd

F32 = mybir.dt.float32
BF16 = mybir.dt.bfloat16
AF = mybir.ActivationFunctionType
ALU = mybir.AluOpType
AX = mybir.AxisListType

ENGS = ("sync", "scalar", "vector", "gpsimd", "tensor")
SEM_ROLL = 30000


class T:
    __slots__ = ("ap", "w", "r")

    def __init__(self, ap):
        self.ap = ap
        self.w = None
        self.r = []

    def __getitem__(self, idx):
        return self.ap[idx]


class Prog:
    def __init__(self, nc, es, n_dma_sems=24):
        self.nc = nc
        self.es = es
        self.q = {e: [] for e in ENGS}
        self.cur_sem = {}
        self.cnt = {}
        self.nsem = 0
        for e in ENGS:
            self._new_eng_sem(e)
        self.dma_sems = [es.enter_context(nc.semaphore(f"dma{i}")) for i in range(2 * n_dma_sems)]
        self.dma_cnt = [0] * (2 * n_dma_sems)
        self.dma_last = [None] * (2 * n_dma_sems)
        self.dma_rr = {"hw": 0, "sw": 0}
        self.n_dma_sems = n_dma_sems
        self.waited = {e: {} for e in ENGS}
        self.semobj = {}
        self.n_inst = {e: 0 for e in ENGS}
        self.n_wait = 0

    def _new_eng_sem(self, e):
        s = self.es.enter_context(self.nc.semaphore(f"s_{e}_{self.nsem}"))
        self.nsem += 1
        self.cur_sem[e] = s
        self.cnt[e] = 0

    def sb(self, name, shape, dt):
        self.uid = getattr(self, "uid", 0) + 1
        es = self.scopes[-1] if getattr(self, "scopes", None) else self.es
        return es.enter_context(self.nc.sbuf_tensor(f"{name}_u{self.uid}", list(shape), dt))

    def push_scope(self):
        if not hasattr(self, "scopes"):
            self.scopes = []
        es = ExitStack()
        es.__enter__()
        self.scopes.append(es)

    def pop_scope(self):
        es = self.scopes.pop()
        es.__exit__(None, None, None)

    def all_events(self):
        evs = [(self.cur_sem[e], self.cnt[e]) for e in ENGS if self.cnt[e] > 0]
        evs += [x for x in self.dma_last if x is not None]
        evs += list(getattr(self, "coll_events", []))
        return evs

    def barrier(self):
        evs = self.all_events()
        for e in ENGS:
            w = self._filter_waits(e, evs)
            if w:
                self.q[e].append((None, w, None, 0))

    def coll(self, kind, groups, src, dst):
        if not hasattr(self, "coll_sem"):
            self.coll_sem = self.es.enter_context(self.nc.semaphore("collsem"))
            self.coll_cnt = 0
        self.coll_cnt += 1
        ev = (self.coll_sem, self.coll_cnt)

        def fn(e):
            return e.collective_compute(kind, ALU.bypass, replica_groups=groups, ins=[src.opt()], outs=[dst.opt()])
        self.q["gpsimd"].append((fn, [], self.coll_sem, 1))
        self.coll_events = [ev]
        return ev

    def ps(self, name, shape, dt=F32):
        return self.es.enter_context(self.nc.psum_tensor(name, list(shape), dt))

    def _collect(self, reads, writes):
        waits = []
        for t in reads:
            if t.w is not None:
                waits.append(t.w)
        for t in writes:
            if t.w is not None:
                waits.append(t.w)
            waits.extend(t.r)
        return waits

    def _filter_waits(self, eng, waits):
        out = {}
        wd = self.waited[eng]
        for (sem, val) in waits:
            k = id(sem)
            self.semobj[k] = sem
            if wd.get(k, 0) >= val:
                continue
            if out.get(k, 0) < val:
                out[k] = val
        res = []
        for k, v in out.items():
            wd[k] = v
            res.append((self.semobj[k], v))
        return res

    def op(self, eng, fn, reads=(), writes=(), extra=()):
        waits = self._collect(reads, writes) + list(extra)
        waits = self._filter_waits(eng, waits)
        if self.cnt[eng] >= SEM_ROLL:
            self._new_eng_sem(eng)
        sem = self.cur_sem[eng]
        self.cnt[eng] += 1
        ev = (sem, self.cnt[eng])
        self.q[eng].append((fn, waits, sem, 1))
        self.n_inst[eng] += 1
        self.n_wait += len(waits)
        for t in reads:
            t.r.append(ev)
        for t in writes:
            t.w = ev
            t.r = []
        return ev

    def group(self, eng, fns, reads=(), writes=()):
        waits = self._collect(reads, writes)
        waits = self._filter_waits(eng, waits)
        if self.cnt[eng] >= SEM_ROLL:
            self._new_eng_sem(eng)
        sem = self.cur_sem[eng]
        self.cnt[eng] += 1
        ev = (sem, self.cnt[eng])
        n = len(fns)
        for i, fn in enumerate(fns):
            self.q[eng].append((fn, waits if i == 0 else [], sem if i == n - 1 else None, 1))
        self.n_inst[eng] += n
        for t in reads:
            t.r.append(ev)
        for t in writes:
            t.w = ev
            t.r = []
        return ev

    def dma(self, queue, out, in_, reads=(), writes=(), extra=(), **kw):
        kind = "sw" if queue == "gpsimd" else "hw"
        i = self.dma_rr[kind] + (self.n_dma_sems if kind == "sw" else 0)
        self.dma_rr[kind] = (self.dma_rr[kind] + 1) % self.n_dma_sems
        waits = self._collect(reads, writes) + list(extra)
        if self.dma_last[i] is not None:
            waits.append(self.dma_last[i])
        waits = self._filter_waits(queue, waits)
        sem = self.dma_sems[i]
        self.dma_cnt[i] += 16
        ev = (sem, self.dma_cnt[i])
        self.dma_last[i] = ev

        def fn(e, out=out, in_=in_, kw=kw):
            return e.dma_start(out=out, in_=in_, **kw)
        self.q[queue].append((fn, waits, sem, 16))
        self.n_inst[queue] += 1
        for t in reads:
            t.r.append(ev)
        for t in writes:
            t.w = ev
            t.r = []
        return ev

    def finish(self, final_events):
        nc = self.nc
        fw = list(final_events)

        def run(e, name):
            for (fn, waits, sem, inc) in self.q[name]:
                for (s, v) in waits:
                    e.wait_ge(s, v)
                if fn is None:
                    continue
                ins = fn(e)
                if sem is not None:
                    ins.then_inc(sem, inc)
            if name == "sync":
                for (s, v) in fw:
                    e.wait_ge(s, v)

        with nc.Block() as block:
            @block.sync
            def _(e):
                run(e, "sync")

            @block.scalar
            def _(e):
                run(e, "scalar")

            @block.vector
            def _(e):
                run(e, "vector")

            @block.gpsimd
            def _(e):
                run(e, "gpsimd")

            @block.tensor
            def _(e):
                run(e, "tensor")


D = 2048
DFF = 5632
KC = D // 128
HC = DFF // 128
TT = 512
ALPHA = 8 ** 0.25
LN_EPS = 1e-5


class Ctx:
    pass


def setup_common(P, nc, banks=None):
    C = Ctx()
    C.banks = banks if banks is not None else [T(P.ps(f"bank{i}", [128, 512], F32)) for i in range(8)]
    C.bank_rr = 0
    C.xb = P.sb("xb", [128, KC, TT], BF16)
    C.xf = P.sb("xf", [128, KC, TT], F32)
    C.h = P.sb("h", [128, HC, TT], BF16)
    C.xb_t = [T(C.xb[:, c, :]) for c in range(KC)]
    C.xf_t = [T(C.xf[:, c, :]) for c in range(KC)]
    C.h_t = [T(C.h[:, c, :]) for c in range(HC)]
    C.w1 = [P.sb(f"w1_{i}", [128, KC, 512], BF16) for i in range(2)]
    C.w1_t = [T(C.w1[i][:]) for i in range(2)]
    C.w2 = [P.sb(f"w2_{i}", [128, HC, 128], BF16) for i in range(2)]
    C.w2_t = [T(C.w2[i][:]) for i in range(2)]
    C.sg = [P.sb(f"sg_{i}", [128, TT], F32) for i in range(2)]
    C.sg_t = [T(C.sg[i][:]) for i in range(2)]
    C.sq = [P.sb(f"sq_{i}", [128, TT], F32) for i in range(2)]
    C.sq_t = [T(C.sq[i][:]) for i in range(2)]
    C.rstd = P.sb("rstd", [128, TT], F32)
    C.rstd_t = T(C.rstd[:])
    C.onesM = P.sb("onesM", [128, 128], F32)
    C.onesM_t = T(C.onesM[:])
    P.op("vector", lambda e: e.memset(C.onesM[:], 1.0 / D), writes=[C.onesM_t])
    C.cnt = {"w1": 0, "w2": 0, "sg": 0, "sq": 0}
    C.wc = None; C.ph = "x"; C.ti = 0
    return C


def wload(P, C, wt, wtt, parts, key, nfree):
    wc = getattr(C, "wc", None)
    if wc is None:
        for (sap, dap) in parts:
            P.dma("gpsimd", sap, dap, writes=[wtt])
        return
    if key not in wc["t"]:
        wc["t"][key] = (P.nc.dram_tensor(f"wsc_{key}", [128, nfree], BF16).ap(), None)
    sc = wc["t"][key][0]
    flat = wt[:].rearrange("p a b -> p (a b)")[:, 0:nfree] if len(wt[:].shape) == 3 else wt[:, 0:nfree]
    if C.ti == 0:
        for (sap, dap) in parts:
            P.dma("gpsimd", sap, dap, writes=[wtt])
        ev = P.dma("sync", sc, flat, reads=[wtt])
        wc["ev"][key] = ev
    else:
        P.dma("sync", flat, sc, writes=[wtt], extra=[wc["ev"][key]])


def next_bank(C):
    b = C.banks[C.bank_rr]
    C.bank_rr = (C.bank_rr + 1) % 8
    return b


def load_vec_cols(P, name, src, n):
    t = P.sb(name, [128, n], F32)
    tt = T(t[:])
    P.dma("sync", t[:], src.rearrange("(c p) -> p c", p=128), writes=[tt], allow_slow_non_contiguous=True)
    return t, tt


def ffn_ln(P, C, w_in, w_out, lng, lng_t, lnb, lnb_t):
    nc = P.nc
    w_in_v = w_in.rearrange("(c p) n -> p c n", p=128)
    w_out_v = w_out.rearrange("(c p) n -> p c n", p=128)
    for hg in range(HC // 2):
        slot = C.cnt["w1"] % 2
        C.cnt["w1"] += 1
        wt, wtt = C.w1[slot], C.w1_t[slot]
        wload(P, C, wt, wtt, [(wt[:, :, 0:256], w_in_v[:, :, hg * 256:(hg + 1) * 256]), (wt[:, :, 256:512], w_in_v[:, :, DFF + hg * 256:DFF + (hg + 1) * 256])],
              f"{C.ph}_a{hg}", KC * 512)
        for j in range(2):
            ht = hg * 2 + j
            pg = next_bank(C)
            pu = next_bank(C)
            fns = []
            for k in range(KC):
                fns.append(lambda e, k=k, pg=pg, j=j, wt=wt: e.matmul(pg[:], wt[:, k, j * 128:(j + 1) * 128], C.xb[:, k, :], start=(k == 0), stop=(k == KC - 1)))
            P.group("tensor", fns, reads=[wtt] + C.xb_t, writes=[pg])
            fns = []
            for k in range(KC):
                fns.append(lambda e, k=k, pu=pu, j=j, wt=wt: e.matmul(pu[:], wt[:, k, 256 + j * 128:256 + (j + 1) * 128], C.xb[:, k, :], start=(k == 0), stop=(k == KC - 1)))
            P.group("tensor", fns, reads=[wtt] + C.xb_t, writes=[pu])
            s = C.cnt["sg"] % 2
            C.cnt["sg"] += 1
            sg, sgt = C.sg[s], C.sg_t[s]
            P.op("scalar", lambda e, sg=sg, pg=pg: e.activation(out=sg[:], in_=pg[:], func=AF.Silu), reads=[pg], writes=[sgt])
            P.op("vector", lambda e, sg=sg, pu=pu, ht=ht: e.scalar_tensor_tensor(out=C.h[:, ht, :], in0=sg[:], scalar=0.5, in1=pu[:], op0=ALU.mult, op1=ALU.mult),
                 reads=[sgt, pu], writes=[C.h_t[ht]])
    for dt_ in range(KC):
        slot = C.cnt["w2"] % 2
        C.cnt["w2"] += 1
        wt, wtt = C.w2[slot], C.w2_t[slot]
        wload(P, C, wt, wtt, [(wt[:, 0:HC // 2, :], w_out_v[:, 0:HC // 2, dt_ * 128:(dt_ + 1) * 128]), (wt[:, HC // 2:, :], w_out_v[:, HC // 2:, dt_ * 128:(dt_ + 1) * 128])],
              f"{C.ph}_b{dt_}", HC * 128)
        py = next_bank(C)
        fns = []
        for k in range(HC):
            fns.append(lambda e, k=k, py=py, wt=wt: e.matmul(py[:], wt[:, k, :], C.h[:, k, :], start=(k == 0), stop=(k == HC - 1)))
        P.group("tensor", fns, reads=[wtt] + C.h_t, writes=[py])
        P.op("vector", lambda e, dt_=dt_, py=py: e.scalar_tensor_tensor(out=C.xf[:, dt_, :], in0=C.xf[:, dt_, :], scalar=ALPHA, in1=py[:], op0=ALU.mult, op1=ALU.add),
             reads=[py], writes=[C.xf_t[dt_]])
    layer_norm(P, C, lng, lng_t, lnb, lnb_t)


def layer_norm(P, C, lng, lng_t, lnb, lnb_t):
    pm = next_bank(C)
    fns = []
    for c in range(KC):
        fns.append(lambda e, c=c: e.matmul(pm[:], C.onesM[:], C.xf[:, c, :], start=(c == 0), stop=(c == KC - 1)))
    P.group("tensor", fns, reads=[C.onesM_t] + C.xf_t, writes=[pm])
    pv = next_bank(C)
    for c in range(KC):
        P.op("vector", lambda e, c=c: e.tensor_tensor(out=C.xf[:, c, :], in0=C.xf[:, c, :], in1=pm[:], op=ALU.subtract),
             reads=[pm], writes=[C.xf_t[c]])
        s = C.cnt["sq"] % 2
        C.cnt["sq"] += 1
        sq, sqt = C.sq[s], C.sq_t[s]
        P.op("scalar", lambda e, c=c, sq=sq: e.activation(out=sq[:], in_=C.xf[:, c, :], func=AF.Square), reads=[C.xf_t[c]], writes=[sqt])
        P.op("tensor", lambda e, c=c, sq=sq: e.matmul(pv[:], C.onesM[:], sq[:], start=(c == 0), stop=(c == KC - 1)),
             reads=[sqt, C.onesM_t], writes=[pv])
    P.op("scalar", lambda e: e.activation(out=C.rstd[:], in_=pv[:], func=AF.Sqrt, bias=LN_EPS), reads=[pv], writes=[C.rstd_t])
    P.op("vector", lambda e: e.reciprocal(out=C.rstd[:], in_=C.rstd[:]), reads=[C.rstd_t], writes=[C.rstd_t])
    for c in range(KC):
        P.op("vector", lambda e, c=c: e.tensor_tensor(out=C.xf[:, c, :], in0=C.xf[:, c, :], in1=C.rstd[:], op=ALU.mult),
             reads=[C.rstd_t], writes=[C.xf_t[c]])
        P.op("scalar", lambda e, c=c: e.activation(out=C.xf[:, c, :], in_=C.xf[:, c, :], func=AF.Identity, scale=lng[:, c:c + 1], bias=lnb[:, c:c + 1]),
             reads=[lng_t, lnb_t], writes=[C.xf_t[c]])
        P.op("gpsimd", lambda e, c=c: e.tensor_copy(out=C.xb[:, c, :], in_=C.xf[:, c, :]), reads=[C.xf_t[c]], writes=[C.xb_t[c]])


def load_x(P, C, xT, t0):
    xv = xT.rearrange("(c p) t -> p c t", p=128)
    for c in range(KC):
        P.dma("sync", C.xf[:, c, :], xv[:, c, t0:t0 + TT], writes=[C.xf_t[c]])
    for c in range(KC):
        P.op("gpsimd", lambda e, c=c: e.tensor_copy(out=C.xb[:, c, :], in_=C.xf[:, c, :]), reads=[C.xf_t[c]], writes=[C.xb_t[c]])


def store_x(P, C, oT, t0):
    ov = oT.rearrange("(c p) t -> p c t", p=128)
    evs = []
    for c in range(KC):
        evs.append(P.dma("sync", ov[:, c, t0:t0 + TT], C.xf[:, c, :], reads=[C.xf_t[c]]))
    return evs


def project(P, C, wp, col_tiles, outT, t0, wslots, evac_rr=[0]):
    wp_v = wp.rearrange("(c p) n -> p c n", p=128)
    evs = []
    for (c0, ncol, r0) in col_tiles:
        i = wslots["cnt"] % len(wslots["t"])
        wslots["cnt"] += 1
        wt, wtt = wslots["t"][i], wslots["tt"][i]
        P.dma("gpsimd", wt[:, :, 0:ncol], wp_v[:, :, c0:c0 + ncol], writes=[wtt])
        pp = next_bank(C)
        fns = []
        for k in range(KC):
            fns.append(lambda e, k=k, pp=pp, wt=wt, ncol=ncol: e.matmul(pp[0:ncol, :], wt[:, k, 0:ncol], C.xb[:, k, :], start=(k == 0), stop=(k == KC - 1)))
        P.group("tensor", fns, reads=[wtt] + C.xb_t, writes=[pp])
        j = wslots["ocnt"] % len(wslots["o"])
        wslots["ocnt"] += 1
        ot, ott = wslots["o"][j], wslots["ot"][j]
        eng = "scalar" if (j % 2 == 0) else "vector"
        if eng == "scalar":
            P.op("scalar", lambda e, ot=ot, pp=pp, ncol=ncol: e.copy(out=ot[0:ncol, :], in_=pp[0:ncol, :]), reads=[pp], writes=[ott])
        else:
            P.op("vector", lambda e, ot=ot, pp=pp, ncol=ncol: e.tensor_copy(out=ot[0:ncol, :], in_=pp[0:ncol, :]), reads=[pp], writes=[ott])
        evs.append(P.dma("sync", outT[r0:r0 + ncol, t0:t0 + TT], ot[0:ncol, :], reads=[ott]))
    return evs


def make_proj_slots(P, n=3, no=3):
    w = [P.sb(f"wp_{i}", [128, KC, 128], BF16) for i in range(n)]
    o = [P.sb(f"po_{i}", [128, TT], F32) for i in range(no)]
    return {"t": w, "tt": [T(x[:]) for x in w], "cnt": 0, "o": o, "ot": [T(x[:]) for x in o], "ocnt": 0}


def col_tiles_for(ranges):
    tiles = []
    r = 0
    for (s, n) in ranges:
        o = 0
        while o < n:
            m = min(128, n - o)
            tiles.append((s + o, m, r))
            r += m
            o += m
    return tiles, r


def build_p1(Ttot, proj_ranges):
    nc = bass.Bass("TRN2", target_bir_lowering=False)
    tiles, NP = col_tiles_for(proj_ranges)
    xT = nc.dram_tensor("xT", [D, Ttot], F32, kind="ExternalInput").ap()
    w1 = nc.dram_tensor("w1", [D, 2 * DFF], F32, kind="ExternalInput").ap()
    w2 = nc.dram_tensor("w2", [DFF, D], F32, kind="ExternalInput").ap()
    lng_d = nc.dram_tensor("lng", [D], F32, kind="ExternalInput").ap()
    lnb_d = nc.dram_tensor("lnb", [D], F32, kind="ExternalInput").ap()
    wp = nc.dram_tensor("wp", [D, NP], F32, kind="ExternalInput").ap()
    x1T = nc.dram_tensor("x1T", [D, Ttot], F32, kind="ExternalOutput").ap()
    pT = nc.dram_tensor("pT", [NP, Ttot], F32, kind="ExternalOutput").ap()
    with ExitStack() as es:
        P = Prog(nc, es)
        C = setup_common(P, nc)
        lng, lng_t = load_vec_cols(P, "lng_s", lng_d, KC)
        lnb, lnb_t = load_vec_cols(P, "lnb_s", lnb_d, KC)
        ws = make_proj_slots(P)
        finals = []
        for ti in range(Ttot // TT):
            t0 = ti * TT
            load_x(P, C, xT, t0)
            ffn_ln(P, C, w1, w2, lng, lng_t, lnb, lnb_t)
            finals += store_x(P, C, x1T, t0)
            finals += project(P, C, wp, tiles, pT, t0, ws)
        P.finish(finals)
    return nc, NP


S = 4096
NH = 8
HN = 512
NCOL = 1824
C0 = math.exp(-0.5)
GN_EPS = 64e-5


def rwkv_consts():
    ident = np.eye(128, dtype=np.float32)
    s = np.arange(128)
    same = (s[:, None] // 64) == (s[None, :] // 64)
    LT = np.where(same & (s[:, None] <= s[None, :]), -C0, 0.0).astype(np.float32)
    BT = np.where(same, -C0, 0.0).astype(np.float32)
    BTc = np.zeros((128, 2), np.float32)
    BTc[:64, 0] = -C0
    BTc[64:, 1] = -C0
    sl = s % 64
    t = np.arange(64)
    mk1 = np.concatenate([(sl[:, None] < t[None, :]), (sl[:, None] <= t[None, :])], 1).astype(np.float32)
    mk3 = (t[None, :] < sl[:, None]).astype(np.float32)
    i2 = (sl[:, None] == t[None, :]).astype(np.float32)
    return {"ident": ident, "LT": LT, "BT": BT, "BTc": BTc, "mk1": mk1, "mk3": mk3, "i2": i2}


def rwkv_phase(P, banks, Gp, prm, cst, msel_d, y):
    ntiles = S // 128
    DBG = False
    STAGE = 99
    SUB = 0
    mu = prm["mu"]; vecs = {n: prm[n] for n in ("w0", "a0", "k_k", "k_a", "r_k", "gn_g", "gn_b")}
    w2 = prm["w2"]; a2 = prm["a2"]; g2 = prm["g2"]
    if True:
        rr = [0]

        def nb():
            b = banks[rr[0]]
            rr[0] = (rr[0] + 1) % 8
            return b

        def mk(name, shape, dt=F32):
            t = P.sb(name, shape, dt)
            return t, T(t[:])

        msel, msel_t = mk("msel", [128, 2]); P.dma("sync", msel[:], msel_d, writes=[msel_t])
        PA = [mk(f"PA{i}", [128, 3360]) for i in range(2)]
        mu_bc, mu_t = mk("mu_bc", [128, NCOL])
        P.dma("sync", mu_bc[:], mu.partition_broadcast(128), writes=[mu_t])
        vb = {}
        for n, ap in vecs.items():
            vb[n] = mk(n + "_bc", [128, HN])
            P.dma("sync", vb[n][0][:], ap.partition_broadcast(128), writes=[vb[n][1]])
        w2s, w2t = mk("w2s", [64, HN]); P.dma("sync", w2s[:], w2, writes=[w2t])
        a2s, a2t = mk("a2s", [64, HN]); P.dma("sync", a2s[:], a2, writes=[a2t])
        g2a, g2at = mk("g2a", [128, HN]); P.dma("sync", g2a[:], g2[0:128, :], writes=[g2at])
        g2b, g2bt = mk("g2b", [32, HN]); P.dma("sync", g2b[:], g2[128:160, :], writes=[g2bt])
        cs = {}
        for n, ap in cst.items():
            cs[n] = mk(n + "_s", list(ap.shape))
            P.dma("sync", cs[n][0][:], ap, writes=[cs[n][1]])
        ident, ident_t = cs["ident"]
        LT, LT_t = cs["LT"]; BT, BT_t = cs["BT"]; BTc, BTc_t = cs["BTc"]
        mk1, mk1_t = cs["mk1"]; mk3, mk3_t = cs["mk3"]; i2, i2_t = cs["i2"]

        Pt = [mk(f"Pt{i}", [128, NCOL]) for i in range(2)]
        Pp = mk("Pp", [128, NCOL])
        names = ["tw", "sw", "ag", "g", "kk", "tmp", "tmp2", "inv", "kmod", "bvec", "cum", "epos", "eexc", "eneg", "eend",
                 "rt", "at", "kt", "bt", "kh", "bh", "X", "W", "U", "Y", "yn", "bon"]
        W_ = {n: mk(n, [128, HN]) for n in names}
        lorT = mk("lorT", [128, 4, 128])
        sgd = mk("sgd", [128, 160])
        st8 = {n: mk(n, [128, NH]) for n in ("n2", "mean", "var", "s8")}
        G1 = mk("G1", [64, NH, 128]); G2 = mk("G2", [64, NH, 128])
        Nf = [mk(f"Nf{i}", [64, NH, 64], BF16) for i in range(2)]
        Tf = [mk(f"Tf{i}", [64, NH, 64], BF16) for i in range(2)]
        Q = mk("Q", [64, NH, 64], BF16)
        Qf = mk("Qf", [64, NH, 64])
        SH = [mk(f"SH{i}", [64, HN]) for i in range(4)]
        CM = mk("CM", [64, NH, 2, 4, 64])
        Apt = mk("Apt", [64, NH, 64])
        pC = mk("pC", [64, NH, 2])
        Hs = [mk(f"H{i}", [64, NH, 64]) for i in range(2)]
        P.op("vector", lambda e: e.memset(Hs[0][0][:], 0.0), writes=[Hs[0][1]])
        hcur = [0]
        finals = []

        def v3(ap):
            return ap.rearrange("p (h n) -> p h n", h=NH)

        def bc8(ap8):
            return ap8.unsqueeze(2).to_broadcast([128, NH, 64])

        def do_chunk(c2, xs, xs_t, at_, at_t, bh, bh_t, kh, kh_t, shs, cm, cm_t):
            U, U_t = W_["U"]; Y, Y_t = W_["Y"]
            X, X_t = W_["X"]; Wt, Wt_t = W_["W"]
            g1, g1_t = G1; g2_, g2_t = G2
            q_, q_t = Q
            apt, apt_t = Apt
            if c2 == 0:
                Vc, Vc_t = xs[0:64, 1024:1536], xs_t
                Ac, Ac_t = at_[0:64, :], at_t
                Bc, Bc_t = bh[0:64, :], bh_t
                Kc, Kc_t = kh[0:64, :], kh_t
            else:
                Vc, Vc_t = shs[0][0][:], shs[0][1]
                Ac, Ac_t = shs[1][0][:], shs[1][1]
                Bc, Bc_t = shs[2][0][:], shs[2][1]
                Kc, Kc_t = shs[3][0][:], shs[3][1]
            hs = lambda h: slice(h * 64, (h + 1) * 64)
            nf0, nf0_t = Nf[0]; tf0, tf0_t = Tf[0]
            pg1 = [nb(), nb()]; pg2 = [nb(), nb()]; pg3 = nb()
            for (pgs, qi) in ((pg1, 0), (pg2, 1)):
                for half in range(2):
                    fns = []
                    for hh in range(4):
                        h = half * 4 + hh
                        fns.append(lambda e, h=h, hh=hh, qi=qi, bk=pgs[half]: e.matmul(bk[0:64, hh * 128:(hh + 1) * 128], cm[:, h, c2, qi, :], cm[:, h, c2, 2:4, :], start=True, stop=True))
                    P.group("tensor", fns, reads=[cm_t], writes=[pgs[half]])
            fns = []
            for h in range(NH):
                fns.append(lambda e, h=h: e.matmul(pg3[0:64, hs(h)], cm[:, h, c2, 2, :], cm[:, h, c2, 0, :], start=True, stop=True))
            P.group("tensor", fns, reads=[cm_t], writes=[pg3])
            mk1b = mk1[0:64, :].unsqueeze(1).to_broadcast([64, 4, 128])
            for half in range(2):
                P.op("vector", lambda e, half=half: e.tensor_tensor(out=g1[:, half * 4:(half + 1) * 4, :], in0=pg1[half][0:64, :].rearrange("p (h t) -> p h t", h=4), in1=mk1b, op=ALU.mult),
                     reads=[pg1[half], mk1_t], writes=[g1_t])
                P.op("vector", lambda e, half=half: e.tensor_tensor(out=g2_[:, half * 4:(half + 1) * 4, :], in0=pg2[half][0:64, :].rearrange("p (h t) -> p h t", h=4), in1=mk1b, op=ALU.mult),
                     reads=[pg2[half], mk1_t], writes=[g2_t])
            P.op("vector", lambda e: e.tensor_tensor(out=nf0[:], in0=pg3[0:64, :].rearrange("p (h t) -> p h t", h=NH), in1=mk3[0:64, :].unsqueeze(1).to_broadcast([64, NH, 64]), op=ALU.mult),
                 reads=[pg3, mk3_t], writes=[nf0_t])
            P.op("gpsimd", lambda e: e.tensor_copy(out=tf0[:], in_=g1[:, :, 0:64]), reads=[g1_t], writes=[tf0_t])
            P.op("gpsimd", lambda e: e.tensor_tensor(out=q_[:], in0=g1[:, :, 0:64], in1=i2[0:64, :].unsqueeze(1).to_broadcast([64, NH, 64]), op=ALU.add), reads=[g1_t, i2_t], writes=[q_t])
            cur = 0
            for lvl in range(5):
                nfc, nfc_t = Nf[cur]; tfc, tfc_t = Tf[cur]
                nfn, nfn_t = Nf[1 - cur]; tfn, tfn_t = Tf[1 - cur]
                last = (lvl == 4)
                pn = nb()
                fns = [(lambda e, h=h, pn=pn, tfc=tfc, nfc=nfc: e.matmul(pn[0:64, hs(h)], tfc[:, h, :], nfc[:, h, :], start=True, stop=True)) for h in range(NH)]
                P.group("tensor", fns, reads=[tfc_t, nfc_t], writes=[pn])
                if not last:
                    ptt = nb()
                    fns = [(lambda e, h=h, ptt=ptt, tfc=tfc, nfc=nfc: e.matmul(ptt[0:64, hs(h)], nfc[:, h, :], tfc[:, h, :], start=True, stop=True)) for h in range(NH)]
                    P.group("tensor", fns, reads=[tfc_t, nfc_t], writes=[ptt])
                P.op("scalar", lambda e, pn=pn, nfn=nfn: e.copy(out=nfn[:].rearrange("p h t -> p (h t)"), in_=pn[0:64, :]), reads=[pn], writes=[nfn_t])
                if not last:
                    P.op("vector", lambda e, ptt=ptt, tfn=tfn: e.tensor_copy(out=tfn[:].rearrange("p h t -> p (h t)"), in_=ptt[0:64, :]), reads=[ptt], writes=[tfn_t])
                pq = nb()
                fns = [(lambda e, h=h, pq=pq, nfn=nfn: e.matmul(pq[0:64, hs(h)], nfn[:, h, :], q_[:, h, :], start=True, stop=True)) for h in range(NH)]
                P.group("tensor", fns, reads=[nfn_t, q_t], writes=[pq])
                P.op("vector", lambda e, pq=pq: e.tensor_tensor(out=q_[:].rearrange("p h t -> p (h t)"), in0=q_[:].rearrange("p h t -> p (h t)"), in1=pq[0:64, :], op=ALU.add), reads=[pq], writes=[q_t])
                cur = 1 - cur
            qf_, qf_t = Qf
            P.op("gpsimd", lambda e: e.tensor_copy(out=qf_[:], in_=q_[:]), reads=[q_t], writes=[qf_t])
            px = nb()
            fns = [(lambda e, h=h, px=px: e.matmul(px[0:64, hs(h)], g2_[:, h, 0:64], Vc[:, hs(h)], start=True, stop=True)) for h in range(NH)]
            P.group("tensor", fns, reads=[g2_t, Vc_t], writes=[px])
            P.op("scalar", lambda e, px=px: e.copy(out=X[0:64, :], in_=px[0:64, :]), reads=[px], writes=[X_t])
            pw_ = nb()
            fns = [(lambda e, h=h, pw_=pw_: e.matmul(pw_[0:64, hs(h)], qf_[:, h, :], X[0:64, hs(h)], start=True, stop=True)) for h in range(NH)]
            P.group("tensor", fns, reads=[qf_t, X_t], writes=[pw_])
            P.op("scalar", lambda e, pw_=pw_: e.copy(out=Wt[0:64, :], in_=pw_[0:64, :]), reads=[pw_], writes=[Wt_t])
            pap = nb()
            fns = [(lambda e, h=h, pap=pap: e.matmul(pap[0:64, hs(h)], Ac[:, hs(h)], qf_[:, h, :], start=True, stop=True)) for h in range(NH)]
            P.group("tensor", fns, reads=[Ac_t, qf_t], writes=[pap])
            P.op("vector", lambda e, pap=pap: e.tensor_copy(out=apt[:].rearrange("p h t -> p (h t)"), in_=pap[0:64, :]), reads=[pap], writes=[apt_t])
            H0, H0_t = Hs[hcur[0]]
            H1, H1_t = Hs[1 - hcur[0]]
            pu = nb()
            fns = [(lambda e, h=h, pu=pu, H0=H0: e.matmul(pu[0:64, hs(h)], apt[:, h, :], H0[:, h, :], start=True, stop=True)) for h in range(NH)]
            P.group("tensor", fns, reads=[apt_t, H0_t], writes=[pu])
            P.op("vector", lambda e, pu=pu: e.tensor_tensor(out=U[0:64, :], in0=pu[0:64, :], in1=Wt[0:64, :], op=ALU.add), reads=[pu, Wt_t], writes=[U_t])
            ph = nb()
            fns = []
            for h in range(NH):
                fns.append(lambda e, h=h, ph=ph: e.matmul(ph[0:64, hs(h)], Bc[:, hs(h)], U[0:64, hs(h)], start=True, stop=False))
                fns.append(lambda e, h=h, ph=ph: e.matmul(ph[0:64, hs(h)], Kc[:, hs(h)], Vc[:, hs(h)], start=False, stop=True))
            P.group("tensor", fns, reads=[Bc_t, Kc_t, U_t, Vc_t], writes=[ph])
            py = nb()
            sl = slice(c2 * 64, (c2 + 1) * 64)
            fns = []
            for h in range(NH):
                fns.append(lambda e, h=h, py=py, H0=H0: e.matmul(py[sl, hs(h)], cm[:, h, c2, 3, :], H0[:, h, :], start=True, stop=False))
                fns.append(lambda e, h=h, py=py: e.matmul(py[sl, hs(h)], g1[:, h, 64:128], U[0:64, hs(h)], start=False, stop=False))
                fns.append(lambda e, h=h, py=py: e.matmul(py[sl, hs(h)], g2_[:, h, 64:128], Vc[:, hs(h)], start=False, stop=True))
            P.group("tensor", fns, reads=[cm_t, H0_t, g1_t, g2_t, U_t, Vc_t], writes=[py])
            P.op("gpsimd", lambda e, H0=H0, H1=H1: e.tensor_tensor(out=H1[:], in0=H0[:], in1=pC[0][:, :, c2:c2 + 1].to_broadcast([64, NH, 64]), op=ALU.mult),
                 reads=[H0_t, pC[1]], writes=[H1_t])
            P.op("vector", lambda e, H1=H1, ph=ph: e.tensor_tensor(out=H1[:].rearrange("p a i -> p (a i)"), in0=H1[:].rearrange("p a i -> p (a i)"), in1=ph[0:64, :], op=ALU.add),
                 reads=[ph], writes=[H1_t])
            P.op("scalar", lambda e, py=py: e.copy(out=Y[sl, :], in_=py[sl, :]), reads=[py], writes=[Y_t])
            hcur[0] = 1 - hcur[0]

        def do_tile(ti):
            t0 = ti * 128
            Pc, Pc_t = Pt[ti % 2]
            pa, pa_t = PA[0]
            pb_, pb_t = PA[1]
            rk_, lt_ = ti // 16, (ti % 16) * 128
            P.dma("sync", pa[:], Gp.rows(rk_, lt_, 128), writes=[pa_t])
            if ti == 0:
                P.op("vector", lambda e: e.memset(pb_[0:1, :], 0.0), writes=[pb_t])
            else:
                P.dma("sync", pb_[0:1, :], Gp.rows((ti - 1) // 16, ((ti - 1) % 16) * 128 + 127, 1), writes=[pb_t])
            P.dma("sync", pb_[1:128, :], Gp.rows(rk_, lt_, 127), writes=[pb_t])
            for (src, src_t, dst, dst_t) in ((pa, pa_t, Pc, Pc_t), (pb_, pb_t, Pp[0], Pp[1])):
                v4 = src[:, 0:3072].rearrange("p (j g n) -> p j g n", j=3, g=2)
                d3 = dst[:, 0:1536].rearrange("p (j n) -> p j n", j=3)
                P.op("vector", lambda e, v4=v4, d3=d3: e.tensor_scalar(out=d3, in0=v4[:, :, 0, :], scalar1=msel[:, 0:1], scalar2=None, op0=ALU.mult), reads=[src_t, msel_t], writes=[dst_t])
                P.op("vector", lambda e, v4=v4, d3=d3: e.scalar_tensor_tensor(out=d3, in0=v4[:, :, 1, :], scalar=msel[:, 1:2], in1=d3, op0=ALU.mult, op1=ALU.add), reads=[src_t, msel_t], writes=[dst_t])
                P.op("gpsimd", lambda e, src=src, dst=dst: e.tensor_copy(out=dst[:, 1536:1824], in_=src[:, 3072:3360]), reads=[src_t], writes=[dst_t])
            P.op("vector", lambda e, Pc=Pc: e.tensor_tensor(out=Pp[0][:], in0=Pp[0][:], in1=Pc[:], op=ALU.subtract), reads=[Pc_t], writes=[Pp[1]])
            P.op("gpsimd", lambda e: e.tensor_tensor(out=Pp[0][:], in0=Pp[0][:], in1=mu_bc[:], op=ALU.mult), reads=[mu_t], writes=[Pp[1]])
            P.op("vector", lambda e, Pc=Pc: e.tensor_tensor(out=Pp[0][:], in0=Pp[0][:], in1=Pc[:], op=ALU.add), reads=[Pc_t], writes=[Pp[1]])
            xs, xs_t = Pp
            r_ = xs[:, 0:512]; k_ = xs[:, 512:1024]; v_ = xs[:, 1024:1536]
            if STAGE == 1:
                finals.append(P.dma('sync', y[t0:t0 + 128, :], xs[:, 0:512], reads=[xs_t])); return
            tw, tw_t = W_["tw"]
            P.op("scalar", lambda e: e.activation(out=tw[:, 0:64], in_=xs[:, 1536:1600], func=AF.Tanh), reads=[xs_t], writes=[tw_t])
            P.op("scalar", lambda e: e.activation(out=sgd[0][:], in_=xs[:, 1664:1824], func=AF.Sigmoid), reads=[xs_t], writes=[sgd[1]])
            pb = nb()
            P.op("tensor", lambda e, pb=pb: e.transpose(pb[0:64, 0:128], tw[:, 0:64], ident[:]), reads=[tw_t, ident_t], writes=[pb])
            P.op("tensor", lambda e, pb=pb: e.transpose(pb[0:64, 128:256], xs[:, 1600:1664], ident[:]), reads=[xs_t, ident_t], writes=[pb])
            P.op("tensor", lambda e, pb=pb: e.transpose(pb[0:128, 256:384], sgd[0][:, 0:128], ident[:]), reads=[sgd[1], ident_t], writes=[pb])
            P.op("tensor", lambda e, pb=pb: e.transpose(pb[0:32, 384:512], sgd[0][:, 128:160], ident[:]), reads=[sgd[1], ident_t], writes=[pb])
            lT, lT_t = lorT
            P.op("vector", lambda e, pb=pb: e.tensor_copy(out=lT[0:64, 0:2, :], in_=pb[0:64, 0:256].rearrange("p (a b) -> p a b", a=2)), reads=[pb], writes=[lT_t])
            P.op("vector", lambda e, pb=pb: e.tensor_copy(out=lT[:, 2, :], in_=pb[:, 256:384]), reads=[pb], writes=[lT_t])
            P.op("vector", lambda e, pb=pb: e.tensor_copy(out=lT[0:32, 3, :], in_=pb[0:32, 384:512]), reads=[pb], writes=[lT_t])
            pw = nb(); pa = nb(); pg = nb()
            P.op("tensor", lambda e, pw=pw: e.matmul(pw[:], lT[0:64, 0, :], w2s[:], start=True, stop=True), reads=[lT_t, w2t], writes=[pw])
            P.op("tensor", lambda e, pa=pa: e.matmul(pa[:], lT[0:64, 1, :], a2s[:], start=True, stop=True), reads=[lT_t, a2t], writes=[pa])
            P.group("tensor", [lambda e, pg=pg: e.matmul(pg[:], lT[:, 2, :], g2a[:], start=True, stop=False),
                               lambda e, pg=pg: e.matmul(pg[:], lT[0:32, 3, :], g2b[:], start=False, stop=True)], reads=[lT_t, g2at, g2bt], writes=[pg])
            sw, sw_t = W_["sw"]; ag, ag_t = W_["ag"]; g_, g_t = W_["g"]
            P.op("vector", lambda e, pw=pw: e.tensor_tensor(out=sw[:], in0=pw[:], in1=vb["w0"][0][:], op=ALU.add), reads=[pw, vb["w0"][1]], writes=[sw_t])
            P.op("scalar", lambda e: e.activation(out=sw[:], in_=sw[:], func=AF.Sigmoid), reads=[sw_t], writes=[sw_t])
            P.op("vector", lambda e, pa=pa: e.tensor_tensor(out=ag[:], in0=pa[:], in1=vb["a0"][0][:], op=ALU.add), reads=[pa, vb["a0"][1]], writes=[ag_t])
            P.op("scalar", lambda e: e.activation(out=ag[:], in_=ag[:], func=AF.Sigmoid), reads=[ag_t], writes=[ag_t])
            P.op("scalar", lambda e, pg=pg: e.copy(out=g_[:], in_=pg[:]), reads=[pg], writes=[g_t])
            if STAGE == 2:
                finals.append(P.dma('sync', y[t0:t0 + 128, :], xs[:, 0:512], reads=[xs_t])); return
            kk, kk_t = W_["kk"]; tmp, tmp_t = W_["tmp"]; tmp2, tmp2_t = W_["tmp2"]
            kmod, kmod_t = W_["kmod"]; bvec, bvec_t = W_["bvec"]
            n2, n2_t = st8["n2"]
            P.op("vector", lambda e: e.tensor_tensor(out=kk[:], in0=k_, in1=vb["k_k"][0][:], op=ALU.mult), reads=[xs_t, vb["k_k"][1]], writes=[kk_t])
            P.op("gpsimd", lambda e: e.tensor_tensor(out=tmp[:], in0=kk[:], in1=kk[:], op=ALU.mult), reads=[kk_t], writes=[tmp_t])
            P.op("vector", lambda e: e.tensor_reduce(out=n2[:], in_=v3(tmp[:]), axis=AX.X, op=ALU.add), reads=[tmp_t], writes=[n2_t])
            P.op("scalar", lambda e: e.activation(out=n2[:], in_=n2[:], func=AF.Sqrt), reads=[n2_t], writes=[n2_t])
            P.op("vector", lambda e: e.tensor_scalar(out=n2[:], in0=n2[:], scalar1=1e-12, scalar2=None, op0=ALU.max), reads=[n2_t], writes=[n2_t])
            P.op("vector", lambda e: e.reciprocal(out=n2[:], in_=n2[:]), reads=[n2_t], writes=[n2_t])
            P.op("vector", lambda e: e.tensor_tensor(out=v3(kk[:]), in0=v3(kk[:]), in1=bc8(n2[:]), op=ALU.mult), reads=[n2_t], writes=[kk_t])
            P.op("vector", lambda e: e.scalar_tensor_tensor(out=tmp[:], in0=ag[:], scalar=-1.0, in1=vb["k_a"][0][:], op0=ALU.add, op1=ALU.mult), reads=[ag_t, vb["k_a"][1]], writes=[tmp_t])
            P.op("vector", lambda e: e.scalar_tensor_tensor(out=kmod[:], in0=tmp[:], scalar=1.0, in1=k_, op0=ALU.add, op1=ALU.mult), reads=[tmp_t, xs_t], writes=[kmod_t])
            P.op("gpsimd", lambda e: e.tensor_tensor(out=bvec[:], in0=kk[:], in1=ag[:], op=ALU.mult), reads=[kk_t, ag_t], writes=[bvec_t])
            if STAGE == 3:
                finals.append(P.dma('sync', y[t0:t0 + 128, :], xs[:, 0:512], reads=[xs_t])); return
            pc = nb(); pt = nb()
            P.op("tensor", lambda e, pc=pc: e.matmul(pc[:], LT[:], sw[:], start=True, stop=True), reads=[LT_t, sw_t], writes=[pc])
            P.op("tensor", lambda e, pt=pt: e.matmul(pt[:], BT[:], sw[:], start=True, stop=True), reads=[BT_t, sw_t], writes=[pt])
            ppc = nb()
            fns = [(lambda e, h=h, ppc=ppc: e.matmul(ppc[0:64, h * 2:h * 2 + 2], sw[:, h * 64:(h + 1) * 64], BTc[:], start=True, stop=True)) for h in range(NH)]
            P.group("tensor", fns, reads=[sw_t, BTc_t], writes=[ppc])
            P.op("scalar", lambda e, ppc=ppc: e.activation(out=pC[0][:].rearrange("p a b -> p (a b)"), in_=ppc[0:64, 0:16], func=AF.Exp), reads=[ppc], writes=[pC[1]])
            cum, cum_t = W_["cum"]
            epos, epos_t = W_["epos"]; eexc, eexc_t = W_["eexc"]; eneg, eneg_t = W_["eneg"]; eend, eend_t = W_["eend"]
            P.op("scalar", lambda e, pc=pc: e.copy(out=cum[:], in_=pc[:]), reads=[pc], writes=[cum_t])
            P.op("scalar", lambda e, pc=pc: e.activation(out=epos[:], in_=pc[:], func=AF.Exp), reads=[pc], writes=[epos_t])
            P.op("scalar", lambda e, pc=pc: e.activation(out=eneg[:], in_=pc[:], func=AF.Exp, scale=-1.0), reads=[pc], writes=[eneg_t])
            P.op("vector", lambda e: e.scalar_tensor_tensor(out=eexc[:], in0=sw[:], scalar=C0, in1=cum[:], op0=ALU.mult, op1=ALU.add), reads=[sw_t, cum_t], writes=[eexc_t])
            P.op("scalar", lambda e: e.activation(out=eexc[:], in_=eexc[:], func=AF.Exp), reads=[eexc_t], writes=[eexc_t])
            P.op("vector", lambda e, pt=pt: e.tensor_tensor(out=eend[:], in0=pt[:], in1=cum[:], op=ALU.subtract), reads=[pt, cum_t], writes=[eend_t])
            P.op("scalar", lambda e: e.activation(out=eend[:], in_=eend[:], func=AF.Exp), reads=[eend_t], writes=[eend_t])
            if STAGE == 4:
                finals.append(P.dma('sync', y[t0:t0 + 128, :], xs[:, 0:512], reads=[xs_t])); return
            rt, rt_t = W_["rt"]; at_, at_t = W_["at"]; kt, kt_t = W_["kt"]; bt, bt_t = W_["bt"]; kh, kh_t = W_["kh"]; bh, bh_t = W_["bh"]
            P.op("vector", lambda e: e.tensor_tensor(out=rt[:], in0=r_, in1=epos[:], op=ALU.mult), reads=[xs_t, epos_t], writes=[rt_t])
            P.op("vector", lambda e: e.scalar_tensor_tensor(out=at_[:], in0=kk[:], scalar=-1.0, in1=eexc[:], op0=ALU.mult, op1=ALU.mult), reads=[kk_t, eexc_t], writes=[at_t])
            P.op("vector", lambda e: e.tensor_tensor(out=kt[:], in0=kmod[:], in1=eneg[:], op=ALU.mult), reads=[kmod_t, eneg_t], writes=[kt_t])
            P.op("gpsimd", lambda e: e.tensor_tensor(out=bt[:], in0=bvec[:], in1=eneg[:], op=ALU.mult), reads=[bvec_t, eneg_t], writes=[bt_t])
            P.op("vector", lambda e: e.tensor_tensor(out=kh[:], in0=kmod[:], in1=eend[:], op=ALU.mult), reads=[kmod_t, eend_t], writes=[kh_t])
            P.op("gpsimd", lambda e: e.tensor_tensor(out=bh[:], in0=bvec[:], in1=eend[:], op=ALU.mult), reads=[bvec_t, eend_t], writes=[bh_t])

            cm, cm_t = CM
            srcs = [(bt, bt_t), (kt, kt_t), (at_, at_t), (rt, rt_t)]
            for h in range(NH):
                pb = nb()
                for q, (src, src_t) in enumerate(srcs):
                    P.op("tensor", lambda e, pb=pb, q=q, src=src, h=h: e.transpose(pb[0:64, q * 128:(q + 1) * 128], src[:, h * 64:(h + 1) * 64], ident[:]),
                         reads=[src_t, ident_t], writes=[pb])
                outap = cm[:, h, :, :, :].rearrange("p c q j -> p q c j")
                inap = pb[0:64, :].rearrange("p (q c j) -> p q c j", q=4, c=2)
                if h % 2 == 0:
                    P.op("vector", lambda e, outap=outap, inap=inap: e.tensor_copy(out=outap, in_=inap), reads=[pb], writes=[cm_t])
                else:
                    P.op("scalar", lambda e, outap=outap, inap=inap: e.copy(out=outap, in_=inap), reads=[pb], writes=[cm_t])
            if STAGE == 5:
                finals.append(P.dma('sync', y[t0:t0 + 128, :], xs[:, 0:512], reads=[xs_t])); return
            shs = []
            for qi, (src_ap, src_t) in enumerate(((xs[64:128, 1024:1536], xs_t), (at_[64:128, :], at_t), (bh[64:128, :], bh_t), (kh[64:128, :], kh_t))):
                d_, d_t = SH[qi]
                P.dma("sync", d_[:], src_ap, reads=[src_t], writes=[d_t])
                shs.append((d_, d_t))
            U, U_t = W_["U"]; Y, Y_t = W_["Y"]
            X, X_t = W_["X"]; Wt, Wt_t = W_["W"]
            g1, g1_t = G1; g2_, g2_t = G2
            q_, q_t = Q
            apt, apt_t = Apt
            for c2 in range(2):
                do_chunk(c2, xs, xs_t, at_, at_t, bh, bh_t, kh, kh_t, shs, cm, cm_t)

            mean, mean_t = st8["mean"]; var, var_t = st8["var"]; s8, s8_t = st8["s8"]
            yn, yn_t = W_["yn"]; bon, bon_t = W_["bon"]
            P.op("vector", lambda e: e.tensor_reduce(out=mean[:], in_=v3(Y[:]), axis=AX.X, op=ALU.add), reads=[Y_t], writes=[mean_t])
            P.op("vector", lambda e: e.tensor_scalar(out=mean[:], in0=mean[:], scalar1=1.0 / 64, scalar2=None, op0=ALU.mult), reads=[mean_t], writes=[mean_t])
            P.op("vector", lambda e: e.tensor_tensor(out=v3(yn[:]), in0=v3(Y[:]), in1=bc8(mean[:]), op=ALU.subtract), reads=[Y_t, mean_t], writes=[yn_t])
            P.op("gpsimd", lambda e: e.tensor_tensor(out=tmp[:], in0=yn[:], in1=yn[:], op=ALU.mult), reads=[yn_t], writes=[tmp_t])
            P.op("vector", lambda e: e.tensor_reduce(out=var[:], in_=v3(tmp[:]), axis=AX.X, op=ALU.add), reads=[tmp_t], writes=[var_t])
            P.op("scalar", lambda e: e.activation(out=var[:], in_=var[:], func=AF.Sqrt, bias=GN_EPS, scale=1.0 / 64), reads=[var_t], writes=[var_t])
            P.op("vector", lambda e: e.reciprocal(out=var[:], in_=var[:]), reads=[var_t], writes=[var_t])
            P.op("vector", lambda e: e.tensor_tensor(out=v3(yn[:]), in0=v3(yn[:]), in1=bc8(var[:]), op=ALU.mult), reads=[var_t], writes=[yn_t])
            P.op("gpsimd", lambda e: e.tensor_tensor(out=yn[:], in0=yn[:], in1=vb["gn_g"][0][:], op=ALU.mult), reads=[vb["gn_g"][1]], writes=[yn_t])
            P.op("vector", lambda e: e.tensor_tensor(out=yn[:], in0=yn[:], in1=vb["gn_b"][0][:], op=ALU.add), reads=[vb["gn_b"][1]], writes=[yn_t])
            P.op("gpsimd", lambda e: e.tensor_tensor(out=tmp2[:], in0=r_, in1=kmod[:], op=ALU.mult), reads=[xs_t, kmod_t], writes=[tmp2_t])
            P.op("gpsimd", lambda e: e.tensor_tensor(out=tmp2[:], in0=tmp2[:], in1=vb["r_k"][0][:], op=ALU.mult), reads=[vb["r_k"][1]], writes=[tmp2_t])
            P.op("vector", lambda e: e.tensor_reduce(out=s8[:], in_=v3(tmp2[:]), axis=AX.X, op=ALU.add), reads=[tmp2_t], writes=[s8_t])
            P.op("vector", lambda e: e.tensor_tensor(out=v3(bon[:]), in0=v3(v_), in1=bc8(s8[:]), op=ALU.mult), reads=[xs_t, s8_t], writes=[bon_t])
            P.op("vector", lambda e: e.tensor_tensor(out=yn[:], in0=yn[:], in1=bon[:], op=ALU.add), reads=[bon_t], writes=[yn_t])
            P.op("vector", lambda e: e.tensor_tensor(out=yn[:], in0=yn[:], in1=g_[:], op=ALU.mult), reads=[g_t], writes=[yn_t])
            finals.append(P.dma("sync", y[t0:t0 + 128, :], yn[:], reads=[yn_t]))
            if DBG:
                Hc, Hc_t = Hs[hcur[0]]
                finals.append(P.dma("sync", dbg[ti], Hc[:].rearrange("p a i -> p (a i)"), reads=[Hc_t]))
        for ti in range(ntiles):
            do_tile(ti)


S = 4096
NSLOT = 16
NQ = NSLOT * 128
AH = 8
IH = 16
TOPK = 256
NEG = -30000.0


def np_rel_bucket(dist):
    max_exact = 16
    d_f = np.maximum(dist, 1).astype(np.float32)
    large = max_exact + (np.log(d_f / max_exact) / np.float32(math.log(128 / max_exact)) * (32 - max_exact)).astype(np.int32)
    large = np.minimum(large, 31)
    return np.where(dist < max_exact, dist, large)


def dsa_consts(e, rel_bias):
    q = np.arange(128)
    tri = (q[None, :] <= q[:, None]).astype(np.float32)
    cm = np.zeros((128, 256), np.float32)
    if e == 0:
        cm[:, 0:128] = tri
    else:
        cm[:, 0:128] = 1.0
        cm[:, 128:256] = tri
    nbig = ((cm - 1.0) * 1e30).astype(np.float32)
    NB = np.zeros((128, 3, AH, 128), np.float32)
    for r in range(3):
        delta = e + 1 - r
        dist = np.maximum(delta * 128 + q[None, :] - q[:, None], 0)
        b = np_rel_bucket(dist.astype(np.int32))
        NB[:, r, :, :] = np.transpose(rel_bias[b], (0, 2, 1))
    cfar = np.broadcast_to(rel_bias[31][None, :], (128, AH)).astype(np.float32).copy()
    return {"cm": cm, "nbig": nbig, "NB": NB, "cfar": cfar, "ident": np.eye(128, dtype=np.float32)}


def dsa_phase(P, banks, G, lng_d, lnb_d, cst, msel_d, y, nslot=NSLOT):
    cm_d, nbig_d, NB_d, cfar_d, ident_d = cst["cm"], cst["nbig"], cst["NB"], cst["cfar"], cst["ident"]
    if True:
        rr = [0]

        def nb():
            b = banks[rr[0]]
            rr[0] = (rr[0] + 1) % 4
            return b
        acc = [banks[6], banks[7]]
        pscb = [banks[4], banks[5]]
        pscn = [0]

        def mk(name, shape, dt=F32):
            t = P.sb(name, shape, dt)
            return t, T(t[:])

        msel, msel_t = mk("msel3", [128, 2]); P.dma("sync", msel[:], msel_d, writes=[msel_t])
        ident, ident_t = mk("ident_s", [128, 128]); P.dma("sync", ident[:], ident_d, writes=[ident_t])
        identb, identb_t = mk("identb", [128, 128], BF16)
        P.op("vector", lambda e: e.tensor_copy(out=identb[:], in_=ident[:]), reads=[ident_t], writes=[identb_t])
        cm, cm_t = mk("cm_s", [128, 256]); P.dma("sync", cm[:], cm_d, writes=[cm_t])
        nbig, nbig_t = mk("nbig_s", [128, 256]); P.dma("sync", nbig[:], nbig_d, writes=[nbig_t])
        cfar, cfar_t = mk("cfar_s", [128, AH]); P.dma("sync", cfar[:], cfar_d, writes=[cfar_t])
        NBf, NBf_t = mk("NBf", [128, 3, AH, 128]); P.dma("sync", NBf[:], NB_d, writes=[NBf_t])
        NBb, NBb_t = mk("NBb", [128, 3, AH, 128], BF16)
        for r in range(3):
            for h in range(AH):
                P.op("vector", lambda e, r=r, h=h: e.tensor_scalar(out=NBb[:, r, h, :], in0=NBf[:, r, h, :], scalar1=cfar[:, h:h + 1], scalar2=None, op0=ALU.subtract),
                     reads=[NBf_t, cfar_t], writes=[NBb_t])
        lng, lng_t = mk("lng_bc", [128, 64]); P.dma("sync", lng[:], lng_d.partition_broadcast(128), writes=[lng_t])
        lnb, lnb_t = mk("lnb_bc", [128, 64]); P.dma("sync", lnb[:], lnb_d.partition_broadcast(128), writes=[lnb_t])

        kT, kT_t = mk("kT_s", [128, 4, S], BF16)
        for r_ in range(2):
            for hp in range(4):
                P.dma("gpsimd", kT[:, hp, r_ * 2048:(r_ + 1) * 2048], G["kT"].rows(r_, hp * 128, 128), writes=[kT_t])
        V1, V1_t = mk("V1", [128, 32, AH, 65], BF16)
        P.op("vector", lambda e: e.memset(V1[:, :, :, 64:65], 1.0), writes=[V1_t])
        for kb in range(32):
            P.dma("gpsimd", V1[:, kb, :, 0:64], G["V"].rows(kb // 16, (kb % 16) * 128, 128).rearrange("p (h d) -> p h d", h=AH), writes=[V1_t])
        ki, ki_t = mk("ki", [128, 32, 64])
        for r_ in range(2):
            P.dma("sync", ki[:, r_ * 16:(r_ + 1) * 16, :], G["kw"].rows(r_, 0, 2048)[:, 0:64].rearrange("(kb p) d -> p kb d", p=128), writes=[ki_t])
        st, st_t = mk("kist", [128, 32]); ksq, ksq_t = mk("kisq", [128, 32, 64])
        bc32 = lambda ap: ap.unsqueeze(2).to_broadcast([128, 32, 64])
        P.op("vector", lambda e: e.tensor_reduce(out=st[:], in_=ki[:], axis=AX.X, op=ALU.add), reads=[ki_t], writes=[st_t])
        P.op("vector", lambda e: e.tensor_scalar(out=st[:], in0=st[:], scalar1=1.0 / 64, scalar2=None, op0=ALU.mult), reads=[st_t], writes=[st_t])
        P.op("vector", lambda e: e.tensor_tensor(out=ki[:], in0=ki[:], in1=bc32(st[:]), op=ALU.subtract), reads=[st_t], writes=[ki_t])
        P.op("vector", lambda e: e.tensor_tensor(out=ksq[:], in0=ki[:], in1=ki[:], op=ALU.mult), reads=[ki_t], writes=[ksq_t])
        P.op("vector", lambda e: e.tensor_reduce(out=st[:], in_=ksq[:], axis=AX.X, op=ALU.add), reads=[ksq_t], writes=[st_t])
        P.op("scalar", lambda e: e.activation(out=st[:], in_=st[:], func=AF.Sqrt, bias=1e-5, scale=1.0 / 64), reads=[st_t], writes=[st_t])
        P.op("vector", lambda e: e.reciprocal(out=st[:], in_=st[:]), reads=[st_t], writes=[st_t])
        P.op("vector", lambda e: e.tensor_tensor(out=ki[:], in0=ki[:], in1=bc32(st[:]), op=ALU.mult), reads=[st_t], writes=[ki_t])
        P.op("vector", lambda e: e.tensor_tensor(out=ki[:], in0=ki[:], in1=lng[:].unsqueeze(1).to_broadcast([128, 32, 64]), op=ALU.mult), reads=[lng_t], writes=[ki_t])
        P.op("vector", lambda e: e.tensor_tensor(out=ki[:], in0=ki[:], in1=lnb[:].unsqueeze(1).to_broadcast([128, 32, 64]), op=ALU.add), reads=[lnb_t], writes=[ki_t])
        kiT, kiT_t = mk("kiT", [64, S])
        for g in range(8):
            pb = nb()
            for j in range(4):
                kb = g * 4 + j
                P.op("tensor", lambda e, pb=pb, j=j, kb=kb: e.transpose(pb[0:64, j * 128:(j + 1) * 128], ki[:, kb, :], ident[:]), reads=[ki_t, ident_t], writes=[pb])
            P.op("scalar", lambda e, pb=pb, g=g: e.copy(out=kiT[:, g * 512:(g + 1) * 512], in_=pb[0:64, :]), reads=[pb], writes=[kiT_t])

        qf, qf_t = mk("qf", [128, 4, 128])
        qf2, qf2_t = mk("qf2", [128, 4, 256])
        qi2, qi2_t = mk("qi2", [64, IH, 256])
        wq2, wq2_t = mk("wq2", [128, 2, IH])
        qTz, qTz_t = mk("qTz", [128, AH, 128], BF16)
        P.op("vector", lambda e: e.memset(qTz[:], 0.0), writes=[qTz_t])
        qiT, qiT_t = mk("qiT", [64, IH, 128])
        wq, wq_t = mk("wq", [128, IH])
        diag = [mk(f"diag{i}", [128, 128]) for i in range(2)]
        stage = [mk(f"stage{i}", [128, 512]) for i in range(3)]
        score, score_t = mk("score", [128, S])
        work, work_t = mk("work", [128, S])
        madd, madd_t = mk("madd", [128, S], BF16)
        mx, mx_t = mk("mx", [128, 8])
        thr, thr_t = mk("thr", [128, 1])
        PT = [mk(f"PT{i}", [128, 4, 128], BF16) for i in range(3)]
        rec, rec_t = mk("rec", [128, AH])
        yo = [mk(f"yo{i}", [128, 512]) for i in range(2)]
        cnt = {"diag": 0, "stage": 0, "PT": 0}
        finals = []

        def do_slot(i):
            q0 = i * 128
            nkb = 2 * i + 2
            nkeys = nkb * 128
            r_ = i // 8
            lt0 = (i - 8 * r_) * 256
            for hp in range(4):
                P.dma("sync", qf2[:, hp, :], G["qT"].rows(r_, hp * 128, 128)[:, lt0:lt0 + 256], writes=[qf2_t])
            for hq in range(4):
                P.dma("sync", qi2[:, hq * 4:(hq + 1) * 4, :], G["qiT"].rows(r_, hq * 256, 256)[:, lt0:lt0 + 256].rearrange("(h d) q -> d h q", d=64), writes=[qi2_t])
            P.dma("sync", wq2[:], G["kw"].rows(r_, lt0, 256)[:, 64:80].rearrange("(c p) n -> p c n", p=128), writes=[wq2_t])
            P.op("vector", lambda e: e.tensor_scalar(out=qf[:], in0=qf2[:, :, 0:128], scalar1=msel[:, 0:1], scalar2=None, op0=ALU.mult), reads=[qf2_t, msel_t], writes=[qf_t])
            P.op("vector", lambda e: e.scalar_tensor_tensor(out=qf[:], in0=qf2[:, :, 128:256], scalar=msel[:, 1:2], in1=qf[:], op0=ALU.mult, op1=ALU.add), reads=[qf2_t, msel_t], writes=[qf_t])
            P.op("vector", lambda e: e.tensor_scalar(out=qiT[:], in0=qi2[:, :, 0:128], scalar1=msel[0:64, 0:1], scalar2=None, op0=ALU.mult), reads=[qi2_t, msel_t], writes=[qiT_t])
            P.op("vector", lambda e: e.scalar_tensor_tensor(out=qiT[:], in0=qi2[:, :, 128:256], scalar=msel[0:64, 1:2], in1=qiT[:], op0=ALU.mult, op1=ALU.add), reads=[qi2_t, msel_t], writes=[qiT_t])
            P.op("vector", lambda e: e.tensor_scalar(out=wq[:], in0=wq2[:, 0, :], scalar1=msel[:, 0:1], scalar2=None, op0=ALU.mult), reads=[wq2_t, msel_t], writes=[wq_t])
            P.op("vector", lambda e: e.scalar_tensor_tensor(out=wq[:], in0=wq2[:, 1, :], scalar=msel[:, 1:2], in1=wq[:], op0=ALU.mult, op1=ALU.add), reads=[wq2_t, msel_t], writes=[wq_t])
            qz = qTz[:].rearrange("p (hp e) q -> p hp e q", e=2)
            P.op("scalar", lambda e: e.mul(out=qz[0:64, :, 0, :], in_=qf[0:64, :, :], mul=0.125), reads=[qf_t], writes=[qTz_t])
            P.op("scalar", lambda e: e.mul(out=qz[64:128, :, 1, :], in_=qf[64:128, :, :], mul=0.125), reads=[qf_t], writes=[qTz_t])
            for g0 in range(0, nkeys, 512):
                n = min(512, nkeys - g0)
                psc = pscb[pscn[0] % 2]; pscn[0] += 1
                for h in range(IH):
                    pd = nb()
                    P.op("tensor", lambda e, pd=pd, h=h, g0=g0, n=n: e.matmul(pd[:, 0:n], qiT[:, h, :], kiT[:, g0:g0 + n], start=True, stop=True), reads=[qiT_t, kiT_t], writes=[pd])
                    sg, sg_t = stage[cnt["stage"] % 3]; cnt["stage"] += 1
                    P.op("scalar", lambda e, pd=pd, sg=sg, n=n: e.activation(out=sg[:, 0:n], in_=pd[:, 0:n], func=AF.Relu), reads=[pd], writes=[sg_t])
                    dg, dg_t = diag[cnt["diag"] % 2]; cnt["diag"] += 1
                    P.op("vector", lambda e, dg=dg, h=h: e.tensor_scalar(out=dg[:], in0=ident[:], scalar1=wq[:, h:h + 1], scalar2=None, op0=ALU.mult), reads=[ident_t, wq_t], writes=[dg_t])
                    P.op("tensor", lambda e, psc=psc, dg=dg, sg=sg, h=h, n=n: e.matmul(psc[:, 0:n], dg[:], sg[:, 0:n], start=(h == 0), stop=(h == IH - 1)), reads=[dg_t, sg_t], writes=[psc])
                P.op("vector", lambda e, psc=psc, g0=g0, n=n: e.tensor_copy(out=score[:, g0:g0 + n], in_=psc[:, 0:n]), reads=[psc], writes=[score_t])
            l0 = nkeys - 256
            P.op("vector", lambda e: e.tensor_tensor(out=score[:, l0:nkeys], in0=score[:, l0:nkeys], in1=cm[:], op=ALU.mult), reads=[cm_t], writes=[score_t])
            P.op("vector", lambda e: e.tensor_tensor(out=score[:, l0:nkeys], in0=score[:, l0:nkeys], in1=nbig[:], op=ALU.add), reads=[nbig_t], writes=[score_t])
            if i == 0:
                P.op("vector", lambda e: e.memset(thr[:], -1e29), writes=[thr_t])
            else:
                P.op("gpsimd", lambda e: e.tensor_copy(out=work[:, 0:nkeys], in_=score[:, 0:nkeys]), reads=[score_t], writes=[work_t])
                for rnd in range(TOPK // 8):
                    P.op("vector", lambda e: e.max(out=mx[:], in_=work[:, 0:nkeys]), reads=[work_t], writes=[mx_t])
                    if rnd < TOPK // 8 - 1:
                        P.op("vector", lambda e: e.match_replace(out=work[:, 0:nkeys], in_to_replace=mx[:], in_values=work[:, 0:nkeys], imm_value=-1e30), reads=[mx_t], writes=[work_t])
                P.op("vector", lambda e: e.tensor_copy(out=thr[:], in_=mx[:, 7:8]), reads=[mx_t], writes=[thr_t])
            P.op("vector", lambda e: e.tensor_scalar(out=madd[:, 0:nkeys], in0=score[:, 0:nkeys], scalar1=thr[:, 0:1], scalar2=NEG, op0=ALU.is_lt, op1=ALU.mult),
                 reads=[score_t, thr_t], writes=[madd_t])
            for kb in range(nkb):
                r = kb - (nkb - 3)
                ks = slice(kb * 128, (kb + 1) * 128)
                for half in range(2):
                    pl = nb()
                    fns = []
                    for hh in range(4):
                        h = half * 4 + hh
                        hp = h // 2
                        cs = slice(hh * 128, (hh + 1) * 128)
                        near = (r >= 0)
                        fns.append(lambda e, pl=pl, cs=cs, hp=hp, h=h, ks=ks: e.matmul(pl[:, cs], kT[:, hp, ks], qTz[:, h, :], start=True, stop=False))
                        fns.append(lambda e, pl=pl, cs=cs, ks=ks, near=near: e.matmul(pl[:, cs], madd[:, ks], identb[:], start=False, stop=(not near)))
                        if near:
                            fns.append(lambda e, pl=pl, cs=cs, r=r, h=h: e.matmul(pl[:, cs], identb[:], NBb[:, r, h, :], start=False, stop=True))
                    P.group("tensor", fns, reads=[kT_t, qTz_t, madd_t, identb_t, NBb_t], writes=[pl])
                    pt_, pt_t = PT[cnt["PT"] % 3]; cnt["PT"] += 1
                    P.op("scalar", lambda e, pl=pl, pt_=pt_: e.activation(out=pt_[:].rearrange("p h q -> p (h q)"), in_=pl[:], func=AF.Exp), reads=[pl], writes=[pt_t])
                    fns = []
                    for hh in range(4):
                        h = half * 4 + hh
                        fns.append(lambda e, hh=hh, h=h, pt_=pt_, kb=kb, half=half: e.matmul(acc[half][:, hh * 65:(hh + 1) * 65], pt_[:, hh, :], V1[:, kb, h, :], start=(kb == 0 and hh == 0), stop=(kb == nkb - 1 and hh == 3), skip_group_check=True))
                    P.group("tensor", fns, reads=[pt_t, V1_t], writes=[acc[half]])
            yo_, yo_t = yo[i % 2]
            for half in range(2):
                a3 = acc[half][:, 0:260].rearrange("p (h d) -> p h d", h=4)
                P.op("vector", lambda e, a3=a3, half=half: e.reciprocal(out=rec[:, half * 4:(half + 1) * 4], in_=a3[:, :, 64]), reads=[acc[half]], writes=[rec_t])
                P.op("vector", lambda e, a3=a3, half=half, yo_=yo_: e.tensor_tensor(out=yo_[:, half * 256:(half + 1) * 256].rearrange("p (h d) -> p h d", h=4), in0=a3[:, :, 0:64],
                                                                                in1=rec[:, half * 4:(half + 1) * 4].unsqueeze(2).to_broadcast([128, 4, 64]), op=ALU.mult),
                     reads=[acc[half], rec_t], writes=[yo_t])
            P.dma("sync", y[q0:q0 + 128, :], yo_[:], reads=[yo_t])

        for i in range(nslot):
            do_slot(i)


RW = 1024
SGD = 512
ATD = 512
SG_OFF = 0
GATE_OFF = 1024


def sg_consts():
    j = np.arange(128)
    return {"sgmask": (j[:, None] <= j[None, :]).astype(np.float32),
            "identf": np.eye(128, dtype=np.float32)}


def wslot(P, C):
    s = C.cnt["w1"] % 2
    C.cnt["w1"] += 1
    return C.w1[s], C.w1_t[s]


def mixer(P, C, M, t0, load_y_branches=None):
    win_v = M.w_in.rearrange("(c p) n -> p c n", p=128)
    wbr_v = M.w_branch
    hb = C.h
    merged = lambda c: hb[:, c, :]
    yrw = lambda c: hb[:, 16 + c, :]
    yat = lambda c: hb[:, 24 + c, :]
    uT = lambda c: hb[:, 28 + c, :]
    ysg = lambda c: hb[:, 32 + c, :]
    ht = C.h_t
    def load_cast(src, nchunks, base):
        sv = src.rearrange("(c p) t -> p c t", p=128)
        for c in range(nchunks):
            s = C.cnt["sq"] % 2
            C.cnt["sq"] += 1
            sq, sqt = C.sq[s], C.sq_t[s]
            P.dma("sync", sq[:], sv[:, c, t0:t0 + TT], writes=[sqt])
            P.op("gpsimd", lambda e, sq=sq, c=c: e.tensor_copy(out=hb[:, base + c, :], in_=sq[:]), reads=[sqt], writes=[ht[base + c]])
    if getattr(M, "fused", False):
        load_y_branches(P, C, M, t0)
        sg_off, gate_off = 3360, 7024
    else:
        load_cast(M.yrwT, 8, 16)
        load_cast(M.yatT, 4, 24)
        sg_off, gate_off = SG_OFF, GATE_OFF
    wt, wtt = wslot(P, C)
    wload(P, C, wt, wtt, [(wt[:], win_v[:, :, sg_off:sg_off + 512])], f"{C.ph}_sgu", KC * 512)
    for c in range(4):
        pu = next_bank(C)
        fns = [(lambda e, k=k, pu=pu, wt=wt, c=c: e.matmul(pu[:], wt[:, k, c * 128:(c + 1) * 128], C.xb[:, k, :], start=(k == 0), stop=(k == KC - 1))) for k in range(KC)]
        P.group("tensor", fns, reads=[wtt] + C.xb_t, writes=[pu])
        P.op("scalar", lambda e, pu=pu, c=c: e.activation(out=uT(c), in_=pu[:], func=AF.Gelu), reads=[pu], writes=[ht[28 + c]])
    wt, wtt = wslot(P, C)
    wload(P, C, wt, wtt, [(wt[:], win_v[:, :, sg_off + 512:sg_off + 1024])], f"{C.ph}_sgv", KC * 512)
    for ch in range(TT // 128):
        ts = slice(ch * 128, (ch + 1) * 128)
        pv = next_bank(C)
        fns = [(lambda e, k=k, pv=pv, wt=wt, ts=ts: e.matmul(pv[:], C.xb[:, k, ts], wt[:, k, :], start=(k == 0), stop=(k == KC - 1))) for k in range(KC)]
        P.group("tensor", fns, reads=[wtt] + C.xb_t, writes=[pv])
        vg, vgt = M.vg, M.vg_t
        P.op("scalar", lambda e, pv=pv: e.activation(out=vg[:], in_=pv[:], func=AF.Gelu), reads=[pv], writes=[vgt])
        st, stt = M.st, M.st_t
        vsq, vsqt = M.vsq, M.vsq_t
        P.op("vector", lambda e: e.tensor_reduce(out=st[:, 0:1], in_=vg[:], axis=AX.X, op=ALU.add), reads=[vgt], writes=[stt])
        P.op("vector", lambda e: e.tensor_scalar(out=st[:, 0:1], in0=st[:, 0:1], scalar1=1.0 / SGD, scalar2=None, op0=ALU.mult), reads=[stt], writes=[stt])
        P.op("vector", lambda e: e.tensor_scalar(out=vg[:], in0=vg[:], scalar1=st[:, 0:1], scalar2=None, op0=ALU.subtract), reads=[stt], writes=[vgt])
        P.op("gpsimd", lambda e: e.tensor_tensor(out=vsq[:], in0=vg[:], in1=vg[:], op=ALU.mult), reads=[vgt], writes=[vsqt])
        P.op("vector", lambda e: e.tensor_reduce(out=st[:, 1:2], in_=vsq[:], axis=AX.X, op=ALU.add), reads=[vsqt], writes=[stt])
        P.op("scalar", lambda e: e.activation(out=st[:, 1:2], in_=st[:, 1:2], func=AF.Sqrt, bias=LN_EPS, scale=1.0 / SGD), reads=[stt], writes=[stt])
        P.op("vector", lambda e: e.reciprocal(out=st[:, 1:2], in_=st[:, 1:2]), reads=[stt], writes=[stt])
        P.op("vector", lambda e: e.scalar_tensor_tensor(out=vg[:], in0=vg[:], scalar=st[:, 1:2], in1=M.sglg[:], op0=ALU.mult, op1=ALU.mult), reads=[stt, M.sglg_t], writes=[vgt])
        vb, vbt = M.vb, M.vb_t
        P.op("vector", lambda e: e.tensor_tensor(out=vb[:], in0=vg[:], in1=M.sglb[:], op=ALU.add), reads=[vgt, M.sglb_t], writes=[vbt])
        for g in range(4):
            pm = next_bank(C)
            P.op("tensor", lambda e, pm=pm, g=g: e.matmul(pm[:, 0:128], vb[:, g * 128:(g + 1) * 128], M.swT[:, g, :], start=True, stop=True), reads=[vbt, M.swT_t], writes=[pm])
            mx_, mxt = M.mx, M.mx_t
            P.op("vector", lambda e, pm=pm, g=g: e.tensor_tensor(out=mx_[:], in0=pm[:, 0:128], in1=M.sgbb[:, g, :], op=ALU.add), reads=[pm, M.sgbb_t], writes=[mxt])
            P.op("vector", lambda e, g=g, ts=ts: e.tensor_tensor(out=ysg(g)[:, ts], in0=uT(g)[:, ts], in1=mx_[:], op=ALU.mult), reads=[mxt, ht[28 + g]], writes=[ht[32 + g]])
    for c in range(KC):
        cs = slice(c * 128, (c + 1) * 128)
        wt, wtt = wslot(P, C)
        parts = [(wt[:, :, b * 128:(b + 1) * 128], win_v[:, :, gate_off + b * D + c * 128:gate_off + b * D + (c + 1) * 128]) for b in range(3)]
        parts.append((wt[:, :, 384:512], wbr_v.rearrange("(c p) n -> p c n", p=128)[:, :, cs]))
        wload(P, C, wt, wtt, parts, f"{C.ph}_g{c}", KC * 512)
        gts = []
        for b in range(3):
            pg = next_bank(C)
            fns = [(lambda e, k=k, pg=pg, wt=wt, b=b: e.matmul(pg[:], wt[:, k, b * 128:(b + 1) * 128], C.xb[:, k, :], start=(k == 0), stop=(k == KC - 1))) for k in range(KC)]
            P.group("tensor", fns, reads=[wtt] + C.xb_t, writes=[pg])
            gt, gtt = M.gate[b], M.gate_t[b]
            P.op("scalar", lambda e, pg=pg, gt=gt, b=b, c=c: e.activation(out=gt[:], in_=pg[:], func=AF.Sigmoid, bias=M.bg[:, b * KC + c:b * KC + c + 1]), reads=[pg, M.bg_t], writes=[gtt])
            gts.append((gt, gtt))
        srcs = [([yrw(k) for k in range(8)], [ht[16 + k] for k in range(8)], 0),
                ([ysg(k) for k in range(4)], [ht[32 + k] for k in range(4)], 8),
                ([yat(k) for k in range(4)], [ht[24 + k] for k in range(4)], 12)]
        for b, (ops, ops_t, kb0) in enumerate(srcs):
            pz = next_bank(C)
            n = len(ops)
            fns = [(lambda e, k=k, pz=pz, wt=wt, ops=ops, kb0=kb0, n=n: e.matmul(pz[:], wt[:, kb0 + k, 384:512], ops[k], start=(k == 0), stop=(k == n - 1))) for k in range(n)]
            P.group("tensor", fns, reads=[wtt] + ops_t, writes=[pz])
            gt, gtt = gts[b]
            if b == 0:
                P.op("vector", lambda e, pz=pz, gt=gt, c=c: e.tensor_tensor(out=M.macc[:], in0=gt[:], in1=pz[:], op=ALU.mult), reads=[gtt, pz], writes=[M.macc_t])
            else:
                P.op("vector", lambda e, pz=pz, gt=gt: e.tensor_tensor(out=gt[:], in0=gt[:], in1=pz[:], op=ALU.mult), reads=[pz], writes=[gtt])
                if b == 1:
                    P.op("gpsimd", lambda e, gt=gt: e.tensor_tensor(out=M.macc[:], in0=M.macc[:], in1=gt[:], op=ALU.add), reads=[gtt], writes=[M.macc_t])
                else:
                    P.op("gpsimd", lambda e, gt=gt, c=c: e.tensor_tensor(out=merged(c), in0=M.macc[:], in1=gt[:], op=ALU.add), reads=[gtt, M.macc_t], writes=[ht[c]])
    wo_v = M.w_o.rearrange("(c p) n -> p c n", p=128)
    for c4 in range(KC // 4):
        wt, wtt = wslot(P, C)
        wload(P, C, wt, wtt, [(wt[:], wo_v[:, :, c4 * 512:(c4 + 1) * 512])], f"{C.ph}_wo{c4}", KC * 512)
        for j in range(4):
            c = c4 * 4 + j
            po = next_bank(C)
            fns = [(lambda e, k=k, po=po, wt=wt, j=j: e.matmul(po[:], wt[:, k, j * 128:(j + 1) * 128], merged(k), start=(k == 0), stop=(k == KC - 1))) for k in range(KC)]
            P.group("tensor", fns, reads=[wtt] + [ht[k] for k in range(KC)], writes=[po])
            P.op("vector", lambda e, c=c, po=po: e.scalar_tensor_tensor(out=C.xf[:, c, :], in0=C.xf[:, c, :], scalar=ALPHA, in1=po[:], op0=ALU.mult, op1=ALU.add),
                 reads=[po], writes=[C.xf_t[c]])
    layer_norm(P, C, M.ln2g, M.ln2g_t, M.ln2b, M.ln2b_t)


def build_p4(Ttot):
    nc = bass.Bass("TRN2", target_bir_lowering=False)
    din = lambda n, sh: nc.dram_tensor(n, sh, F32, kind="ExternalInput").ap()
    x1T = din("x1T", [D, Ttot]); yrwT = din("yrwT", [RW, Ttot]); yatT = din("yatT", [ATD, Ttot])
    w_in = din("w_in", [D, 7168]); b_gate = din("b_gate", [3 * D])
    sg_ln_g = din("sg_ln_g", [SGD]); sg_ln_b = din("sg_ln_b", [SGD]); sgwT = din("sgwT", [4, 128, 128]); sg_b = din("sg_b", [4, 128])
    w_branch = din("w_branch", [D, D]); w_o = din("w_o", [D, D])
    ln2g_d = din("ln2g", [D]); ln2b_d = din("ln2b", [D])
    w1 = din("w1", [D, 2 * DFF]); w2 = din("w2", [DFF, D]); ln3g_d = din("ln3g", [D]); ln3b_d = din("ln3b", [D])
    sgmask_d = din("sgmask", [128, 128])
    x3T = nc.dram_tensor("x3T", [D, Ttot], F32, kind="ExternalOutput").ap()
    import os
    DBG = "DBG" in os.environ
    if DBG:
        x2T = nc.dram_tensor("x2T", [D, Ttot], F32, kind="ExternalOutput").ap()
    with ExitStack() as es:
        P = Prog(nc, es)
        C = setup_common(P, nc)
        M = Ctx()
        M.w_in, M.w_branch, M.w_o, M.yrwT, M.yatT = w_in, w_branch, w_o, yrwT, yatT
        M.ln2g, M.ln2g_t = load_vec_cols(P, "ln2g_s", ln2g_d, KC)
        M.ln2b, M.ln2b_t = load_vec_cols(P, "ln2b_s", ln2b_d, KC)
        ln3g, ln3g_t = load_vec_cols(P, "ln3g_s", ln3g_d, KC)
        ln3b, ln3b_t = load_vec_cols(P, "ln3b_s", ln3b_d, KC)
        M.bg, M.bg_t = load_vec_cols(P, "bg_s", b_gate, 3 * KC)

        def mk(name, shape, dt=F32):
            t = P.sb(name, shape, dt)
            return t, T(t[:])
        M.sglg, M.sglg_t = mk("sglg", [128, SGD]); P.dma("sync", M.sglg[:], sg_ln_g.partition_broadcast(128), writes=[M.sglg_t])
        M.sglb, M.sglb_t = mk("sglb", [128, SGD]); P.dma("sync", M.sglb[:], sg_ln_b.partition_broadcast(128), writes=[M.sglb_t])
        M.sgbb, M.sgbb_t = mk("sgbb", [128, 4, 128])
        for g in range(4):
            P.dma("sync", M.sgbb[:, g, :], sg_b[g].partition_broadcast(128), writes=[M.sgbb_t])
        swf, swf_t = mk("swf", [128, 4, 128]); P.dma("sync", swf[:], sgwT.rearrange("g j i -> j g i"), writes=[swf_t])
        smk, smk_t = mk("smk", [128, 128]); P.dma("sync", smk[:], sgmask_d, writes=[smk_t])
        M.swT, M.swT_t = mk("swT", [128, 4, 128], BF16)
        P.op("vector", lambda e: e.tensor_tensor(out=M.swT[:], in0=swf[:], in1=smk[:].unsqueeze(1).to_broadcast([128, 4, 128]), op=ALU.mult), reads=[swf_t, smk_t], writes=[M.swT_t])
        M.vg, M.vg_t = mk("vg", [128, SGD]); M.vsq, M.vsq_t = mk("vsq", [128, SGD]); M.vb, M.vb_t = mk("vb", [128, SGD], BF16)
        M.st, M.st_t = mk("sgst", [128, 2]); M.mx, M.mx_t = mk("sgmx", [128, 128])
        M.gate = []; M.gate_t = []
        for b in range(3):
            g_, g_t = mk(f"gate{b}", [128, TT]); M.gate.append(g_); M.gate_t.append(g_t)
        M.macc, M.macc_t = mk("macc", [128, TT])
        finals = []
        for ti in range(Ttot // TT):
            t0 = ti * TT
            load_x(P, C, x1T, t0)
            mixer(P, C, M, t0)
            if DBG:
                finals += store_x(P, C, x2T, t0)
            ffn_ln(P, C, w1, w2, ln3g, ln3g_t, ln3b, ln3b_t)
            finals += store_x(P, C, x3T, t0)
        P.finish(finals)
    return nc


NL = 4
RW_OFF = 0
ATT_OFF = 4384
SGC_OFF = 3360
GATEC_OFF = 7024


class Exch:
    def __init__(self, nc, name, rows, cols, cr):
        self.S = nc.dram_tensor("S_" + name, [rows, cols], F32).ap()
        self.cr = cr
        self.nch = rows // cr
        self.G = [nc.dram_tensor(f"G_{name}{k}", [2 * cr, cols], F32).ap() for k in range(self.nch)]

    def pairs(self):
        return [(self.S[k * self.cr:(k + 1) * self.cr, :], self.G[k]) for k in range(self.nch)]

    def rows(self, rank, r0, n):
        k, off = r0 // self.cr, r0 % self.cr
        assert off + n <= self.cr
        return self.G[k][rank * self.cr + off:rank * self.cr + off + n, :]


def project_tok(P, C, w_ap, c0, ncol, out_ap, oc0, t0, stg):
    wv = w_ap.rearrange("(c p) n -> p c n", p=128)
    wt, wtt = wslot(P, C)
    if ncol == 512:
        wload(P, C, wt, wtt, [(wt[:, :, 0:ncol], wv[:, :, c0:c0 + ncol])], f"{C.ph}_pt{c0}", KC * 512)
    else:
        P.dma("gpsimd", wt[:, :, 0:ncol], wv[:, :, c0:c0 + ncol], writes=[wtt])
    for ch in range(TT // 128):
        ts = slice(ch * 128, (ch + 1) * 128)
        pp = next_bank(C)
        fns = [(lambda e, k=k, pp=pp, wt=wt, ts=ts: e.matmul(pp[:, 0:ncol], C.xb[:, k, ts], wt[:, k, 0:ncol], start=(k == 0), stop=(k == KC - 1))) for k in range(KC)]
        P.group("tensor", fns, reads=[wtt] + C.xb_t, writes=[pp])
        j = stg["cnt"] % len(stg["o"])
        stg["cnt"] += 1
        ot, ott = stg["o"][j], stg["ot"][j]
        if j % 2 == 0:
            P.op("scalar", lambda e, ot=ot, pp=pp: e.copy(out=ot[:, 0:ncol], in_=pp[:, 0:ncol]), reads=[pp], writes=[ott])
        else:
            P.op("vector", lambda e, ot=ot, pp=pp: e.tensor_copy(out=ot[:, 0:ncol], in_=pp[:, 0:ncol]), reads=[pp], writes=[ott])
        P.dma("sync", out_ap[t0 + ch * 128:t0 + (ch + 1) * 128, oc0:oc0 + ncol], ot[:, 0:ncol], reads=[ott])


def project_feat(P, C, w_ap, c0, nrows, out_ap, r0, t0, stg):
    wv = w_ap.rearrange("(c p) n -> p c n", p=128)
    for g0 in range(0, nrows, 512):
        n = min(512, nrows - g0)
        wt, wtt = wslot(P, C)
        wload(P, C, wt, wtt, [(wt[:, :, 0:n], wv[:, :, c0 + g0:c0 + g0 + n])], f"{C.ph}_pf{c0 + g0}", KC * 512)
        for j0 in range(0, n, 128):
            pp = next_bank(C)
            fns = [(lambda e, k=k, pp=pp, wt=wt, j0=j0: e.matmul(pp[:], wt[:, k, j0:j0 + 128], C.xb[:, k, :], start=(k == 0), stop=(k == KC - 1))) for k in range(KC)]
            P.group("tensor", fns, reads=[wtt] + C.xb_t, writes=[pp])
            j = stg["cnt"] % len(stg["o"])
            stg["cnt"] += 1
            ot, ott = stg["o"][j], stg["ot"][j]
            if j % 2 == 0:
                P.op("scalar", lambda e, ot=ot, pp=pp: e.copy(out=ot[:], in_=pp[:]), reads=[pp], writes=[ott])
            else:
                P.op("vector", lambda e, ot=ot, pp=pp: e.tensor_copy(out=ot[:], in_=pp[:]), reads=[pp], writes=[ott])
            P.dma("sync", out_ap[r0 + g0 + j0:r0 + g0 + j0 + 128, t0:t0 + TT], ot[:], reads=[ott])


def load_y_branches(P, C, M, t0):
    hb, ht = C.h, C.h_t
    for ch in range(TT // 128):
        tok = t0 + ch * 128
        j = (t0 // 128 + ch)
        e_ = j % 2
        cands = []
        for hf in range(2):
            yc, yct = M.ycand[hf]
            P.dma("sync", yc[:, 0:512], M.G_yrw.rows(0, hf * 2048 + tok, 128), writes=[yct])
            P.dma("sync", yc[:, 512:1024], M.G_yrw.rows(1, hf * 2048 + tok, 128), writes=[yct])
            slot = hf * 8 + j // 2
            P.dma("sync", yc[:, 1024:1536], M.G_yat.rows(e_, slot * 128, 128), writes=[yct])
            cands.append((yc, yct))
        ys, yst = M.ysel
        P.op("vector", lambda e, ys=ys, c0=cands[0][0]: e.tensor_scalar(out=ys[:], in0=c0[:], scalar1=M.msel[:, 0:1], scalar2=None, op0=ALU.mult), reads=[cands[0][1], M.msel_t], writes=[yst])
        P.op("vector", lambda e, ys=ys, c1=cands[1][0]: e.scalar_tensor_tensor(out=ys[:], in0=c1[:], scalar=M.msel[:, 1:2], in1=ys[:], op0=ALU.mult, op1=ALU.add), reads=[cands[1][1], M.msel_t], writes=[yst])
        for g in range(3):
            pb = next_bank(C)
            for q in range(4):
                c = g * 4 + q
                P.op("tensor", lambda e, pb=pb, q=q, c=c, ys=ys: e.transpose(pb[:, q * 128:(q + 1) * 128], ys[:, c * 128:(c + 1) * 128], M.identf[:]), reads=[yst, M.identf_t], writes=[pb])
            outap = hb[:, 16 + g * 4:16 + g * 4 + 4, ch * 128:(ch + 1) * 128]
            inap = pb[:].rearrange("p (q t) -> p q t", q=4)
            if g % 2 == 0:
                P.op("scalar", lambda e, outap=outap, inap=inap: e.copy(out=outap, in_=inap), reads=[pb], writes=[ht[16 + g * 4 + q] for q in range(4)])
            else:
                P.op("vector", lambda e, outap=outap, inap=inap: e.tensor_copy(out=outap, in_=inap), reads=[pb], writes=[ht[16 + g * 4 + q] for q in range(4)])


def build_fused(nlayers=NL, L=NL):
    nc = bass.Bass("TRN2", target_bir_lowering=False)
    din = lambda n, sh: nc.dram_tensor(n, sh, F32, kind="ExternalInput").ap()
    I = {}
    I["xT"] = din("xT", [D, 2048])
    for n, sh in (("ffn1_w_in", [L, D, 2 * DFF]), ("ffn1_w_out", [L, DFF, D]), ("ln1_g", [L, D]), ("ln1_b", [L, D]),
                  ("w_in", [L, D, 13168]), ("b_gate", [L, 3 * D]),
                  ("mu_c", [L, 1824]), ("w0_c", [L, 512]), ("a0_c", [L, 512]), ("k_k_c", [L, 512]), ("k_a_c", [L, 512]), ("r_k_c", [L, 512]),
                  ("gn_g_c", [L, 512]), ("gn_b_c", [L, 512]), ("w2_c", [L, 64, 512]), ("a2_c", [L, 64, 512]), ("g2_c", [L, 160, 512]),
                  ("sg_ln_g", [L, 512]), ("sg_ln_b", [L, 512]), ("sgwT", [L, 4, 128, 128]), ("sg_b", [L, 4, 128]),
                  ("idx_ln_g", [L, 64]), ("idx_ln_b", [L, 64]), ("w_branch", [L, D, D]), ("w_o", [L, D, D]),
                  ("ln2_g", [L, D]), ("ln2_b", [L, D]), ("ffn2_w_in", [L, D, 2 * DFF]), ("ffn2_w_out", [L, DFF, D]), ("ln3_g", [L, D]), ("ln3_b", [L, D]),
                  ("msel", [128, 2]), ("sgmask", [128, 128])):
        I[n] = din(n, sh)
    rc = {n: din("rc_" + n, list(v.shape)) for n, v in rwkv_consts().items()}
    dc = {n: din("dc_" + n, list(v.shape)) for n, v in dsa_consts(0, np.zeros((32, 8), np.float32)).items()}
    xoT = nc.dram_tensor("xoT", [D, 2048], F32, kind="ExternalOutput").ap()
    dt_ = lambda n, sh: nc.dram_tensor(n, sh, F32).ap()
    x1s = dt_("x1s", [D, 2048]); xcur = dt_("xcur", [D, 2048])
    X_prw = Exch(nc, "prw", 2048, 3360, 128); X_qT = Exch(nc, "qT", 512, 2048, 256); X_kT = Exch(nc, "kT", 512, 2048, 256)
    X_qiT = Exch(nc, "qiT", 1024, 2048, 256); X_V = Exch(nc, "V", 2048, 512, 1024); X_kw = Exch(nc, "kw", 2048, 80, 2048)
    X_yrw = Exch(nc, "yrw", 4096, 512, 1024); X_yat = Exch(nc, "yat", 2048, 512, 1024)
    S_prw, S_qT, S_kT, S_qiT, S_V, S_kw, S_yrw, S_yat = X_prw.S, X_qT.S, X_kT.S, X_qiT.S, X_V.S, X_kw.S, X_yrw.S, X_yat.S
    groups = [[0, 1], [2, 3], [4, 5], [6, 7]]
    with ExitStack() as es:
        P = Prog(nc, es)
        banks = [T(P.ps(f"bank{i}", [128, 512], F32)) for i in range(8)]

        def mk(name, shape, dt=F32):
            t = P.sb(name, shape, dt)
            return t, T(t[:])

        def exchange(pairs):
            P.barrier()
            for (s_, g_) in pairs:
                P.coll("AllGather", groups, s_, g_)
            P.barrier()

        WC = {"t": {}, "ev": {}}
        for l in range(nlayers):
            P.push_scope()
            C = setup_common(P, nc, banks)
            C.wc = WC; C.ph = "p1"
            lng, lng_t = load_vec_cols(P, "ln1g", I["ln1_g"][l], KC)
            lnb, lnb_t = load_vec_cols(P, "ln1b", I["ln1_b"][l], KC)
            stg = {"o": [], "ot": [], "cnt": 0}
            for i in range(3):
                o_, ot_ = mk(f"stg{i}", [128, TT]); stg["o"].append(o_); stg["ot"].append(ot_)
            xsrc = I["xT"] if l == 0 else xcur
            w_in_l = I["w_in"][l]
            for ti in range(2048 // TT):
                t0 = ti * TT
                C.ti = ti
                load_x(P, C, xsrc, t0)
                ffn_ln(P, C, I["ffn1_w_in"][l], I["ffn1_w_out"][l], lng, lng_t, lnb, lnb_t)
                store_x(P, C, x1s, t0)
                for c0 in range(0, 3360, 512):
                    project_tok(P, C, w_in_l, RW_OFF + c0, min(512, 3360 - c0), S_prw, c0, t0, stg)
                project_feat(P, C, w_in_l, ATT_OFF + 0, 512, S_qT, 0, t0, stg)
                project_feat(P, C, w_in_l, ATT_OFF + 512, 512, S_kT, 0, t0, stg)
                project_tok(P, C, w_in_l, ATT_OFF + 1024, 512, S_V, 0, t0, stg)
                project_feat(P, C, w_in_l, ATT_OFF + 1536, 1024, S_qiT, 0, t0, stg)
                project_tok(P, C, w_in_l, ATT_OFF + 2560, 80, S_kw, 0, t0, stg)
            exchange(X_prw.pairs() + X_qT.pairs() + X_kT.pairs() + X_qiT.pairs() + X_V.pairs() + X_kw.pairs())
            P.pop_scope()
            P.push_scope()
            prm = {"mu": I["mu_c"][l], "w0": I["w0_c"][l], "a0": I["a0_c"][l], "k_k": I["k_k_c"][l], "k_a": I["k_a_c"][l], "r_k": I["r_k_c"][l],
                   "gn_g": I["gn_g_c"][l], "gn_b": I["gn_b_c"][l], "w2": I["w2_c"][l], "a2": I["a2_c"][l], "g2": I["g2_c"][l]}
            rwkv_phase(P, banks, X_prw, prm, rc, I["msel"], S_yrw)
            P.barrier()
            P.pop_scope()
            P.push_scope()
            dsa_phase(P, banks, {"kT": X_kT, "V": X_V, "kw": X_kw, "qT": X_qT, "qiT": X_qiT}, I["idx_ln_g"][l], I["idx_ln_b"][l], dc, I["msel"], S_yat)
            exchange(X_yrw.pairs() + X_yat.pairs())
            P.pop_scope()
            P.push_scope()
            C = setup_common(P, nc, banks)
            C.wc = WC; C.ph = "p4"
            M = Ctx()
            M.w_in, M.w_branch, M.w_o = I["w_in"][l], I["w_branch"][l], I["w_o"][l]
            M.G_yrw, M.G_yat = X_yrw, X_yat
            M.ln2g, M.ln2g_t = load_vec_cols(P, "ln2g", I["ln2_g"][l], KC)
            M.ln2b, M.ln2b_t = load_vec_cols(P, "ln2b", I["ln2_b"][l], KC)
            ln3g, ln3g_t = load_vec_cols(P, "ln3g", I["ln3_g"][l], KC)
            ln3b, ln3b_t = load_vec_cols(P, "ln3b", I["ln3_b"][l], KC)
            M.bg, M.bg_t = load_vec_cols(P, "bg", I["b_gate"][l], 3 * KC)
            M.msel, M.msel_t = mk("msel4", [128, 2]); P.dma("sync", M.msel[:], I["msel"], writes=[M.msel_t])
            M.identf, M.identf_t = mk("identf", [128, 128]); P.dma("sync", M.identf[:], rc["ident"], writes=[M.identf_t])
            M.sglg, M.sglg_t = mk("sglg", [128, SGD]); P.dma("sync", M.sglg[:], I["sg_ln_g"][l].partition_broadcast(128), writes=[M.sglg_t])
            M.sglb, M.sglb_t = mk("sglb", [128, SGD]); P.dma("sync", M.sglb[:], I["sg_ln_b"][l].partition_broadcast(128), writes=[M.sglb_t])
            M.sgbb, M.sgbb_t = mk("sgbb", [128, 4, 128])
            for g in range(4):
                P.dma("sync", M.sgbb[:, g, :], I["sg_b"][l][g].partition_broadcast(128), writes=[M.sgbb_t])
            swf, swf_t = mk("swf", [128, 4, 128]); P.dma("sync", swf[:], I["sgwT"][l].rearrange("g j i -> j g i"), writes=[swf_t])
            smk, smk_t = mk("smk", [128, 128]); P.dma("sync", smk[:], I["sgmask"], writes=[smk_t])
            M.swT, M.swT_t = mk("swT", [128, 4, 128], BF16)
            P.op("vector", lambda e, M=M, swf=swf, smk=smk: e.tensor_tensor(out=M.swT[:], in0=swf[:], in1=smk[:].unsqueeze(1).to_broadcast([128, 4, 128]), op=ALU.mult), reads=[swf_t, smk_t], writes=[M.swT_t])
            M.vg, M.vg_t = mk("vg", [128, SGD]); M.vsq, M.vsq_t = mk("vsq", [128, SGD]); M.vb, M.vb_t = mk("vb", [128, SGD], BF16)
            M.st, M.st_t = mk("sgst", [128, 2]); M.mx, M.mx_t = mk("sgmx", [128, 128])
            M.gate = []; M.gate_t = []
            for b in range(3):
                g_, g_t = mk(f"gate{b}", [128, TT]); M.gate.append(g_); M.gate_t.append(g_t)
            M.macc, M.macc_t = mk("macc", [128, TT])
            M.ycand = [mk(f"ycand{i}", [128, 1536]) for i in range(2)]
            M.ysel = mk("ysel", [128, 1536])
            M.fused = True
            for ti in range(2048 // TT):
                t0 = ti * TT
                C.ti = ti
                load_x(P, C, x1s, t0)
                mixer(P, C, M, t0, load_y_branches)
                ffn_ln(P, C, I["ffn2_w_in"][l], I["ffn2_w_out"][l], ln3g, ln3g_t, ln3b, ln3b_t)
                fin = store_x(P, C, xoT if l == nlayers - 1 else xcur, t0)
            P.barrier()
            P.pop_scope()
        P.finish(P.all_events())
    return nc


_FUSED = {}


def _c(a):
    return np.ascontiguousarray(a, dtype=np.float32)


def make_in_maps(inp):
    x = np.asarray(inp["x"], dtype=np.float32)
    B, S_, D_ = x.shape
    xf = x.reshape(B * S_, D_)
    NCORE = 8
    TSH = (B * S_) // NCORE
    shared = {}
    for k in ("ffn1_w_in", "ffn1_w_out", "ln1_g", "ln1_b", "w_in", "b_gate", "sg_ln_g", "sg_ln_b", "sg_b", "idx_ln_g", "idx_ln_b",
              "w_branch", "w_o", "ln2_g", "ln2_b", "ffn2_w_in", "ffn2_w_out", "ln3_g", "ln3_b"):
        shared[k] = _c(inp[k])
    shared["sgwT"] = _c(np.transpose(np.asarray(inp["sg_w"], dtype=np.float32), (0, 1, 3, 2)))
    shared["sgmask"] = _c(sg_consts()["sgmask"])
    for n, v in rwkv_consts().items():
        shared["rc_" + n] = _c(v)
    rel_bias = np.asarray(inp["rel_bias"], dtype=np.float32)
    L = shared["w_in"].shape[0]
    per_e = []
    for e in range(2):
        m = {}
        hs = slice(e * 512, (e + 1) * 512)
        cols = np.r_[e * 512:(e + 1) * 512, 1024 + e * 512:1024 + (e + 1) * 512, 2048 + e * 512:2048 + (e + 1) * 512, 3072:3360]
        m["mu_c"] = _c(np.asarray(inp["rwkv_mu"])[:, cols])
        for n, src in (("w0_c", "rwkv_w0"), ("a0_c", "rwkv_a0"), ("k_k_c", "rwkv_k_k"), ("k_a_c", "rwkv_k_a"), ("gn_g_c", "rwkv_gn_g"), ("gn_b_c", "rwkv_gn_b")):
            m[n] = _c(np.asarray(inp[src])[:, hs])
        m["r_k_c"] = _c(np.asarray(inp["rwkv_r_k"]).reshape(L, -1)[:, hs])
        for n, src in (("w2_c", "rwkv_w2"), ("a2_c", "rwkv_a2"), ("g2_c", "rwkv_g2")):
            m[n] = _c(np.asarray(inp[src])[:, :, hs])
        for n, v in dsa_consts(e, rel_bias).items():
            m["dc_" + n] = _c(v)
        ms = np.zeros((128, 2), np.float32)
        ms[:, e] = 1.0
        m["msel"] = ms
        per_e.append(m)
    in_maps = []
    for c in range(NCORE):
        m = dict(shared)
        m.update(per_e[c % 2])
        m["xT"] = _c(xf[c * TSH:(c + 1) * TSH].T)
        in_maps.append(m)
    return in_maps, (B, S_, D_)


def kernel(**inp):
    if "nc" not in _FUSED:
        _FUSED["nc"] = build_fused()
    in_maps, (B, S_, D_) = make_in_maps(inp)
    cores = list(range(8))
    res = run_bass_kernel_spmd(_FUSED["nc"], in_maps, core_ids=cores)
    out = np.concatenate([np.ascontiguousarray(res.results[c]["xoT"].T) for c in cores], axis=0).reshape(B, S_, D_)
    return out.astype(np.float32)
```

```python
import math


import numpy as np
from contextlib import ExitStack
import concourse.bass as bass
import concourse.mybir as mybir
from concourse.bass_utils import run_bass_kernel_spmd

F32 = mybir.dt.float32
BF16 = mybir.dt.bfloat16
AF = mybir.ActivationFunctionType
ALU = mybir.AluOpType
AX = mybir.AxisListType

ENGS = ("sync", "scalar", "vector", "gpsimd", "tensor")
SEM_ROLL = 30000


class T:
    __slots__ = ("ap", "w", "r")

    def __init__(self, ap):
        self.ap = ap
        self.w = None
        self.r = []

    def __getitem__(self, idx):
        return self.ap[idx]


class Prog:
    def __init__(self, nc, es, n_dma_sems=24):
        self.nc = nc
        self.es = es
        self.q = {e: [] for e in ENGS}
        self.cur_sem = {}
        self.cnt = {}
        self.nsem = 0
        for e in ENGS:
            self._new_eng_sem(e)
        self.dma_sems = [es.enter_context(nc.semaphore(f"dma{i}")) for i in range(2 * n_dma_sems)]
        self.dma_cnt = [0] * (2 * n_dma_sems)
        self.dma_last = [None] * (2 * n_dma_sems)
        self.dma_rr = {"hw": 0, "sw": 0}
        self.n_dma_sems = n_dma_sems
        self.waited = {e: {} for e in ENGS}
        self.semobj = {}
        self.n_inst = {e: 0 for e in ENGS}
        self.n_wait = 0

    def _new_eng_sem(self, e):
        s = self.es.enter_context(self.nc.semaphore(f"s_{e}_{self.nsem}"))
        self.nsem += 1
        self.cur_sem[e] = s
        self.cnt[e] = 0

    def sb(self, name, shape, dt):
        self.uid = getattr(self, "uid", 0) + 1
        es = self.scopes[-1] if getattr(self, "scopes", None) else self.es
        return es.enter_context(self.nc.sbuf_tensor(f"{name}_u{self.uid}", list(shape), dt))

    def push_scope(self):
        if not hasattr(self, "scopes"):
            self.scopes = []
        es = ExitStack()
        es.__enter__()
        self.scopes.append(es)

    def pop_scope(self):
        es = self.scopes.pop()
        es.__exit__(None, None, None)

    def all_events(self):
        evs = [(self.cur_sem[e], self.cnt[e]) for e in ENGS if self.cnt[e] > 0]
        evs += [x for x in self.dma_last if x is not None]
        evs += list(getattr(self, "coll_events", []))
        return evs

    def barrier(self):
        evs = self.all_events()
        for e in ENGS:
            w = self._filter_waits(e, evs)
            if w:
                self.q[e].append((None, w, None, 0))

    def coll(self, kind, groups, src, dst):
        if not hasattr(self, "coll_sem"):
            self.coll_sem = self.es.enter_context(self.nc.semaphore("collsem"))
            self.coll_cnt = 0
        self.coll_cnt += 1
        ev = (self.coll_sem, self.coll_cnt)

        def fn(e):
            return e.collective_compute(kind, ALU.bypass, replica_groups=groups, ins=[src.opt()], outs=[dst.opt()])
        self.q["gpsimd"].append((fn, [], self.coll_sem, 1))
        self.coll_events = [ev]
        return ev

    def ps(self, name, shape, dt=F32):
        return self.es.enter_context(self.nc.psum_tensor(name, list(shape), dt))

    def _collect(self, reads, writes):
        waits = []
        for t in reads:
            if t.w is not None:
                waits.append(t.w)
        for t in writes:
            if t.w is not None:
                waits.append(t.w)
            waits.extend(t.r)
        return waits

    def _filter_waits(self, eng, waits):
        out = {}
        wd = self.waited[eng]
        for (sem, val) in waits:
            k = id(sem)
            self.semobj[k] = sem
            if wd.get(k, 0) >= val:
                continue
            if out.get(k, 0) < val:
                out[k] = val
        res = []
        for k, v in out.items():
            wd[k] = v
            res.append((self.semobj[k], v))
        return res

    def op(self, eng, fn, reads=(), writes=(), extra=()):
        waits = self._collect(reads, writes) + list(extra)
        waits = self._filter_waits(eng, waits)
        if self.cnt[eng] >= SEM_ROLL:
            self._new_eng_sem(eng)
        sem = self.cur_sem[eng]
        self.cnt[eng] += 1
        ev = (sem, self.cnt[eng])
        self.q[eng].append((fn, waits, sem, 1))
        self.n_inst[eng] += 1
        self.n_wait += len(waits)
        for t in reads:
            t.r.append(ev)
        for t in writes:
            t.w = ev
            t.r = []
        return ev

    def group(self, eng, fns, reads=(), writes=()):
        waits = self._collect(reads, writes)
        waits = self._filter_waits(eng, waits)
        if self.cnt[eng] >= SEM_ROLL:
            self._new_eng_sem(eng)
        sem = self.cur_sem[eng]
        self.cnt[eng] += 1
        ev = (sem, self.cnt[eng])
        n = len(fns)
        for i, fn in enumerate(fns):
            self.q[eng].append((fn, waits if i == 0 else [], sem if i == n - 1 else None, 1))
        self.n_inst[eng] += n
        for t in reads:
            t.r.append(ev)
        for t in writes:
            t.w = ev
            t.r = []
        return ev

    def dma(self, queue, out, in_, reads=(), writes=(), extra=(), **kw):
        kind = "sw" if queue == "gpsimd" else "hw"
        i = self.dma_rr[kind] + (self.n_dma_sems if kind == "sw" else 0)
        self.dma_rr[kind] = (self.dma_rr[kind] + 1) % self.n_dma_sems
        waits = self._collect(reads, writes) + list(extra)
        if self.dma_last[i] is not None:
            waits.append(self.dma_last[i])
        waits = self._filter_waits(queue, waits)
        sem = self.dma_sems[i]
        self.dma_cnt[i] += 16
        ev = (sem, self.dma_cnt[i])
        self.dma_last[i] = ev

        def fn(e, out=out, in_=in_, kw=kw):
            return e.dma_start(out=out, in_=in_, **kw)
        self.q[queue].append((fn, waits, sem, 16))
        self.n_inst[queue] += 1
        for t in reads:
            t.r.append(ev)
        for t in writes:
            t.w = ev
            t.r = []
        return ev

    def finish(self, final_events):
        nc = self.nc
        fw = list(final_events)

        def run(e, name):
            for (fn, waits, sem, inc) in self.q[name]:
                for (s, v) in waits:
                    e.wait_ge(s, v)
                if fn is None:
                    continue
                ins = fn(e)
                if sem is not None:
                    ins.then_inc(sem, inc)
            if name == "sync":
                for (s, v) in fw:
                    e.wait_ge(s, v)

        with nc.Block() as block:
            @block.sync
            def _(e):
                run(e, "sync")

            @block.scalar
            def _(e):
                run(e, "scalar")

            @block.vector
            def _(e):
                run(e, "vector")

            @block.gpsimd
            def _(e):
                run(e, "gpsimd")

            @block.tensor
            def _(e):
                run(e, "tensor")


D = 2048
DFF = 5632
KC = D // 128
HC = DFF // 128
TT = 512
ALPHA = 8 ** 0.25
LN_EPS = 1e-5


class Ctx:
    pass


def setup_common(P, nc, banks=None):
    C = Ctx()
    C.banks = banks if banks is not None else [T(P.ps(f"bank{i}", [128, 512], F32)) for i in range(8)]
    C.bank_rr = 0
    C.xb = P.sb("xb", [128, KC, TT], BF16)
    C.xf = P.sb("xf", [128, KC, TT], F32)
    C.h = P.sb("h", [128, HC, TT], BF16)
    C.xb_t = [T(C.xb[:, c, :]) for c in range(KC)]
    C.xf_t = [T(C.xf[:, c, :]) for c in range(KC)]
    C.h_t = [T(C.h[:, c, :]) for c in range(HC)]
    C.w1 = [P.sb(f"w1_{i}", [128, KC, 512], BF16) for i in range(2)]
    C.w1_t = [T(C.w1[i][:]) for i in range(2)]
    C.w2 = [P.sb(f"w2_{i}", [128, HC, 128], BF16) for i in range(2)]
    C.w2_t = [T(C.w2[i][:]) for i in range(2)]
    C.sg = [P.sb(f"sg_{i}", [128, TT], F32) for i in range(2)]
    C.sg_t = [T(C.sg[i][:]) for i in range(2)]
    C.sq = [P.sb(f"sq_{i}", [128, TT], F32) for i in range(2)]
    C.sq_t = [T(C.sq[i][:]) for i in range(2)]
    C.rstd = P.sb("rstd", [128, TT], F32)
    C.rstd_t = T(C.rstd[:])
    C.onesM = P.sb("onesM", [128, 128], F32)
    C.onesM_t = T(C.onesM[:])
    P.op("vector", lambda e: e.memset(C.onesM[:], 1.0 / D), writes=[C.onesM_t])
    C.cnt = {"w1": 0, "w2": 0, "sg": 0, "sq": 0}
    C.wc = None; C.ph = "x"; C.ti = 0
    return C


def wload(P, C, wt, wtt, parts, key, nfree):
    wc = getattr(C, "wc", None)
    if wc is None:
        for (sap, dap) in parts:
            P.dma("gpsimd", sap, dap, writes=[wtt])
        return
    if key not in wc["t"]:
        wc["t"][key] = (P.nc.dram_tensor(f"wsc_{key}", [128, nfree], BF16).ap(), None)
    sc = wc["t"][key][0]
    flat = wt[:].rearrange("p a b -> p (a b)")[:, 0:nfree] if len(wt[:].shape) == 3 else wt[:, 0:nfree]
    if C.ti == 0:
        for (sap, dap) in parts:
            P.dma("gpsimd", sap, dap, writes=[wtt])
        ev = P.dma("sync", sc, flat, reads=[wtt])
        wc["ev"][key] = ev
    else:
        P.dma("sync", flat, sc, writes=[wtt], extra=[wc["ev"][key]])


def next_bank(C):
    b = C.banks[C.bank_rr]
    C.bank_rr = (C.bank_rr + 1) % 8
    return b


def load_vec_cols(P, name, src, n):
    t = P.sb(name, [128, n], F32)
    tt = T(t[:])
    P.dma("sync", t[:], src.rearrange("(c p) -> p c", p=128), writes=[tt], allow_slow_non_contiguous=True)
    return t, tt


def ffn_ln(P, C, w_in, w_out, lng, lng_t, lnb, lnb_t):
    nc = P.nc
    w_in_v = w_in.rearrange("(c p) n -> p c n", p=128)
    w_out_v = w_out.rearrange("(c p) n -> p c n", p=128)
    for hg in range(HC // 2):
        slot = C.cnt["w1"] % 2
        C.cnt["w1"] += 1
        wt, wtt = C.w1[slot], C.w1_t[slot]
        wload(P, C, wt, wtt, [(wt[:, :, 0:256], w_in_v[:, :, hg * 256:(hg + 1) * 256]), (wt[:, :, 256:512], w_in_v[:, :, DFF + hg * 256:DFF + (hg + 1) * 256])],
              f"{C.ph}_a{hg}", KC * 512)
        for j in range(2):
            ht = hg * 2 + j
            pg = next_bank(C)
            pu = next_bank(C)
            fns = []
            for k in range(KC):
                fns.append(lambda e, k=k, pg=pg, j=j, wt=wt: e.matmul(pg[:], wt[:, k, j * 128:(j + 1) * 128], C.xb[:, k, :], start=(k == 0), stop=(k == KC - 1)))
            P.group("tensor", fns, reads=[wtt] + C.xb_t, writes=[pg])
            fns = []
            for k in range(KC):
                fns.append(lambda e, k=k, pu=pu, j=j, wt=wt: e.matmul(pu[:], wt[:, k, 256 + j * 128:256 + (j + 1) * 128], C.xb[:, k, :], start=(k == 0), stop=(k == KC - 1)))
            P.group("tensor", fns, reads=[wtt] + C.xb_t, writes=[pu])
            s = C.cnt["sg"] % 2
            C.cnt["sg"] += 1
            sg, sgt = C.sg[s], C.sg_t[s]
            P.op("scalar", lambda e, sg=sg, pg=pg: e.activation(out=sg[:], in_=pg[:], func=AF.Silu), reads=[pg], writes=[sgt])
            P.op("vector", lambda e, sg=sg, pu=pu, ht=ht: e.scalar_tensor_tensor(out=C.h[:, ht, :], in0=sg[:], scalar=0.5, in1=pu[:], op0=ALU.mult, op1=ALU.mult),
                 reads=[sgt, pu], writes=[C.h_t[ht]])
    for dt_ in range(KC):
        slot = C.cnt["w2"] % 2
        C.cnt["w2"] += 1
        wt, wtt = C.w2[slot], C.w2_t[slot]
        wload(P, C, wt, wtt, [(wt[:, 0:HC // 2, :], w_out_v[:, 0:HC // 2, dt_ * 128:(dt_ + 1) * 128]), (wt[:, HC // 2:, :], w_out_v[:, HC // 2:, dt_ * 128:(dt_ + 1) * 128])],
              f"{C.ph}_b{dt_}", HC * 128)
        py = next_bank(C)
        fns = []
        for k in range(HC):
            fns.append(lambda e, k=k, py=py, wt=wt: e.matmul(py[:], wt[:, k, :], C.h[:, k, :], start=(k == 0), stop=(k == HC - 1)))
        P.group("tensor", fns, reads=[wtt] + C.h_t, writes=[py])
        P.op("vector", lambda e, dt_=dt_, py=py: e.scalar_tensor_tensor(out=C.xf[:, dt_, :], in0=C.xf[:, dt_, :], scalar=ALPHA, in1=py[:], op0=ALU.mult, op1=ALU.add),
             reads=[py], writes=[C.xf_t[dt_]])
    layer_norm(P, C, lng, lng_t, lnb, lnb_t)


def layer_norm(P, C, lng, lng_t, lnb, lnb_t):
    pm = next_bank(C)
    fns = []
    for c in range(KC):
        fns.append(lambda e, c=c: e.matmul(pm[:], C.onesM[:], C.xf[:, c, :], start=(c == 0), stop=(c == KC - 1)))
    P.group("tensor", fns, reads=[C.onesM_t] + C.xf_t, writes=[pm])
    pv = next_bank(C)
    for c in range(KC):
        P.op("vector", lambda e, c=c: e.tensor_tensor(out=C.xf[:, c, :], in0=C.xf[:, c, :], in1=pm[:], op=ALU.subtract),
             reads=[pm], writes=[C.xf_t[c]])
        s = C.cnt["sq"] % 2
        C.cnt["sq"] += 1
        sq, sqt = C.sq[s], C.sq_t[s]
        P.op("scalar", lambda e, c=c, sq=sq: e.activation(out=sq[:], in_=C.xf[:, c, :], func=AF.Square), reads=[C.xf_t[c]], writes=[sqt])
        P.op("tensor", lambda e, c=c, sq=sq: e.matmul(pv[:], C.onesM[:], sq[:], start=(c == 0), stop=(c == KC - 1)),
             reads=[sqt, C.onesM_t], writes=[pv])
    P.op("scalar", lambda e: e.activation(out=C.rstd[:], in_=pv[:], func=AF.Sqrt, bias=LN_EPS), reads=[pv], writes=[C.rstd_t])
    P.op("vector", lambda e: e.reciprocal(out=C.rstd[:], in_=C.rstd[:]), reads=[C.rstd_t], writes=[C.rstd_t])
    for c in range(KC):
        P.op("vector", lambda e, c=c: e.tensor_tensor(out=C.xf[:, c, :], in0=C.xf[:, c, :], in1=C.rstd[:], op=ALU.mult),
             reads=[C.rstd_t], writes=[C.xf_t[c]])
        P.op("scalar", lambda e, c=c: e.activation(out=C.xf[:, c, :], in_=C.xf[:, c, :], func=AF.Identity, scale=lng[:, c:c + 1], bias=lnb[:, c:c + 1]),
             reads=[lng_t, lnb_t], writes=[C.xf_t[c]])
        P.op("gpsimd", lambda e, c=c: e.tensor_copy(out=C.xb[:, c, :], in_=C.xf[:, c, :]), reads=[C.xf_t[c]], writes=[C.xb_t[c]])


def load_x(P, C, xT, t0):
    xv = xT.rearrange("(c p) t -> p c t", p=128)
    for c in range(KC):
        P.dma("sync", C.xf[:, c, :], xv[:, c, t0:t0 + TT], writes=[C.xf_t[c]])
    for c in range(KC):
        P.op("gpsimd", lambda e, c=c: e.tensor_copy(out=C.xb[:, c, :], in_=C.xf[:, c, :]), reads=[C.xf_t[c]], writes=[C.xb_t[c]])


def store_x(P, C, oT, t0):
    ov = oT.rearrange("(c p) t -> p c t", p=128)
    evs = []
    for c in range(KC):
        evs.append(P.dma("sync", ov[:, c, t0:t0 + TT], C.xf[:, c, :], reads=[C.xf_t[c]]))
    return evs


def project(P, C, wp, col_tiles, outT, t0, wslots, evac_rr=[0]):
    wp_v = wp.rearrange("(c p) n -> p c n", p=128)
    evs = []
    for (c0, ncol, r0) in col_tiles:
        i = wslots["cnt"] % len(wslots["t"])
        wslots["cnt"] += 1
        wt, wtt = wslots["t"][i], wslots["tt"][i]
        P.dma("gpsimd", wt[:, :, 0:ncol], wp_v[:, :, c0:c0 + ncol], writes=[wtt])
        pp = next_bank(C)
        fns = []
        for k in range(KC):
            fns.append(lambda e, k=k, pp=pp, wt=wt, ncol=ncol: e.matmul(pp[0:ncol, :], wt[:, k, 0:ncol], C.xb[:, k, :], start=(k == 0), stop=(k == KC - 1)))
        P.group("tensor", fns, reads=[wtt] + C.xb_t, writes=[pp])
        j = wslots["ocnt"] % len(wslots["o"])
        wslots["ocnt"] += 1
        ot, ott = wslots["o"][j], wslots["ot"][j]
        eng = "scalar" if (j % 2 == 0) else "vector"
        if eng == "scalar":
            P.op("scalar", lambda e, ot=ot, pp=pp, ncol=ncol: e.copy(out=ot[0:ncol, :], in_=pp[0:ncol, :]), reads=[pp], writes=[ott])
        else:
            P.op("vector", lambda e, ot=ot, pp=pp, ncol=ncol: e.tensor_copy(out=ot[0:ncol, :], in_=pp[0:ncol, :]), reads=[pp], writes=[ott])
        evs.append(P.dma("sync", outT[r0:r0 + ncol, t0:t0 + TT], ot[0:ncol, :], reads=[ott]))
    return evs


def make_proj_slots(P, n=3, no=3):
    w = [P.sb(f"wp_{i}", [128, KC, 128], BF16) for i in range(n)]
    o = [P.sb(f"po_{i}", [128, TT], F32) for i in range(no)]
    return {"t": w, "tt": [T(x[:]) for x in w], "cnt": 0, "o": o, "ot": [T(x[:]) for x in o], "ocnt": 0}


def col_tiles_for(ranges):
    tiles = []
    r = 0
    for (s, n) in ranges:
        o = 0
        while o < n:
            m = min(128, n - o)
            tiles.append((s + o, m, r))
            r += m
            o += m
    return tiles, r


def build_p1(Ttot, proj_ranges):
    nc = bass.Bass("TRN2", target_bir_lowering=False)
    tiles, NP = col_tiles_for(proj_ranges)
    xT = nc.dram_tensor("xT", [D, Ttot], F32, kind="ExternalInput").ap()
    w1 = nc.dram_tensor("w1", [D, 2 * DFF], F32, kind="ExternalInput").ap()
    w2 = nc.dram_tensor("w2", [DFF, D], F32, kind="ExternalInput").ap()
    lng_d = nc.dram_tensor("lng", [D], F32, kind="ExternalInput").ap()
    lnb_d = nc.dram_tensor("lnb", [D], F32, kind="ExternalInput").ap()
    wp = nc.dram_tensor("wp", [D, NP], F32, kind="ExternalInput").ap()
    x1T = nc.dram_tensor("x1T", [D, Ttot], F32, kind="ExternalOutput").ap()
    pT = nc.dram_tensor("pT", [NP, Ttot], F32, kind="ExternalOutput").ap()
    with ExitStack() as es:
        P = Prog(nc, es)
        C = setup_common(P, nc)
        lng, lng_t = load_vec_cols(P, "lng_s", lng_d, KC)
        lnb, lnb_t = load_vec_cols(P, "lnb_s", lnb_d, KC)
        ws = make_proj_slots(P)
        finals = []
        for ti in range(Ttot // TT):
            t0 = ti * TT
            load_x(P, C, xT, t0)
            ffn_ln(P, C, w1, w2, lng, lng_t, lnb, lnb_t)
            finals += store_x(P, C, x1T, t0)
            finals += project(P, C, wp, tiles, pT, t0, ws)
        P.finish(finals)
    return nc, NP


S = 4096
NH = 8
HN = 512
NCOL = 1824
C0 = math.exp(-0.5)
GN_EPS = 64e-5


def rwkv_consts():
    ident = np.eye(128, dtype=np.float32)
    s = np.arange(128)
    same = (s[:, None] // 64) == (s[None, :] // 64)
    LT = np.where(same & (s[:, None] <= s[None, :]), -C0, 0.0).astype(np.float32)
    BT = np.where(same, -C0, 0.0).astype(np.float32)
    BTc = np.zeros((128, 2), np.float32)
    BTc[:64, 0] = -C0
    BTc[64:, 1] = -C0
    sl = s % 64
    t = np.arange(64)
    mk1 = np.concatenate([(sl[:, None] < t[None, :]), (sl[:, None] <= t[None, :])], 1).astype(np.float32)
    mk3 = (t[None, :] < sl[:, None]).astype(np.float32)
    i2 = (sl[:, None] == t[None, :]).astype(np.float32)
    return {"ident": ident, "LT": LT, "BT": BT, "BTc": BTc, "mk1": mk1, "mk3": mk3, "i2": i2}


def rwkv_phase(P, banks, Gp, prm, cst, msel_d, y):
    ntiles = S // 128
    DBG = False
    STAGE = 99
    SUB = 0
    mu = prm["mu"]; vecs = {n: prm[n] for n in ("w0", "a0", "k_k", "k_a", "r_k", "gn_g", "gn_b")}
    w2 = prm["w2"]; a2 = prm["a2"]; g2 = prm["g2"]
    if True:
        rr = [0]

        def nb():
            b = banks[rr[0]]
            rr[0] = (rr[0] + 1) % 8
            return b

        def mk(name, shape, dt=F32):
            t = P.sb(name, shape, dt)
            return t, T(t[:])

        msel, msel_t = mk("msel", [128, 2]); P.dma("sync", msel[:], msel_d, writes=[msel_t])
        PA = [mk(f"PA{i}", [128, 3360]) for i in range(2)]
        mu_bc, mu_t = mk("mu_bc", [128, NCOL])
        P.dma("sync", mu_bc[:], mu.partition_broadcast(128), writes=[mu_t])
        vb = {}
        for n, ap in vecs.items():
            vb[n] = mk(n + "_bc", [128, HN])
            P.dma("sync", vb[n][0][:], ap.partition_broadcast(128), writes=[vb[n][1]])
        w2s, w2t = mk("w2s", [64, HN]); P.dma("sync", w2s[:], w2, writes=[w2t])
        a2s, a2t = mk("a2s", [64, HN]); P.dma("sync", a2s[:], a2, writes=[a2t])
        g2a, g2at = mk("g2a", [128, HN]); P.dma("sync", g2a[:], g2[0:128, :], writes=[g2at])
        g2b, g2bt = mk("g2b", [32, HN]); P.dma("sync", g2b[:], g2[128:160, :], writes=[g2bt])
        cs = {}
        for n, ap in cst.items():
            cs[n] = mk(n + "_s", list(ap.shape))
            P.dma("sync", cs[n][0][:], ap, writes=[cs[n][1]])
        ident, ident_t = cs["ident"]
        LT, LT_t = cs["LT"]; BT, BT_t = cs["BT"]; BTc, BTc_t = cs["BTc"]
        mk1, mk1_t = cs["mk1"]; mk3, mk3_t = cs["mk3"]; i2, i2_t = cs["i2"]

        Pt = [mk(f"Pt{i}", [128, NCOL]) for i in range(2)]
        Pp = mk("Pp", [128, NCOL])
        names = ["tw", "sw", "ag", "g", "kk", "tmp", "tmp2", "inv", "kmod", "bvec", "cum", "epos", "eexc", "eneg", "eend",
                 "rt", "at", "kt", "bt", "kh", "bh", "X", "W", "U", "Y", "yn", "bon"]
        W_ = {n: mk(n, [128, HN]) for n in names}
        lorT = mk("lorT", [128, 4, 128])
        sgd = mk("sgd", [128, 160])
        st8 = {n: mk(n, [128, NH]) for n in ("n2", "mean", "var", "s8")}
        G1 = mk("G1", [64, NH, 128]); G2 = mk("G2", [64, NH, 128])
        Nf = [mk(f"Nf{i}", [64, NH, 64], BF16) for i in range(2)]
        Tf = [mk(f"Tf{i}", [64, NH, 64], BF16) for i in range(2)]
        Q = mk("Q", [64, NH, 64], BF16)
        Qf = mk("Qf", [64, NH, 64])
        SH = [mk(f"SH{i}", [64, HN]) for i in range(4)]
        CM = mk("CM", [64, NH, 2, 4, 64])
        Apt = mk("Apt", [64, NH, 64])
        pC = mk("pC", [64, NH, 2])
        Hs = [mk(f"H{i}", [64, NH, 64]) for i in range(2)]
        P.op("vector", lambda e: e.memset(Hs[0][0][:], 0.0), writes=[Hs[0][1]])
        hcur = [0]
        finals = []

        def v3(ap):
            return ap.rearrange("p (h n) -> p h n", h=NH)

        def bc8(ap8):
            return ap8.unsqueeze(2).to_broadcast([128, NH, 64])

        def do_chunk(c2, xs, xs_t, at_, at_t, bh, bh_t, kh, kh_t, shs, cm, cm_t):
            U, U_t = W_["U"]; Y, Y_t = W_["Y"]
            X, X_t = W_["X"]; Wt, Wt_t = W_["W"]
            g1, g1_t = G1; g2_, g2_t = G2
            q_, q_t = Q
            apt, apt_t = Apt
            if c2 == 0:
                Vc, Vc_t = xs[0:64, 1024:1536], xs_t
                Ac, Ac_t = at_[0:64, :], at_t
                Bc, Bc_t = bh[0:64, :], bh_t
                Kc, Kc_t = kh[0:64, :], kh_t
            else:
                Vc, Vc_t = shs[0][0][:], shs[0][1]
                Ac, Ac_t = shs[1][0][:], shs[1][1]
                Bc, Bc_t = shs[2][0][:], shs[2][1]
                Kc, Kc_t = shs[3][0][:], shs[3][1]
            hs = lambda h: slice(h * 64, (h + 1) * 64)
            nf0, nf0_t = Nf[0]; tf0, tf0_t = Tf[0]
            pg1 = [nb(), nb()]; pg2 = [nb(), nb()]; pg3 = nb()
            for (pgs, qi) in ((pg1, 0), (pg2, 1)):
                for half in range(2):
                    fns = []
                    for hh in range(4):
                        h = half * 4 + hh
                        fns.append(lambda e, h=h, hh=hh, qi=qi, bk=pgs[half]: e.matmul(bk[0:64, hh * 128:(hh + 1) * 128], cm[:, h, c2, qi, :], cm[:, h, c2, 2:4, :], start=True, stop=True))
                    P.group("tensor", fns, reads=[cm_t], writes=[pgs[half]])
            fns = []
            for h in range(NH):
                fns.append(lambda e, h=h: e.matmul(pg3[0:64, hs(h)], cm[:, h, c2, 2, :], cm[:, h, c2, 0, :], start=True, stop=True))
            P.group("tensor", fns, reads=[cm_t], writes=[pg3])
            mk1b = mk1[0:64, :].unsqueeze(1).to_broadcast([64, 4, 128])
            for half in range(2):
                P.op("vector", lambda e, half=half: e.tensor_tensor(out=g1[:, half * 4:(half + 1) * 4, :], in0=pg1[half][0:64, :].rearrange("p (h t) -> p h t", h=4), in1=mk1b, op=ALU.mult),
                     reads=[pg1[half], mk1_t], writes=[g1_t])
                P.op("vector", lambda e, half=half: e.tensor_tensor(out=g2_[:, half * 4:(half + 1) * 4, :], in0=pg2[half][0:64, :].rearrange("p (h t) -> p h t", h=4), in1=mk1b, op=ALU.mult),
                     reads=[pg2[half], mk1_t], writes=[g2_t])
            P.op("vector", lambda e: e.tensor_tensor(out=nf0[:], in0=pg3[0:64, :].rearrange("p (h t) -> p h t", h=NH), in1=mk3[0:64, :].unsqueeze(1).to_broadcast([64, NH, 64]), op=ALU.mult),
                 reads=[pg3, mk3_t], writes=[nf0_t])
            P.op("gpsimd", lambda e: e.tensor_copy(out=tf0[:], in_=g1[:, :, 0:64]), reads=[g1_t], writes=[tf0_t])
            P.op("gpsimd", lambda e: e.tensor_tensor(out=q_[:], in0=g1[:, :, 0:64], in1=i2[0:64, :].unsqueeze(1).to_broadcast([64, NH, 64]), op=ALU.add), reads=[g1_t, i2_t], writes=[q_t])
            cur = 0
            for lvl in range(5):
                nfc, nfc_t = Nf[cur]; tfc, tfc_t = Tf[cur]
                nfn, nfn_t = Nf[1 - cur]; tfn, tfn_t = Tf[1 - cur]
                last = (lvl == 4)
                pn = nb()
                fns = [(lambda e, h=h, pn=pn, tfc=tfc, nfc=nfc: e.matmul(pn[0:64, hs(h)], tfc[:, h, :], nfc[:, h, :], start=True, stop=True)) for h in range(NH)]
                P.group("tensor", fns, reads=[tfc_t, nfc_t], writes=[pn])
                if not last:
                    ptt = nb()
                    fns = [(lambda e, h=h, ptt=ptt, tfc=tfc, nfc=nfc: e.matmul(ptt[0:64, hs(h)], nfc[:, h, :], tfc[:, h, :], start=True, stop=True)) for h in range(NH)]
                    P.group("tensor", fns, reads=[tfc_t, nfc_t], writes=[ptt])
                P.op("scalar", lambda e, pn=pn, nfn=nfn: e.copy(out=nfn[:].rearrange("p h t -> p (h t)"), in_=pn[0:64, :]), reads=[pn], writes=[nfn_t])
                if not last:
                    P.op("vector", lambda e, ptt=ptt, tfn=tfn: e.tensor_copy(out=tfn[:].rearrange("p h t -> p (h t)"), in_=ptt[0:64, :]), reads=[ptt], writes=[tfn_t])
                pq = nb()
                fns = [(lambda e, h=h, pq=pq, nfn=nfn: e.matmul(pq[0:64, hs(h)], nfn[:, h, :], q_[:, h, :], start=True, stop=True)) for h in range(NH)]
                P.group("tensor", fns, reads=[nfn_t, q_t], writes=[pq])
                P.op("vector", lambda e, pq=pq: e.tensor_tensor(out=q_[:].rearrange("p h t -> p (h t)"), in0=q_[:].rearrange("p h t -> p (h t)"), in1=pq[0:64, :], op=ALU.add), reads=[pq], writes=[q_t])
                cur = 1 - cur
            qf_, qf_t = Qf
            P.op("gpsimd", lambda e: e.tensor_copy(out=qf_[:], in_=q_[:]), reads=[q_t], writes=[qf_t])
            px = nb()
            fns = [(lambda e, h=h, px=px: e.matmul(px[0:64, hs(h)], g2_[:, h, 0:64], Vc[:, hs(h)], start=True, stop=True)) for h in range(NH)]
            P.group("tensor", fns, reads=[g2_t, Vc_t], writes=[px])
            P.op("scalar", lambda e, px=px: e.copy(out=X[0:64, :], in_=px[0:64, :]), reads=[px], writes=[X_t])
            pw_ = nb()
            fns = [(lambda e, h=h, pw_=pw_: e.matmul(pw_[0:64, hs(h)], qf_[:, h, :], X[0:64, hs(h)], start=True, stop=True)) for h in range(NH)]
            P.group("tensor", fns, reads=[qf_t, X_t], writes=[pw_])
            P.op("scalar", lambda e, pw_=pw_: e.copy(out=Wt[0:64, :], in_=pw_[0:64, :]), reads=[pw_], writes=[Wt_t])
            pap = nb()
            fns = [(lambda e, h=h, pap=pap: e.matmul(pap[0:64, hs(h)], Ac[:, hs(h)], qf_[:, h, :], start=True, stop=True)) for h in range(NH)]
            P.group("tensor", fns, reads=[Ac_t, qf_t], writes=[pap])
            P.op("vector", lambda e, pap=pap: e.tensor_copy(out=apt[:].rearrange("p h t -> p (h t)"), in_=pap[0:64, :]), reads=[pap], writes=[apt_t])
            H0, H0_t = Hs[hcur[0]]
            H1, H1_t = Hs[1 - hcur[0]]
            pu = nb()
            fns = [(lambda e, h=h, pu=pu, H0=H0: e.matmul(pu[0:64, hs(h)], apt[:, h, :], H0[:, h, :], start=True, stop=True)) for h in range(NH)]
            P.group("tensor", fns, reads=[apt_t, H0_t], writes=[pu])
            P.op("vector", lambda e, pu=pu: e.tensor_tensor(out=U[0:64, :], in0=pu[0:64, :], in1=Wt[0:64, :], op=ALU.add), reads=[pu, Wt_t], writes=[U_t])
            ph = nb()
            fns = []
            for h in range(NH):
                fns.append(lambda e, h=h, ph=ph: e.matmul(ph[0:64, hs(h)], Bc[:, hs(h)], U[0:64, hs(h)], start=True, stop=False))
                fns.append(lambda e, h=h, ph=ph: e.matmul(ph[0:64, hs(h)], Kc[:, hs(h)], Vc[:, hs(h)], start=False, stop=True))
            P.group("tensor", fns, reads=[Bc_t, Kc_t, U_t, Vc_t], writes=[ph])
            py = nb()
            sl = slice(c2 * 64, (c2 + 1) * 64)
            fns = []
            for h in range(NH):
                fns.append(lambda e, h=h, py=py, H0=H0: e.matmul(py[sl, hs(h)], cm[:, h, c2, 3, :], H0[:, h, :], start=True, stop=False))
                fns.append(lambda e, h=h, py=py: e.matmul(py[sl, hs(h)], g1[:, h, 64:128], U[0:64, hs(h)], start=False, stop=False))
                fns.append(lambda e, h=h, py=py: e.matmul(py[sl, hs(h)], g2_[:, h, 64:128], Vc[:, hs(h)], start=False, stop=True))
            P.group("tensor", fns, reads=[cm_t, H0_t, g1_t, g2_t, U_t, Vc_t], writes=[py])
            P.op("gpsimd", lambda e, H0=H0, H1=H1: e.tensor_tensor(out=H1[:], in0=H0[:], in1=pC[0][:, :, c2:c2 + 1].to_broadcast([64, NH, 64]), op=ALU.mult),
                 reads=[H0_t, pC[1]], writes=[H1_t])
            P.op("vector", lambda e, H1=H1, ph=ph: e.tensor_tensor(out=H1[:].rearrange("p a i -> p (a i)"), in0=H1[:].rearrange("p a i -> p (a i)"), in1=ph[0:64, :], op=ALU.add),
                 reads=[ph], writes=[H1_t])
            P.op("scalar", lambda e, py=py: e.copy(out=Y[sl, :], in_=py[sl, :]), reads=[py], writes=[Y_t])
            hcur[0] = 1 - hcur[0]

        def do_tile(ti):
            t0 = ti * 128
            Pc, Pc_t = Pt[ti % 2]
            pa, pa_t = PA[0]
            pb_, pb_t = PA[1]
            rk_, lt_ = ti // 16, (ti % 16) * 128
            P.dma("sync", pa[:], Gp.rows(rk_, lt_, 128), writes=[pa_t])
            if ti == 0:
                P.op("vector", lambda e: e.memset(pb_[0:1, :], 0.0), writes=[pb_t])
            else:
                P.dma("sync", pb_[0:1, :], Gp.rows((ti - 1) // 16, ((ti - 1) % 16) * 128 + 127, 1), writes=[pb_t])
            P.dma("sync", pb_[1:128, :], Gp.rows(rk_, lt_, 127), writes=[pb_t])
            for (src, src_t, dst, dst_t) in ((pa, pa_t, Pc, Pc_t), (pb_, pb_t, Pp[0], Pp[1])):
                v4 = src[:, 0:3072].rearrange("p (j g n) -> p j g n", j=3, g=2)
                d3 = dst[:, 0:1536].rearrange("p (j n) -> p j n", j=3)
                P.op("vector", lambda e, v4=v4, d3=d3: e.tensor_scalar(out=d3, in0=v4[:, :, 0, :], scalar1=msel[:, 0:1], scalar2=None, op0=ALU.mult), reads=[src_t, msel_t], writes=[dst_t])
                P.op("vector", lambda e, v4=v4, d3=d3: e.scalar_tensor_tensor(out=d3, in0=v4[:, :, 1, :], scalar=msel[:, 1:2], in1=d3, op0=ALU.mult, op1=ALU.add), reads=[src_t, msel_t], writes=[dst_t])
                P.op("gpsimd", lambda e, src=src, dst=dst: e.tensor_copy(out=dst[:, 1536:1824], in_=src[:, 3072:3360]), reads=[src_t], writes=[dst_t])
            P.op("vector", lambda e, Pc=Pc: e.tensor_tensor(out=Pp[0][:], in0=Pp[0][:], in1=Pc[:], op=ALU.subtract), reads=[Pc_t], writes=[Pp[1]])
            P.op("gpsimd", lambda e: e.tensor_tensor(out=Pp[0][:], in0=Pp[0][:], in1=mu_bc[:], op=ALU.mult), reads=[mu_t], writes=[Pp[1]])
            P.op("vector", lambda e, Pc=Pc: e.tensor_tensor(out=Pp[0][:], in0=Pp[0][:], in1=Pc[:], op=ALU.add), reads=[Pc_t], writes=[Pp[1]])
            xs, xs_t = Pp
            r_ = xs[:, 0:512]; k_ = xs[:, 512:1024]; v_ = xs[:, 1024:1536]
            if STAGE == 1:
                finals.append(P.dma('sync', y[t0:t0 + 128, :], xs[:, 0:512], reads=[xs_t])); return
            tw, tw_t = W_["tw"]
            P.op("scalar", lambda e: e.activation(out=tw[:, 0:64], in_=xs[:, 1536:1600], func=AF.Tanh), reads=[xs_t], writes=[tw_t])
            P.op("scalar", lambda e: e.activation(out=sgd[0][:], in_=xs[:, 1664:1824], func=AF.Sigmoid), reads=[xs_t], writes=[sgd[1]])
            pb = nb()
            P.op("tensor", lambda e, pb=pb: e.transpose(pb[0:64, 0:128], tw[:, 0:64], ident[:]), reads=[tw_t, ident_t], writes=[pb])
            P.op("tensor", lambda e, pb=pb: e.transpose(pb[0:64, 128:256], xs[:, 1600:1664], ident[:]), reads=[xs_t, ident_t], writes=[pb])
            P.op("tensor", lambda e, pb=pb: e.transpose(pb[0:128, 256:384], sgd[0][:, 0:128], ident[:]), reads=[sgd[1], ident_t], writes=[pb])
            P.op("tensor", lambda e, pb=pb: e.transpose(pb[0:32, 384:512], sgd[0][:, 128:160], ident[:]), reads=[sgd[1], ident_t], writes=[pb])
            lT, lT_t = lorT
            P.op("vector", lambda e, pb=pb: e.tensor_copy(out=lT[0:64, 0:2, :], in_=pb[0:64, 0:256].rearrange("p (a b) -> p a b", a=2)), reads=[pb], writes=[lT_t])
            P.op("vector", lambda e, pb=pb: e.tensor_copy(out=lT[:, 2, :], in_=pb[:, 256:384]), reads=[pb], writes=[lT_t])
            P.op("vector", lambda e, pb=pb: e.tensor_copy(out=lT[0:32, 3, :], in_=pb[0:32, 384:512]), reads=[pb], writes=[lT_t])
            pw = nb(); pa = nb(); pg = nb()
            P.op("tensor", lambda e, pw=pw: e.matmul(pw[:], lT[0:64, 0, :], w2s[:], start=True, stop=True), reads=[lT_t, w2t], writes=[pw])
            P.op("tensor", lambda e, pa=pa: e.matmul(pa[:], lT[0:64, 1, :], a2s[:], start=True, stop=True), reads=[lT_t, a2t], writes=[pa])
            P.group("tensor", [lambda e, pg=pg: e.matmul(pg[:], lT[:, 2, :], g2a[:], start=True, stop=False),
                               lambda e, pg=pg: e.matmul(pg[:], lT[0:32, 3, :], g2b[:], start=False, stop=True)], reads=[lT_t, g2at, g2bt], writes=[pg])
            sw, sw_t = W_["sw"]; ag, ag_t = W_["ag"]; g_, g_t = W_["g"]
            P.op("vector", lambda e, pw=pw: e.tensor_tensor(out=sw[:], in0=pw[:], in1=vb["w0"][0][:], op=ALU.add), reads=[pw, vb["w0"][1]], writes=[sw_t])
            P.op("scalar", lambda e: e.activation(out=sw[:], in_=sw[:], func=AF.Sigmoid), reads=[sw_t], writes=[sw_t])
            P.op("vector", lambda e, pa=pa: e.tensor_tensor(out=ag[:], in0=pa[:], in1=vb["a0"][0][:], op=ALU.add), reads=[pa, vb["a0"][1]], writes=[ag_t])
            P.op("scalar", lambda e: e.activation(out=ag[:], in_=ag[:], func=AF.Sigmoid), reads=[ag_t], writes=[ag_t])
            P.op("scalar", lambda e, pg=pg: e.copy(out=g_[:], in_=pg[:]), reads=[pg], writes=[g_t])
            if STAGE == 2:
                finals.append(P.dma('sync', y[t0:t0 + 128, :], xs[:, 0:512], reads=[xs_t])); return
            kk, kk_t = W_["kk"]; tmp, tmp_t = W_["tmp"]; tmp2, tmp2_t = W_["tmp2"]
            kmod, kmod_t = W_["kmod"]; bvec, bvec_t = W_["bvec"]
            n2, n2_t = st8["n2"]
            P.op("vector", lambda e: e.tensor_tensor(out=kk[:], in0=k_, in1=vb["k_k"][0][:], op=ALU.mult), reads=[xs_t, vb["k_k"][1]], writes=[kk_t])
            P.op("gpsimd", lambda e: e.tensor_tensor(out=tmp[:], in0=kk[:], in1=kk[:], op=ALU.mult), reads=[kk_t], writes=[tmp_t])
            P.op("vector", lambda e: e.tensor_reduce(out=n2[:], in_=v3(tmp[:]), axis=AX.X, op=ALU.add), reads=[tmp_t], writes=[n2_t])
            P.op("scalar", lambda e: e.activation(out=n2[:], in_=n2[:], func=AF.Sqrt), reads=[n2_t], writes=[n2_t])
            P.op("vector", lambda e: e.tensor_scalar(out=n2[:], in0=n2[:], scalar1=1e-12, scalar2=None, op0=ALU.max), reads=[n2_t], writes=[n2_t])
            P.op("vector", lambda e: e.reciprocal(out=n2[:], in_=n2[:]), reads=[n2_t], writes=[n2_t])
            P.op("vector", lambda e: e.tensor_tensor(out=v3(kk[:]), in0=v3(kk[:]), in1=bc8(n2[:]), op=ALU.mult), reads=[n2_t], writes=[kk_t])
            P.op("vector", lambda e: e.scalar_tensor_tensor(out=tmp[:], in0=ag[:], scalar=-1.0, in1=vb["k_a"][0][:], op0=ALU.add, op1=ALU.mult), reads=[ag_t, vb["k_a"][1]], writes=[tmp_t])
            P.op("vector", lambda e: e.scalar_tensor_tensor(out=kmod[:], in0=tmp[:], scalar=1.0, in1=k_, op0=ALU.add, op1=ALU.mult), reads=[tmp_t, xs_t], writes=[kmod_t])
            P.op("gpsimd", lambda e: e.tensor_tensor(out=bvec[:], in0=kk[:], in1=ag[:], op=ALU.mult), reads=[kk_t, ag_t], writes=[bvec_t])
            if STAGE == 3:
                finals.append(P.dma('sync', y[t0:t0 + 128, :], xs[:, 0:512], reads=[xs_t])); return
            pc = nb(); pt = nb()
            P.op("tensor", lambda e, pc=pc: e.matmul(pc[:], LT[:], sw[:], start=True, stop=True), reads=[LT_t, sw_t], writes=[pc])
            P.op("tensor", lambda e, pt=pt: e.matmul(pt[:], BT[:], sw[:], start=True, stop=True), reads=[BT_t, sw_t], writes=[pt])
            ppc = nb()
            fns = [(lambda e, h=h, ppc=ppc: e.matmul(ppc[0:64, h * 2:h * 2 + 2], sw[:, h * 64:(h + 1) * 64], BTc[:], start=True, stop=True)) for h in range(NH)]
            P.group("tensor", fns, reads=[sw_t, BTc_t], writes=[ppc])
            P.op("scalar", lambda e, ppc=ppc: e.activation(out=pC[0][:].rearrange("p a b -> p (a b)"), in_=ppc[0:64, 0:16], func=AF.Exp), reads=[ppc], writes=[pC[1]])
            cum, cum_t = W_["cum"]
            epos, epos_t = W_["epos"]; eexc, eexc_t = W_["eexc"]; eneg, eneg_t = W_["eneg"]; eend, eend_t = W_["eend"]
            P.op("scalar", lambda e, pc=pc: e.copy(out=cum[:], in_=pc[:]), reads=[pc], writes=[cum_t])
            P.op("scalar", lambda e, pc=pc: e.activation(out=epos[:], in_=pc[:], func=AF.Exp), reads=[pc], writes=[epos_t])
            P.op("scalar", lambda e, pc=pc: e.activation(out=eneg[:], in_=pc[:], func=AF.Exp, scale=-1.0), reads=[pc], writes=[eneg_t])
            P.op("vector", lambda e: e.scalar_tensor_tensor(out=eexc[:], in0=sw[:], scalar=C0, in1=cum[:], op0=ALU.mult, op1=ALU.add), reads=[sw_t, cum_t], writes=[eexc_t])
            P.op("scalar", lambda e: e.activation(out=eexc[:], in_=eexc[:], func=AF.Exp), reads=[eexc_t], writes=[eexc_t])
            P.op("vector", lambda e, pt=pt: e.tensor_tensor(out=eend[:], in0=pt[:], in1=cum[:], op=ALU.subtract), reads=[pt, cum_t], writes=[eend_t])
            P.op("scalar", lambda e: e.activation(out=eend[:], in_=eend[:], func=AF.Exp), reads=[eend_t], writes=[eend_t])
            if STAGE == 4:
                finals.append(P.dma('sync', y[t0:t0 + 128, :], xs[:, 0:512], reads=[xs_t])); return
            rt, rt_t = W_["rt"]; at_, at_t = W_["at"]; kt, kt_t = W_["kt"]; bt, bt_t = W_["bt"]; kh, kh_t = W_["kh"]; bh, bh_t = W_["bh"]
            P.op("vector", lambda e: e.tensor_tensor(out=rt[:], in0=r_, in1=epos[:], op=ALU.mult), reads=[xs_t, epos_t], writes=[rt_t])
            P.op("vector", lambda e: e.scalar_tensor_tensor(out=at_[:], in0=kk[:], scalar=-1.0, in1=eexc[:], op0=ALU.mult, op1=ALU.mult), reads=[kk_t, eexc_t], writes=[at_t])
            P.op("vector", lambda e: e.tensor_tensor(out=kt[:], in0=kmod[:], in1=eneg[:], op=ALU.mult), reads=[kmod_t, eneg_t], writes=[kt_t])
            P.op("gpsimd", lambda e: e.tensor_tensor(out=bt[:], in0=bvec[:], in1=eneg[:], op=ALU.mult), reads=[bvec_t, eneg_t], writes=[bt_t])
            P.op("vector", lambda e: e.tensor_tensor(out=kh[:], in0=kmod[:], in1=eend[:], op=ALU.mult), reads=[kmod_t, eend_t], writes=[kh_t])
            P.op("gpsimd", lambda e: e.tensor_tensor(out=bh[:], in0=bvec[:], in1=eend[:], op=ALU.mult), reads=[bvec_t, eend_t], writes=[bh_t])

            cm, cm_t = CM
            srcs = [(bt, bt_t), (kt, kt_t), (at_, at_t), (rt, rt_t)]
            for h in range(NH):
                pb = nb()
                for q, (src, src_t) in enumerate(srcs):
                    P.op("tensor", lambda e, pb=pb, q=q, src=src, h=h: e.transpose(pb[0:64, q * 128:(q + 1) * 128], src[:, h * 64:(h + 1) * 64], ident[:]),
                         reads=[src_t, ident_t], writes=[pb])
                outap = cm[:, h, :, :, :].rearrange("p c q j -> p q c j")
                inap = pb[0:64, :].rearrange("p (q c j) -> p q c j", q=4, c=2)
                if h % 2 == 0:
                    P.op("vector", lambda e, outap=outap, inap=inap: e.tensor_copy(out=outap, in_=inap), reads=[pb], writes=[cm_t])
                else:
                    P.op("scalar", lambda e, outap=outap, inap=inap: e.copy(out=outap, in_=inap), reads=[pb], writes=[cm_t])
            if STAGE == 5:
                finals.append(P.dma('sync', y[t0:t0 + 128, :], xs[:, 0:512], reads=[xs_t])); return
            shs = []
            for qi, (src_ap, src_t) in enumerate(((xs[64:128, 1024:1536], xs_t), (at_[64:128, :], at_t), (bh[64:128, :], bh_t), (kh[64:128, :], kh_t))):
                d_, d_t = SH[qi]
                P.dma("sync", d_[:], src_ap, reads=[src_t], writes=[d_t])
                shs.append((d_, d_t))
            U, U_t = W_["U"]; Y, Y_t = W_["Y"]
            X, X_t = W_["X"]; Wt, Wt_t = W_["W"]
            g1, g1_t = G1; g2_, g2_t = G2
            q_, q_t = Q
            apt, apt_t = Apt
            for c2 in range(2):
                do_chunk(c2, xs, xs_t, at_, at_t, bh, bh_t, kh, kh_t, shs, cm, cm_t)

            mean, mean_t = st8["mean"]; var, var_t = st8["var"]; s8, s8_t = st8["s8"]
            yn, yn_t = W_["yn"]; bon, bon_t = W_["bon"]
            P.op("vector", lambda e: e.tensor_reduce(out=mean[:], in_=v3(Y[:]), axis=AX.X, op=ALU.add), reads=[Y_t], writes=[mean_t])
            P.op("vector", lambda e: e.tensor_scalar(out=mean[:], in0=mean[:], scalar1=1.0 / 64, scalar2=None, op0=ALU.mult), reads=[mean_t], writes=[mean_t])
            P.op("vector", lambda e: e.tensor_tensor(out=v3(yn[:]), in0=v3(Y[:]), in1=bc8(mean[:]), op=ALU.subtract), reads=[Y_t, mean_t], writes=[yn_t])
            P.op("gpsimd", lambda e: e.tensor_tensor(out=tmp[:], in0=yn[:], in1=yn[:], op=ALU.mult), reads=[yn_t], writes=[tmp_t])
            P.op("vector", lambda e: e.tensor_reduce(out=var[:], in_=v3(tmp[:]), axis=AX.X, op=ALU.add), reads=[tmp_t], writes=[var_t])
            P.op("scalar", lambda e: e.activation(out=var[:], in_=var[:], func=AF.Sqrt, bias=GN_EPS, scale=1.0 / 64), reads=[var_t], writes=[var_t])
            P.op("vector", lambda e: e.reciprocal(out=var[:], in_=var[:]), reads=[var_t], writes=[var_t])
            P.op("vector", lambda e: e.tensor_tensor(out=v3(yn[:]), in0=v3(yn[:]), in1=bc8(var[:]), op=ALU.mult), reads=[var_t], writes=[yn_t])
            P.op("gpsimd", lambda e: e.tensor_tensor(out=yn[:], in0=yn[:], in1=vb["gn_g"][0][:], op=ALU.mult), reads=[vb["gn_g"][1]], writes=[yn_t])
            P.op("vector", lambda e: e.tensor_tensor(out=yn[:], in0=yn[:], in1=vb["gn_b"][0][:], op=ALU.add), reads=[vb["gn_b"][1]], writes=[yn_t])
            P.op("gpsimd", lambda e: e.tensor_tensor(out=tmp2[:], in0=r_, in1=kmod[:], op=ALU.mult), reads=[xs_t, kmod_t], writes=[tmp2_t])
            P.op("gpsimd", lambda e: e.tensor_tensor(out=tmp2[:], in0=tmp2[:], in1=vb["r_k"][0][:], op=ALU.mult), reads=[vb["r_k"][1]], writes=[tmp2_t])
            P.op("vector", lambda e: e.tensor_reduce(out=s8[:], in_=v3(tmp2[:]), axis=AX.X, op=ALU.add), reads=[tmp2_t], writes=[s8_t])
            P.op("vector", lambda e: e.tensor_tensor(out=v3(bon[:]), in0=v3(v_), in1=bc8(s8[:]), op=ALU.mult), reads=[xs_t, s8_t], writes=[bon_t])
            P.op("vector", lambda e: e.tensor_tensor(out=yn[:], in0=yn[:], in1=bon[:], op=ALU.add), reads=[bon_t], writes=[yn_t])
            P.op("vector", lambda e: e.tensor_tensor(out=yn[:], in0=yn[:], in1=g_[:], op=ALU.mult), reads=[g_t], writes=[yn_t])
            finals.append(P.dma("sync", y[t0:t0 + 128, :], yn[:], reads=[yn_t]))
            if DBG:
                Hc, Hc_t = Hs[hcur[0]]
                finals.append(P.dma("sync", dbg[ti], Hc[:].rearrange("p a i -> p (a i)"), reads=[Hc_t]))
        for ti in range(ntiles):
            do_tile(ti)


S = 4096
NSLOT = 16
NQ = NSLOT * 128
AH = 8
IH = 16
TOPK = 256
NEG = -30000.0


def np_rel_bucket(dist):
    max_exact = 16
    d_f = np.maximum(dist, 1).astype(np.float32)
    large = max_exact + (np.log(d_f / max_exact) / np.float32(math.log(128 / max_exact)) * (32 - max_exact)).astype(np.int32)
    large = np.minimum(large, 31)
    return np.where(dist < max_exact, dist, large)


def dsa_consts(e, rel_bias):
    q = np.arange(128)
    tri = (q[None, :] <= q[:, None]).astype(np.float32)
    cm = np.zeros((128, 256), np.float32)
    if e == 0:
        cm[:, 0:128] = tri
    else:
        cm[:, 0:128] = 1.0
        cm[:, 128:256] = tri
    nbig = ((cm - 1.0) * 1e30).astype(np.float32)
    NB = np.zeros((128, 3, AH, 128), np.float32)
    for r in range(3):
        delta = e + 1 - r
        dist = np.maximum(delta * 128 + q[None, :] - q[:, None], 0)
        b = np_rel_bucket(dist.astype(np.int32))
        NB[:, r, :, :] = np.transpose(rel_bias[b], (0, 2, 1))
    cfar = np.broadcast_to(rel_bias[31][None, :], (128, AH)).astype(np.float32).copy()
    return {"cm": cm, "nbig": nbig, "NB": NB, "cfar": cfar, "ident": np.eye(128, dtype=np.float32)}


def dsa_phase(P, banks, G, lng_d, lnb_d, cst, msel_d, y, nslot=NSLOT):
    cm_d, nbig_d, NB_d, cfar_d, ident_d = cst["cm"], cst["nbig"], cst["NB"], cst["cfar"], cst["ident"]
    if True:
        rr = [0]

        def nb():
            b = banks[rr[0]]
            rr[0] = (rr[0] + 1) % 4
            return b
        acc = [banks[6], banks[7]]
        pscb = [banks[4], banks[5]]
        pscn = [0]

        def mk(name, shape, dt=F32):
            t = P.sb(name, shape, dt)
            return t, T(t[:])

        msel, msel_t = mk("msel3", [128, 2]); P.dma("sync", msel[:], msel_d, writes=[msel_t])
        ident, ident_t = mk("ident_s", [128, 128]); P.dma("sync", ident[:], ident_d, writes=[ident_t])
        identb, identb_t = mk("identb", [128, 128], BF16)
        P.op("vector", lambda e: e.tensor_copy(out=identb[:], in_=ident[:]), reads=[ident_t], writes=[identb_t])
        cm, cm_t = mk("cm_s", [128, 256]); P.dma("sync", cm[:], cm_d, writes=[cm_t])
        nbig, nbig_t = mk("nbig_s", [128, 256]); P.dma("sync", nbig[:], nbig_d, writes=[nbig_t])
        cfar, cfar_t = mk("cfar_s", [128, AH]); P.dma("sync", cfar[:], cfar_d, writes=[cfar_t])
        NBb, NBb_t = mk("NBb", [128, 3, AH, 128], BF16)
        kT, kT_t = mk("kT_s", [128, 4, S], BF16)
        V1, V1_t = mk("V1", [128, 32, AH, 65], BF16)
        kiT, kiT_t = mk("kiT", [64, S])
        P.push_scope()
        NBf, NBf_t = mk("NBf", [128, 3, AH, 128]); P.dma("sync", NBf[:], NB_d, writes=[NBf_t])
        for r in range(3):
            for h in range(AH):
                P.op("vector", lambda e, r=r, h=h: e.tensor_scalar(out=NBb[:, r, h, :], in0=NBf[:, r, h, :], scalar1=cfar[:, h:h + 1], scalar2=None, op0=ALU.subtract),
                     reads=[NBf_t, cfar_t], writes=[NBb_t])
        lng, lng_t = mk("lng_bc", [128, 64]); P.dma("sync", lng[:], lng_d.partition_broadcast(128), writes=[lng_t])
        lnb, lnb_t = mk("lnb_bc", [128, 64]); P.dma("sync", lnb[:], lnb_d.partition_broadcast(128), writes=[lnb_t])

        for r_ in range(2):
            for hp in range(4):
                P.dma("gpsimd", kT[:, hp, r_ * 2048:(r_ + 1) * 2048], G["kT"].rows(r_, hp * 128, 128), writes=[kT_t])
        P.op("vector", lambda e: e.memset(V1[:, :, :, 64:65], 1.0), writes=[V1_t])
        for kb in range(32):
            P.dma("gpsimd", V1[:, kb, :, 0:64], G["V"].rows(kb // 16, (kb % 16) * 128, 128).rearrange("p (h d) -> p h d", h=AH), writes=[V1_t])
        ki, ki_t = mk("ki", [128, 32, 64])
        for r_ in range(2):
            P.dma("sync", ki[:, r_ * 16:(r_ + 1) * 16, :], G["kw"].rows(r_, 0, 2048)[:, 0:64].rearrange("(kb p) d -> p kb d", p=128), writes=[ki_t])
        st, st_t = mk("kist", [128, 32]); ksq, ksq_t = mk("kisq", [128, 32, 64])
        bc32 = lambda ap: ap.unsqueeze(2).to_broadcast([128, 32, 64])
        P.op("vector", lambda e: e.tensor_reduce(out=st[:], in_=ki[:], axis=AX.X, op=ALU.add), reads=[ki_t], writes=[st_t])
        P.op("vector", lambda e: e.tensor_scalar(out=st[:], in0=st[:], scalar1=1.0 / 64, scalar2=None, op0=ALU.mult), reads=[st_t], writes=[st_t])
        P.op("vector", lambda e: e.tensor_tensor(out=ki[:], in0=ki[:], in1=bc32(st[:]), op=ALU.subtract), reads=[st_t], writes=[ki_t])
        P.op("vector", lambda e: e.tensor_tensor(out=ksq[:], in0=ki[:], in1=ki[:], op=ALU.mult), reads=[ki_t], writes=[ksq_t])
        P.op("vector", lambda e: e.tensor_reduce(out=st[:], in_=ksq[:], axis=AX.X, op=ALU.add), reads=[ksq_t], writes=[st_t])
        P.op("scalar", lambda e: e.activation(out=st[:], in_=st[:], func=AF.Sqrt, bias=1e-5, scale=1.0 / 64), reads=[st_t], writes=[st_t])
        P.op("vector", lambda e: e.reciprocal(out=st[:], in_=st[:]), reads=[st_t], writes=[st_t])
        P.op("vector", lambda e: e.tensor_tensor(out=ki[:], in0=ki[:], in1=bc32(st[:]), op=ALU.mult), reads=[st_t], writes=[ki_t])
        P.op("vector", lambda e: e.tensor_tensor(out=ki[:], in0=ki[:], in1=lng[:].unsqueeze(1).to_broadcast([128, 32, 64]), op=ALU.mult), reads=[lng_t], writes=[ki_t])
        P.op("vector", lambda e: e.tensor_tensor(out=ki[:], in0=ki[:], in1=lnb[:].unsqueeze(1).to_broadcast([128, 32, 64]), op=ALU.add), reads=[lnb_t], writes=[ki_t])
        for g in range(8):
            pb = nb()
            for j in range(4):
                kb = g * 4 + j
                P.op("tensor", lambda e, pb=pb, j=j, kb=kb: e.transpose(pb[0:64, j * 128:(j + 1) * 128], ki[:, kb, :], ident[:]), reads=[ki_t, ident_t], writes=[pb])
            P.op("scalar", lambda e, pb=pb, g=g: e.copy(out=kiT[:, g * 512:(g + 1) * 512], in_=pb[0:64, :]), reads=[pb], writes=[kiT_t])

        P.barrier()
        P.pop_scope()
        qf, qf_t = mk("qf", [128, 4, 128])
        qfb, qfb_t = mk("qfb", [128, 4, 128])
        qf2, qf2_t = mk("qf2", [128, 4, 256])
        qi2 = [mk(f"qi2h{i}", [64, IH, 128]) for i in range(1)]
        qib, qib_t = mk("qib", [64, IH, 128])
        wq2, wq2_t = mk("wq2", [128, 2, IH])
        wqb, wqb_t = mk("wqb", [128, IH])
        qTz, qTz_t = mk("qTz", [128, AH, 128], BF16)
        P.op("vector", lambda e: e.memset(qTz[:], 0.0), writes=[qTz_t])
        qiT, qiT_t = mk("qiT", [64, IH, 128])
        wq, wq_t = mk("wq", [128, IH])
        diagall, diagall_t = mk("diagall", [128, IH, 128])
        stage = [mk(f"stage{i}", [128, 512]) for i in range(2)]
        score2 = [mk(f"score{i}", [128, S]) for i in range(2)]
        work, work_t = mk("work", [128, S])
        madd2 = [mk(f"madd{i}", [128, S], BF16) for i in range(2)]
        mx, mx_t = mk("mx", [128, 8])
        thr, thr_t = mk("thr", [128, 1])
        PT = [mk(f"PT{i}", [128, 4, 128], BF16) for i in range(2)]
        rec, rec_t = mk("rec", [128, AH])
        yo = [mk(f"yo{i}", [128, 512]) for i in range(1)]
        cnt = {"stage": 0, "PT": 0}

        def blend2(c0, c0_t, c1, c1_t, tb, tb_t, out, out_t, np_):
            P.op("scalar", lambda e: e.activation(out=out, in_=c0, func=AF.Identity, scale=msel[0:np_, 0:1]), reads=[c0_t, msel_t], writes=[out_t])
            P.op("scalar", lambda e: e.activation(out=tb, in_=c1, func=AF.Identity, scale=msel[0:np_, 1:2]), reads=[c1_t, msel_t], writes=[tb_t])
            P.op("gpsimd", lambda e: e.tensor_tensor(out=out, in0=out, in1=tb, op=ALU.add), reads=[tb_t], writes=[out_t])

        def st_idx(i):
            nkb = 2 * i + 2
            nkeys = nkb * 128
            r_ = i // 8
            lt0 = (i - 8 * r_) * 256
            score, score_t = score2[i % 2]
            qh, qh_t = qi2[0]
            for cand in range(2):
                for hq in range(4):
                    P.dma("sync", qh[:, hq * 4:(hq + 1) * 4, :], G["qiT"].rows(r_, hq * 256, 256)[:, lt0 + cand * 128:lt0 + (cand + 1) * 128].rearrange("(h d) q -> d h q", d=64), writes=[qh_t])
                if cand == 0:
                    P.op("scalar", lambda e: e.activation(out=qiT[:], in_=qh[:], func=AF.Identity, scale=msel[0:64, 0:1]), reads=[qh_t, msel_t], writes=[qiT_t])
                else:
                    P.op("scalar", lambda e: e.activation(out=qib[:], in_=qh[:], func=AF.Identity, scale=msel[0:64, 1:2]), reads=[qh_t, msel_t], writes=[qib_t])
                    P.op("gpsimd", lambda e: e.tensor_tensor(out=qiT[:], in0=qiT[:], in1=qib[:], op=ALU.add), reads=[qib_t], writes=[qiT_t])
            P.dma("sync", wq2[:], G["kw"].rows(r_, lt0, 256)[:, 64:80].rearrange("(c p) n -> p c n", p=128), writes=[wq2_t])
            blend2(wq2[:, 0, :], wq2_t, wq2[:, 1, :], wq2_t, wqb[:], wqb_t, wq[:], wq_t, 128)
            for h in range(IH):
                P.op("scalar", lambda e, h=h: e.activation(out=diagall[:, h, :], in_=ident[:], func=AF.Identity, scale=wq[:, h:h + 1]), reads=[ident_t, wq_t], writes=[diagall_t])
            for g0 in range(0, nkeys, 512):
                n = min(512, nkeys - g0)
                psc = pscb[pscn[0] % 2]; pscn[0] += 1
                for h in range(IH):
                    pd = nb()
                    P.op("tensor", lambda e, pd=pd, h=h, g0=g0, n=n: e.matmul(pd[:, 0:n], qiT[:, h, :], kiT[:, g0:g0 + n], start=True, stop=True), reads=[qiT_t, kiT_t], writes=[pd])
                    sg, sg_t = stage[cnt["stage"] % 2]; cnt["stage"] += 1
                    P.op("scalar", lambda e, pd=pd, sg=sg, n=n: e.activation(out=sg[:, 0:n], in_=pd[:, 0:n], func=AF.Relu), reads=[pd], writes=[sg_t])
                    P.op("tensor", lambda e, psc=psc, sg=sg, h=h, n=n: e.matmul(psc[:, 0:n], diagall[:, h, :], sg[:, 0:n], start=(h == 0), stop=(h == IH - 1)), reads=[diagall_t, sg_t], writes=[psc])
                P.op("scalar", lambda e, psc=psc, g0=g0, n=n, score=score: e.copy(out=score[:, g0:g0 + n], in_=psc[:, 0:n]), reads=[psc], writes=[score_t])
            l0 = nkeys - 256
            P.op("gpsimd", lambda e, score=score: e.tensor_tensor(out=score[:, l0:nkeys], in0=score[:, l0:nkeys], in1=cm[:], op=ALU.mult), reads=[cm_t], writes=[score_t])
            P.op("gpsimd", lambda e, score=score: e.tensor_tensor(out=score[:, l0:nkeys], in0=score[:, l0:nkeys], in1=nbig[:], op=ALU.add), reads=[nbig_t], writes=[score_t])

        def st_topk(i):
            nkeys = (2 * i + 2) * 128
            score, score_t = score2[i % 2]
            madd, madd_t = madd2[i % 2]
            if i == 0:
                P.op("vector", lambda e: e.memset(thr[:], -1e29), writes=[thr_t])
            else:
                P.op("gpsimd", lambda e, score=score: e.tensor_copy(out=work[:, 0:nkeys], in_=score[:, 0:nkeys]), reads=[score_t], writes=[work_t])
                for rnd in range(TOPK // 8):
                    P.op("vector", lambda e: e.max(out=mx[:], in_=work[:, 0:nkeys]), reads=[work_t], writes=[mx_t])
                    if rnd < TOPK // 8 - 1:
                        P.op("vector", lambda e: e.match_replace(out=work[:, 0:nkeys], in_to_replace=mx[:], in_values=work[:, 0:nkeys], imm_value=-1e30), reads=[mx_t], writes=[work_t])
                P.op("vector", lambda e: e.tensor_copy(out=thr[:], in_=mx[:, 7:8]), reads=[mx_t], writes=[thr_t])
            P.op("vector", lambda e, score=score, madd=madd: e.tensor_scalar(out=madd[:, 0:nkeys], in0=score[:, 0:nkeys], scalar1=thr[:, 0:1], scalar2=NEG, op0=ALU.is_lt, op1=ALU.mult),
                 reads=[score_t, thr_t], writes=[madd_t])

        def st_attn(i):
            q0 = i * 128
            nkb = 2 * i + 2
            r_ = i // 8
            lt0 = (i - 8 * r_) * 256
            madd, madd_t = madd2[i % 2]
            for hp in range(4):
                P.dma("sync", qf2[:, hp, :], G["qT"].rows(r_, hp * 128, 128)[:, lt0:lt0 + 256], writes=[qf2_t])
            blend2(qf2[:, :, 0:128], qf2_t, qf2[:, :, 128:256], qf2_t, qfb[:], qfb_t, qf[:], qf_t, 128)
            qz = qTz[:].rearrange("p (hp e) q -> p hp e q", e=2)
            P.op("scalar", lambda e: e.mul(out=qz[0:64, :, 0, :], in_=qf[0:64, :, :], mul=0.125), reads=[qf_t], writes=[qTz_t])
            P.op("scalar", lambda e: e.mul(out=qz[64:128, :, 1, :], in_=qf[64:128, :, :], mul=0.125), reads=[qf_t], writes=[qTz_t])
            for kb in range(nkb):
                r = kb - (nkb - 3)
                ks = slice(kb * 128, (kb + 1) * 128)
                for half in range(2):
                    pl = nb()
                    fns = []
                    for hh in range(4):
                        h = half * 4 + hh
                        hp = h // 2
                        cs = slice(hh * 128, (hh + 1) * 128)
                        near = (r >= 0)
                        fns.append(lambda e, pl=pl, cs=cs, hp=hp, h=h, ks=ks: e.matmul(pl[:, cs], kT[:, hp, ks], qTz[:, h, :], start=True, stop=False))
                        fns.append(lambda e, pl=pl, cs=cs, ks=ks, near=near, madd=madd: e.matmul(pl[:, cs], madd[:, ks], identb[:], start=False, stop=(not near)))
                        if near:
                            fns.append(lambda e, pl=pl, cs=cs, r=r, h=h: e.matmul(pl[:, cs], identb[:], NBb[:, r, h, :], start=False, stop=True))
                    P.group("tensor", fns, reads=[kT_t, qTz_t, madd_t, identb_t, NBb_t], writes=[pl])
                    pt_, pt_t = PT[cnt["PT"] % 2]; cnt["PT"] += 1
                    P.op("scalar", lambda e, pl=pl, pt_=pt_: e.activation(out=pt_[:].rearrange("p h q -> p (h q)"), in_=pl[:], func=AF.Exp), reads=[pl], writes=[pt_t])
                    fns = []
                    for hh in range(4):
                        h = half * 4 + hh
                        fns.append(lambda e, hh=hh, h=h, pt_=pt_, kb=kb, half=half: e.matmul(acc[half][:, hh * 65:(hh + 1) * 65], pt_[:, hh, :], V1[:, kb, h, :], start=(kb == 0 and hh == 0), stop=(kb == nkb - 1 and hh == 3), skip_group_check=True))
                    P.group("tensor", fns, reads=[pt_t, V1_t], writes=[acc[half]])
            yo_, yo_t = yo[0]
            for half in range(2):
                a3 = acc[half][:, 0:260].rearrange("p (h d) -> p h d", h=4)
                P.op("vector", lambda e, a3=a3, half=half: e.reciprocal(out=rec[:, half * 4:(half + 1) * 4], in_=a3[:, :, 64]), reads=[acc[half]], writes=[rec_t])
                P.op("vector", lambda e, a3=a3, half=half, yo_=yo_: e.tensor_tensor(out=yo_[:, half * 256:(half + 1) * 256].rearrange("p (h d) -> p h d", h=4), in0=a3[:, :, 0:64],
                                                                                in1=rec[:, half * 4:(half + 1) * 4].unsqueeze(2).to_broadcast([128, 4, 64]), op=ALU.mult),
                     reads=[acc[half], rec_t], writes=[yo_t])
            P.dma("sync", y[q0:q0 + 128, :], yo_[:], reads=[yo_t])

        for s_ in range(-1, nslot + 1):
            if 0 <= s_ + 1 < nslot:
                st_idx(s_ + 1)
            if 0 <= s_ < nslot:
                st_topk(s_)
            if 0 <= s_ - 1 < nslot:
                st_attn(s_ - 1)


RW = 1024
SGD = 512
ATD = 512
SG_OFF = 0
GATE_OFF = 1024


def sg_consts():
    j = np.arange(128)
    return {"sgmask": (j[:, None] <= j[None, :]).astype(np.float32),
            "identf": np.eye(128, dtype=np.float32)}


def wslot(P, C):
    s = C.cnt["w1"] % 2
    C.cnt["w1"] += 1
    return C.w1[s], C.w1_t[s]


def mixer(P, C, M, t0, load_y_branches=None):
    win_v = M.w_in.rearrange("(c p) n -> p c n", p=128)
    wbr_v = M.w_branch
    hb = C.h
    merged = lambda c: hb[:, c, :]
    yrw = lambda c: hb[:, 16 + c, :]
    yat = lambda c: hb[:, 24 + c, :]
    uT = lambda c: hb[:, 28 + c, :]
    ysg = lambda c: hb[:, 32 + c, :]
    ht = C.h_t
    def load_cast(src, nchunks, base):
        sv = src.rearrange("(c p) t -> p c t", p=128)
        for c in range(nchunks):
            s = C.cnt["sq"] % 2
            C.cnt["sq"] += 1
            sq, sqt = C.sq[s], C.sq_t[s]
            P.dma("sync", sq[:], sv[:, c, t0:t0 + TT], writes=[sqt])
            P.op("gpsimd", lambda e, sq=sq, c=c: e.tensor_copy(out=hb[:, base + c, :], in_=sq[:]), reads=[sqt], writes=[ht[base + c]])
    if getattr(M, "fused", False):
        load_y_branches(P, C, M, t0)
        sg_off, gate_off = 3360, 7024
    else:
        load_cast(M.yrwT, 8, 16)
        load_cast(M.yatT, 4, 24)
        sg_off, gate_off = SG_OFF, GATE_OFF
    wt, wtt = wslot(P, C)
    wload(P, C, wt, wtt, [(wt[:], win_v[:, :, sg_off:sg_off + 512])], f"{C.ph}_sgu", KC * 512)
    for c in range(4):
        pu = next_bank(C)
        fns = [(lambda e, k=k, pu=pu, wt=wt, c=c: e.matmul(pu[:], wt[:, k, c * 128:(c + 1) * 128], C.xb[:, k, :], start=(k == 0), stop=(k == KC - 1))) for k in range(KC)]
        P.group("tensor", fns, reads=[wtt] + C.xb_t, writes=[pu])
        P.op("scalar", lambda e, pu=pu, c=c: e.activation(out=uT(c), in_=pu[:], func=AF.Gelu), reads=[pu], writes=[ht[28 + c]])
    wt, wtt = wslot(P, C)
    wload(P, C, wt, wtt, [(wt[:], win_v[:, :, sg_off + 512:sg_off + 1024])], f"{C.ph}_sgv", KC * 512)
    for ch in range(TT // 128):
        ts = slice(ch * 128, (ch + 1) * 128)
        pv = next_bank(C)
        fns = [(lambda e, k=k, pv=pv, wt=wt, ts=ts: e.matmul(pv[:], C.xb[:, k, ts], wt[:, k, :], start=(k == 0), stop=(k == KC - 1))) for k in range(KC)]
        P.group("tensor", fns, reads=[wtt] + C.xb_t, writes=[pv])
        vg, vgt = M.vg, M.vg_t
        P.op("scalar", lambda e, pv=pv: e.activation(out=vg[:], in_=pv[:], func=AF.Gelu), reads=[pv], writes=[vgt])
        st, stt = M.st, M.st_t
        vsq, vsqt = M.vsq, M.vsq_t
        P.op("vector", lambda e: e.tensor_reduce(out=st[:, 0:1], in_=vg[:], axis=AX.X, op=ALU.add), reads=[vgt], writes=[stt])
        P.op("vector", lambda e: e.tensor_scalar(out=st[:, 0:1], in0=st[:, 0:1], scalar1=1.0 / SGD, scalar2=None, op0=ALU.mult), reads=[stt], writes=[stt])
        P.op("vector", lambda e: e.tensor_scalar(out=vg[:], in0=vg[:], scalar1=st[:, 0:1], scalar2=None, op0=ALU.subtract), reads=[stt], writes=[vgt])
        P.op("gpsimd", lambda e: e.tensor_tensor(out=vsq[:], in0=vg[:], in1=vg[:], op=ALU.mult), reads=[vgt], writes=[vsqt])
        P.op("vector", lambda e: e.tensor_reduce(out=st[:, 1:2], in_=vsq[:], axis=AX.X, op=ALU.add), reads=[vsqt], writes=[stt])
        P.op("scalar", lambda e: e.activation(out=st[:, 1:2], in_=st[:, 1:2], func=AF.Sqrt, bias=LN_EPS, scale=1.0 / SGD), reads=[stt], writes=[stt])
        P.op("vector", lambda e: e.reciprocal(out=st[:, 1:2], in_=st[:, 1:2]), reads=[stt], writes=[stt])
        P.op("vector", lambda e: e.scalar_tensor_tensor(out=vg[:], in0=vg[:], scalar=st[:, 1:2], in1=M.sglg[:], op0=ALU.mult, op1=ALU.mult), reads=[stt, M.sglg_t], writes=[vgt])
        vb, vbt = M.vb, M.vb_t
        P.op("vector", lambda e: e.tensor_tensor(out=vb[:], in0=vg[:], in1=M.sglb[:], op=ALU.add), reads=[vgt, M.sglb_t], writes=[vbt])
        for g in range(4):
            pm = next_bank(C)
            P.op("tensor", lambda e, pm=pm, g=g: e.matmul(pm[:, 0:128], vb[:, g * 128:(g + 1) * 128], M.swT[:, g, :], start=True, stop=True), reads=[vbt, M.swT_t], writes=[pm])
            mx_, mxt = M.mx, M.mx_t
            P.op("vector", lambda e, pm=pm, g=g: e.tensor_tensor(out=mx_[:], in0=pm[:, 0:128], in1=M.sgbb[:, g, :], op=ALU.add), reads=[pm, M.sgbb_t], writes=[mxt])
            P.op("vector", lambda e, g=g, ts=ts: e.tensor_tensor(out=ysg(g)[:, ts], in0=uT(g)[:, ts], in1=mx_[:], op=ALU.mult), reads=[mxt, ht[28 + g]], writes=[ht[32 + g]])
    for c in range(KC):
        cs = slice(c * 128, (c + 1) * 128)
        wt, wtt = wslot(P, C)
        parts = [(wt[:, :, b * 128:(b + 1) * 128], win_v[:, :, gate_off + b * D + c * 128:gate_off + b * D + (c + 1) * 128]) for b in range(3)]
        parts.append((wt[:, :, 384:512], wbr_v.rearrange("(c p) n -> p c n", p=128)[:, :, cs]))
        wload(P, C, wt, wtt, parts, f"{C.ph}_g{c}", KC * 512)
        gts = []
        for b in range(3):
            pg = next_bank(C)
            fns = [(lambda e, k=k, pg=pg, wt=wt, b=b: e.matmul(pg[:], wt[:, k, b * 128:(b + 1) * 128], C.xb[:, k, :], start=(k == 0), stop=(k == KC - 1))) for k in range(KC)]
            P.group("tensor", fns, reads=[wtt] + C.xb_t, writes=[pg])
            gt, gtt = M.gate[b], M.gate_t[b]
            P.op("scalar", lambda e, pg=pg, gt=gt, b=b, c=c: e.activation(out=gt[:], in_=pg[:], func=AF.Sigmoid, bias=M.bg[:, b * KC + c:b * KC + c + 1]), reads=[pg, M.bg_t], writes=[gtt])
            gts.append((gt, gtt))
        srcs = [([yrw(k) for k in range(8)], [ht[16 + k] for k in range(8)], 0),
                ([ysg(k) for k in range(4)], [ht[32 + k] for k in range(4)], 8),
                ([yat(k) for k in range(4)], [ht[24 + k] for k in range(4)], 12)]
        for b, (ops, ops_t, kb0) in enumerate(srcs):
            pz = next_bank(C)
            n = len(ops)
            fns = [(lambda e, k=k, pz=pz, wt=wt, ops=ops, kb0=kb0, n=n: e.matmul(pz[:], wt[:, kb0 + k, 384:512], ops[k], start=(k == 0), stop=(k == n - 1))) for k in range(n)]
            P.group("tensor", fns, reads=[wtt] + ops_t, writes=[pz])
            gt, gtt = gts[b]
            if b == 0:
                P.op("vector", lambda e, pz=pz, gt=gt, c=c: e.tensor_tensor(out=M.macc[:], in0=gt[:], in1=pz[:], op=ALU.mult), reads=[gtt, pz], writes=[M.macc_t])
            else:
                P.op("vector", lambda e, pz=pz, gt=gt: e.tensor_tensor(out=gt[:], in0=gt[:], in1=pz[:], op=ALU.mult), reads=[pz], writes=[gtt])
                if b == 1:
                    P.op("gpsimd", lambda e, gt=gt: e.tensor_tensor(out=M.macc[:], in0=M.macc[:], in1=gt[:], op=ALU.add), reads=[gtt], writes=[M.macc_t])
                else:
                    P.op("gpsimd", lambda e, gt=gt, c=c: e.tensor_tensor(out=merged(c), in0=M.macc[:], in1=gt[:], op=ALU.add), reads=[gtt, M.macc_t], writes=[ht[c]])
    wo_v = M.w_o.rearrange("(c p) n -> p c n", p=128)
    for c4 in range(KC // 4):
        wt, wtt = wslot(P, C)
        wload(P, C, wt, wtt, [(wt[:], wo_v[:, :, c4 * 512:(c4 + 1) * 512])], f"{C.ph}_wo{c4}", KC * 512)
        for j in range(4):
            c = c4 * 4 + j
            po = next_bank(C)
            fns = [(lambda e, k=k, po=po, wt=wt, j=j: e.matmul(po[:], wt[:, k, j * 128:(j + 1) * 128], merged(k), start=(k == 0), stop=(k == KC - 1))) for k in range(KC)]
            P.group("tensor", fns, reads=[wtt] + [ht[k] for k in range(KC)], writes=[po])
            P.op("vector", lambda e, c=c, po=po: e.scalar_tensor_tensor(out=C.xf[:, c, :], in0=C.xf[:, c, :], scalar=ALPHA, in1=po[:], op0=ALU.mult, op1=ALU.add),
                 reads=[po], writes=[C.xf_t[c]])
    layer_norm(P, C, M.ln2g, M.ln2g_t, M.ln2b, M.ln2b_t)


def build_p4(Ttot):
    nc = bass.Bass("TRN2", target_bir_lowering=False)
    din = lambda n, sh: nc.dram_tensor(n, sh, F32, kind="ExternalInput").ap()
    x1T = din("x1T", [D, Ttot]); yrwT = din("yrwT", [RW, Ttot]); yatT = din("yatT", [ATD, Ttot])
    w_in = din("w_in", [D, 7168]); b_gate = din("b_gate", [3 * D])
    sg_ln_g = din("sg_ln_g", [SGD]); sg_ln_b = din("sg_ln_b", [SGD]); sgwT = din("sgwT", [4, 128, 128]); sg_b = din("sg_b", [4, 128])
    w_branch = din("w_branch", [D, D]); w_o = din("w_o", [D, D])
    ln2g_d = din("ln2g", [D]); ln2b_d = din("ln2b", [D])
    w1 = din("w1", [D, 2 * DFF]); w2 = din("w2", [DFF, D]); ln3g_d = din("ln3g", [D]); ln3b_d = din("ln3b", [D])
    sgmask_d = din("sgmask", [128, 128])
    x3T = nc.dram_tensor("x3T", [D, Ttot], F32, kind="ExternalOutput").ap()
    import os
    DBG = "DBG" in os.environ
    if DBG:
        x2T = nc.dram_tensor("x2T", [D, Ttot], F32, kind="ExternalOutput").ap()
    with ExitStack() as es:
        P = Prog(nc, es)
        C = setup_common(P, nc)
        M = Ctx()
        M.w_in, M.w_branch, M.w_o, M.yrwT, M.yatT = w_in, w_branch, w_o, yrwT, yatT
        M.ln2g, M.ln2g_t = load_vec_cols(P, "ln2g_s", ln2g_d, KC)
        M.ln2b, M.ln2b_t = load_vec_cols(P, "ln2b_s", ln2b_d, KC)
        ln3g, ln3g_t = load_vec_cols(P, "ln3g_s", ln3g_d, KC)
        ln3b, ln3b_t = load_vec_cols(P, "ln3b_s", ln3b_d, KC)
        M.bg, M.bg_t = load_vec_cols(P, "bg_s", b_gate, 3 * KC)

        def mk(name, shape, dt=F32):
            t = P.sb(name, shape, dt)
            return t, T(t[:])
        M.sglg, M.sglg_t = mk("sglg", [128, SGD]); P.dma("sync", M.sglg[:], sg_ln_g.partition_broadcast(128), writes=[M.sglg_t])
        M.sglb, M.sglb_t = mk("sglb", [128, SGD]); P.dma("sync", M.sglb[:], sg_ln_b.partition_broadcast(128), writes=[M.sglb_t])
        M.sgbb, M.sgbb_t = mk("sgbb", [128, 4, 128])
        for g in range(4):
            P.dma("sync", M.sgbb[:, g, :], sg_b[g].partition_broadcast(128), writes=[M.sgbb_t])
        swf, swf_t = mk("swf", [128, 4, 128]); P.dma("sync", swf[:], sgwT.rearrange("g j i -> j g i"), writes=[swf_t])
        smk, smk_t = mk("smk", [128, 128]); P.dma("sync", smk[:], sgmask_d, writes=[smk_t])
        M.swT, M.swT_t = mk("swT", [128, 4, 128], BF16)
        P.op("vector", lambda e: e.tensor_tensor(out=M.swT[:], in0=swf[:], in1=smk[:].unsqueeze(1).to_broadcast([128, 4, 128]), op=ALU.mult), reads=[swf_t, smk_t], writes=[M.swT_t])
        M.vg, M.vg_t = mk("vg", [128, SGD]); M.vsq, M.vsq_t = mk("vsq", [128, SGD]); M.vb, M.vb_t = mk("vb", [128, SGD], BF16)
        M.st, M.st_t = mk("sgst", [128, 2]); M.mx, M.mx_t = mk("sgmx", [128, 128])
        M.gate = []; M.gate_t = []
        for b in range(3):
            g_, g_t = mk(f"gate{b}", [128, TT]); M.gate.append(g_); M.gate_t.append(g_t)
        M.macc, M.macc_t = mk("macc", [128, TT])
        finals = []
        for ti in range(Ttot // TT):
            t0 = ti * TT
            load_x(P, C, x1T, t0)
            mixer(P, C, M, t0)
            if DBG:
                finals += store_x(P, C, x2T, t0)
            ffn_ln(P, C, w1, w2, ln3g, ln3g_t, ln3b, ln3b_t)
            finals += store_x(P, C, x3T, t0)
        P.finish(finals)
    return nc


NL = 4
RW_OFF = 0
ATT_OFF = 4384
SGC_OFF = 3360
GATEC_OFF = 7024


class Exch:
    def __init__(self, nc, name, rows, cols, cr):
        self.S = nc.dram_tensor("S_" + name, [rows, cols], F32).ap()
        self.cr = cr
        self.nch = rows // cr
        self.G = [nc.dram_tensor(f"G_{name}{k}", [2 * cr, cols], F32).ap() for k in range(self.nch)]

    def pairs(self):
        return [(self.S[k * self.cr:(k + 1) * self.cr, :], self.G[k]) for k in range(self.nch)]

    def rows(self, rank, r0, n):
        k, off = r0 // self.cr, r0 % self.cr
        assert off + n <= self.cr
        return self.G[k][rank * self.cr + off:rank * self.cr + off + n, :]


def project_tok(P, C, w_ap, c0, ncol, out_ap, oc0, t0, stg):
    wv = w_ap.rearrange("(c p) n -> p c n", p=128)
    wt, wtt = wslot(P, C)
    if ncol == 512:
        wload(P, C, wt, wtt, [(wt[:, :, 0:ncol], wv[:, :, c0:c0 + ncol])], f"{C.ph}_pt{c0}", KC * 512)
    else:
        P.dma("gpsimd", wt[:, :, 0:ncol], wv[:, :, c0:c0 + ncol], writes=[wtt])
    for ch in range(TT // 128):
        ts = slice(ch * 128, (ch + 1) * 128)
        pp = next_bank(C)
        fns = [(lambda e, k=k, pp=pp, wt=wt, ts=ts: e.matmul(pp[:, 0:ncol], C.xb[:, k, ts], wt[:, k, 0:ncol], start=(k == 0), stop=(k == KC - 1))) for k in range(KC)]
        P.group("tensor", fns, reads=[wtt] + C.xb_t, writes=[pp])
        j = stg["cnt"] % len(stg["o"])
        stg["cnt"] += 1
        ot, ott = stg["o"][j], stg["ot"][j]
        if j % 2 == 0:
            P.op("scalar", lambda e, ot=ot, pp=pp: e.copy(out=ot[:, 0:ncol], in_=pp[:, 0:ncol]), reads=[pp], writes=[ott])
        else:
            P.op("vector", lambda e, ot=ot, pp=pp: e.tensor_copy(out=ot[:, 0:ncol], in_=pp[:, 0:ncol]), reads=[pp], writes=[ott])
        P.dma("sync", out_ap[t0 + ch * 128:t0 + (ch + 1) * 128, oc0:oc0 + ncol], ot[:, 0:ncol], reads=[ott])


def project_feat(P, C, w_ap, c0, nrows, out_ap, r0, t0, stg):
    wv = w_ap.rearrange("(c p) n -> p c n", p=128)
    for g0 in range(0, nrows, 512):
        n = min(512, nrows - g0)
        wt, wtt = wslot(P, C)
        wload(P, C, wt, wtt, [(wt[:, :, 0:n], wv[:, :, c0 + g0:c0 + g0 + n])], f"{C.ph}_pf{c0 + g0}", KC * 512)
        for j0 in range(0, n, 128):
            pp = next_bank(C)
            fns = [(lambda e, k=k, pp=pp, wt=wt, j0=j0: e.matmul(pp[:], wt[:, k, j0:j0 + 128], C.xb[:, k, :], start=(k == 0), stop=(k == KC - 1))) for k in range(KC)]
            P.group("tensor", fns, reads=[wtt] + C.xb_t, writes=[pp])
            j = stg["cnt"] % len(stg["o"])
            stg["cnt"] += 1
            ot, ott = stg["o"][j], stg["ot"][j]
            if j % 2 == 0:
                P.op("scalar", lambda e, ot=ot, pp=pp: e.copy(out=ot[:], in_=pp[:]), reads=[pp], writes=[ott])
            else:
                P.op("vector", lambda e, ot=ot, pp=pp: e.tensor_copy(out=ot[:], in_=pp[:]), reads=[pp], writes=[ott])
            P.dma("sync", out_ap[r0 + g0 + j0:r0 + g0 + j0 + 128, t0:t0 + TT], ot[:], reads=[ott])


def load_y_branches(P, C, M, t0):
    hb, ht = C.h, C.h_t
    for ch in range(TT // 128):
        tok = t0 + ch * 128
        j = (t0 // 128 + ch)
        e_ = j % 2
        cands = []
        for hf in range(2):
            yc, yct = M.ycand[hf]
            P.dma("sync", yc[:, 0:512], M.G_yrw.rows(0, hf * 2048 + tok, 128), writes=[yct])
            P.dma("sync", yc[:, 512:1024], M.G_yrw.rows(1, hf * 2048 + tok, 128), writes=[yct])
            slot = hf * 8 + j // 2
            P.dma("sync", yc[:, 1024:1536], M.G_yat.rows(e_, slot * 128, 128), writes=[yct])
            cands.append((yc, yct))
        ys, yst = M.ysel
        P.op("vector", lambda e, ys=ys, c0=cands[0][0]: e.tensor_scalar(out=ys[:], in0=c0[:], scalar1=M.msel[:, 0:1], scalar2=None, op0=ALU.mult), reads=[cands[0][1], M.msel_t], writes=[yst])
        P.op("vector", lambda e, ys=ys, c1=cands[1][0]: e.scalar_tensor_tensor(out=ys[:], in0=c1[:], scalar=M.msel[:, 1:2], in1=ys[:], op0=ALU.mult, op1=ALU.add), reads=[cands[1][1], M.msel_t], writes=[yst])
        for g in range(3):
            pb = next_bank(C)
            for q in range(4):
                c = g * 4 + q
                P.op("tensor", lambda e, pb=pb, q=q, c=c, ys=ys: e.transpose(pb[:, q * 128:(q + 1) * 128], ys[:, c * 128:(c + 1) * 128], M.identf[:]), reads=[yst, M.identf_t], writes=[pb])
            outap = hb[:, 16 + g * 4:16 + g * 4 + 4, ch * 128:(ch + 1) * 128]
            inap = pb[:].rearrange("p (q t) -> p q t", q=4)
            if g % 2 == 0:
                P.op("scalar", lambda e, outap=outap, inap=inap: e.copy(out=outap, in_=inap), reads=[pb], writes=[ht[16 + g * 4 + q] for q in range(4)])
            else:
                P.op("vector", lambda e, outap=outap, inap=inap: e.tensor_copy(out=outap, in_=inap), reads=[pb], writes=[ht[16 + g * 4 + q] for q in range(4)])


def build_fused(nlayers=NL, L=NL):
    nc = bass.Bass("TRN2", target_bir_lowering=False)
    din = lambda n, sh: nc.dram_tensor(n, sh, F32, kind="ExternalInput").ap()
    I = {}
    I["xT"] = din("xT", [D, 2048])
    for n, sh in (("ffn1_w_in", [L, D, 2 * DFF]), ("ffn1_w_out", [L, DFF, D]), ("ln1_g", [L, D]), ("ln1_b", [L, D]),
                  ("w_in", [L, D, 13168]), ("b_gate", [L, 3 * D]),
                  ("mu_c", [L, 1824]), ("w0_c", [L, 512]), ("a0_c", [L, 512]), ("k_k_c", [L, 512]), ("k_a_c", [L, 512]), ("r_k_c", [L, 512]),
                  ("gn_g_c", [L, 512]), ("gn_b_c", [L, 512]), ("w2_c", [L, 64, 512]), ("a2_c", [L, 64, 512]), ("g2_c", [L, 160, 512]),
                  ("sg_ln_g", [L, 512]), ("sg_ln_b", [L, 512]), ("sgwT", [L, 4, 128, 128]), ("sg_b", [L, 4, 128]),
                  ("idx_ln_g", [L, 64]), ("idx_ln_b", [L, 64]), ("w_branch", [L, D, D]), ("w_o", [L, D, D]),
                  ("ln2_g", [L, D]), ("ln2_b", [L, D]), ("ffn2_w_in", [L, D, 2 * DFF]), ("ffn2_w_out", [L, DFF, D]), ("ln3_g", [L, D]), ("ln3_b", [L, D]),
                  ("msel", [128, 2]), ("sgmask", [128, 128])):
        I[n] = din(n, sh)
    rc = {n: din("rc_" + n, list(v.shape)) for n, v in rwkv_consts().items()}
    dc = {n: din("dc_" + n, list(v.shape)) for n, v in dsa_consts(0, np.zeros((32, 8), np.float32)).items()}
    xoT = nc.dram_tensor("xoT", [D, 2048], F32, kind="ExternalOutput").ap()
    dt_ = lambda n, sh: nc.dram_tensor(n, sh, F32).ap()
    x1s = dt_("x1s", [D, 2048]); xcur = dt_("xcur", [D, 2048])
    X_prw = Exch(nc, "prw", 2048, 3360, 128); X_qT = Exch(nc, "qT", 512, 2048, 256); X_kT = Exch(nc, "kT", 512, 2048, 256)
    X_qiT = Exch(nc, "qiT", 1024, 2048, 256); X_V = Exch(nc, "V", 2048, 512, 1024); X_kw = Exch(nc, "kw", 2048, 80, 2048)
    X_yrw = Exch(nc, "yrw", 4096, 512, 1024); X_yat = Exch(nc, "yat", 2048, 512, 1024)
    S_prw, S_qT, S_kT, S_qiT, S_V, S_kw, S_yrw, S_yat = X_prw.S, X_qT.S, X_kT.S, X_qiT.S, X_V.S, X_kw.S, X_yrw.S, X_yat.S
    groups = [[0, 1], [2, 3], [4, 5], [6, 7]]
    with ExitStack() as es:
        P = Prog(nc, es)
        banks = [T(P.ps(f"bank{i}", [128, 512], F32)) for i in range(8)]

        def mk(name, shape, dt=F32):
            t = P.sb(name, shape, dt)
            return t, T(t[:])

        def exchange(pairs):
            P.barrier()
            for (s_, g_) in pairs:
                P.coll("AllGather", groups, s_, g_)
            P.barrier()

        WC = {"t": {}, "ev": {}}
        for l in range(nlayers):
            P.push_scope()
            C = setup_common(P, nc, banks)
            C.wc = WC; C.ph = "p1"
            lng, lng_t = load_vec_cols(P, "ln1g", I["ln1_g"][l], KC)
            lnb, lnb_t = load_vec_cols(P, "ln1b", I["ln1_b"][l], KC)
            stg = {"o": [], "ot": [], "cnt": 0}
            for i in range(3):
                o_, ot_ = mk(f"stg{i}", [128, TT]); stg["o"].append(o_); stg["ot"].append(ot_)
            xsrc = I["xT"] if l == 0 else xcur
            w_in_l = I["w_in"][l]
            for ti in range(2048 // TT):
                t0 = ti * TT
                C.ti = ti
                load_x(P, C, xsrc, t0)
                ffn_ln(P, C, I["ffn1_w_in"][l], I["ffn1_w_out"][l], lng, lng_t, lnb, lnb_t)
                store_x(P, C, x1s, t0)
                for c0 in range(0, 3360, 512):
                    project_tok(P, C, w_in_l, RW_OFF + c0, min(512, 3360 - c0), S_prw, c0, t0, stg)
                project_feat(P, C, w_in_l, ATT_OFF + 0, 512, S_qT, 0, t0, stg)
                project_feat(P, C, w_in_l, ATT_OFF + 512, 512, S_kT, 0, t0, stg)
                project_tok(P, C, w_in_l, ATT_OFF + 1024, 512, S_V, 0, t0, stg)
                project_feat(P, C, w_in_l, ATT_OFF + 1536, 1024, S_qiT, 0, t0, stg)
                project_tok(P, C, w_in_l, ATT_OFF + 2560, 80, S_kw, 0, t0, stg)
            exchange(X_prw.pairs() + X_qT.pairs() + X_kT.pairs() + X_qiT.pairs() + X_V.pairs() + X_kw.pairs())
            P.pop_scope()
            P.push_scope()
            prm = {"mu": I["mu_c"][l], "w0": I["w0_c"][l], "a0": I["a0_c"][l], "k_k": I["k_k_c"][l], "k_a": I["k_a_c"][l], "r_k": I["r_k_c"][l],
                   "gn_g": I["gn_g_c"][l], "gn_b": I["gn_b_c"][l], "w2": I["w2_c"][l], "a2": I["a2_c"][l], "g2": I["g2_c"][l]}
            rwkv_phase(P, banks, X_prw, prm, rc, I["msel"], S_yrw)
            P.barrier()
            P.pop_scope()
            P.push_scope()
            dsa_phase(P, banks, {"kT": X_kT, "V": X_V, "kw": X_kw, "qT": X_qT, "qiT": X_qiT}, I["idx_ln_g"][l], I["idx_ln_b"][l], dc, I["msel"], S_yat)
            exchange(X_yrw.pairs() + X_yat.pairs())
            P.pop_scope()
            P.push_scope()
            C = setup_common(P, nc, banks)
            C.wc = WC; C.ph = "p4"
            M = Ctx()
            M.w_in, M.w_branch, M.w_o = I["w_in"][l], I["w_branch"][l], I["w_o"][l]
            M.G_yrw, M.G_yat = X_yrw, X_yat
            M.ln2g, M.ln2g_t = load_vec_cols(P, "ln2g", I["ln2_g"][l], KC)
            M.ln2b, M.ln2b_t = load_vec_cols(P, "ln2b", I["ln2_b"][l], KC)
            ln3g, ln3g_t = load_vec_cols(P, "ln3g", I["ln3_g"][l], KC)
            ln3b, ln3b_t = load_vec_cols(P, "ln3b", I["ln3_b"][l], KC)
            M.bg, M.bg_t = load_vec_cols(P, "bg", I["b_gate"][l], 3 * KC)
            M.msel, M.msel_t = mk("msel4", [128, 2]); P.dma("sync", M.msel[:], I["msel"], writes=[M.msel_t])
            M.identf, M.identf_t = mk("identf", [128, 128]); P.dma("sync", M.identf[:], rc["ident"], writes=[M.identf_t])
            M.sglg, M.sglg_t = mk("sglg", [128, SGD]); P.dma("sync", M.sglg[:], I["sg_ln_g"][l].partition_broadcast(128), writes=[M.sglg_t])
            M.sglb, M.sglb_t = mk("sglb", [128, SGD]); P.dma("sync", M.sglb[:], I["sg_ln_b"][l].partition_broadcast(128), writes=[M.sglb_t])
            M.sgbb, M.sgbb_t = mk("sgbb", [128, 4, 128])
            for g in range(4):
                P.dma("sync", M.sgbb[:, g, :], I["sg_b"][l][g].partition_broadcast(128), writes=[M.sgbb_t])
            swf, swf_t = mk("swf", [128, 4, 128]); P.dma("sync", swf[:], I["sgwT"][l].rearrange("g j i -> j g i"), writes=[swf_t])
            smk, smk_t = mk("smk", [128, 128]); P.dma("sync", smk[:], I["sgmask"], writes=[smk_t])
            M.swT, M.swT_t = mk("swT", [128, 4, 128], BF16)
            P.op("vector", lambda e, M=M, swf=swf, smk=smk: e.tensor_tensor(out=M.swT[:], in0=swf[:], in1=smk[:].unsqueeze(1).to_broadcast([128, 4, 128]), op=ALU.mult), reads=[swf_t, smk_t], writes=[M.swT_t])
            M.vg, M.vg_t = mk("vg", [128, SGD]); M.vsq, M.vsq_t = mk("vsq", [128, SGD]); M.vb, M.vb_t = mk("vb", [128, SGD], BF16)
            M.st, M.st_t = mk("sgst", [128, 2]); M.mx, M.mx_t = mk("sgmx", [128, 128])
            M.gate = []; M.gate_t = []
            for b in range(3):
                g_, g_t = mk(f"gate{b}", [128, TT]); M.gate.append(g_); M.gate_t.append(g_t)
            M.macc, M.macc_t = mk("macc", [128, TT])
            M.ycand = [mk(f"ycand{i}", [128, 1536]) for i in range(2)]
            M.ysel = mk("ysel", [128, 1536])
            M.fused = True
            for ti in range(2048 // TT):
                t0 = ti * TT
                C.ti = ti
                load_x(P, C, x1s, t0)
                mixer(P, C, M, t0, load_y_branches)
                ffn_ln(P, C, I["ffn2_w_in"][l], I["ffn2_w_out"][l], ln3g, ln3g_t, ln3b, ln3b_t)
                fin = store_x(P, C, xoT if l == nlayers - 1 else xcur, t0)
            P.barrier()
            P.pop_scope()
        P.finish(P.all_events())
    return nc


_FUSED = {}


def _c(a):
    return np.ascontiguousarray(a, dtype=np.float32)


def make_in_maps(inp):
    x = np.asarray(inp["x"], dtype=np.float32)
    B, S_, D_ = x.shape
    xf = x.reshape(B * S_, D_)
    NCORE = 8
    TSH = (B * S_) // NCORE
    shared = {}
    for k in ("ffn1_w_in", "ffn1_w_out", "ln1_g", "ln1_b", "w_in", "b_gate", "sg_ln_g", "sg_ln_b", "sg_b", "idx_ln_g", "idx_ln_b",
              "w_branch", "w_o", "ln2_g", "ln2_b", "ffn2_w_in", "ffn2_w_out", "ln3_g", "ln3_b"):
        shared[k] = _c(inp[k])
    shared["sgwT"] = _c(np.transpose(np.asarray(inp["sg_w"], dtype=np.float32), (0, 1, 3, 2)))
    shared["sgmask"] = _c(sg_consts()["sgmask"])
    for n, v in rwkv_consts().items():
        shared["rc_" + n] = _c(v)
    rel_bias = np.asarray(inp["rel_bias"], dtype=np.float32)
    L = shared["w_in"].shape[0]
    per_e = []
    for e in range(2):
        m = {}
        hs = slice(e * 512, (e + 1) * 512)
        cols = np.r_[e * 512:(e + 1) * 512, 1024 + e * 512:1024 + (e + 1) * 512, 2048 + e * 512:2048 + (e + 1) * 512, 3072:3360]
        m["mu_c"] = _c(np.asarray(inp["rwkv_mu"])[:, cols])
        for n, src in (("w0_c", "rwkv_w0"), ("a0_c", "rwkv_a0"), ("k_k_c", "rwkv_k_k"), ("k_a_c", "rwkv_k_a"), ("gn_g_c", "rwkv_gn_g"), ("gn_b_c", "rwkv_gn_b")):
            m[n] = _c(np.asarray(inp[src])[:, hs])
        m["r_k_c"] = _c(np.asarray(inp["rwkv_r_k"]).reshape(L, -1)[:, hs])
        for n, src in (("w2_c", "rwkv_w2"), ("a2_c", "rwkv_a2"), ("g2_c", "rwkv_g2")):
            m[n] = _c(np.asarray(inp[src])[:, :, hs])
        for n, v in dsa_consts(e, rel_bias).items():
            m["dc_" + n] = _c(v)
        ms = np.zeros((128, 2), np.float32)
        ms[:, e] = 1.0
        m["msel"] = ms
        per_e.append(m)
    in_maps = []
    for c in range(NCORE):
        m = dict(shared)
        m.update(per_e[c % 2])
        m["xT"] = _c(xf[c * TSH:(c + 1) * TSH].T)
        in_maps.append(m)
    return in_maps, (B, S_, D_)


def kernel(**inp):
    if "nc" not in _FUSED:
        _FUSED["nc"] = build_fused()
    in_maps, (B, S_, D_) = make_in_maps(inp)
    cores = list(range(8))
    res = run_bass_kernel_spmd(_FUSED["nc"], in_maps, core_ids=cores)
    out = np.concatenate([np.ascontiguousarray(res.results[c]["xoT"].T) for c in cores], axis=0).reshape(B, S_, D_)
    return out.astype(np.float32)
```

```python
import math


import numpy as np
from contextlib import ExitStack
import concourse.bass as bass
import concourse.mybir as mybir
from concourse.bass_utils import run_bass_kernel_spmd

F32 = mybir.dt.float32
BF16 = mybir.dt.bfloat16
AF = mybir.ActivationFunctionType
ALU = mybir.AluOpType
AX = mybir.AxisListType

ENGS = ("sync", "scalar", "vector", "gpsimd", "tensor")
SEM_ROLL = 30000


class T:
    __slots__ = ("ap", "w", "r")

    def __init__(self, ap):
        self.ap = ap
        self.w = None
        self.r = []

    def __getitem__(self, idx):
        return self.ap[idx]


class Prog:
    def __init__(self, nc, es, n_dma_sems=24):
        self.nc = nc
        self.es = es
        self.q = {e: [] for e in ENGS}
        self.cur_sem = {}
        self.cnt = {}
        self.nsem = 0
        for e in ENGS:
            self._new_eng_sem(e)
        self.dma_sems = [es.enter_context(nc.semaphore(f"dma{i}")) for i in range(2 * n_dma_sems)]
        self.dma_cnt = [0] * (2 * n_dma_sems)
        self.dma_last = [None] * (2 * n_dma_sems)
        self.dma_rr = {"hw": 0, "sw": 0}
        self.n_dma_sems = n_dma_sems
        self.waited = {e: {} for e in ENGS}
        self.semobj = {}
        self.n_inst = {e: 0 for e in ENGS}
        self.n_wait = 0

    def _new_eng_sem(self, e):
        s = self.es.enter_context(self.nc.semaphore(f"s_{e}_{self.nsem}"))
        self.nsem += 1
        self.cur_sem[e] = s
        self.cnt[e] = 0

    def sb(self, name, shape, dt):
        self.uid = getattr(self, "uid", 0) + 1
        es = self.scopes[-1] if getattr(self, "scopes", None) else self.es
        return es.enter_context(self.nc.sbuf_tensor(f"{name}_u{self.uid}", list(shape), dt))

    def push_scope(self):
        if not hasattr(self, "scopes"):
            self.scopes = []
        es = ExitStack()
        es.__enter__()
        self.scopes.append(es)

    def pop_scope(self):
        es = self.scopes.pop()
        es.__exit__(None, None, None)

    def all_events(self):
        evs = [(self.cur_sem[e], self.cnt[e]) for e in ENGS if self.cnt[e] > 0]
        evs += [x for x in self.dma_last if x is not None]
        evs += list(getattr(self, "coll_events", []))
        return evs

    def barrier(self):
        evs = self.all_events()
        for e in ENGS:
            w = self._filter_waits(e, evs)
            if w:
                self.q[e].append((None, w, None, 0))

    def coll(self, kind, groups, src, dst):
        if not hasattr(self, "coll_sem"):
            self.coll_sem = self.es.enter_context(self.nc.semaphore("collsem"))
            self.coll_cnt = 0
        self.coll_cnt += 1
        ev = (self.coll_sem, self.coll_cnt)

        def fn(e):
            return e.collective_compute(kind, ALU.bypass, replica_groups=groups, ins=[src.opt()], outs=[dst.opt()])
        self.q["gpsimd"].append((fn, [], self.coll_sem, 1))
        self.coll_events = [ev]
        return ev

    def ps(self, name, shape, dt=F32):
        return self.es.enter_context(self.nc.psum_tensor(name, list(shape), dt))

    def _collect(self, reads, writes):
        waits = []
        for t in reads:
            if t.w is not None:
                waits.append(t.w)
        for t in writes:
            if t.w is not None:
                waits.append(t.w)
            waits.extend(t.r)
        return waits

    def _filter_waits(self, eng, waits):
        out = {}
        wd = self.waited[eng]
        for (sem, val) in waits:
            k = id(sem)
            self.semobj[k] = sem
            if wd.get(k, 0) >= val:
                continue
            if out.get(k, 0) < val:
                out[k] = val
        res = []
        for k, v in out.items():
            wd[k] = v
            res.append((self.semobj[k], v))
        return res

    def op(self, eng, fn, reads=(), writes=(), extra=()):
        waits = self._collect(reads, writes) + list(extra)
        waits = self._filter_waits(eng, waits)
        if self.cnt[eng] >= SEM_ROLL:
            self._new_eng_sem(eng)
        sem = self.cur_sem[eng]
        self.cnt[eng] += 1
        ev = (sem, self.cnt[eng])
        self.q[eng].append((fn, waits, sem, 1))
        self.n_inst[eng] += 1
        self.n_wait += len(waits)
        for t in reads:
            t.r.append(ev)
        for t in writes:
            t.w = ev
            t.r = []
        return ev

    def group(self, eng, fns, reads=(), writes=()):
        waits = self._collect(reads, writes)
        waits = self._filter_waits(eng, waits)
        if self.cnt[eng] >= SEM_ROLL:
            self._new_eng_sem(eng)
        sem = self.cur_sem[eng]
        self.cnt[eng] += 1
        ev = (sem, self.cnt[eng])
        n = len(fns)
        for i, fn in enumerate(fns):
            self.q[eng].append((fn, waits if i == 0 else [], sem if i == n - 1 else None, 1))
        self.n_inst[eng] += n
        for t in reads:
            t.r.append(ev)
        for t in writes:
            t.w = ev
            t.r = []
        return ev

    def dma(self, queue, out, in_, reads=(), writes=(), extra=(), **kw):
        kind = "sw" if queue == "gpsimd" else "hw"
        i = self.dma_rr[kind] + (self.n_dma_sems if kind == "sw" else 0)
        self.dma_rr[kind] = (self.dma_rr[kind] + 1) % self.n_dma_sems
        waits = self._collect(reads, writes) + list(extra)
        if self.dma_last[i] is not None:
            waits.append(self.dma_last[i])
        waits = self._filter_waits(queue, waits)
        sem = self.dma_sems[i]
        self.dma_cnt[i] += 16
        ev = (sem, self.dma_cnt[i])
        self.dma_last[i] = ev

        def fn(e, out=out, in_=in_, kw=kw):
            return e.dma_start(out=out, in_=in_, **kw)
        self.q[queue].append((fn, waits, sem, 16))
        self.n_inst[queue] += 1
        for t in reads:
            t.r.append(ev)
        for t in writes:
            t.w = ev
            t.r = []
        return ev

    def finish(self, final_events):
        nc = self.nc
        fw = list(final_events)

        def run(e, name):
            for (fn, waits, sem, inc) in self.q[name]:
                for (s, v) in waits:
                    e.wait_ge(s, v)
                if fn is None:
                    continue
                ins = fn(e)
                if sem is not None:
                    ins.then_inc(sem, inc)
            if name == "sync":
                for (s, v) in fw:
                    e.wait_ge(s, v)

        with nc.Block() as block:
            @block.sync
            def _(e):
                run(e, "sync")

            @block.scalar
            def _(e):
                run(e, "scalar")

            @block.vector
            def _(e):
                run(e, "vector")

            @block.gpsimd
            def _(e):
                run(e, "gpsimd")

            @block.tensor
            def _(e):
                run(e, "tensor")


D = 2048
DFF = 5632
KC = D // 128
HC = DFF // 128
TT = 512
ALPHA = 8 ** 0.25
LN_EPS = 1e-5


class Ctx:
    pass


def setup_common(P, nc, banks=None):
    C = Ctx()
    C.banks = banks if banks is not None else [T(P.ps(f"bank{i}", [128, 512], F32)) for i in range(8)]
    C.bank_rr = 0
    C.xb = P.sb("xb", [128, KC, TT], BF16)
    C.xf = P.sb("xf", [128, KC, TT], F32)
    C.h = P.sb("h", [128, HC, TT], BF16)
    C.xb_t = [T(C.xb[:, c, :]) for c in range(KC)]
    C.xf_t = [T(C.xf[:, c, :]) for c in range(KC)]
    C.h_t = [T(C.h[:, c, :]) for c in range(HC)]
    C.w1 = [P.sb(f"w1_{i}", [128, KC, 512], BF16) for i in range(2)]
    C.w1_t = [T(C.w1[i][:]) for i in range(2)]
    C.w2 = [P.sb(f"w2_{i}", [128, HC, 128], BF16) for i in range(2)]
    C.w2_t = [T(C.w2[i][:]) for i in range(2)]
    C.sg = [P.sb(f"sg_{i}", [128, TT], F32) for i in range(2)]
    C.sg_t = [T(C.sg[i][:]) for i in range(2)]
    C.sq = [P.sb(f"sq_{i}", [128, TT], F32) for i in range(2)]
    C.sq_t = [T(C.sq[i][:]) for i in range(2)]
    C.rstd = P.sb("rstd", [128, TT], F32)
    C.rstd_t = T(C.rstd[:])
    C.onesM = P.sb("onesM", [128, 128], F32)
    C.onesM_t = T(C.onesM[:])
    P.op("vector", lambda e: e.memset(C.onesM[:], 1.0 / D), writes=[C.onesM_t])
    C.cnt = {"w1": 0, "w2": 0, "sg": 0, "sq": 0}
    C.wc = None; C.ph = "x"; C.ti = 0
    return C


def wload(P, C, wt, wtt, parts, key, nfree):
    wc = getattr(C, "wc", None)
    if wc is None:
        for (sap, dap) in parts:
            P.dma("gpsimd", sap, dap, writes=[wtt])
        return
    if key not in wc["t"]:
        wc["t"][key] = (P.nc.dram_tensor(f"wsc_{key}", [128, nfree], BF16).ap(), None)
    sc = wc["t"][key][0]
    flat = wt[:].rearrange("p a b -> p (a b)")[:, 0:nfree] if len(wt[:].shape) == 3 else wt[:, 0:nfree]
    if C.ti == 0:
        for (sap, dap) in parts:
            P.dma("gpsimd", sap, dap, writes=[wtt])
        ev = P.dma("sync", sc, flat, reads=[wtt])
        wc["ev"][key] = ev
    else:
        P.dma("sync", flat, sc, writes=[wtt], extra=[wc["ev"][key]])


def next_bank(C):
    b = C.banks[C.bank_rr]
    C.bank_rr = (C.bank_rr + 1) % 8
    return b


def load_vec_cols(P, name, src, n):
    t = P.sb(name, [128, n], F32)
    tt = T(t[:])
    P.dma("sync", t[:], src.rearrange("(c p) -> p c", p=128), writes=[tt], allow_slow_non_contiguous=True)
    return t, tt


def ffn_ln(P, C, w_in, w_out, lng, lng_t, lnb, lnb_t):
    nc = P.nc
    w_in_v = w_in.rearrange("(c p) n -> p c n", p=128)
    w_out_v = w_out.rearrange("(c p) n -> p c n", p=128)
    for hg in range(HC // 2):
        slot = C.cnt["w1"] % 2
        C.cnt["w1"] += 1
        wt, wtt = C.w1[slot], C.w1_t[slot]
        wload(P, C, wt, wtt, [(wt[:, :, 0:256], w_in_v[:, :, hg * 256:(hg + 1) * 256]), (wt[:, :, 256:512], w_in_v[:, :, DFF + hg * 256:DFF + (hg + 1) * 256])],
              f"{C.ph}_a{hg}", KC * 512)
        for j in range(2):
            ht = hg * 2 + j
            pg = next_bank(C)
            pu = next_bank(C)
            fns = []
            for k in range(KC):
                fns.append(lambda e, k=k, pg=pg, j=j, wt=wt: e.matmul(pg[:], wt[:, k, j * 128:(j + 1) * 128], C.xb[:, k, :], start=(k == 0), stop=(k == KC - 1)))
            P.group("tensor", fns, reads=[wtt] + C.xb_t, writes=[pg])
            fns = []
            for k in range(KC):
                fns.append(lambda e, k=k, pu=pu, j=j, wt=wt: e.matmul(pu[:], wt[:, k, 256 + j * 128:256 + (j + 1) * 128], C.xb[:, k, :], start=(k == 0), stop=(k == KC - 1)))
            P.group("tensor", fns, reads=[wtt] + C.xb_t, writes=[pu])
            s = C.cnt["sg"] % 2
            C.cnt["sg"] += 1
            sg, sgt = C.sg[s], C.sg_t[s]
            P.op("scalar", lambda e, sg=sg, pg=pg: e.activation(out=sg[:], in_=pg[:], func=AF.Silu), reads=[pg], writes=[sgt])
            P.op("vector", lambda e, sg=sg, pu=pu, ht=ht: e.scalar_tensor_tensor(out=C.h[:, ht, :], in0=sg[:], scalar=0.5, in1=pu[:], op0=ALU.mult, op1=ALU.mult),
                 reads=[sgt, pu], writes=[C.h_t[ht]])
    for dt_ in range(KC):
        slot = C.cnt["w2"] % 2
        C.cnt["w2"] += 1
        wt, wtt = C.w2[slot], C.w2_t[slot]
        wload(P, C, wt, wtt, [(wt[:, 0:HC // 2, :], w_out_v[:, 0:HC // 2, dt_ * 128:(dt_ + 1) * 128]), (wt[:, HC // 2:, :], w_out_v[:, HC // 2:, dt_ * 128:(dt_ + 1) * 128])],
              f"{C.ph}_b{dt_}", HC * 128)
        py = next_bank(C)
        fns = []
        for k in range(HC):
            fns.append(lambda e, k=k, py=py, wt=wt: e.matmul(py[:], wt[:, k, :], C.h[:, k, :], start=(k == 0), stop=(k == HC - 1)))
        P.group("tensor", fns, reads=[wtt] + C.h_t, writes=[py])
        P.op("vector", lambda e, dt_=dt_, py=py: e.scalar_tensor_tensor(out=C.xf[:, dt_, :], in0=C.xf[:, dt_, :], scalar=ALPHA, in1=py[:], op0=ALU.mult, op1=ALU.add),
             reads=[py], writes=[C.xf_t[dt_]])
    layer_norm(P, C, lng, lng_t, lnb, lnb_t)


def layer_norm(P, C, lng, lng_t, lnb, lnb_t):
    pm = next_bank(C)
    fns = []
    for c in range(KC):
        fns.append(lambda e, c=c: e.matmul(pm[:], C.onesM[:], C.xf[:, c, :], start=(c == 0), stop=(c == KC - 1)))
    P.group("tensor", fns, reads=[C.onesM_t] + C.xf_t, writes=[pm])
    pv = next_bank(C)
    for c in range(KC):
        P.op("vector", lambda e, c=c: e.tensor_tensor(out=C.xf[:, c, :], in0=C.xf[:, c, :], in1=pm[:], op=ALU.subtract),
             reads=[pm], writes=[C.xf_t[c]])
        s = C.cnt["sq"] % 2
        C.cnt["sq"] += 1
        sq, sqt = C.sq[s], C.sq_t[s]
        P.op("scalar", lambda e, c=c, sq=sq: e.activation(out=sq[:], in_=C.xf[:, c, :], func=AF.Square), reads=[C.xf_t[c]], writes=[sqt])
        P.op("tensor", lambda e, c=c, sq=sq: e.matmul(pv[:], C.onesM[:], sq[:], start=(c == 0), stop=(c == KC - 1)),
             reads=[sqt, C.onesM_t], writes=[pv])
    P.op("scalar", lambda e: e.activation(out=C.rstd[:], in_=pv[:], func=AF.Sqrt, bias=LN_EPS), reads=[pv], writes=[C.rstd_t])
    P.op("vector", lambda e: e.reciprocal(out=C.rstd[:], in_=C.rstd[:]), reads=[C.rstd_t], writes=[C.rstd_t])
    for c in range(KC):
        P.op("vector", lambda e, c=c: e.tensor_tensor(out=C.xf[:, c, :], in0=C.xf[:, c, :], in1=C.rstd[:], op=ALU.mult),
             reads=[C.rstd_t], writes=[C.xf_t[c]])
        P.op("scalar", lambda e, c=c: e.activation(out=C.xf[:, c, :], in_=C.xf[:, c, :], func=AF.Identity, scale=lng[:, c:c + 1], bias=lnb[:, c:c + 1]),
             reads=[lng_t, lnb_t], writes=[C.xf_t[c]])
        P.op("gpsimd", lambda e, c=c: e.tensor_copy(out=C.xb[:, c, :], in_=C.xf[:, c, :]), reads=[C.xf_t[c]], writes=[C.xb_t[c]])


def load_x(P, C, xT, t0):
    xv = xT.rearrange("(c p) t -> p c t", p=128)
    for c in range(KC):
        P.dma("sync", C.xf[:, c, :], xv[:, c, t0:t0 + TT], writes=[C.xf_t[c]])
    for c in range(KC):
        P.op("gpsimd", lambda e, c=c: e.tensor_copy(out=C.xb[:, c, :], in_=C.xf[:, c, :]), reads=[C.xf_t[c]], writes=[C.xb_t[c]])


def store_x(P, C, oT, t0):
    ov = oT.rearrange("(c p) t -> p c t", p=128)
    evs = []
    for c in range(KC):
        evs.append(P.dma("sync", ov[:, c, t0:t0 + TT], C.xf[:, c, :], reads=[C.xf_t[c]]))
    return evs


def project(P, C, wp, col_tiles, outT, t0, wslots, evac_rr=[0]):
    wp_v = wp.rearrange("(c p) n -> p c n", p=128)
    evs = []
    for (c0, ncol, r0) in col_tiles:
        i = wslots["cnt"] % len(wslots["t"])
        wslots["cnt"] += 1
        wt, wtt = wslots["t"][i], wslots["tt"][i]
        P.dma("gpsimd", wt[:, :, 0:ncol], wp_v[:, :, c0:c0 + ncol], writes=[wtt])
        pp = next_bank(C)
        fns = []
        for k in range(KC):
            fns.append(lambda e, k=k, pp=pp, wt=wt, ncol=ncol: e.matmul(pp[0:ncol, :], wt[:, k, 0:ncol], C.xb[:, k, :], start=(k == 0), stop=(k == KC - 1)))
        P.group("tensor", fns, reads=[wtt] + C.xb_t, writes=[pp])
        j = wslots["ocnt"] % len(wslots["o"])
        wslots["ocnt"] += 1
        ot, ott = wslots["o"][j], wslots["ot"][j]
        eng = "scalar" if (j % 2 == 0) else "vector"
        if eng == "scalar":
            P.op("scalar", lambda e, ot=ot, pp=pp, ncol=ncol: e.copy(out=ot[0:ncol, :], in_=pp[0:ncol, :]), reads=[pp], writes=[ott])
        else:
            P.op("vector", lambda e, ot=ot, pp=pp, ncol=ncol: e.tensor_copy(out=ot[0:ncol, :], in_=pp[0:ncol, :]), reads=[pp], writes=[ott])
        evs.append(P.dma("sync", outT[r0:r0 + ncol, t0:t0 + TT], ot[0:ncol, :], reads=[ott]))
    return evs


def make_proj_slots(P, n=3, no=3):
    w = [P.sb(f"wp_{i}", [128, KC, 128], BF16) for i in range(n)]
    o = [P.sb(f"po_{i}", [128, TT], F32) for i in range(no)]
    return {"t": w, "tt": [T(x[:]) for x in w], "cnt": 0, "o": o, "ot": [T(x[:]) for x in o], "ocnt": 0}


def col_tiles_for(ranges):
    tiles = []
    r = 0
    for (s, n) in ranges:
        o = 0
        while o < n:
            m = min(128, n - o)
            tiles.append((s + o, m, r))
            r += m
            o += m
    return tiles, r


def build_p1(Ttot, proj_ranges):
    nc = bass.Bass("TRN2", target_bir_lowering=False)
    tiles, NP = col_tiles_for(proj_ranges)
    xT = nc.dram_tensor("xT", [D, Ttot], F32, kind="ExternalInput").ap()
    w1 = nc.dram_tensor("w1", [D, 2 * DFF], F32, kind="ExternalInput").ap()
    w2 = nc.dram_tensor("w2", [DFF, D], F32, kind="ExternalInput").ap()
    lng_d = nc.dram_tensor("lng", [D], F32, kind="ExternalInput").ap()
    lnb_d = nc.dram_tensor("lnb", [D], F32, kind="ExternalInput").ap()
    wp = nc.dram_tensor("wp", [D, NP], F32, kind="ExternalInput").ap()
    x1T = nc.dram_tensor("x1T", [D, Ttot], F32, kind="ExternalOutput").ap()
    pT = nc.dram_tensor("pT", [NP, Ttot], F32, kind="ExternalOutput").ap()
    with ExitStack() as es:
        P = Prog(nc, es)
        C = setup_common(P, nc)
        lng, lng_t = load_vec_cols(P, "lng_s", lng_d, KC)
        lnb, lnb_t = load_vec_cols(P, "lnb_s", lnb_d, KC)
        ws = make_proj_slots(P)
        finals = []
        for ti in range(Ttot // TT):
            t0 = ti * TT
            load_x(P, C, xT, t0)
            ffn_ln(P, C, w1, w2, lng, lng_t, lnb, lnb_t)
            finals += store_x(P, C, x1T, t0)
            finals += project(P, C, wp, tiles, pT, t0, ws)
        P.finish(finals)
    return nc, NP


S = 4096
NH = 8
HN = 512
NCOL = 1824
C0 = math.exp(-0.5)
GN_EPS = 64e-5


def rwkv_consts():
    ident = np.eye(128, dtype=np.float32)
    s = np.arange(128)
    same = (s[:, None] // 64) == (s[None, :] // 64)
    LT = np.where(same & (s[:, None] <= s[None, :]), -C0, 0.0).astype(np.float32)
    BT = np.where(same, -C0, 0.0).astype(np.float32)
    BTc = np.zeros((128, 2), np.float32)
    BTc[:64, 0] = -C0
    BTc[64:, 1] = -C0
    sl = s % 64
    t = np.arange(64)
    mk1 = np.concatenate([(sl[:, None] < t[None, :]), (sl[:, None] <= t[None, :])], 1).astype(np.float32)
    mk3 = (t[None, :] < sl[:, None]).astype(np.float32)
    i2 = (sl[:, None] == t[None, :]).astype(np.float32)
    return {"ident": ident, "LT": LT, "BT": BT, "BTc": BTc, "mk1": mk1, "mk3": mk3, "i2": i2}


def rwkv_phase(P, banks, Gp, prm, cst, msel_d, y):
    ntiles = S // 128
    DBG = False
    STAGE = 99
    SUB = 0
    mu = prm["mu"]; vecs = {n: prm[n] for n in ("w0", "a0", "k_k", "k_a", "r_k", "gn_g", "gn_b")}
    w2 = prm["w2"]; a2 = prm["a2"]; g2 = prm["g2"]
    if True:
        rr = [0]

        def nb():
            b = banks[rr[0]]
            rr[0] = (rr[0] + 1) % 8
            return b

        def mk(name, shape, dt=F32):
            t = P.sb(name, shape, dt)
            return t, T(t[:])

        msel, msel_t = mk("msel", [128, 2]); P.dma("sync", msel[:], msel_d, writes=[msel_t])
        PA = [mk(f"PA{i}", [128, 3360]) for i in range(2)]
        mu_bc, mu_t = mk("mu_bc", [128, NCOL])
        P.dma("sync", mu_bc[:], mu.partition_broadcast(128), writes=[mu_t])
        vb = {}
        for n, ap in vecs.items():
            vb[n] = mk(n + "_bc", [128, HN])
            P.dma("sync", vb[n][0][:], ap.partition_broadcast(128), writes=[vb[n][1]])
        w2s, w2t = mk("w2s", [64, HN]); P.dma("sync", w2s[:], w2, writes=[w2t])
        a2s, a2t = mk("a2s", [64, HN]); P.dma("sync", a2s[:], a2, writes=[a2t])
        g2a, g2at = mk("g2a", [128, HN]); P.dma("sync", g2a[:], g2[0:128, :], writes=[g2at])
        g2b, g2bt = mk("g2b", [32, HN]); P.dma("sync", g2b[:], g2[128:160, :], writes=[g2bt])
        cs = {}
        for n, ap in cst.items():
            cs[n] = mk(n + "_s", list(ap.shape))
            P.dma("sync", cs[n][0][:], ap, writes=[cs[n][1]])
        ident, ident_t = cs["ident"]
        LT, LT_t = cs["LT"]; BT, BT_t = cs["BT"]; BTc, BTc_t = cs["BTc"]
        mk1, mk1_t = cs["mk1"]; mk3, mk3_t = cs["mk3"]; i2, i2_t = cs["i2"]

        Pt = [mk(f"Pt{i}", [128, NCOL]) for i in range(2)]
        Pp = mk("Pp", [128, NCOL])
        names = ["tw", "sw", "ag", "g", "kk", "tmp", "tmp2", "inv", "kmod", "bvec", "cum", "epos", "eexc", "eneg", "eend",
                 "rt", "at", "kt", "bt", "kh", "bh", "X", "W", "U", "Y", "yn", "bon"]
        W_ = {n: mk(n, [128, HN]) for n in names}
        lorT = mk("lorT", [128, 4, 128])
        sgd = mk("sgd", [128, 160])
        st8 = {n: mk(n, [128, NH]) for n in ("n2", "mean", "var", "s8")}
        G1 = mk("G1", [64, NH, 128]); G2 = mk("G2", [64, NH, 128])
        Nf = [mk(f"Nf{i}", [64, NH, 64], BF16) for i in range(2)]
        Tf = [mk(f"Tf{i}", [64, NH, 64], BF16) for i in range(2)]
        Q = mk("Q", [64, NH, 64], BF16)
        Qf = mk("Qf", [64, NH, 64])
        SH = [mk(f"SH{i}", [64, HN]) for i in range(4)]
        CM = mk("CM", [64, NH, 2, 4, 64])
        Apt = mk("Apt", [64, NH, 64])
        pC = mk("pC", [64, NH, 2])
        Hs = [mk(f"H{i}", [64, NH, 64]) for i in range(2)]
        P.op("vector", lambda e: e.memset(Hs[0][0][:], 0.0), writes=[Hs[0][1]])
        hcur = [0]
        finals = []

        def v3(ap):
            return ap.rearrange("p (h n) -> p h n", h=NH)

        def bc8(ap8):
            return ap8.unsqueeze(2).to_broadcast([128, NH, 64])

        def do_chunk(c2, xs, xs_t, at_, at_t, bh, bh_t, kh, kh_t, shs, cm, cm_t):
            U, U_t = W_["U"]; Y, Y_t = W_["Y"]
            X, X_t = W_["X"]; Wt, Wt_t = W_["W"]
            g1, g1_t = G1; g2_, g2_t = G2
            q_, q_t = Q
            apt, apt_t = Apt
            if c2 == 0:
                Vc, Vc_t = xs[0:64, 1024:1536], xs_t
                Ac, Ac_t = at_[0:64, :], at_t
                Bc, Bc_t = bh[0:64, :], bh_t
                Kc, Kc_t = kh[0:64, :], kh_t
            else:
                Vc, Vc_t = shs[0][0][:], shs[0][1]
                Ac, Ac_t = shs[1][0][:], shs[1][1]
                Bc, Bc_t = shs[2][0][:], shs[2][1]
                Kc, Kc_t = shs[3][0][:], shs[3][1]
            hs = lambda h: slice(h * 64, (h + 1) * 64)
            nf0, nf0_t = Nf[0]; tf0, tf0_t = Tf[0]
            pg1 = [nb(), nb()]; pg2 = [nb(), nb()]; pg3 = nb()
            for (pgs, qi) in ((pg1, 0), (pg2, 1)):
                for half in range(2):
                    fns = []
                    for hh in range(4):
                        h = half * 4 + hh
                        fns.append(lambda e, h=h, hh=hh, qi=qi, bk=pgs[half]: e.matmul(bk[0:64, hh * 128:(hh + 1) * 128], cm[:, h, c2, qi, :], cm[:, h, c2, 2:4, :], start=True, stop=True))
                    P.group("tensor", fns, reads=[cm_t], writes=[pgs[half]])
            fns = []
            for h in range(NH):
                fns.append(lambda e, h=h: e.matmul(pg3[0:64, hs(h)], cm[:, h, c2, 2, :], cm[:, h, c2, 0, :], start=True, stop=True))
            P.group("tensor", fns, reads=[cm_t], writes=[pg3])
            mk1b = mk1[0:64, :].unsqueeze(1).to_broadcast([64, 4, 128])
            for half in range(2):
                P.op("vector", lambda e, half=half: e.tensor_tensor(out=g1[:, half * 4:(half + 1) * 4, :], in0=pg1[half][0:64, :].rearrange("p (h t) -> p h t", h=4), in1=mk1b, op=ALU.mult),
                     reads=[pg1[half], mk1_t], writes=[g1_t])
                P.op("vector", lambda e, half=half: e.tensor_tensor(out=g2_[:, half * 4:(half + 1) * 4, :], in0=pg2[half][0:64, :].rearrange("p (h t) -> p h t", h=4), in1=mk1b, op=ALU.mult),
                     reads=[pg2[half], mk1_t], writes=[g2_t])
            P.op("vector", lambda e: e.tensor_tensor(out=nf0[:], in0=pg3[0:64, :].rearrange("p (h t) -> p h t", h=NH), in1=mk3[0:64, :].unsqueeze(1).to_broadcast([64, NH, 64]), op=ALU.mult),
                 reads=[pg3, mk3_t], writes=[nf0_t])
            P.op("gpsimd", lambda e: e.tensor_copy(out=tf0[:], in_=g1[:, :, 0:64]), reads=[g1_t], writes=[tf0_t])
            P.op("gpsimd", lambda e: e.tensor_tensor(out=q_[:], in0=g1[:, :, 0:64], in1=i2[0:64, :].unsqueeze(1).to_broadcast([64, NH, 64]), op=ALU.add), reads=[g1_t, i2_t], writes=[q_t])
            cur = 0
            for lvl in range(5):
                nfc, nfc_t = Nf[cur]; tfc, tfc_t = Tf[cur]
                nfn, nfn_t = Nf[1 - cur]; tfn, tfn_t = Tf[1 - cur]
                last = (lvl == 4)
                pn = nb()
                fns = [(lambda e, h=h, pn=pn, tfc=tfc, nfc=nfc: e.matmul(pn[0:64, hs(h)], tfc[:, h, :], nfc[:, h, :], start=True, stop=True)) for h in range(NH)]
                P.group("tensor", fns, reads=[tfc_t, nfc_t], writes=[pn])
                if not last:
                    ptt = nb()
                    fns = [(lambda e, h=h, ptt=ptt, tfc=tfc, nfc=nfc: e.matmul(ptt[0:64, hs(h)], nfc[:, h, :], tfc[:, h, :], start=True, stop=True)) for h in range(NH)]
                    P.group("tensor", fns, reads=[tfc_t, nfc_t], writes=[ptt])
                P.op("scalar", lambda e, pn=pn, nfn=nfn: e.copy(out=nfn[:].rearrange("p h t -> p (h t)"), in_=pn[0:64, :]), reads=[pn], writes=[nfn_t])
                if not last:
                    P.op("vector", lambda e, ptt=ptt, tfn=tfn: e.tensor_copy(out=tfn[:].rearrange("p h t -> p (h t)"), in_=ptt[0:64, :]), reads=[ptt], writes=[tfn_t])
                pq = nb()
                fns = [(lambda e, h=h, pq=pq, nfn=nfn: e.matmul(pq[0:64, hs(h)], nfn[:, h, :], q_[:, h, :], start=True, stop=True)) for h in range(NH)]
                P.group("tensor", fns, reads=[nfn_t, q_t], writes=[pq])
                P.op("vector", lambda e, pq=pq: e.tensor_tensor(out=q_[:].rearrange("p h t -> p (h t)"), in0=q_[:].rearrange("p h t -> p (h t)"), in1=pq[0:64, :], op=ALU.add), reads=[pq], writes=[q_t])
                cur = 1 - cur
            qf_, qf_t = Qf
            P.op("gpsimd", lambda e: e.tensor_copy(out=qf_[:], in_=q_[:]), reads=[q_t], writes=[qf_t])
            px = nb()
            fns = [(lambda e, h=h, px=px: e.matmul(px[0:64, hs(h)], g2_[:, h, 0:64], Vc[:, hs(h)], start=True, stop=True)) for h in range(NH)]
            P.group("tensor", fns, reads=[g2_t, Vc_t], writes=[px])
            P.op("scalar", lambda e, px=px: e.copy(out=X[0:64, :], in_=px[0:64, :]), reads=[px], writes=[X_t])
            pw_ = nb()
            fns = [(lambda e, h=h, pw_=pw_: e.matmul(pw_[0:64, hs(h)], qf_[:, h, :], X[0:64, hs(h)], start=True, stop=True)) for h in range(NH)]
            P.group("tensor", fns, reads=[qf_t, X_t], writes=[pw_])
            P.op("scalar", lambda e, pw_=pw_: e.copy(out=Wt[0:64, :], in_=pw_[0:64, :]), reads=[pw_], writes=[Wt_t])
            pap = nb()
            fns = [(lambda e, h=h, pap=pap: e.matmul(pap[0:64, hs(h)], Ac[:, hs(h)], qf_[:, h, :], start=True, stop=True)) for h in range(NH)]
            P.group("tensor", fns, reads=[Ac_t, qf_t], writes=[pap])
            P.op("vector", lambda e, pap=pap: e.tensor_copy(out=apt[:].rearrange("p h t -> p (h t)"), in_=pap[0:64, :]), reads=[pap], writes=[apt_t])
            H0, H0_t = Hs[hcur[0]]
            H1, H1_t = Hs[1 - hcur[0]]
            pu = nb()
            fns = [(lambda e, h=h, pu=pu, H0=H0: e.matmul(pu[0:64, hs(h)], apt[:, h, :], H0[:, h, :], start=True, stop=True)) for h in range(NH)]
            P.group("tensor", fns, reads=[apt_t, H0_t], writes=[pu])
            P.op("vector", lambda e, pu=pu: e.tensor_tensor(out=U[0:64, :], in0=pu[0:64, :], in1=Wt[0:64, :], op=ALU.add), reads=[pu, Wt_t], writes=[U_t])
            ph = nb()
            fns = []
            for h in range(NH):
                fns.append(lambda e, h=h, ph=ph: e.matmul(ph[0:64, hs(h)], Bc[:, hs(h)], U[0:64, hs(h)], start=True, stop=False))
                fns.append(lambda e, h=h, ph=ph: e.matmul(ph[0:64, hs(h)], Kc[:, hs(h)], Vc[:, hs(h)], start=False, stop=True))
            P.group("tensor", fns, reads=[Bc_t, Kc_t, U_t, Vc_t], writes=[ph])
            py = nb()
            sl = slice(c2 * 64, (c2 + 1) * 64)
            fns = []
            for h in range(NH):
                fns.append(lambda e, h=h, py=py, H0=H0: e.matmul(py[sl, hs(h)], cm[:, h, c2, 3, :], H0[:, h, :], start=True, stop=False))
                fns.append(lambda e, h=h, py=py: e.matmul(py[sl, hs(h)], g1[:, h, 64:128], U[0:64, hs(h)], start=False, stop=False))
                fns.append(lambda e, h=h, py=py: e.matmul(py[sl, hs(h)], g2_[:, h, 64:128], Vc[:, hs(h)], start=False, stop=True))
            P.group("tensor", fns, reads=[cm_t, H0_t, g1_t, g2_t, U_t, Vc_t], writes=[py])
            P.op("gpsimd", lambda e, H0=H0, H1=H1: e.tensor_tensor(out=H1[:], in0=H0[:], in1=pC[0][:, :, c2:c2 + 1].to_broadcast([64, NH, 64]), op=ALU.mult),
                 reads=[H0_t, pC[1]], writes=[H1_t])
            P.op("vector", lambda e, H1=H1, ph=ph: e.tensor_tensor(out=H1[:].rearrange("p a i -> p (a i)"), in0=H1[:].rearrange("p a i -> p (a i)"), in1=ph[0:64, :], op=ALU.add),
                 reads=[ph], writes=[H1_t])
            P.op("scalar", lambda e, py=py: e.copy(out=Y[sl, :], in_=py[sl, :]), reads=[py], writes=[Y_t])
            hcur[0] = 1 - hcur[0]

        def do_tile(ti):
            t0 = ti * 128
            Pc, Pc_t = Pt[ti % 2]
            pa, pa_t = PA[0]
            pb_, pb_t = PA[1]
            rk_, lt_ = ti // 16, (ti % 16) * 128
            P.dma("sync", pa[:], Gp.rows(rk_, lt_, 128), writes=[pa_t])
            if ti == 0:
                P.op("vector", lambda e: e.memset(pb_[0:1, :], 0.0), writes=[pb_t])
            else:
                P.dma("sync", pb_[0:1, :], Gp.rows((ti - 1) // 16, ((ti - 1) % 16) * 128 + 127, 1), writes=[pb_t])
            P.dma("sync", pb_[1:128, :], Gp.rows(rk_, lt_, 127), writes=[pb_t])
            for (src, src_t, dst, dst_t) in ((pa, pa_t, Pc, Pc_t), (pb_, pb_t, Pp[0], Pp[1])):
                v4 = src[:, 0:3072].rearrange("p (j g n) -> p j g n", j=3, g=2)
                d3 = dst[:, 0:1536].rearrange("p (j n) -> p j n", j=3)
                P.op("vector", lambda e, v4=v4, d3=d3: e.tensor_scalar(out=d3, in0=v4[:, :, 0, :], scalar1=msel[:, 0:1], scalar2=None, op0=ALU.mult), reads=[src_t, msel_t], writes=[dst_t])
                P.op("vector", lambda e, v4=v4, d3=d3: e.scalar_tensor_tensor(out=d3, in0=v4[:, :, 1, :], scalar=msel[:, 1:2], in1=d3, op0=ALU.mult, op1=ALU.add), reads=[src_t, msel_t], writes=[dst_t])
                P.op("gpsimd", lambda e, src=src, dst=dst: e.tensor_copy(out=dst[:, 1536:1824], in_=src[:, 3072:3360]), reads=[src_t], writes=[dst_t])
            P.op("vector", lambda e, Pc=Pc: e.tensor_tensor(out=Pp[0][:], in0=Pp[0][:], in1=Pc[:], op=ALU.subtract), reads=[Pc_t], writes=[Pp[1]])
            P.op("gpsimd", lambda e: e.tensor_tensor(out=Pp[0][:], in0=Pp[0][:], in1=mu_bc[:], op=ALU.mult), reads=[mu_t], writes=[Pp[1]])
            P.op("vector", lambda e, Pc=Pc: e.tensor_tensor(out=Pp[0][:], in0=Pp[0][:], in1=Pc[:], op=ALU.add), reads=[Pc_t], writes=[Pp[1]])
            xs, xs_t = Pp
            r_ = xs[:, 0:512]; k_ = xs[:, 512:1024]; v_ = xs[:, 1024:1536]
            if STAGE == 1:
                finals.append(P.dma('sync', y[t0:t0 + 128, :], xs[:, 0:512], reads=[xs_t])); return
            tw, tw_t = W_["tw"]
            P.op("scalar", lambda e: e.activation(out=tw[:, 0:64], in_=xs[:, 1536:1600], func=AF.Tanh), reads=[xs_t], writes=[tw_t])
            P.op("scalar", lambda e: e.activation(out=sgd[0][:], in_=xs[:, 1664:1824], func=AF.Sigmoid), reads=[xs_t], writes=[sgd[1]])
            pb = nb()
            P.op("tensor", lambda e, pb=pb: e.transpose(pb[0:64, 0:128], tw[:, 0:64], ident[:]), reads=[tw_t, ident_t], writes=[pb])
            P.op("tensor", lambda e, pb=pb: e.transpose(pb[0:64, 128:256], xs[:, 1600:1664], ident[:]), reads=[xs_t, ident_t], writes=[pb])
            P.op("tensor", lambda e, pb=pb: e.transpose(pb[0:128, 256:384], sgd[0][:, 0:128], ident[:]), reads=[sgd[1], ident_t], writes=[pb])
            P.op("tensor", lambda e, pb=pb: e.transpose(pb[0:32, 384:512], sgd[0][:, 128:160], ident[:]), reads=[sgd[1], ident_t], writes=[pb])
            lT, lT_t = lorT
            P.op("vector", lambda e, pb=pb: e.tensor_copy(out=lT[0:64, 0:2, :], in_=pb[0:64, 0:256].rearrange("p (a b) -> p a b", a=2)), reads=[pb], writes=[lT_t])
            P.op("vector", lambda e, pb=pb: e.tensor_copy(out=lT[:, 2, :], in_=pb[:, 256:384]), reads=[pb], writes=[lT_t])
            P.op("vector", lambda e, pb=pb: e.tensor_copy(out=lT[0:32, 3, :], in_=pb[0:32, 384:512]), reads=[pb], writes=[lT_t])
            pw = nb(); pa = nb(); pg = nb()
            P.op("tensor", lambda e, pw=pw: e.matmul(pw[:], lT[0:64, 0, :], w2s[:], start=True, stop=True), reads=[lT_t, w2t], writes=[pw])
            P.op("tensor", lambda e, pa=pa: e.matmul(pa[:], lT[0:64, 1, :], a2s[:], start=True, stop=True), reads=[lT_t, a2t], writes=[pa])
            P.group("tensor", [lambda e, pg=pg: e.matmul(pg[:], lT[:, 2, :], g2a[:], start=True, stop=False),
                               lambda e, pg=pg: e.matmul(pg[:], lT[0:32, 3, :], g2b[:], start=False, stop=True)], reads=[lT_t, g2at, g2bt], writes=[pg])
            sw, sw_t = W_["sw"]; ag, ag_t = W_["ag"]; g_, g_t = W_["g"]
            P.op("vector", lambda e, pw=pw: e.tensor_tensor(out=sw[:], in0=pw[:], in1=vb["w0"][0][:], op=ALU.add), reads=[pw, vb["w0"][1]], writes=[sw_t])
            P.op("scalar", lambda e: e.activation(out=sw[:], in_=sw[:], func=AF.Sigmoid), reads=[sw_t], writes=[sw_t])
            P.op("vector", lambda e, pa=pa: e.tensor_tensor(out=ag[:], in0=pa[:], in1=vb["a0"][0][:], op=ALU.add), reads=[pa, vb["a0"][1]], writes=[ag_t])
            P.op("scalar", lambda e: e.activation(out=ag[:], in_=ag[:], func=AF.Sigmoid), reads=[ag_t], writes=[ag_t])
            P.op("scalar", lambda e, pg=pg: e.copy(out=g_[:], in_=pg[:]), reads=[pg], writes=[g_t])
            if STAGE == 2:
                finals.append(P.dma('sync', y[t0:t0 + 128, :], xs[:, 0:512], reads=[xs_t])); return
            kk, kk_t = W_["kk"]; tmp, tmp_t = W_["tmp"]; tmp2, tmp2_t = W_["tmp2"]
            kmod, kmod_t = W_["kmod"]; bvec, bvec_t = W_["bvec"]
            n2, n2_t = st8["n2"]
            P.op("vector", lambda e: e.tensor_tensor(out=kk[:], in0=k_, in1=vb["k_k"][0][:], op=ALU.mult), reads=[xs_t, vb["k_k"][1]], writes=[kk_t])
            P.op("gpsimd", lambda e: e.tensor_tensor(out=tmp[:], in0=kk[:], in1=kk[:], op=ALU.mult), reads=[kk_t], writes=[tmp_t])
            P.op("vector", lambda e: e.tensor_reduce(out=n2[:], in_=v3(tmp[:]), axis=AX.X, op=ALU.add), reads=[tmp_t], writes=[n2_t])
            P.op("scalar", lambda e: e.activation(out=n2[:], in_=n2[:], func=AF.Sqrt), reads=[n2_t], writes=[n2_t])
            P.op("vector", lambda e: e.tensor_scalar(out=n2[:], in0=n2[:], scalar1=1e-12, scalar2=None, op0=ALU.max), reads=[n2_t], writes=[n2_t])
            P.op("vector", lambda e: e.reciprocal(out=n2[:], in_=n2[:]), reads=[n2_t], writes=[n2_t])
            P.op("vector", lambda e: e.tensor_tensor(out=v3(kk[:]), in0=v3(kk[:]), in1=bc8(n2[:]), op=ALU.mult), reads=[n2_t], writes=[kk_t])
            P.op("vector", lambda e: e.scalar_tensor_tensor(out=tmp[:], in0=ag[:], scalar=-1.0, in1=vb["k_a"][0][:], op0=ALU.add, op1=ALU.mult), reads=[ag_t, vb["k_a"][1]], writes=[tmp_t])
            P.op("vector", lambda e: e.scalar_tensor_tensor(out=kmod[:], in0=tmp[:], scalar=1.0, in1=k_, op0=ALU.add, op1=ALU.mult), reads=[tmp_t, xs_t], writes=[kmod_t])
            P.op("gpsimd", lambda e: e.tensor_tensor(out=bvec[:], in0=kk[:], in1=ag[:], op=ALU.mult), reads=[kk_t, ag_t], writes=[bvec_t])
            if STAGE == 3:
                finals.append(P.dma('sync', y[t0:t0 + 128, :], xs[:, 0:512], reads=[xs_t])); return
            pc = nb(); pt = nb()
            P.op("tensor", lambda e, pc=pc: e.matmul(pc[:], LT[:], sw[:], start=True, stop=True), reads=[LT_t, sw_t], writes=[pc])
            P.op("tensor", lambda e, pt=pt: e.matmul(pt[:], BT[:], sw[:], start=True, stop=True), reads=[BT_t, sw_t], writes=[pt])
            ppc = nb()
            fns = [(lambda e, h=h, ppc=ppc: e.matmul(ppc[0:64, h * 2:h * 2 + 2], sw[:, h * 64:(h + 1) * 64], BTc[:], start=True, stop=True)) for h in range(NH)]
            P.group("tensor", fns, reads=[sw_t, BTc_t], writes=[ppc])
            P.op("scalar", lambda e, ppc=ppc: e.activation(out=pC[0][:].rearrange("p a b -> p (a b)"), in_=ppc[0:64, 0:16], func=AF.Exp), reads=[ppc], writes=[pC[1]])
            cum, cum_t = W_["cum"]
            epos, epos_t = W_["epos"]; eexc, eexc_t = W_["eexc"]; eneg, eneg_t = W_["eneg"]; eend, eend_t = W_["eend"]
            P.op("scalar", lambda e, pc=pc: e.copy(out=cum[:], in_=pc[:]), reads=[pc], writes=[cum_t])
            P.op("scalar", lambda e, pc=pc: e.activation(out=epos[:], in_=pc[:], func=AF.Exp), reads=[pc], writes=[epos_t])
            P.op("scalar", lambda e, pc=pc: e.activation(out=eneg[:], in_=pc[:], func=AF.Exp, scale=-1.0), reads=[pc], writes=[eneg_t])
            P.op("vector", lambda e: e.scalar_tensor_tensor(out=eexc[:], in0=sw[:], scalar=C0, in1=cum[:], op0=ALU.mult, op1=ALU.add), reads=[sw_t, cum_t], writes=[eexc_t])
            P.op("scalar", lambda e: e.activation(out=eexc[:], in_=eexc[:], func=AF.Exp), reads=[eexc_t], writes=[eexc_t])
            P.op("vector", lambda e, pt=pt: e.tensor_tensor(out=eend[:], in0=pt[:], in1=cum[:], op=ALU.subtract), reads=[pt, cum_t], writes=[eend_t])
            P.op("scalar", lambda e: e.activation(out=eend[:], in_=eend[:], func=AF.Exp), reads=[eend_t], writes=[eend_t])
            if STAGE == 4:
                finals.append(P.dma('sync', y[t0:t0 + 128, :], xs[:, 0:512], reads=[xs_t])); return
            rt, rt_t = W_["rt"]; at_, at_t = W_["at"]; kt, kt_t = W_["kt"]; bt, bt_t = W_["bt"]; kh, kh_t = W_["kh"]; bh, bh_t = W_["bh"]
            P.op("vector", lambda e: e.tensor_tensor(out=rt[:], in0=r_, in1=epos[:], op=ALU.mult), reads=[xs_t, epos_t], writes=[rt_t])
            P.op("vector", lambda e: e.scalar_tensor_tensor(out=at_[:], in0=kk[:], scalar=-1.0, in1=eexc[:], op0=ALU.mult, op1=ALU.mult), reads=[kk_t, eexc_t], writes=[at_t])
            P.op("vector", lambda e: e.tensor_tensor(out=kt[:], in0=kmod[:], in1=eneg[:], op=ALU.mult), reads=[kmod_t, eneg_t], writes=[kt_t])
            P.op("gpsimd", lambda e: e.tensor_tensor(out=bt[:], in0=bvec[:], in1=eneg[:], op=ALU.mult), reads=[bvec_t, eneg_t], writes=[bt_t])
            P.op("vector", lambda e: e.tensor_tensor(out=kh[:], in0=kmod[:], in1=eend[:], op=ALU.mult), reads=[kmod_t, eend_t], writes=[kh_t])
            P.op("gpsimd", lambda e: e.tensor_tensor(out=bh[:], in0=bvec[:], in1=eend[:], op=ALU.mult), reads=[bvec_t, eend_t], writes=[bh_t])

            cm, cm_t = CM
            srcs = [(bt, bt_t), (kt, kt_t), (at_, at_t), (rt, rt_t)]
            for h in range(NH):
                pb = nb()
                for q, (src, src_t) in enumerate(srcs):
                    P.op("tensor", lambda e, pb=pb, q=q, src=src, h=h: e.transpose(pb[0:64, q * 128:(q + 1) * 128], src[:, h * 64:(h + 1) * 64], ident[:]),
                         reads=[src_t, ident_t], writes=[pb])
                outap = cm[:, h, :, :, :].rearrange("p c q j -> p q c j")
                inap = pb[0:64, :].rearrange("p (q c j) -> p q c j", q=4, c=2)
                if h % 2 == 0:
                    P.op("vector", lambda e, outap=outap, inap=inap: e.tensor_copy(out=outap, in_=inap), reads=[pb], writes=[cm_t])
                else:
                    P.op("scalar", lambda e, outap=outap, inap=inap: e.copy(out=outap, in_=inap), reads=[pb], writes=[cm_t])
            if STAGE == 5:
                finals.append(P.dma('sync', y[t0:t0 + 128, :], xs[:, 0:512], reads=[xs_t])); return
            shs = []
            for qi, (src_ap, src_t) in enumerate(((xs[64:128, 1024:1536], xs_t), (at_[64:128, :], at_t), (bh[64:128, :], bh_t), (kh[64:128, :], kh_t))):
                d_, d_t = SH[qi]
                P.dma("sync", d_[:], src_ap, reads=[src_t], writes=[d_t])
                shs.append((d_, d_t))
            U, U_t = W_["U"]; Y, Y_t = W_["Y"]
            X, X_t = W_["X"]; Wt, Wt_t = W_["W"]
            g1, g1_t = G1; g2_, g2_t = G2
            q_, q_t = Q
            apt, apt_t = Apt
            for c2 in range(2):
                do_chunk(c2, xs, xs_t, at_, at_t, bh, bh_t, kh, kh_t, shs, cm, cm_t)

            mean, mean_t = st8["mean"]; var, var_t = st8["var"]; s8, s8_t = st8["s8"]
            yn, yn_t = W_["yn"]; bon, bon_t = W_["bon"]
            P.op("vector", lambda e: e.tensor_reduce(out=mean[:], in_=v3(Y[:]), axis=AX.X, op=ALU.add), reads=[Y_t], writes=[mean_t])
            P.op("vector", lambda e: e.tensor_scalar(out=mean[:], in0=mean[:], scalar1=1.0 / 64, scalar2=None, op0=ALU.mult), reads=[mean_t], writes=[mean_t])
            P.op("vector", lambda e: e.tensor_tensor(out=v3(yn[:]), in0=v3(Y[:]), in1=bc8(mean[:]), op=ALU.subtract), reads=[Y_t, mean_t], writes=[yn_t])
            P.op("gpsimd", lambda e: e.tensor_tensor(out=tmp[:], in0=yn[:], in1=yn[:], op=ALU.mult), reads=[yn_t], writes=[tmp_t])
            P.op("vector", lambda e: e.tensor_reduce(out=var[:], in_=v3(tmp[:]), axis=AX.X, op=ALU.add), reads=[tmp_t], writes=[var_t])
            P.op("scalar", lambda e: e.activation(out=var[:], in_=var[:], func=AF.Sqrt, bias=GN_EPS, scale=1.0 / 64), reads=[var_t], writes=[var_t])
            P.op("vector", lambda e: e.reciprocal(out=var[:], in_=var[:]), reads=[var_t], writes=[var_t])
            P.op("vector", lambda e: e.tensor_tensor(out=v3(yn[:]), in0=v3(yn[:]), in1=bc8(var[:]), op=ALU.mult), reads=[var_t], writes=[yn_t])
            P.op("gpsimd", lambda e: e.tensor_tensor(out=yn[:], in0=yn[:], in1=vb["gn_g"][0][:], op=ALU.mult), reads=[vb["gn_g"][1]], writes=[yn_t])
            P.op("vector", lambda e: e.tensor_tensor(out=yn[:], in0=yn[:], in1=vb["gn_b"][0][:], op=ALU.add), reads=[vb["gn_b"][1]], writes=[yn_t])
            P.op("gpsimd", lambda e: e.tensor_tensor(out=tmp2[:], in0=r_, in1=kmod[:], op=ALU.mult), reads=[xs_t, kmod_t], writes=[tmp2_t])
            P.op("gpsimd", lambda e: e.tensor_tensor(out=tmp2[:], in0=tmp2[:], in1=vb["r_k"][0][:], op=ALU.mult), reads=[vb["r_k"][1]], writes=[tmp2_t])
            P.op("vector", lambda e: e.tensor_reduce(out=s8[:], in_=v3(tmp2[:]), axis=AX.X, op=ALU.add), reads=[tmp2_t], writes=[s8_t])
            P.op("vector", lambda e: e.tensor_tensor(out=v3(bon[:]), in0=v3(v_), in1=bc8(s8[:]), op=ALU.mult), reads=[xs_t, s8_t], writes=[bon_t])
            P.op("vector", lambda e: e.tensor_tensor(out=yn[:], in0=yn[:], in1=bon[:], op=ALU.add), reads=[bon_t], writes=[yn_t])
            P.op("vector", lambda e: e.tensor_tensor(out=yn[:], in0=yn[:], in1=g_[:], op=ALU.mult), reads=[g_t], writes=[yn_t])
            finals.append(P.dma("sync", y[t0:t0 + 128, :], yn[:], reads=[yn_t]))
            if DBG:
                Hc, Hc_t = Hs[hcur[0]]
                finals.append(P.dma("sync", dbg[ti], Hc[:].rearrange("p a i -> p (a i)"), reads=[Hc_t]))
        for ti in range(ntiles):
            do_tile(ti)


S = 4096
NSLOT = 16
NQ = NSLOT * 128
AH = 8
IH = 16
TOPK = 256
NEG = -30000.0


def np_rel_bucket(dist):
    max_exact = 16
    d_f = np.maximum(dist, 1).astype(np.float32)
    large = max_exact + (np.log(d_f / max_exact) / np.float32(math.log(128 / max_exact)) * (32 - max_exact)).astype(np.int32)
    large = np.minimum(large, 31)
    return np.where(dist < max_exact, dist, large)


def dsa_consts(e, rel_bias):
    q = np.arange(128)
    tri = (q[None, :] <= q[:, None]).astype(np.float32)
    cm = np.zeros((128, 256), np.float32)
    if e == 0:
        cm[:, 0:128] = tri
    else:
        cm[:, 0:128] = 1.0
        cm[:, 128:256] = tri
    nbig = ((cm - 1.0) * 1e30).astype(np.float32)
    NB = np.zeros((128, 3, AH, 128), np.float32)
    for r in range(3):
        delta = e + 1 - r
        dist = np.maximum(delta * 128 + q[None, :] - q[:, None], 0)
        b = np_rel_bucket(dist.astype(np.int32))
        NB[:, r, :, :] = np.transpose(rel_bias[b], (0, 2, 1))
    cfar = np.broadcast_to(rel_bias[31][None, :], (128, AH)).astype(np.float32).copy()
    return {"cm": cm, "nbig": nbig, "NB": NB, "cfar": cfar, "ident": np.eye(128, dtype=np.float32)}


def dsa_phase(P, banks, G, lng_d, lnb_d, cst, msel_d, y, nslot=NSLOT):
    cm_d, nbig_d, NB_d, cfar_d, ident_d = cst["cm"], cst["nbig"], cst["NB"], cst["cfar"], cst["ident"]
    if True:
        rr = [0]

        def nb():
            b = banks[rr[0]]
            rr[0] = (rr[0] + 1) % 4
            return b
        acc = [banks[6], banks[7]]
        pscb = [banks[4], banks[5]]
        pscn = [0]

        def mk(name, shape, dt=F32):
            t = P.sb(name, shape, dt)
            return t, T(t[:])

        msel, msel_t = mk("msel3", [128, 2]); P.dma("sync", msel[:], msel_d, writes=[msel_t])
        ident, ident_t = mk("ident_s", [128, 128]); P.dma("sync", ident[:], ident_d, writes=[ident_t])
        identb, identb_t = mk("identb", [128, 128], BF16)
        P.op("vector", lambda e: e.tensor_copy(out=identb[:], in_=ident[:]), reads=[ident_t], writes=[identb_t])
        cm, cm_t = mk("cm_s", [128, 256]); P.dma("sync", cm[:], cm_d, writes=[cm_t])
        nbig, nbig_t = mk("nbig_s", [128, 256]); P.dma("sync", nbig[:], nbig_d, writes=[nbig_t])
        cfar, cfar_t = mk("cfar_s", [128, AH]); P.dma("sync", cfar[:], cfar_d, writes=[cfar_t])
        NBb, NBb_t = mk("NBb", [128, 3, AH, 128], BF16)
        kT, kT_t = mk("kT_s", [128, 4, S], BF16)
        V1, V1_t = mk("V1", [128, 32, AH, 65], BF16)
        kiT, kiT_t = mk("kiT", [64, S], BF16)
        P.push_scope()
        NBf, NBf_t = mk("NBf", [128, 3, AH, 128]); P.dma("sync", NBf[:], NB_d, writes=[NBf_t])
        for r in range(3):
            for h in range(AH):
                P.op("vector", lambda e, r=r, h=h: e.tensor_scalar(out=NBb[:, r, h, :], in0=NBf[:, r, h, :], scalar1=cfar[:, h:h + 1], scalar2=None, op0=ALU.subtract),
                     reads=[NBf_t, cfar_t], writes=[NBb_t])
        lng, lng_t = mk("lng_bc", [128, 64]); P.dma("sync", lng[:], lng_d.partition_broadcast(128), writes=[lng_t])
        lnb, lnb_t = mk("lnb_bc", [128, 64]); P.dma("sync", lnb[:], lnb_d.partition_broadcast(128), writes=[lnb_t])

        for r_ in range(2):
            for hp in range(4):
                P.dma("gpsimd", kT[:, hp, r_ * 2048:(r_ + 1) * 2048], G["kT"].rows(r_, hp * 128, 128), writes=[kT_t])
        P.op("vector", lambda e: e.memset(V1[:, :, :, 64:65], 1.0), writes=[V1_t])
        for kb in range(32):
            P.dma("gpsimd", V1[:, kb, :, 0:64], G["V"].rows(kb // 16, (kb % 16) * 128, 128).rearrange("p (h d) -> p h d", h=AH), writes=[V1_t])
        ki, ki_t = mk("ki", [128, 32, 64])
        for r_ in range(2):
            P.dma("sync", ki[:, r_ * 16:(r_ + 1) * 16, :], G["kw"].rows(r_, 0, 2048)[:, 0:64].rearrange("(kb p) d -> p kb d", p=128), writes=[ki_t])
        st, st_t = mk("kist", [128, 32]); ksq, ksq_t = mk("kisq", [128, 32, 64])
        bc32 = lambda ap: ap.unsqueeze(2).to_broadcast([128, 32, 64])
        P.op("vector", lambda e: e.tensor_reduce(out=st[:], in_=ki[:], axis=AX.X, op=ALU.add), reads=[ki_t], writes=[st_t])
        P.op("vector", lambda e: e.tensor_scalar(out=st[:], in0=st[:], scalar1=1.0 / 64, scalar2=None, op0=ALU.mult), reads=[st_t], writes=[st_t])
        P.op("vector", lambda e: e.tensor_tensor(out=ki[:], in0=ki[:], in1=bc32(st[:]), op=ALU.subtract), reads=[st_t], writes=[ki_t])
        P.op("vector", lambda e: e.tensor_tensor(out=ksq[:], in0=ki[:], in1=ki[:], op=ALU.mult), reads=[ki_t], writes=[ksq_t])
        P.op("vector", lambda e: e.tensor_reduce(out=st[:], in_=ksq[:], axis=AX.X, op=ALU.add), reads=[ksq_t], writes=[st_t])
        P.op("scalar", lambda e: e.activation(out=st[:], in_=st[:], func=AF.Sqrt, bias=1e-5, scale=1.0 / 64), reads=[st_t], writes=[st_t])
        P.op("vector", lambda e: e.reciprocal(out=st[:], in_=st[:]), reads=[st_t], writes=[st_t])
        P.op("vector", lambda e: e.tensor_tensor(out=ki[:], in0=ki[:], in1=bc32(st[:]), op=ALU.mult), reads=[st_t], writes=[ki_t])
        P.op("vector", lambda e: e.tensor_tensor(out=ki[:], in0=ki[:], in1=lng[:].unsqueeze(1).to_broadcast([128, 32, 64]), op=ALU.mult), reads=[lng_t], writes=[ki_t])
        P.op("vector", lambda e: e.tensor_tensor(out=ki[:], in0=ki[:], in1=lnb[:].unsqueeze(1).to_broadcast([128, 32, 64]), op=ALU.add), reads=[lnb_t], writes=[ki_t])
        for g in range(8):
            pb = nb()
            for j in range(4):
                kb = g * 4 + j
                P.op("tensor", lambda e, pb=pb, j=j, kb=kb: e.transpose(pb[0:64, j * 128:(j + 1) * 128], ki[:, kb, :], ident[:]), reads=[ki_t, ident_t], writes=[pb])
            P.op("scalar", lambda e, pb=pb, g=g: e.copy(out=kiT[:, g * 512:(g + 1) * 512], in_=pb[0:64, :]), reads=[pb], writes=[kiT_t])

        P.barrier()
        P.pop_scope()
        qf, qf_t = mk("qf", [128, 4, 128])
        qfb, qfb_t = mk("qfb", [128, 4, 128])
        qf2, qf2_t = mk("qf2", [128, 4, 256])
        qi2 = [mk(f"qi2h{i}", [64, IH, 128]) for i in range(1)]
        qib, qib_t = mk("qib", [64, IH, 128])
        qia, qia_t = mk("qia", [64, IH, 128])
        wq2, wq2_t = mk("wq2", [128, 2, IH])
        wqb, wqb_t = mk("wqb", [128, IH])
        qTz, qTz_t = mk("qTz", [128, AH, 128], BF16)
        P.op("vector", lambda e: e.memset(qTz[:], 0.0), writes=[qTz_t])
        qiT, qiT_t = mk("qiT", [64, IH, 128], BF16)
        wq, wq_t = mk("wq", [128, IH])
        diagall, diagall_t = mk("diagall", [128, IH, 128], BF16)
        stage = [mk(f"stage{i}", [128, 512], BF16) for i in range(2)]
        score2 = [mk(f"score{i}", [128, S]) for i in range(2)]
        work, work_t = mk("work", [128, S])
        madd2 = [mk(f"madd{i}", [128, S], BF16) for i in range(2)]
        mx, mx_t = mk("mx", [128, 8])
        thr, thr_t = mk("thr", [128, 1])
        PT = [mk(f"PT{i}", [128, 4, 128], BF16) for i in range(2)]
        rec, rec_t = mk("rec", [128, AH])
        yo = [mk(f"yo{i}", [128, 512]) for i in range(1)]
        cnt = {"stage": 0, "PT": 0}

        def blend2(c0, c0_t, c1, c1_t, tb, tb_t, out, out_t, np_):
            P.op("scalar", lambda e: e.activation(out=out, in_=c0, func=AF.Identity, scale=msel[0:np_, 0:1]), reads=[c0_t, msel_t], writes=[out_t])
            P.op("scalar", lambda e: e.activation(out=tb, in_=c1, func=AF.Identity, scale=msel[0:np_, 1:2]), reads=[c1_t, msel_t], writes=[tb_t])
            P.op("gpsimd", lambda e: e.tensor_tensor(out=out, in0=out, in1=tb, op=ALU.add), reads=[tb_t], writes=[out_t])

        def st_idx(i):
            nkb = 2 * i + 2
            nkeys = nkb * 128
            r_ = i // 8
            lt0 = (i - 8 * r_) * 256
            score, score_t = score2[i % 2]
            qh, qh_t = qi2[0]
            for cand in range(2):
                for hq in range(4):
                    P.dma("sync", qh[:, hq * 4:(hq + 1) * 4, :], G["qiT"].rows(r_, hq * 256, 256)[:, lt0 + cand * 128:lt0 + (cand + 1) * 128].rearrange("(h d) q -> d h q", d=64), writes=[qh_t])
                if cand == 0:
                    P.op("scalar", lambda e: e.activation(out=qia[:], in_=qh[:], func=AF.Identity, scale=msel[0:64, 0:1]), reads=[qh_t, msel_t], writes=[qia_t])
                else:
                    P.op("scalar", lambda e: e.activation(out=qib[:], in_=qh[:], func=AF.Identity, scale=msel[0:64, 1:2]), reads=[qh_t, msel_t], writes=[qib_t])
                    P.op("gpsimd", lambda e: e.tensor_tensor(out=qiT[:], in0=qia[:], in1=qib[:], op=ALU.add), reads=[qia_t, qib_t], writes=[qiT_t])
            P.dma("sync", wq2[:], G["kw"].rows(r_, lt0, 256)[:, 64:80].rearrange("(c p) n -> p c n", p=128), writes=[wq2_t])
            blend2(wq2[:, 0, :], wq2_t, wq2[:, 1, :], wq2_t, wqb[:], wqb_t, wq[:], wq_t, 128)
            for h in range(IH):
                P.op("scalar", lambda e, h=h: e.activation(out=diagall[:, h, :], in_=ident[:], func=AF.Identity, scale=wq[:, h:h + 1]), reads=[ident_t, wq_t], writes=[diagall_t])
            for g0 in range(0, nkeys, 512):
                n = min(512, nkeys - g0)
                psc = pscb[pscn[0] % 2]; pscn[0] += 1
                for h in range(IH):
                    pd = nb()
                    P.op("tensor", lambda e, pd=pd, h=h, g0=g0, n=n: e.matmul(pd[:, 0:n], qiT[:, h, :], kiT[:, g0:g0 + n], start=True, stop=True), reads=[qiT_t, kiT_t], writes=[pd])
                    sg, sg_t = stage[cnt["stage"] % 2]; cnt["stage"] += 1
                    P.op("scalar", lambda e, pd=pd, sg=sg, n=n: e.activation(out=sg[:, 0:n], in_=pd[:, 0:n], func=AF.Relu), reads=[pd], writes=[sg_t])
                    P.op("tensor", lambda e, psc=psc, sg=sg, h=h, n=n: e.matmul(psc[:, 0:n], diagall[:, h, :], sg[:, 0:n], start=(h == 0), stop=(h == IH - 1)), reads=[diagall_t, sg_t], writes=[psc])
                P.op("scalar", lambda e, psc=psc, g0=g0, n=n, score=score: e.copy(out=score[:, g0:g0 + n], in_=psc[:, 0:n]), reads=[psc], writes=[score_t])
            l0 = nkeys - 256
            P.op("gpsimd", lambda e, score=score: e.tensor_tensor(out=score[:, l0:nkeys], in0=score[:, l0:nkeys], in1=cm[:], op=ALU.mult), reads=[cm_t], writes=[score_t])
            P.op("gpsimd", lambda e, score=score: e.tensor_tensor(out=score[:, l0:nkeys], in0=score[:, l0:nkeys], in1=nbig[:], op=ALU.add), reads=[nbig_t], writes=[score_t])

        def st_topk(i):
            nkeys = (2 * i + 2) * 128
            score, score_t = score2[i % 2]
            madd, madd_t = madd2[i % 2]
            if i == 0:
                P.op("vector", lambda e: e.memset(thr[:], -1e29), writes=[thr_t])
            else:
                P.op("gpsimd", lambda e, score=score: e.tensor_copy(out=work[:, 0:nkeys], in_=score[:, 0:nkeys]), reads=[score_t], writes=[work_t])
                for rnd in range(TOPK // 8):
                    P.op("vector", lambda e: e.max(out=mx[:], in_=work[:, 0:nkeys]), reads=[work_t], writes=[mx_t])
                    if rnd < TOPK // 8 - 1:
                        P.op("vector", lambda e: e.match_replace(out=work[:, 0:nkeys], in_to_replace=mx[:], in_values=work[:, 0:nkeys], imm_value=-1e30), reads=[mx_t], writes=[work_t])
                P.op("vector", lambda e: e.tensor_copy(out=thr[:], in_=mx[:, 7:8]), reads=[mx_t], writes=[thr_t])
            P.op("vector", lambda e, score=score, madd=madd: e.tensor_scalar(out=madd[:, 0:nkeys], in0=score[:, 0:nkeys], scalar1=thr[:, 0:1], scalar2=NEG, op0=ALU.is_lt, op1=ALU.mult),
                 reads=[score_t, thr_t], writes=[madd_t])

        def st_attn(i):
            q0 = i * 128
            nkb = 2 * i + 2
            r_ = i // 8
            lt0 = (i - 8 * r_) * 256
            madd, madd_t = madd2[i % 2]
            for hp in range(4):
                P.dma("sync", qf2[:, hp, :], G["qT"].rows(r_, hp * 128, 128)[:, lt0:lt0 + 256], writes=[qf2_t])
            blend2(qf2[:, :, 0:128], qf2_t, qf2[:, :, 128:256], qf2_t, qfb[:], qfb_t, qf[:], qf_t, 128)
            qz = qTz[:].rearrange("p (hp e) q -> p hp e q", e=2)
            P.op("scalar", lambda e: e.mul(out=qz[0:64, :, 0, :], in_=qf[0:64, :, :], mul=0.125), reads=[qf_t], writes=[qTz_t])
            P.op("scalar", lambda e: e.mul(out=qz[64:128, :, 1, :], in_=qf[64:128, :, :], mul=0.125), reads=[qf_t], writes=[qTz_t])
            for kb in range(nkb):
                r = kb - (nkb - 3)
                ks = slice(kb * 128, (kb + 1) * 128)
                for half in range(2):
                    pl = nb()
                    fns = []
                    for hh in range(4):
                        h = half * 4 + hh
                        hp = h // 2
                        cs = slice(hh * 128, (hh + 1) * 128)
                        near = (r >= 0)
                        fns.append(lambda e, pl=pl, cs=cs, hp=hp, h=h, ks=ks: e.matmul(pl[:, cs], kT[:, hp, ks], qTz[:, h, :], start=True, stop=False))
                        fns.append(lambda e, pl=pl, cs=cs, ks=ks, near=near, madd=madd: e.matmul(pl[:, cs], madd[:, ks], identb[:], start=False, stop=(not near)))
                        if near:
                            fns.append(lambda e, pl=pl, cs=cs, r=r, h=h: e.matmul(pl[:, cs], identb[:], NBb[:, r, h, :], start=False, stop=True))
                    P.group("tensor", fns, reads=[kT_t, qTz_t, madd_t, identb_t, NBb_t], writes=[pl])
                    pt_, pt_t = PT[cnt["PT"] % 2]; cnt["PT"] += 1
                    P.op("scalar", lambda e, pl=pl, pt_=pt_: e.activation(out=pt_[:].rearrange("p h q -> p (h q)"), in_=pl[:], func=AF.Exp), reads=[pl], writes=[pt_t])
                    fns = []
                    for hh in range(4):
                        h = half * 4 + hh
                        fns.append(lambda e, hh=hh, h=h, pt_=pt_, kb=kb, half=half: e.matmul(acc[half][:, hh * 65:(hh + 1) * 65], pt_[:, hh, :], V1[:, kb, h, :], start=(kb == 0 and hh == 0), stop=(kb == nkb - 1 and hh == 3), skip_group_check=True))
                    P.group("tensor", fns, reads=[pt_t, V1_t], writes=[acc[half]])
            yo_, yo_t = yo[0]
            for half in range(2):
                a3 = acc[half][:, 0:260].rearrange("p (h d) -> p h d", h=4)
                P.op("vector", lambda e, a3=a3, half=half: e.reciprocal(out=rec[:, half * 4:(half + 1) * 4], in_=a3[:, :, 64]), reads=[acc[half]], writes=[rec_t])
                P.op("vector", lambda e, a3=a3, half=half, yo_=yo_: e.tensor_tensor(out=yo_[:, half * 256:(half + 1) * 256].rearrange("p (h d) -> p h d", h=4), in0=a3[:, :, 0:64],
                                                                                in1=rec[:, half * 4:(half + 1) * 4].unsqueeze(2).to_broadcast([128, 4, 64]), op=ALU.mult),
                     reads=[acc[half], rec_t], writes=[yo_t])
            P.dma("sync", y[q0:q0 + 128, :], yo_[:], reads=[yo_t])

        for s_ in range(-1, nslot + 1):
            if 0 <= s_ + 1 < nslot:
                st_idx(s_ + 1)
            if 0 <= s_ < nslot:
                st_topk(s_)
            if 0 <= s_ - 1 < nslot:
                st_attn(s_ - 1)


RW = 1024
SGD = 512
ATD = 512
SG_OFF = 0
GATE_OFF = 1024


def sg_consts():
    j = np.arange(128)
    return {"sgmask": (j[:, None] <= j[None, :]).astype(np.float32),
            "identf": np.eye(128, dtype=np.float32)}


def wslot(P, C):
    s = C.cnt["w1"] % 2
    C.cnt["w1"] += 1
    return C.w1[s], C.w1_t[s]


def mixer(P, C, M, t0, load_y_branches=None):
    win_v = M.w_in.rearrange("(c p) n -> p c n", p=128)
    wbr_v = M.w_branch
    hb = C.h
    merged = lambda c: hb[:, c, :]
    yrw = lambda c: hb[:, 16 + c, :]
    yat = lambda c: hb[:, 24 + c, :]
    uT = lambda c: hb[:, 28 + c, :]
    ysg = lambda c: hb[:, 32 + c, :]
    ht = C.h_t
    def load_cast(src, nchunks, base):
        sv = src.rearrange("(c p) t -> p c t", p=128)
        for c in range(nchunks):
            s = C.cnt["sq"] % 2
            C.cnt["sq"] += 1
            sq, sqt = C.sq[s], C.sq_t[s]
            P.dma("sync", sq[:], sv[:, c, t0:t0 + TT], writes=[sqt])
            P.op("gpsimd", lambda e, sq=sq, c=c: e.tensor_copy(out=hb[:, base + c, :], in_=sq[:]), reads=[sqt], writes=[ht[base + c]])
    if getattr(M, "fused", False):
        load_y_branches(P, C, M, t0)
        sg_off, gate_off = 3360, 7024
    else:
        load_cast(M.yrwT, 8, 16)
        load_cast(M.yatT, 4, 24)
        sg_off, gate_off = SG_OFF, GATE_OFF
    wt, wtt = wslot(P, C)
    wload(P, C, wt, wtt, [(wt[:], win_v[:, :, sg_off:sg_off + 512])], f"{C.ph}_sgu", KC * 512)
    for c in range(4):
        pu = next_bank(C)
        fns = [(lambda e, k=k, pu=pu, wt=wt, c=c: e.matmul(pu[:], wt[:, k, c * 128:(c + 1) * 128], C.xb[:, k, :], start=(k == 0), stop=(k == KC - 1))) for k in range(KC)]
        P.group("tensor", fns, reads=[wtt] + C.xb_t, writes=[pu])
        P.op("scalar", lambda e, pu=pu, c=c: e.activation(out=uT(c), in_=pu[:], func=AF.Gelu), reads=[pu], writes=[ht[28 + c]])
    wt, wtt = wslot(P, C)
    wload(P, C, wt, wtt, [(wt[:], win_v[:, :, sg_off + 512:sg_off + 1024])], f"{C.ph}_sgv", KC * 512)
    for ch in range(TT // 128):
        ts = slice(ch * 128, (ch + 1) * 128)
        pv = next_bank(C)
        fns = [(lambda e, k=k, pv=pv, wt=wt, ts=ts: e.matmul(pv[:], C.xb[:, k, ts], wt[:, k, :], start=(k == 0), stop=(k == KC - 1))) for k in range(KC)]
        P.group("tensor", fns, reads=[wtt] + C.xb_t, writes=[pv])
        vg, vgt = M.vg, M.vg_t
        P.op("scalar", lambda e, pv=pv: e.activation(out=vg[:], in_=pv[:], func=AF.Gelu), reads=[pv], writes=[vgt])
        st, stt = M.st, M.st_t
        vsq, vsqt = M.vsq, M.vsq_t
        P.op("vector", lambda e: e.tensor_reduce(out=st[:, 0:1], in_=vg[:], axis=AX.X, op=ALU.add), reads=[vgt], writes=[stt])
        P.op("vector", lambda e: e.tensor_scalar(out=st[:, 0:1], in0=st[:, 0:1], scalar1=1.0 / SGD, scalar2=None, op0=ALU.mult), reads=[stt], writes=[stt])
        P.op("vector", lambda e: e.tensor_scalar(out=vg[:], in0=vg[:], scalar1=st[:, 0:1], scalar2=None, op0=ALU.subtract), reads=[stt], writes=[vgt])
        P.op("gpsimd", lambda e: e.tensor_tensor(out=vsq[:], in0=vg[:], in1=vg[:], op=ALU.mult), reads=[vgt], writes=[vsqt])
        P.op("vector", lambda e: e.tensor_reduce(out=st[:, 1:2], in_=vsq[:], axis=AX.X, op=ALU.add), reads=[vsqt], writes=[stt])
        P.op("scalar", lambda e: e.activation(out=st[:, 1:2], in_=st[:, 1:2], func=AF.Sqrt, bias=LN_EPS, scale=1.0 / SGD), reads=[stt], writes=[stt])
        P.op("vector", lambda e: e.reciprocal(out=st[:, 1:2], in_=st[:, 1:2]), reads=[stt], writes=[stt])
        P.op("vector", lambda e: e.scalar_tensor_tensor(out=vg[:], in0=vg[:], scalar=st[:, 1:2], in1=M.sglg[:], op0=ALU.mult, op1=ALU.mult), reads=[stt, M.sglg_t], writes=[vgt])
        vb, vbt = M.vb, M.vb_t
        P.op("vector", lambda e: e.tensor_tensor(out=vb[:], in0=vg[:], in1=M.sglb[:], op=ALU.add), reads=[vgt, M.sglb_t], writes=[vbt])
        for g in range(4):
            pm = next_bank(C)
            P.op("tensor", lambda e, pm=pm, g=g: e.matmul(pm[:, 0:128], vb[:, g * 128:(g + 1) * 128], M.swT[:, g, :], start=True, stop=True), reads=[vbt, M.swT_t], writes=[pm])
            mx_, mxt = M.mx, M.mx_t
            P.op("vector", lambda e, pm=pm, g=g: e.tensor_tensor(out=mx_[:], in0=pm[:, 0:128], in1=M.sgbb[:, g, :], op=ALU.add), reads=[pm, M.sgbb_t], writes=[mxt])
            P.op("vector", lambda e, g=g, ts=ts: e.tensor_tensor(out=ysg(g)[:, ts], in0=uT(g)[:, ts], in1=mx_[:], op=ALU.mult), reads=[mxt, ht[28 + g]], writes=[ht[32 + g]])
    for c in range(KC):
        cs = slice(c * 128, (c + 1) * 128)
        wt, wtt = wslot(P, C)
        parts = [(wt[:, :, b * 128:(b + 1) * 128], win_v[:, :, gate_off + b * D + c * 128:gate_off + b * D + (c + 1) * 128]) for b in range(3)]
        parts.append((wt[:, :, 384:512], wbr_v.rearrange("(c p) n -> p c n", p=128)[:, :, cs]))
        wload(P, C, wt, wtt, parts, f"{C.ph}_g{c}", KC * 512)
        gts = []
        for b in range(3):
            pg = next_bank(C)
            fns = [(lambda e, k=k, pg=pg, wt=wt, b=b: e.matmul(pg[:], wt[:, k, b * 128:(b + 1) * 128], C.xb[:, k, :], start=(k == 0), stop=(k == KC - 1))) for k in range(KC)]
            P.group("tensor", fns, reads=[wtt] + C.xb_t, writes=[pg])
            gt, gtt = M.gate[b], M.gate_t[b]
            P.op("scalar", lambda e, pg=pg, gt=gt, b=b, c=c: e.activation(out=gt[:], in_=pg[:], func=AF.Sigmoid, bias=M.bg[:, b * KC + c:b * KC + c + 1]), reads=[pg, M.bg_t], writes=[gtt])
            gts.append((gt, gtt))
        srcs = [([yrw(k) for k in range(8)], [ht[16 + k] for k in range(8)], 0),
                ([ysg(k) for k in range(4)], [ht[32 + k] for k in range(4)], 8),
                ([yat(k) for k in range(4)], [ht[24 + k] for k in range(4)], 12)]
        for b, (ops, ops_t, kb0) in enumerate(srcs):
            pz = next_bank(C)
            n = len(ops)
            fns = [(lambda e, k=k, pz=pz, wt=wt, ops=ops, kb0=kb0, n=n: e.matmul(pz[:], wt[:, kb0 + k, 384:512], ops[k], start=(k == 0), stop=(k == n - 1))) for k in range(n)]
            P.group("tensor", fns, reads=[wtt] + ops_t, writes=[pz])
            gt, gtt = gts[b]
            if b == 0:
                P.op("vector", lambda e, pz=pz, gt=gt, c=c: e.tensor_tensor(out=M.macc[:], in0=gt[:], in1=pz[:], op=ALU.mult), reads=[gtt, pz], writes=[M.macc_t])
            else:
                P.op("vector", lambda e, pz=pz, gt=gt: e.tensor_tensor(out=gt[:], in0=gt[:], in1=pz[:], op=ALU.mult), reads=[pz], writes=[gtt])
                if b == 1:
                    P.op("gpsimd", lambda e, gt=gt: e.tensor_tensor(out=M.macc[:], in0=M.macc[:], in1=gt[:], op=ALU.add), reads=[gtt], writes=[M.macc_t])
                else:
                    P.op("gpsimd", lambda e, gt=gt, c=c: e.tensor_tensor(out=merged(c), in0=M.macc[:], in1=gt[:], op=ALU.add), reads=[gtt, M.macc_t], writes=[ht[c]])
    wo_v = M.w_o.rearrange("(c p) n -> p c n", p=128)
    for c4 in range(KC // 4):
        wt, wtt = wslot(P, C)
        wload(P, C, wt, wtt, [(wt[:], wo_v[:, :, c4 * 512:(c4 + 1) * 512])], f"{C.ph}_wo{c4}", KC * 512)
        for j in range(4):
            c = c4 * 4 + j
            po = next_bank(C)
            fns = [(lambda e, k=k, po=po, wt=wt, j=j: e.matmul(po[:], wt[:, k, j * 128:(j + 1) * 128], merged(k), start=(k == 0), stop=(k == KC - 1))) for k in range(KC)]
            P.group("tensor", fns, reads=[wtt] + [ht[k] for k in range(KC)], writes=[po])
            P.op("vector", lambda e, c=c, po=po: e.scalar_tensor_tensor(out=C.xf[:, c, :], in0=C.xf[:, c, :], scalar=ALPHA, in1=po[:], op0=ALU.mult, op1=ALU.add),
                 reads=[po], writes=[C.xf_t[c]])
    layer_norm(P, C, M.ln2g, M.ln2g_t, M.ln2b, M.ln2b_t)


def build_p4(Ttot):
    nc = bass.Bass("TRN2", target_bir_lowering=False)
    din = lambda n, sh: nc.dram_tensor(n, sh, F32, kind="ExternalInput").ap()
    x1T = din("x1T", [D, Ttot]); yrwT = din("yrwT", [RW, Ttot]); yatT = din("yatT", [ATD, Ttot])
    w_in = din("w_in", [D, 7168]); b_gate = din("b_gate", [3 * D])
    sg_ln_g = din("sg_ln_g", [SGD]); sg_ln_b = din("sg_ln_b", [SGD]); sgwT = din("sgwT", [4, 128, 128]); sg_b = din("sg_b", [4, 128])
    w_branch = din("w_branch", [D, D]); w_o = din("w_o", [D, D])
    ln2g_d = din("ln2g", [D]); ln2b_d = din("ln2b", [D])
    w1 = din("w1", [D, 2 * DFF]); w2 = din("w2", [DFF, D]); ln3g_d = din("ln3g", [D]); ln3b_d = din("ln3b", [D])
    sgmask_d = din("sgmask", [128, 128])
    x3T = nc.dram_tensor("x3T", [D, Ttot], F32, kind="ExternalOutput").ap()
    import os
    DBG = "DBG" in os.environ
    if DBG:
        x2T = nc.dram_tensor("x2T", [D, Ttot], F32, kind="ExternalOutput").ap()
    with ExitStack() as es:
        P = Prog(nc, es)
        C = setup_common(P, nc)
        M = Ctx()
        M.w_in, M.w_branch, M.w_o, M.yrwT, M.yatT = w_in, w_branch, w_o, yrwT, yatT
        M.ln2g, M.ln2g_t = load_vec_cols(P, "ln2g_s", ln2g_d, KC)
        M.ln2b, M.ln2b_t = load_vec_cols(P, "ln2b_s", ln2b_d, KC)
        ln3g, ln3g_t = load_vec_cols(P, "ln3g_s", ln3g_d, KC)
        ln3b, ln3b_t = load_vec_cols(P, "ln3b_s", ln3b_d, KC)
        M.bg, M.bg_t = load_vec_cols(P, "bg_s", b_gate, 3 * KC)

        def mk(name, shape, dt=F32):
            t = P.sb(name, shape, dt)
            return t, T(t[:])
        M.sglg, M.sglg_t = mk("sglg", [128, SGD]); P.dma("sync", M.sglg[:], sg_ln_g.partition_broadcast(128), writes=[M.sglg_t])
        M.sglb, M.sglb_t = mk("sglb", [128, SGD]); P.dma("sync", M.sglb[:], sg_ln_b.partition_broadcast(128), writes=[M.sglb_t])
        M.sgbb, M.sgbb_t = mk("sgbb", [128, 4, 128])
        for g in range(4):
            P.dma("sync", M.sgbb[:, g, :], sg_b[g].partition_broadcast(128), writes=[M.sgbb_t])
        swf, swf_t = mk("swf", [128, 4, 128]); P.dma("sync", swf[:], sgwT.rearrange("g j i -> j g i"), writes=[swf_t])
        smk, smk_t = mk("smk", [128, 128]); P.dma("sync", smk[:], sgmask_d, writes=[smk_t])
        M.swT, M.swT_t = mk("swT", [128, 4, 128], BF16)
        P.op("vector", lambda e: e.tensor_tensor(out=M.swT[:], in0=swf[:], in1=smk[:].unsqueeze(1).to_broadcast([128, 4, 128]), op=ALU.mult), reads=[swf_t, smk_t], writes=[M.swT_t])
        M.vg, M.vg_t = mk("vg", [128, SGD]); M.vsq, M.vsq_t = mk("vsq", [128, SGD]); M.vb, M.vb_t = mk("vb", [128, SGD], BF16)
        M.st, M.st_t = mk("sgst", [128, 2]); M.mx, M.mx_t = mk("sgmx", [128, 128])
        M.gate = []; M.gate_t = []
        for b in range(3):
            g_, g_t = mk(f"gate{b}", [128, TT]); M.gate.append(g_); M.gate_t.append(g_t)
        M.macc, M.macc_t = mk("macc", [128, TT])
        finals = []
        for ti in range(Ttot // TT):
            t0 = ti * TT
            load_x(P, C, x1T, t0)
            mixer(P, C, M, t0)
            if DBG:
                finals += store_x(P, C, x2T, t0)
            ffn_ln(P, C, w1, w2, ln3g, ln3g_t, ln3b, ln3b_t)
            finals += store_x(P, C, x3T, t0)
        P.finish(finals)
    return nc


NL = 4
RW_OFF = 0
ATT_OFF = 4384
SGC_OFF = 3360
GATEC_OFF = 7024


class Exch:
    def __init__(self, nc, name, rows, cols, cr):
        self.S = nc.dram_tensor("S_" + name, [rows, cols], F32).ap()
        self.cr = cr
        self.nch = rows // cr
        self.G = [nc.dram_tensor(f"G_{name}{k}", [2 * cr, cols], F32).ap() for k in range(self.nch)]

    def pairs(self):
        return [(self.S[k * self.cr:(k + 1) * self.cr, :], self.G[k]) for k in range(self.nch)]

    def rows(self, rank, r0, n):
        k, off = r0 // self.cr, r0 % self.cr
        assert off + n <= self.cr
        return self.G[k][rank * self.cr + off:rank * self.cr + off + n, :]


def project_tok(P, C, w_ap, c0, ncol, out_ap, oc0, t0, stg):
    wv = w_ap.rearrange("(c p) n -> p c n", p=128)
    wt, wtt = wslot(P, C)
    if ncol == 512:
        wload(P, C, wt, wtt, [(wt[:, :, 0:ncol], wv[:, :, c0:c0 + ncol])], f"{C.ph}_pt{c0}", KC * 512)
    else:
        P.dma("gpsimd", wt[:, :, 0:ncol], wv[:, :, c0:c0 + ncol], writes=[wtt])
    for ch in range(TT // 128):
        ts = slice(ch * 128, (ch + 1) * 128)
        pp = next_bank(C)
        fns = [(lambda e, k=k, pp=pp, wt=wt, ts=ts: e.matmul(pp[:, 0:ncol], C.xb[:, k, ts], wt[:, k, 0:ncol], start=(k == 0), stop=(k == KC - 1))) for k in range(KC)]
        P.group("tensor", fns, reads=[wtt] + C.xb_t, writes=[pp])
        j = stg["cnt"] % len(stg["o"])
        stg["cnt"] += 1
        ot, ott = stg["o"][j], stg["ot"][j]
        if j % 2 == 0:
            P.op("scalar", lambda e, ot=ot, pp=pp: e.copy(out=ot[:, 0:ncol], in_=pp[:, 0:ncol]), reads=[pp], writes=[ott])
        else:
            P.op("vector", lambda e, ot=ot, pp=pp: e.tensor_copy(out=ot[:, 0:ncol], in_=pp[:, 0:ncol]), reads=[pp], writes=[ott])
        P.dma("sync", out_ap[t0 + ch * 128:t0 + (ch + 1) * 128, oc0:oc0 + ncol], ot[:, 0:ncol], reads=[ott])


def project_feat(P, C, w_ap, c0, nrows, out_ap, r0, t0, stg):
    wv = w_ap.rearrange("(c p) n -> p c n", p=128)
    for g0 in range(0, nrows, 512):
        n = min(512, nrows - g0)
        wt, wtt = wslot(P, C)
        wload(P, C, wt, wtt, [(wt[:, :, 0:n], wv[:, :, c0 + g0:c0 + g0 + n])], f"{C.ph}_pf{c0 + g0}", KC * 512)
        for j0 in range(0, n, 128):
            pp = next_bank(C)
            fns = [(lambda e, k=k, pp=pp, wt=wt, j0=j0: e.matmul(pp[:], wt[:, k, j0:j0 + 128], C.xb[:, k, :], start=(k == 0), stop=(k == KC - 1))) for k in range(KC)]
            P.group("tensor", fns, reads=[wtt] + C.xb_t, writes=[pp])
            j = stg["cnt"] % len(stg["o"])
            stg["cnt"] += 1
            ot, ott = stg["o"][j], stg["ot"][j]
            if j % 2 == 0:
                P.op("scalar", lambda e, ot=ot, pp=pp: e.copy(out=ot[:], in_=pp[:]), reads=[pp], writes=[ott])
            else:
                P.op("vector", lambda e, ot=ot, pp=pp: e.tensor_copy(out=ot[:], in_=pp[:]), reads=[pp], writes=[ott])
            P.dma("sync", out_ap[r0 + g0 + j0:r0 + g0 + j0 + 128, t0:t0 + TT], ot[:], reads=[ott])


def load_y_branches(P, C, M, t0):
    hb, ht = C.h, C.h_t
    for ch in range(TT // 128):
        tok = t0 + ch * 128
        j = (t0 // 128 + ch)
        e_ = j % 2
        cands = []
        for hf in range(2):
            yc, yct = M.ycand[hf]
            P.dma("sync", yc[:, 0:512], M.G_yrw.rows(0, hf * 2048 + tok, 128), writes=[yct])
            P.dma("sync", yc[:, 512:1024], M.G_yrw.rows(1, hf * 2048 + tok, 128), writes=[yct])
            slot = hf * 8 + j // 2
            P.dma("sync", yc[:, 1024:1536], M.G_yat.rows(e_, slot * 128, 128), writes=[yct])
            cands.append((yc, yct))
        ys, yst = M.ysel
        P.op("vector", lambda e, ys=ys, c0=cands[0][0]: e.tensor_scalar(out=ys[:], in0=c0[:], scalar1=M.msel[:, 0:1], scalar2=None, op0=ALU.mult), reads=[cands[0][1], M.msel_t], writes=[yst])
        P.op("vector", lambda e, ys=ys, c1=cands[1][0]: e.scalar_tensor_tensor(out=ys[:], in0=c1[:], scalar=M.msel[:, 1:2], in1=ys[:], op0=ALU.mult, op1=ALU.add), reads=[cands[1][1], M.msel_t], writes=[yst])
        for g in range(3):
            pb = next_bank(C)
            for q in range(4):
                c = g * 4 + q
                P.op("tensor", lambda e, pb=pb, q=q, c=c, ys=ys: e.transpose(pb[:, q * 128:(q + 1) * 128], ys[:, c * 128:(c + 1) * 128], M.identf[:]), reads=[yst, M.identf_t], writes=[pb])
            outap = hb[:, 16 + g * 4:16 + g * 4 + 4, ch * 128:(ch + 1) * 128]
            inap = pb[:].rearrange("p (q t) -> p q t", q=4)
            if g % 2 == 0:
                P.op("scalar", lambda e, outap=outap, inap=inap: e.copy(out=outap, in_=inap), reads=[pb], writes=[ht[16 + g * 4 + q] for q in range(4)])
            else:
                P.op("vector", lambda e, outap=outap, inap=inap: e.tensor_copy(out=outap, in_=inap), reads=[pb], writes=[ht[16 + g * 4 + q] for q in range(4)])


def build_fused(nlayers=NL, L=NL):
    nc = bass.Bass("TRN2", target_bir_lowering=False)
    din = lambda n, sh: nc.dram_tensor(n, sh, F32, kind="ExternalInput").ap()
    I = {}
    I["xT"] = din("xT", [D, 2048])
    for n, sh in (("ffn1_w_in", [L, D, 2 * DFF]), ("ffn1_w_out", [L, DFF, D]), ("ln1_g", [L, D]), ("ln1_b", [L, D]),
                  ("w_in", [L, D, 13168]), ("b_gate", [L, 3 * D]),
                  ("mu_c", [L, 1824]), ("w0_c", [L, 512]), ("a0_c", [L, 512]), ("k_k_c", [L, 512]), ("k_a_c", [L, 512]), ("r_k_c", [L, 512]),
                  ("gn_g_c", [L, 512]), ("gn_b_c", [L, 512]), ("w2_c", [L, 64, 512]), ("a2_c", [L, 64, 512]), ("g2_c", [L, 160, 512]),
                  ("sg_ln_g", [L, 512]), ("sg_ln_b", [L, 512]), ("sgwT", [L, 4, 128, 128]), ("sg_b", [L, 4, 128]),
                  ("idx_ln_g", [L, 64]), ("idx_ln_b", [L, 64]), ("w_branch", [L, D, D]), ("w_o", [L, D, D]),
                  ("ln2_g", [L, D]), ("ln2_b", [L, D]), ("ffn2_w_in", [L, D, 2 * DFF]), ("ffn2_w_out", [L, DFF, D]), ("ln3_g", [L, D]), ("ln3_b", [L, D]),
                  ("msel", [128, 2]), ("sgmask", [128, 128])):
        I[n] = din(n, sh)
    rc = {n: din("rc_" + n, list(v.shape)) for n, v in rwkv_consts().items()}
    dc = {n: din("dc_" + n, list(v.shape)) for n, v in dsa_consts(0, np.zeros((32, 8), np.float32)).items()}
    xoT = nc.dram_tensor("xoT", [D, 2048], F32, kind="ExternalOutput").ap()
    dt_ = lambda n, sh: nc.dram_tensor(n, sh, F32).ap()
    x1s = dt_("x1s", [D, 2048]); xcur = dt_("xcur", [D, 2048])
    X_prw = Exch(nc, "prw", 2048, 3360, 128); X_qT = Exch(nc, "qT", 512, 2048, 256); X_kT = Exch(nc, "kT", 512, 2048, 256)
    X_qiT = Exch(nc, "qiT", 1024, 2048, 256); X_V = Exch(nc, "V", 2048, 512, 1024); X_kw = Exch(nc, "kw", 2048, 80, 2048)
    X_yrw = Exch(nc, "yrw", 4096, 512, 1024); X_yat = Exch(nc, "yat", 2048, 512, 1024)
    S_prw, S_qT, S_kT, S_qiT, S_V, S_kw, S_yrw, S_yat = X_prw.S, X_qT.S, X_kT.S, X_qiT.S, X_V.S, X_kw.S, X_yrw.S, X_yat.S
    groups = [[0, 1], [2, 3], [4, 5], [6, 7]]
    with ExitStack() as es:
        P = Prog(nc, es)
        banks = [T(P.ps(f"bank{i}", [128, 512], F32)) for i in range(8)]

        def mk(name, shape, dt=F32):
            t = P.sb(name, shape, dt)
            return t, T(t[:])

        def exchange(pairs):
            P.barrier()
            for (s_, g_) in pairs:
                P.coll("AllGather", groups, s_, g_)
            P.barrier()

        WC = {"t": {}, "ev": {}}
        for l in range(nlayers):
            P.push_scope()
            C = setup_common(P, nc, banks)
            C.wc = WC; C.ph = "p1"
            lng, lng_t = load_vec_cols(P, "ln1g", I["ln1_g"][l], KC)
            lnb, lnb_t = load_vec_cols(P, "ln1b", I["ln1_b"][l], KC)
            stg = {"o": [], "ot": [], "cnt": 0}
            for i in range(3):
                o_, ot_ = mk(f"stg{i}", [128, TT]); stg["o"].append(o_); stg["ot"].append(ot_)
            xsrc = I["xT"] if l == 0 else xcur
            w_in_l = I["w_in"][l]
            for ti in range(2048 // TT):
                t0 = ti * TT
                C.ti = ti
                load_x(P, C, xsrc, t0)
                ffn_ln(P, C, I["ffn1_w_in"][l], I["ffn1_w_out"][l], lng, lng_t, lnb, lnb_t)
                store_x(P, C, x1s, t0)
                for c0 in range(0, 3360, 512):
                    project_tok(P, C, w_in_l, RW_OFF + c0, min(512, 3360 - c0), S_prw, c0, t0, stg)
                project_feat(P, C, w_in_l, ATT_OFF + 0, 512, S_qT, 0, t0, stg)
                project_feat(P, C, w_in_l, ATT_OFF + 512, 512, S_kT, 0, t0, stg)
                project_tok(P, C, w_in_l, ATT_OFF + 1024, 512, S_V, 0, t0, stg)
                project_feat(P, C, w_in_l, ATT_OFF + 1536, 1024, S_qiT, 0, t0, stg)
                project_tok(P, C, w_in_l, ATT_OFF + 2560, 80, S_kw, 0, t0, stg)
            exchange(X_prw.pairs() + X_qT.pairs() + X_kT.pairs() + X_qiT.pairs() + X_V.pairs() + X_kw.pairs())
            P.pop_scope()
            P.push_scope()
            prm = {"mu": I["mu_c"][l], "w0": I["w0_c"][l], "a0": I["a0_c"][l], "k_k": I["k_k_c"][l], "k_a": I["k_a_c"][l], "r_k": I["r_k_c"][l],
                   "gn_g": I["gn_g_c"][l], "gn_b": I["gn_b_c"][l], "w2": I["w2_c"][l], "a2": I["a2_c"][l], "g2": I["g2_c"][l]}
            rwkv_phase(P, banks, X_prw, prm, rc, I["msel"], S_yrw)
            P.barrier()
            P.pop_scope()
            P.push_scope()
            dsa_phase(P, banks, {"kT": X_kT, "V": X_V, "kw": X_kw, "qT": X_qT, "qiT": X_qiT}, I["idx_ln_g"][l], I["idx_ln_b"][l], dc, I["msel"], S_yat)
            exchange(X_yrw.pairs() + X_yat.pairs())
            P.pop_scope()
            P.push_scope()
            C = setup_common(P, nc, banks)
            C.wc = WC; C.ph = "p4"
            M = Ctx()
            M.w_in, M.w_branch, M.w_o = I["w_in"][l], I["w_branch"][l], I["w_o"][l]
            M.G_yrw, M.G_yat = X_yrw, X_yat
            M.ln2g, M.ln2g_t = load_vec_cols(P, "ln2g", I["ln2_g"][l], KC)
            M.ln2b, M.ln2b_t = load_vec_cols(P, "ln2b", I["ln2_b"][l], KC)
            ln3g, ln3g_t = load_vec_cols(P, "ln3g", I["ln3_g"][l], KC)
            ln3b, ln3b_t = load_vec_cols(P, "ln3b", I["ln3_b"][l], KC)
            M.bg, M.bg_t = load_vec_cols(P, "bg", I["b_gate"][l], 3 * KC)
            M.msel, M.msel_t = mk("msel4", [128, 2]); P.dma("sync", M.msel[:], I["msel"], writes=[M.msel_t])
            M.identf, M.identf_t = mk("identf", [128, 128]); P.dma("sync", M.identf[:], rc["ident"], writes=[M.identf_t])
            M.sglg, M.sglg_t = mk("sglg", [128, SGD]); P.dma("sync", M.sglg[:], I["sg_ln_g"][l].partition_broadcast(128), writes=[M.sglg_t])
            M.sglb, M.sglb_t = mk("sglb", [128, SGD]); P.dma("sync", M.sglb[:], I["sg_ln_b"][l].partition_broadcast(128), writes=[M.sglb_t])
            M.sgbb, M.sgbb_t = mk("sgbb", [128, 4, 128])
            for g in range(4):
                P.dma("sync", M.sgbb[:, g, :], I["sg_b"][l][g].partition_broadcast(128), writes=[M.sgbb_t])
            swf, swf_t = mk("swf", [128, 4, 128]); P.dma("sync", swf[:], I["sgwT"][l].rearrange("g j i -> j g i"), writes=[swf_t])
            smk, smk_t = mk("smk", [128, 128]); P.dma("sync", smk[:], I["sgmask"], writes=[smk_t])
            M.swT, M.swT_t = mk("swT", [128, 4, 128], BF16)
            P.op("vector", lambda e, M=M, swf=swf, smk=smk: e.tensor_tensor(out=M.swT[:], in0=swf[:], in1=smk[:].unsqueeze(1).to_broadcast([128, 4, 128]), op=ALU.mult), reads=[swf_t, smk_t], writes=[M.swT_t])
            M.vg, M.vg_t = mk("vg", [128, SGD]); M.vsq, M.vsq_t = mk("vsq", [128, SGD]); M.vb, M.vb_t = mk("vb", [128, SGD], BF16)
            M.st, M.st_t = mk("sgst", [128, 2]); M.mx, M.mx_t = mk("sgmx", [128, 128])
            M.gate = []; M.gate_t = []
            for b in range(3):
                g_, g_t = mk(f"gate{b}", [128, TT]); M.gate.append(g_); M.gate_t.append(g_t)
            M.macc, M.macc_t = mk("macc", [128, TT])
            M.ycand = [mk(f"ycand{i}", [128, 1536]) for i in range(2)]
            M.ysel = mk("ysel", [128, 1536])
            M.fused = True
            for ti in range(2048 // TT):
                t0 = ti * TT
                C.ti = ti
                load_x(P, C, x1s, t0)
                mixer(P, C, M, t0, load_y_branches)
                ffn_ln(P, C, I["ffn2_w_in"][l], I["ffn2_w_out"][l], ln3g, ln3g_t, ln3b, ln3b_t)
                fin = store_x(P, C, xoT if l == nlayers - 1 else xcur, t0)
            P.barrier()
            P.pop_scope()
        P.finish(P.all_events())
    return nc


_FUSED = {}


def _c(a):
    return np.ascontiguousarray(a, dtype=np.float32)


def make_in_maps(inp):
    x = np.asarray(inp["x"], dtype=np.float32)
    B, S_, D_ = x.shape
    xf = x.reshape(B * S_, D_)
    NCORE = 8
    TSH = (B * S_) // NCORE
    shared = {}
    for k in ("ffn1_w_in", "ffn1_w_out", "ln1_g", "ln1_b", "w_in", "b_gate", "sg_ln_g", "sg_ln_b", "sg_b", "idx_ln_g", "idx_ln_b",
              "w_branch", "w_o", "ln2_g", "ln2_b", "ffn2_w_in", "ffn2_w_out", "ln3_g", "ln3_b"):
        shared[k] = _c(inp[k])
    shared["sgwT"] = _c(np.transpose(np.asarray(inp["sg_w"], dtype=np.float32), (0, 1, 3, 2)))
    shared["sgmask"] = _c(sg_consts()["sgmask"])
    for n, v in rwkv_consts().items():
        shared["rc_" + n] = _c(v)
    rel_bias = np.asarray(inp["rel_bias"], dtype=np.float32)
    L = shared["w_in"].shape[0]
    per_e = []
    for e in range(2):
        m = {}
        hs = slice(e * 512, (e + 1) * 512)
        cols = np.r_[e * 512:(e + 1) * 512, 1024 + e * 512:1024 + (e + 1) * 512, 2048 + e * 512:2048 + (e + 1) * 512, 3072:3360]
        m["mu_c"] = _c(np.asarray(inp["rwkv_mu"])[:, cols])
        for n, src in (("w0_c", "rwkv_w0"), ("a0_c", "rwkv_a0"), ("k_k_c", "rwkv_k_k"), ("k_a_c", "rwkv_k_a"), ("gn_g_c", "rwkv_gn_g"), ("gn_b_c", "rwkv_gn_b")):
            m[n] = _c(np.asarray(inp[src])[:, hs])
        m["r_k_c"] = _c(np.asarray(inp["rwkv_r_k"]).reshape(L, -1)[:, hs])
        for n, src in (("w2_c", "rwkv_w2"), ("a2_c", "rwkv_a2"), ("g2_c", "rwkv_g2")):
            m[n] = _c(np.asarray(inp[src])[:, :, hs])
        for n, v in dsa_consts(e, rel_bias).items():
            m["dc_" + n] = _c(v)
        ms = np.zeros((128, 2), np.float32)
        ms[:, e] = 1.0
        m["msel"] = ms
        per_e.append(m)
    in_maps = []
    for c in range(NCORE):
        m = dict(shared)
        m.update(per_e[c % 2])
        m["xT"] = _c(xf[c * TSH:(c + 1) * TSH].T)
        in_maps.append(m)
    return in_maps, (B, S_, D_)


def kernel(**inp):
    if "nc" not in _FUSED:
        _FUSED["nc"] = build_fused()
    in_maps, (B, S_, D_) = make_in_maps(inp)
    cores = list(range(8))
    res = run_bass_kernel_spmd(_FUSED["nc"], in_maps, core_ids=cores)
    out = np.concatenate([np.ascontiguousarray(res.results[c]["xoT"].T) for c in cores], axis=0).reshape(B, S_, D_)
    return out.astype(np.float32)
```

```python
import math


import numpy as np
from contextlib import ExitStack
import concourse.bass as bass
import concourse.mybir as mybir
from concourse.bass_utils import run_bass_kernel_spmd

F32 = mybir.dt.float32
BF16 = mybir.dt.bfloat16
AF = mybir.ActivationFunctionType
ALU = mybir.AluOpType
AX = mybir.AxisListType

ENGS = ("sync", "scalar", "vector", "gpsimd", "tensor")
SEM_ROLL = 30000


class T:
    __slots__ = ("ap", "w", "r")

    def __init__(self, ap):
        self.ap = ap
        self.w = None
        self.r = []

    def __getitem__(self, idx):
        return self.ap[idx]


class Prog:
    def __init__(self, nc, es, n_dma_sems=24):
        self.nc = nc
        self.es = es
        self.q = {e: [] for e in ENGS}
        self.cur_sem = {}
        self.cnt = {}
        self.nsem = 0
        for e in ENGS:
            self._new_eng_sem(e)
        self.dma_sems = [es.enter_context(nc.semaphore(f"dma{i}")) for i in range(2 * n_dma_sems)]
        self.dma_cnt = [0] * (2 * n_dma_sems)
        self.dma_last = [None] * (2 * n_dma_sems)
        self.dma_rr = {"hw": 0, "sw": 0}
        self.n_dma_sems = n_dma_sems
        self.waited = {e: {} for e in ENGS}
        self.semobj = {}
        self.n_inst = {e: 0 for e in ENGS}
        self.n_wait = 0

    def _new_eng_sem(self, e):
        s = self.es.enter_context(self.nc.semaphore(f"s_{e}_{self.nsem}"))
        self.nsem += 1
        self.cur_sem[e] = s
        self.cnt[e] = 0

    def sb(self, name, shape, dt):
        self.uid = getattr(self, "uid", 0) + 1
        es = self.scopes[-1] if getattr(self, "scopes", None) else self.es
        return es.enter_context(self.nc.sbuf_tensor(f"{name}_u{self.uid}", list(shape), dt))

    def push_scope(self):
        if not hasattr(self, "scopes"):
            self.scopes = []
        es = ExitStack()
        es.__enter__()
        self.scopes.append(es)

    def pop_scope(self):
        es = self.scopes.pop()
        es.__exit__(None, None, None)

    def all_events(self):
        evs = [(self.cur_sem[e], self.cnt[e]) for e in ENGS if self.cnt[e] > 0]
        evs += [x for x in self.dma_last if x is not None]
        evs += list(getattr(self, "coll_events", []))
        return evs

    def barrier(self):
        evs = self.all_events()
        for e in ENGS:
            w = self._filter_waits(e, evs)
            if w:
                self.q[e].append((None, w, None, 0))

    def coll(self, kind, groups, src, dst):
        if not hasattr(self, "coll_sem"):
            self.coll_sem = self.es.enter_context(self.nc.semaphore("collsem"))
            self.coll_cnt = 0
        self.coll_cnt += 1
        ev = (self.coll_sem, self.coll_cnt)

        def fn(e):
            return e.collective_compute(kind, ALU.bypass, replica_groups=groups, ins=[src.opt()], outs=[dst.opt()])
        self.q["gpsimd"].append((fn, [], self.coll_sem, 1))
        self.coll_events = [ev]
        return ev

    def ps(self, name, shape, dt=F32):
        return self.es.enter_context(self.nc.psum_tensor(name, list(shape), dt))

    def _collect(self, reads, writes):
        waits = []
        for t in reads:
            if t.w is not None:
                waits.append(t.w)
        for t in writes:
            if t.w is not None:
                waits.append(t.w)
            waits.extend(t.r)
        return waits

    def _filter_waits(self, eng, waits):
        out = {}
        wd = self.waited[eng]
        for (sem, val) in waits:
            k = id(sem)
            self.semobj[k] = sem
            if wd.get(k, 0) >= val:
                continue
            if out.get(k, 0) < val:
                out[k] = val
        res = []
        for k, v in out.items():
            wd[k] = v
            res.append((self.semobj[k], v))
        return res

    def op(self, eng, fn, reads=(), writes=(), extra=()):
        waits = self._collect(reads, writes) + list(extra)
        waits = self._filter_waits(eng, waits)
        if self.cnt[eng] >= SEM_ROLL:
            self._new_eng_sem(eng)
        sem = self.cur_sem[eng]
        self.cnt[eng] += 1
        ev = (sem, self.cnt[eng])
        self.q[eng].append((fn, waits, sem, 1))
        self.n_inst[eng] += 1
        self.n_wait += len(waits)
        for t in reads:
            t.r.append(ev)
        for t in writes:
            t.w = ev
            t.r = []
        return ev

    def group(self, eng, fns, reads=(), writes=()):
        waits = self._collect(reads, writes)
        waits = self._filter_waits(eng, waits)
        if self.cnt[eng] >= SEM_ROLL:
            self._new_eng_sem(eng)
        sem = self.cur_sem[eng]
        self.cnt[eng] += 1
        ev = (sem, self.cnt[eng])
        n = len(fns)
        for i, fn in enumerate(fns):
            self.q[eng].append((fn, waits if i == 0 else [], sem if i == n - 1 else None, 1))
        self.n_inst[eng] += n
        for t in reads:
            t.r.append(ev)
        for t in writes:
            t.w = ev
            t.r = []
        return ev

    def dma(self, queue, out, in_, reads=(), writes=(), extra=(), **kw):
        kind = "sw" if queue == "gpsimd" else "hw"
        i = self.dma_rr[kind] + (self.n_dma_sems if kind == "sw" else 0)
        self.dma_rr[kind] = (self.dma_rr[kind] + 1) % self.n_dma_sems
        waits = self._collect(reads, writes) + list(extra)
        if self.dma_last[i] is not None:
            waits.append(self.dma_last[i])
        waits = self._filter_waits(queue, waits)
        sem = self.dma_sems[i]
        self.dma_cnt[i] += 16
        ev = (sem, self.dma_cnt[i])
        self.dma_last[i] = ev

        def fn(e, out=out, in_=in_, kw=kw):
            return e.dma_start(out=out, in_=in_, **kw)
        self.q[queue].append((fn, waits, sem, 16))
        self.n_inst[queue] += 1
        for t in reads:
            t.r.append(ev)
        for t in writes:
            t.w = ev
            t.r = []
        return ev

    def finish(self, final_events):
        nc = self.nc
        fw = list(final_events)

        def run(e, name):
            for (fn, waits, sem, inc) in self.q[name]:
                for (s, v) in waits:
                    e.wait_ge(s, v)
                if fn is None:
                    continue
                ins = fn(e)
                if sem is not None:
                    ins.then_inc(sem, inc)
            if name == "sync":
                for (s, v) in fw:
                    e.wait_ge(s, v)

        with nc.Block() as block:
            @block.sync
            def _(e):
                run(e, "sync")

            @block.scalar
            def _(e):
                run(e, "scalar")

            @block.vector
            def _(e):
                run(e, "vector")

            @block.gpsimd
            def _(e):
                run(e, "gpsimd")

            @block.tensor
            def _(e):
                run(e, "tensor")


D = 2048
DFF = 5632
KC = D // 128
HC = DFF // 128
TT = 512
ALPHA = 8 ** 0.25
LN_EPS = 1e-5


class Ctx:
    pass


def setup_common(P, nc, banks=None):
    C = Ctx()
    C.banks = banks if banks is not None else [T(P.ps(f"bank{i}", [128, 512], F32)) for i in range(8)]
    C.bank_rr = 0
    C.xb = P.sb("xb", [128, KC, TT], BF16)
    C.xf = P.sb("xf", [128, KC, TT], F32)
    C.h = P.sb("h", [128, HC, TT], BF16)
    C.xb_t = [T(C.xb[:, c, :]) for c in range(KC)]
    C.xf_t = [T(C.xf[:, c, :]) for c in range(KC)]
    C.h_t = [T(C.h[:, c, :]) for c in range(HC)]
    C.w1 = [P.sb(f"w1_{i}", [128, KC, 512], BF16) for i in range(2)]
    C.w1_t = [T(C.w1[i][:]) for i in range(2)]
    C.w2 = [P.sb(f"w2_{i}", [128, HC, 128], BF16) for i in range(2)]
    C.w2_t = [T(C.w2[i][:]) for i in range(2)]
    C.sg = [P.sb(f"sg_{i}", [128, TT], F32) for i in range(2)]
    C.sg_t = [T(C.sg[i][:]) for i in range(2)]
    C.sq = [P.sb(f"sq_{i}", [128, TT], F32) for i in range(2)]
    C.sq_t = [T(C.sq[i][:]) for i in range(2)]
    C.rstd = P.sb("rstd", [128, TT], F32)
    C.rstd_t = T(C.rstd[:])
    C.onesM = P.sb("onesM", [128, 128], F32)
    C.onesM_t = T(C.onesM[:])
    P.op("vector", lambda e: e.memset(C.onesM[:], 1.0 / D), writes=[C.onesM_t])
    C.cnt = {"w1": 0, "w2": 0, "sg": 0, "sq": 0}
    C.wc = None; C.ph = "x"; C.ti = 0
    return C


def wload(P, C, wt, wtt, parts, key, nfree):
    wc = getattr(C, "wc", None)
    if wc is None:
        for (sap, dap) in parts:
            P.dma("gpsimd", sap, dap, writes=[wtt])
        return
    if key not in wc["t"]:
        wc["t"][key] = (P.nc.dram_tensor(f"wsc_{key}", [128, nfree], BF16).ap(), None)
    sc = wc["t"][key][0]
    flat = wt[:].rearrange("p a b -> p (a b)")[:, 0:nfree] if len(wt[:].shape) == 3 else wt[:, 0:nfree]
    if C.ti == 0:
        for (sap, dap) in parts:
            P.dma("gpsimd", sap, dap, writes=[wtt])
        ev = P.dma("sync", sc, flat, reads=[wtt])
        wc["ev"][key] = ev
    else:
        P.dma("sync", flat, sc, writes=[wtt], extra=[wc["ev"][key]])


def next_bank(C):
    b = C.banks[C.bank_rr]
    C.bank_rr = (C.bank_rr + 1) % 8
    return b


def load_vec_cols(P, name, src, n):
    t = P.sb(name, [128, n], F32)
    tt = T(t[:])
    P.dma("sync", t[:], src.rearrange("(c p) -> p c", p=128), writes=[tt], allow_slow_non_contiguous=True)
    return t, tt


def ffn_ln(P, C, w_in, w_out, lng, lng_t, lnb, lnb_t):
    nc = P.nc
    w_in_v = w_in.rearrange("(c p) n -> p c n", p=128)
    w_out_v = w_out.rearrange("(c p) n -> p c n", p=128)
    for hg in range(HC // 2):
        slot = C.cnt["w1"] % 2
        C.cnt["w1"] += 1
        wt, wtt = C.w1[slot], C.w1_t[slot]
        wload(P, C, wt, wtt, [(wt[:, :, 0:256], w_in_v[:, :, hg * 256:(hg + 1) * 256]), (wt[:, :, 256:512], w_in_v[:, :, DFF + hg * 256:DFF + (hg + 1) * 256])],
              f"{C.ph}_a{hg}", KC * 512)
        for j in range(2):
            ht = hg * 2 + j
            pg = next_bank(C)
            pu = next_bank(C)
            fns = []
            for k in range(KC):
                fns.append(lambda e, k=k, pg=pg, j=j, wt=wt: e.matmul(pg[:], wt[:, k, j * 128:(j + 1) * 128], C.xb[:, k, :], start=(k == 0), stop=(k == KC - 1)))
            P.group("tensor", fns, reads=[wtt] + C.xb_t, writes=[pg])
            fns = []
            for k in range(KC):
                fns.append(lambda e, k=k, pu=pu, j=j, wt=wt: e.matmul(pu[:], wt[:, k, 256 + j * 128:256 + (j + 1) * 128], C.xb[:, k, :], start=(k == 0), stop=(k == KC - 1)))
            P.group("tensor", fns, reads=[wtt] + C.xb_t, writes=[pu])
            s = C.cnt["sg"] % 2
            C.cnt["sg"] += 1
            sg, sgt = C.sg[s], C.sg_t[s]
            P.op("scalar", lambda e, sg=sg, pg=pg: e.activation(out=sg[:], in_=pg[:], func=AF.Silu), reads=[pg], writes=[sgt])
            P.op("vector", lambda e, sg=sg, pu=pu, ht=ht: e.scalar_tensor_tensor(out=C.h[:, ht, :], in0=sg[:], scalar=0.5, in1=pu[:], op0=ALU.mult, op1=ALU.mult),
                 reads=[sgt, pu], writes=[C.h_t[ht]])
    for dt_ in range(KC):
        slot = C.cnt["w2"] % 2
        C.cnt["w2"] += 1
        wt, wtt = C.w2[slot], C.w2_t[slot]
        wload(P, C, wt, wtt, [(wt[:, 0:HC // 2, :], w_out_v[:, 0:HC // 2, dt_ * 128:(dt_ + 1) * 128]), (wt[:, HC // 2:, :], w_out_v[:, HC // 2:, dt_ * 128:(dt_ + 1) * 128])],
              f"{C.ph}_b{dt_}", HC * 128)
        py = next_bank(C)
        fns = []
        for k in range(HC):
            fns.append(lambda e, k=k, py=py, wt=wt: e.matmul(py[:], wt[:, k, :], C.h[:, k, :], start=(k == 0), stop=(k == HC - 1)))
        P.group("tensor", fns, reads=[wtt] + C.h_t, writes=[py])
        P.op("vector", lambda e, dt_=dt_, py=py: e.scalar_tensor_tensor(out=C.xf[:, dt_, :], in0=C.xf[:, dt_, :], scalar=ALPHA, in1=py[:], op0=ALU.mult, op1=ALU.add),
             reads=[py], writes=[C.xf_t[dt_]])
    layer_norm(P, C, lng, lng_t, lnb, lnb_t)


def layer_norm(P, C, lng, lng_t, lnb, lnb_t):
    pm = next_bank(C)
    fns = []
    for c in range(KC):
        fns.append(lambda e, c=c: e.matmul(pm[:], C.onesM[:], C.xf[:, c, :], start=(c == 0), stop=(c == KC - 1)))
    P.group("tensor", fns, reads=[C.onesM_t] + C.xf_t, writes=[pm])
    pv = next_bank(C)
    for c in range(KC):
        P.op("vector", lambda e, c=c: e.tensor_tensor(out=C.xf[:, c, :], in0=C.xf[:, c, :], in1=pm[:], op=ALU.subtract),
             reads=[pm], writes=[C.xf_t[c]])
        s = C.cnt["sq"] % 2
        C.cnt["sq"] += 1
        sq, sqt = C.sq[s], C.sq_t[s]
        P.op("scalar", lambda e, c=c, sq=sq: e.activation(out=sq[:], in_=C.xf[:, c, :], func=AF.Square), reads=[C.xf_t[c]], writes=[sqt])
        P.op("tensor", lambda e, c=c, sq=sq: e.matmul(pv[:], C.onesM[:], sq[:], start=(c == 0), stop=(c == KC - 1)),
             reads=[sqt, C.onesM_t], writes=[pv])
    P.op("scalar", lambda e: e.activation(out=C.rstd[:], in_=pv[:], func=AF.Sqrt, bias=LN_EPS), reads=[pv], writes=[C.rstd_t])
    P.op("vector", lambda e: e.reciprocal(out=C.rstd[:], in_=C.rstd[:]), reads=[C.rstd_t], writes=[C.rstd_t])
    for c in range(KC):
        P.op("vector", lambda e, c=c: e.tensor_tensor(out=C.xf[:, c, :], in0=C.xf[:, c, :], in1=C.rstd[:], op=ALU.mult),
             reads=[C.rstd_t], writes=[C.xf_t[c]])
        P.op("scalar", lambda e, c=c: e.activation(out=C.xf[:, c, :], in_=C.xf[:, c, :], func=AF.Identity, scale=lng[:, c:c + 1], bias=lnb[:, c:c + 1]),
             reads=[lng_t, lnb_t], writes=[C.xf_t[c]])
        P.op("gpsimd", lambda e, c=c: e.tensor_copy(out=C.xb[:, c, :], in_=C.xf[:, c, :]), reads=[C.xf_t[c]], writes=[C.xb_t[c]])


def load_x(P, C, xT, t0):
    xv = xT.rearrange("(c p) t -> p c t", p=128)
    for c in range(KC):
        P.dma("sync", C.xf[:, c, :], xv[:, c, t0:t0 + TT], writes=[C.xf_t[c]])
    for c in range(KC):
        P.op("gpsimd", lambda e, c=c: e.tensor_copy(out=C.xb[:, c, :], in_=C.xf[:, c, :]), reads=[C.xf_t[c]], writes=[C.xb_t[c]])


def store_x(P, C, oT, t0):
    ov = oT.rearrange("(c p) t -> p c t", p=128)
    evs = []
    for c in range(KC):
        evs.append(P.dma("sync", ov[:, c, t0:t0 + TT], C.xf[:, c, :], reads=[C.xf_t[c]]))
    return evs


def project(P, C, wp, col_tiles, outT, t0, wslots, evac_rr=[0]):
    wp_v = wp.rearrange("(c p) n -> p c n", p=128)
    evs = []
    for (c0, ncol, r0) in col_tiles:
        i = wslots["cnt"] % len(wslots["t"])
        wslots["cnt"] += 1
        wt, wtt = wslots["t"][i], wslots["tt"][i]
        P.dma("gpsimd", wt[:, :, 0:ncol], wp_v[:, :, c0:c0 + ncol], writes=[wtt])
        pp = next_bank(C)
        fns = []
        for k in range(KC):
            fns.append(lambda e, k=k, pp=pp, wt=wt, ncol=ncol: e.matmul(pp[0:ncol, :], wt[:, k, 0:ncol], C.xb[:, k, :], start=(k == 0), stop=(k == KC - 1)))
        P.group("tensor", fns, reads=[wtt] + C.xb_t, writes=[pp])
        j = wslots["ocnt"] % len(wslots["o"])
        wslots["ocnt"] += 1
        ot, ott = wslots["o"][j], wslots["ot"][j]
        eng = "scalar" if (j % 2 == 0) else "vector"
        if eng == "scalar":
            P.op("scalar", lambda e, ot=ot, pp=pp, ncol=ncol: e.copy(out=ot[0:ncol, :], in_=pp[0:ncol, :]), reads=[pp], writes=[ott])
        else:
            P.op("vector", lambda e, ot=ot, pp=pp, ncol=ncol: e.tensor_copy(out=ot[0:ncol, :], in_=pp[0:ncol, :]), reads=[pp], writes=[ott])
        evs.append(P.dma("sync", outT[r0:r0 + ncol, t0:t0 + TT], ot[0:ncol, :], reads=[ott]))
    return evs


def make_proj_slots(P, n=3, no=3):
    w = [P.sb(f"wp_{i}", [128, KC, 128], BF16) for i in range(n)]
    o = [P.sb(f"po_{i}", [128, TT], F32) for i in range(no)]
    return {"t": w, "tt": [T(x[:]) for x in w], "cnt": 0, "o": o, "ot": [T(x[:]) for x in o], "ocnt": 0}


def col_tiles_for(ranges):
    tiles = []
    r = 0
    for (s, n) in ranges:
        o = 0
        while o < n:
            m = min(128, n - o)
            tiles.append((s + o, m, r))
            r += m
            o += m
    return tiles, r


def build_p1(Ttot, proj_ranges):
    nc = bass.Bass("TRN2", target_bir_lowering=False)
    tiles, NP = col_tiles_for(proj_ranges)
    xT = nc.dram_tensor("xT", [D, Ttot], F32, kind="ExternalInput").ap()
    w1 = nc.dram_tensor("w1", [D, 2 * DFF], F32, kind="ExternalInput").ap()
    w2 = nc.dram_tensor("w2", [DFF, D], F32, kind="ExternalInput").ap()
    lng_d = nc.dram_tensor("lng", [D], F32, kind="ExternalInput").ap()
    lnb_d = nc.dram_tensor("lnb", [D], F32, kind="ExternalInput").ap()
    wp = nc.dram_tensor("wp", [D, NP], F32, kind="ExternalInput").ap()
    x1T = nc.dram_tensor("x1T", [D, Ttot], F32, kind="ExternalOutput").ap()
    pT = nc.dram_tensor("pT", [NP, Ttot], F32, kind="ExternalOutput").ap()
    with ExitStack() as es:
        P = Prog(nc, es)
        C = setup_common(P, nc)
        lng, lng_t = load_vec_cols(P, "lng_s", lng_d, KC)
        lnb, lnb_t = load_vec_cols(P, "lnb_s", lnb_d, KC)
        ws = make_proj_slots(P)
        finals = []
        for ti in range(Ttot // TT):
            t0 = ti * TT
            load_x(P, C, xT, t0)
            ffn_ln(P, C, w1, w2, lng, lng_t, lnb, lnb_t)
            finals += store_x(P, C, x1T, t0)
            finals += project(P, C, wp, tiles, pT, t0, ws)
        P.finish(finals)
    return nc, NP


S = 4096
NH = 8
HN = 512
NCOL = 1824
C0 = math.exp(-0.5)
GN_EPS = 64e-5


def rwkv_consts():
    ident = np.eye(128, dtype=np.float32)
    s = np.arange(128)
    same = (s[:, None] // 64) == (s[None, :] // 64)
    LT = np.where(same & (s[:, None] <= s[None, :]), -C0, 0.0).astype(np.float32)
    BT = np.where(same, -C0, 0.0).astype(np.float32)
    BTc = np.zeros((128, 2), np.float32)
    BTc[:64, 0] = -C0
    BTc[64:, 1] = -C0
    sl = s % 64
    t = np.arange(64)
    mk1 = np.concatenate([(sl[:, None] < t[None, :]), (sl[:, None] <= t[None, :])], 1).astype(np.float32)
    mk3 = (t[None, :] < sl[:, None]).astype(np.float32)
    i2 = (sl[:, None] == t[None, :]).astype(np.float32)
    return {"ident": ident, "LT": LT, "BT": BT, "BTc": BTc, "mk1": mk1, "mk3": mk3, "i2": i2}


def rwkv_phase(P, banks, Gp, prm, cst, msel_d, y):
    ntiles = S // 128
    DBG = False
    STAGE = 99
    SUB = 0
    mu = prm["mu"]; vecs = {n: prm[n] for n in ("w0", "a0", "k_k", "k_a", "r_k", "gn_g", "gn_b")}
    w2 = prm["w2"]; a2 = prm["a2"]; g2 = prm["g2"]
    if True:
        rr = [0]

        def nb():
            b = banks[rr[0]]
            rr[0] = (rr[0] + 1) % 8
            return b

        def mk(name, shape, dt=F32):
            t = P.sb(name, shape, dt)
            return t, T(t[:])

        msel, msel_t = mk("msel", [128, 2]); P.dma("sync", msel[:], msel_d, writes=[msel_t])
        PA = [mk(f"PA{i}", [128, 3360]) for i in range(2)]
        mu_bc, mu_t = mk("mu_bc", [128, NCOL])
        P.dma("sync", mu_bc[:], mu.partition_broadcast(128), writes=[mu_t])
        vb = {}
        for n, ap in vecs.items():
            vb[n] = mk(n + "_bc", [128, HN])
            P.dma("sync", vb[n][0][:], ap.partition_broadcast(128), writes=[vb[n][1]])
        w2s, w2t = mk("w2s", [64, HN]); P.dma("sync", w2s[:], w2, writes=[w2t])
        a2s, a2t = mk("a2s", [64, HN]); P.dma("sync", a2s[:], a2, writes=[a2t])
        g2a, g2at = mk("g2a", [128, HN]); P.dma("sync", g2a[:], g2[0:128, :], writes=[g2at])
        g2b, g2bt = mk("g2b", [32, HN]); P.dma("sync", g2b[:], g2[128:160, :], writes=[g2bt])
        cs = {}
        for n, ap in cst.items():
            cs[n] = mk(n + "_s", list(ap.shape))
            P.dma("sync", cs[n][0][:], ap, writes=[cs[n][1]])
        ident, ident_t = cs["ident"]
        LT, LT_t = cs["LT"]; BT, BT_t = cs["BT"]; BTc, BTc_t = cs["BTc"]
        mk1, mk1_t = cs["mk1"]; mk3, mk3_t = cs["mk3"]; i2, i2_t = cs["i2"]

        Pt = [mk(f"Pt{i}", [128, NCOL]) for i in range(2)]
        Pp = mk("Pp", [128, NCOL])
        names = ["tw", "sw", "ag", "g", "kk", "tmp", "tmp2", "inv", "kmod", "bvec", "cum", "epos", "eexc", "eneg", "eend",
                 "rt", "at", "kt", "bt", "kh", "bh", "X", "W", "U", "Y", "yn", "bon"]
        W_ = {n: mk(n, [128, HN]) for n in names}
        lorT = mk("lorT", [128, 4, 128])
        sgd = mk("sgd", [128, 160])
        st8 = {n: mk(n, [128, NH]) for n in ("n2", "mean", "var", "s8")}
        G1c = [mk(f"G1_{c}", [64, NH, 128]) for c in range(2)]; G2c = [mk(f"G2_{c}", [64, NH, 128]) for c in range(2)]
        Nfc = [[mk(f"Nf{c}_{i}", [64, NH, 64], BF16) for i in range(2)] for c in range(2)]
        Tfc = [[mk(f"Tf{c}_{i}", [64, NH, 64], BF16) for i in range(2)] for c in range(2)]
        Qc = [mk(f"Q{c}", [64, NH, 64], BF16) for c in range(2)]
        Qfc = [mk(f"Qf{c}", [64, NH, 64]) for c in range(2)]
        Xc = [mk(f"X{c}", [64, HN]) for c in range(2)]; Wc = [mk(f"Wt{c}", [64, HN]) for c in range(2)]
        SH = [mk(f"SH{i}", [64, HN]) for i in range(4)]
        CM = mk("CM", [64, NH, 2, 4, 64])
        Aptc = [mk(f"Apt{c}", [64, NH, 64]) for c in range(2)]
        pC = mk("pC", [64, NH, 2])
        Hs = [mk(f"H{i}", [64, NH, 64]) for i in range(2)]
        P.op("vector", lambda e: e.memset(Hs[0][0][:], 0.0), writes=[Hs[0][1]])
        hcur = [0]
        finals = []

        def v3(ap):
            return ap.rearrange("p (h n) -> p h n", h=NH)

        def bc8(ap8):
            return ap8.unsqueeze(2).to_broadcast([128, NH, 64])

        def pre_chunk(c2, xs, xs_t, at_, at_t, bh, bh_t, kh, kh_t, shs, cm, cm_t):
            U, U_t = W_["U"]; Y, Y_t = W_["Y"]
            X, X_t = Xc[c2]; Wt, Wt_t = Wc[c2]
            g1, g1_t = G1c[c2]; g2_, g2_t = G2c[c2]
            q_, q_t = Qc[c2]
            qf_, qf_t = Qfc[c2]
            apt, apt_t = Aptc[c2]
            Nf = Nfc[c2]; Tf = Tfc[c2]
            if c2 == 0:
                Vc, Vc_t = xs[0:64, 1024:1536], xs_t
                Ac, Ac_t = at_[0:64, :], at_t
                Bc, Bc_t = bh[0:64, :], bh_t
                Kc, Kc_t = kh[0:64, :], kh_t
            else:
                Vc, Vc_t = shs[0][0][:], shs[0][1]
                Ac, Ac_t = shs[1][0][:], shs[1][1]
                Bc, Bc_t = shs[2][0][:], shs[2][1]
                Kc, Kc_t = shs[3][0][:], shs[3][1]
            hs = lambda h: slice(h * 64, (h + 1) * 64)
            nf0, nf0_t = Nf[0]; tf0, tf0_t = Tf[0]
            pg1 = [nb(), nb()]; pg2 = [nb(), nb()]; pg3 = nb()
            for (pgs, qi) in ((pg1, 0), (pg2, 1)):
                for half in range(2):
                    fns = []
                    for hh in range(4):
                        h = half * 4 + hh
                        fns.append(lambda e, h=h, hh=hh, qi=qi, bk=pgs[half]: e.matmul(bk[0:64, hh * 128:(hh + 1) * 128], cm[:, h, c2, qi, :], cm[:, h, c2, 2:4, :], start=True, stop=True))
                    P.group("tensor", fns, reads=[cm_t], writes=[pgs[half]])
            fns = []
            for h in range(NH):
                fns.append(lambda e, h=h: e.matmul(pg3[0:64, hs(h)], cm[:, h, c2, 2, :], cm[:, h, c2, 0, :], start=True, stop=True))
            P.group("tensor", fns, reads=[cm_t], writes=[pg3])
            mk1b = mk1[0:64, :].unsqueeze(1).to_broadcast([64, 4, 128])
            for half in range(2):
                P.op("vector", lambda e, half=half: e.tensor_tensor(out=g1[:, half * 4:(half + 1) * 4, :], in0=pg1[half][0:64, :].rearrange("p (h t) -> p h t", h=4), in1=mk1b, op=ALU.mult),
                     reads=[pg1[half], mk1_t], writes=[g1_t])
                P.op("vector", lambda e, half=half: e.tensor_tensor(out=g2_[:, half * 4:(half + 1) * 4, :], in0=pg2[half][0:64, :].rearrange("p (h t) -> p h t", h=4), in1=mk1b, op=ALU.mult),
                     reads=[pg2[half], mk1_t], writes=[g2_t])
            P.op("vector", lambda e: e.tensor_tensor(out=nf0[:], in0=pg3[0:64, :].rearrange("p (h t) -> p h t", h=NH), in1=mk3[0:64, :].unsqueeze(1).to_broadcast([64, NH, 64]), op=ALU.mult),
                 reads=[pg3, mk3_t], writes=[nf0_t])
            P.op("gpsimd", lambda e: e.tensor_copy(out=tf0[:], in_=g1[:, :, 0:64]), reads=[g1_t], writes=[tf0_t])
            P.op("gpsimd", lambda e: e.tensor_tensor(out=q_[:], in0=g1[:, :, 0:64], in1=i2[0:64, :].unsqueeze(1).to_broadcast([64, NH, 64]), op=ALU.add), reads=[g1_t, i2_t], writes=[q_t])
            yield
            cur = 0
            for lvl in range(5):
                nfc, nfc_t = Nf[cur]; tfc, tfc_t = Tf[cur]
                nfn, nfn_t = Nf[1 - cur]; tfn, tfn_t = Tf[1 - cur]
                last = (lvl == 4)
                pn = nb()
                fns = [(lambda e, h=h, pn=pn, tfc=tfc, nfc=nfc: e.matmul(pn[0:64, hs(h)], tfc[:, h, :], nfc[:, h, :], start=True, stop=True)) for h in range(NH)]
                P.group("tensor", fns, reads=[tfc_t, nfc_t], writes=[pn])
                if not last:
                    ptt = nb()
                    fns = [(lambda e, h=h, ptt=ptt, tfc=tfc, nfc=nfc: e.matmul(ptt[0:64, hs(h)], nfc[:, h, :], tfc[:, h, :], start=True, stop=True)) for h in range(NH)]
                    P.group("tensor", fns, reads=[tfc_t, nfc_t], writes=[ptt])
                P.op("scalar", lambda e, pn=pn, nfn=nfn: e.copy(out=nfn[:].rearrange("p h t -> p (h t)"), in_=pn[0:64, :]), reads=[pn], writes=[nfn_t])
                if not last:
                    P.op("vector", lambda e, ptt=ptt, tfn=tfn: e.tensor_copy(out=tfn[:].rearrange("p h t -> p (h t)"), in_=ptt[0:64, :]), reads=[ptt], writes=[tfn_t])
                yield
                pq = nb()
                fns = [(lambda e, h=h, pq=pq, nfn=nfn: e.matmul(pq[0:64, hs(h)], nfn[:, h, :], q_[:, h, :], start=True, stop=True)) for h in range(NH)]
                P.group("tensor", fns, reads=[nfn_t, q_t], writes=[pq])
                P.op("vector", lambda e, pq=pq: e.tensor_tensor(out=q_[:].rearrange("p h t -> p (h t)"), in0=q_[:].rearrange("p h t -> p (h t)"), in1=pq[0:64, :], op=ALU.add), reads=[pq], writes=[q_t])
                cur = 1 - cur
                yield
            P.op("gpsimd", lambda e: e.tensor_copy(out=qf_[:], in_=q_[:]), reads=[q_t], writes=[qf_t])
            px = nb()
            fns = [(lambda e, h=h, px=px: e.matmul(px[0:64, hs(h)], g2_[:, h, 0:64], Vc[:, hs(h)], start=True, stop=True)) for h in range(NH)]
            P.group("tensor", fns, reads=[g2_t, Vc_t], writes=[px])
            P.op("scalar", lambda e, px=px: e.copy(out=X[0:64, :], in_=px[0:64, :]), reads=[px], writes=[X_t])
            yield
            pw_ = nb()
            fns = [(lambda e, h=h, pw_=pw_: e.matmul(pw_[0:64, hs(h)], qf_[:, h, :], X[0:64, hs(h)], start=True, stop=True)) for h in range(NH)]
            P.group("tensor", fns, reads=[qf_t, X_t], writes=[pw_])
            P.op("scalar", lambda e, pw_=pw_: e.copy(out=Wt[0:64, :], in_=pw_[0:64, :]), reads=[pw_], writes=[Wt_t])
            yield
            pap = nb()
            fns = [(lambda e, h=h, pap=pap: e.matmul(pap[0:64, hs(h)], Ac[:, hs(h)], qf_[:, h, :], start=True, stop=True)) for h in range(NH)]
            P.group("tensor", fns, reads=[Ac_t, qf_t], writes=[pap])
            P.op("vector", lambda e, pap=pap: e.tensor_copy(out=apt[:].rearrange("p h t -> p (h t)"), in_=pap[0:64, :]), reads=[pap], writes=[apt_t])
            yield

        def seq_chunk(c2, xs, xs_t, at_, at_t, bh, bh_t, kh, kh_t, shs, cm, cm_t):
            U, U_t = W_["U"]; Y, Y_t = W_["Y"]
            X, X_t = Xc[c2]; Wt, Wt_t = Wc[c2]
            g1, g1_t = G1c[c2]; g2_, g2_t = G2c[c2]
            q_, q_t = Qc[c2]
            qf_, qf_t = Qfc[c2]
            apt, apt_t = Aptc[c2]
            Nf = Nfc[c2]; Tf = Tfc[c2]
            if c2 == 0:
                Vc, Vc_t = xs[0:64, 1024:1536], xs_t
                Ac, Ac_t = at_[0:64, :], at_t
                Bc, Bc_t = bh[0:64, :], bh_t
                Kc, Kc_t = kh[0:64, :], kh_t
            else:
                Vc, Vc_t = shs[0][0][:], shs[0][1]
                Ac, Ac_t = shs[1][0][:], shs[1][1]
                Bc, Bc_t = shs[2][0][:], shs[2][1]
                Kc, Kc_t = shs[3][0][:], shs[3][1]
            hs = lambda h: slice(h * 64, (h + 1) * 64)
            H0, H0_t = Hs[hcur[0]]
            H1, H1_t = Hs[1 - hcur[0]]
            pu = nb()
            fns = [(lambda e, h=h, pu=pu, H0=H0: e.matmul(pu[0:64, hs(h)], apt[:, h, :], H0[:, h, :], start=True, stop=True)) for h in range(NH)]
            P.group("tensor", fns, reads=[apt_t, H0_t], writes=[pu])
            P.op("vector", lambda e, pu=pu: e.tensor_tensor(out=U[0:64, :], in0=pu[0:64, :], in1=Wt[0:64, :], op=ALU.add), reads=[pu, Wt_t], writes=[U_t])
            ph = nb()
            fns = []
            for h in range(NH):
                fns.append(lambda e, h=h, ph=ph: e.matmul(ph[0:64, hs(h)], Bc[:, hs(h)], U[0:64, hs(h)], start=True, stop=False))
                fns.append(lambda e, h=h, ph=ph: e.matmul(ph[0:64, hs(h)], Kc[:, hs(h)], Vc[:, hs(h)], start=False, stop=True))
            P.group("tensor", fns, reads=[Bc_t, Kc_t, U_t, Vc_t], writes=[ph])
            py = nb()
            sl = slice(c2 * 64, (c2 + 1) * 64)
            fns = []
            for h in range(NH):
                fns.append(lambda e, h=h, py=py, H0=H0: e.matmul(py[sl, hs(h)], cm[:, h, c2, 3, :], H0[:, h, :], start=True, stop=False))
                fns.append(lambda e, h=h, py=py: e.matmul(py[sl, hs(h)], g1[:, h, 64:128], U[0:64, hs(h)], start=False, stop=False))
                fns.append(lambda e, h=h, py=py: e.matmul(py[sl, hs(h)], g2_[:, h, 64:128], Vc[:, hs(h)], start=False, stop=True))
            P.group("tensor", fns, reads=[cm_t, H0_t, g1_t, g2_t, U_t, Vc_t], writes=[py])
            P.op("gpsimd", lambda e, H0=H0, H1=H1: e.tensor_tensor(out=H1[:], in0=H0[:], in1=pC[0][:, :, c2:c2 + 1].to_broadcast([64, NH, 64]), op=ALU.mult),
                 reads=[H0_t, pC[1]], writes=[H1_t])
            P.op("vector", lambda e, H1=H1, ph=ph: e.tensor_tensor(out=H1[:].rearrange("p a i -> p (a i)"), in0=H1[:].rearrange("p a i -> p (a i)"), in1=ph[0:64, :], op=ALU.add),
                 reads=[ph], writes=[H1_t])
            P.op("scalar", lambda e, py=py: e.copy(out=Y[sl, :], in_=py[sl, :]), reads=[py], writes=[Y_t])
            hcur[0] = 1 - hcur[0]

        def do_tile(ti):
            t0 = ti * 128
            Pc, Pc_t = Pt[ti % 2]
            pa, pa_t = PA[0]
            pb_, pb_t = PA[1]
            rk_, lt_ = ti // 16, (ti % 16) * 128
            P.dma("sync", pa[:], Gp.rows(rk_, lt_, 128), writes=[pa_t])
            if ti == 0:
                P.op("vector", lambda e: e.memset(pb_[0:1, :], 0.0), writes=[pb_t])
            else:
                P.dma("sync", pb_[0:1, :], Gp.rows((ti - 1) // 16, ((ti - 1) % 16) * 128 + 127, 1), writes=[pb_t])
            P.dma("sync", pb_[1:128, :], Gp.rows(rk_, lt_, 127), writes=[pb_t])
            for (src, src_t, dst, dst_t) in ((pa, pa_t, Pc, Pc_t), (pb_, pb_t, Pp[0], Pp[1])):
                v4 = src[:, 0:3072].rearrange("p (j g n) -> p j g n", j=3, g=2)
                d3 = dst[:, 0:1536].rearrange("p (j n) -> p j n", j=3)
                P.op("vector", lambda e, v4=v4, d3=d3: e.tensor_scalar(out=d3, in0=v4[:, :, 0, :], scalar1=msel[:, 0:1], scalar2=None, op0=ALU.mult), reads=[src_t, msel_t], writes=[dst_t])
                P.op("vector", lambda e, v4=v4, d3=d3: e.scalar_tensor_tensor(out=d3, in0=v4[:, :, 1, :], scalar=msel[:, 1:2], in1=d3, op0=ALU.mult, op1=ALU.add), reads=[src_t, msel_t], writes=[dst_t])
                P.op("gpsimd", lambda e, src=src, dst=dst: e.tensor_copy(out=dst[:, 1536:1824], in_=src[:, 3072:3360]), reads=[src_t], writes=[dst_t])
            P.op("vector", lambda e, Pc=Pc: e.tensor_tensor(out=Pp[0][:], in0=Pp[0][:], in1=Pc[:], op=ALU.subtract), reads=[Pc_t], writes=[Pp[1]])
            P.op("gpsimd", lambda e: e.tensor_tensor(out=Pp[0][:], in0=Pp[0][:], in1=mu_bc[:], op=ALU.mult), reads=[mu_t], writes=[Pp[1]])
            P.op("vector", lambda e, Pc=Pc: e.tensor_tensor(out=Pp[0][:], in0=Pp[0][:], in1=Pc[:], op=ALU.add), reads=[Pc_t], writes=[Pp[1]])
            xs, xs_t = Pp
            r_ = xs[:, 0:512]; k_ = xs[:, 512:1024]; v_ = xs[:, 1024:1536]
            if STAGE == 1:
                finals.append(P.dma('sync', y[t0:t0 + 128, :], xs[:, 0:512], reads=[xs_t])); return
            tw, tw_t = W_["tw"]
            P.op("scalar", lambda e: e.activation(out=tw[:, 0:64], in_=xs[:, 1536:1600], func=AF.Tanh), reads=[xs_t], writes=[tw_t])
            P.op("scalar", lambda e: e.activation(out=sgd[0][:], in_=xs[:, 1664:1824], func=AF.Sigmoid), reads=[xs_t], writes=[sgd[1]])
            pb = nb()
            P.op("tensor", lambda e, pb=pb: e.transpose(pb[0:64, 0:128], tw[:, 0:64], ident[:]), reads=[tw_t, ident_t], writes=[pb])
            P.op("tensor", lambda e, pb=pb: e.transpose(pb[0:64, 128:256], xs[:, 1600:1664], ident[:]), reads=[xs_t, ident_t], writes=[pb])
            P.op("tensor", lambda e, pb=pb: e.transpose(pb[0:128, 256:384], sgd[0][:, 0:128], ident[:]), reads=[sgd[1], ident_t], writes=[pb])
            P.op("tensor", lambda e, pb=pb: e.transpose(pb[0:32, 384:512], sgd[0][:, 128:160], ident[:]), reads=[sgd[1], ident_t], writes=[pb])
            lT, lT_t = lorT
            P.op("vector", lambda e, pb=pb: e.tensor_copy(out=lT[0:64, 0:2, :], in_=pb[0:64, 0:256].rearrange("p (a b) -> p a b", a=2)), reads=[pb], writes=[lT_t])
            P.op("vector", lambda e, pb=pb: e.tensor_copy(out=lT[:, 2, :], in_=pb[:, 256:384]), reads=[pb], writes=[lT_t])
            P.op("vector", lambda e, pb=pb: e.tensor_copy(out=lT[0:32, 3, :], in_=pb[0:32, 384:512]), reads=[pb], writes=[lT_t])
            pw = nb(); pa = nb(); pg = nb()
            P.op("tensor", lambda e, pw=pw: e.matmul(pw[:], lT[0:64, 0, :], w2s[:], start=True, stop=True), reads=[lT_t, w2t], writes=[pw])
            P.op("tensor", lambda e, pa=pa: e.matmul(pa[:], lT[0:64, 1, :], a2s[:], start=True, stop=True), reads=[lT_t, a2t], writes=[pa])
            P.group("tensor", [lambda e, pg=pg: e.matmul(pg[:], lT[:, 2, :], g2a[:], start=True, stop=False),
                               lambda e, pg=pg: e.matmul(pg[:], lT[0:32, 3, :], g2b[:], start=False, stop=True)], reads=[lT_t, g2at, g2bt], writes=[pg])
            sw, sw_t = W_["sw"]; ag, ag_t = W_["ag"]; g_, g_t = W_["g"]
            P.op("vector", lambda e, pw=pw: e.tensor_tensor(out=sw[:], in0=pw[:], in1=vb["w0"][0][:], op=ALU.add), reads=[pw, vb["w0"][1]], writes=[sw_t])
            P.op("scalar", lambda e: e.activation(out=sw[:], in_=sw[:], func=AF.Sigmoid), reads=[sw_t], writes=[sw_t])
            P.op("vector", lambda e, pa=pa: e.tensor_tensor(out=ag[:], in0=pa[:], in1=vb["a0"][0][:], op=ALU.add), reads=[pa, vb["a0"][1]], writes=[ag_t])
            P.op("scalar", lambda e: e.activation(out=ag[:], in_=ag[:], func=AF.Sigmoid), reads=[ag_t], writes=[ag_t])
            P.op("scalar", lambda e, pg=pg: e.copy(out=g_[:], in_=pg[:]), reads=[pg], writes=[g_t])
            if STAGE == 2:
                finals.append(P.dma('sync', y[t0:t0 + 128, :], xs[:, 0:512], reads=[xs_t])); return
            kk, kk_t = W_["kk"]; tmp, tmp_t = W_["tmp"]; tmp2, tmp2_t = W_["tmp2"]
            kmod, kmod_t = W_["kmod"]; bvec, bvec_t = W_["bvec"]
            n2, n2_t = st8["n2"]
            P.op("vector", lambda e: e.tensor_tensor(out=kk[:], in0=k_, in1=vb["k_k"][0][:], op=ALU.mult), reads=[xs_t, vb["k_k"][1]], writes=[kk_t])
            P.op("gpsimd", lambda e: e.tensor_tensor(out=tmp[:], in0=kk[:], in1=kk[:], op=ALU.mult), reads=[kk_t], writes=[tmp_t])
            P.op("vector", lambda e: e.tensor_reduce(out=n2[:], in_=v3(tmp[:]), axis=AX.X, op=ALU.add), reads=[tmp_t], writes=[n2_t])
            P.op("scalar", lambda e: e.activation(out=n2[:], in_=n2[:], func=AF.Sqrt), reads=[n2_t], writes=[n2_t])
            P.op("vector", lambda e: e.tensor_scalar(out=n2[:], in0=n2[:], scalar1=1e-12, scalar2=None, op0=ALU.max), reads=[n2_t], writes=[n2_t])
            P.op("vector", lambda e: e.reciprocal(out=n2[:], in_=n2[:]), reads=[n2_t], writes=[n2_t])
            P.op("vector", lambda e: e.tensor_tensor(out=v3(kk[:]), in0=v3(kk[:]), in1=bc8(n2[:]), op=ALU.mult), reads=[n2_t], writes=[kk_t])
            P.op("vector", lambda e: e.scalar_tensor_tensor(out=tmp[:], in0=ag[:], scalar=-1.0, in1=vb["k_a"][0][:], op0=ALU.add, op1=ALU.mult), reads=[ag_t, vb["k_a"][1]], writes=[tmp_t])
            P.op("vector", lambda e: e.scalar_tensor_tensor(out=kmod[:], in0=tmp[:], scalar=1.0, in1=k_, op0=ALU.add, op1=ALU.mult), reads=[tmp_t, xs_t], writes=[kmod_t])
            P.op("gpsimd", lambda e: e.tensor_tensor(out=bvec[:], in0=kk[:], in1=ag[:], op=ALU.mult), reads=[kk_t, ag_t], writes=[bvec_t])
            if STAGE == 3:
                finals.append(P.dma('sync', y[t0:t0 + 128, :], xs[:, 0:512], reads=[xs_t])); return
            pc = nb(); pt = nb()
            P.op("tensor", lambda e, pc=pc: e.matmul(pc[:], LT[:], sw[:], start=True, stop=True), reads=[LT_t, sw_t], writes=[pc])
            P.op("tensor", lambda e, pt=pt: e.matmul(pt[:], BT[:], sw[:], start=True, stop=True), reads=[BT_t, sw_t], writes=[pt])
            ppc = nb()
            fns = [(lambda e, h=h, ppc=ppc: e.matmul(ppc[0:64, h * 2:h * 2 + 2], sw[:, h * 64:(h + 1) * 64], BTc[:], start=True, stop=True)) for h in range(NH)]
            P.group("tensor", fns, reads=[sw_t, BTc_t], writes=[ppc])
            P.op("scalar", lambda e, ppc=ppc: e.activation(out=pC[0][:].rearrange("p a b -> p (a b)"), in_=ppc[0:64, 0:16], func=AF.Exp), reads=[ppc], writes=[pC[1]])
            cum, cum_t = W_["cum"]
            epos, epos_t = W_["epos"]; eexc, eexc_t = W_["eexc"]; eneg, eneg_t = W_["eneg"]; eend, eend_t = W_["eend"]
            P.op("scalar", lambda e, pc=pc: e.copy(out=cum[:], in_=pc[:]), reads=[pc], writes=[cum_t])
            P.op("scalar", lambda e, pc=pc: e.activation(out=epos[:], in_=pc[:], func=AF.Exp), reads=[pc], writes=[epos_t])
            P.op("scalar", lambda e, pc=pc: e.activation(out=eneg[:], in_=pc[:], func=AF.Exp, scale=-1.0), reads=[pc], writes=[eneg_t])
            P.op("vector", lambda e: e.scalar_tensor_tensor(out=eexc[:], in0=sw[:], scalar=C0, in1=cum[:], op0=ALU.mult, op1=ALU.add), reads=[sw_t, cum_t], writes=[eexc_t])
            P.op("scalar", lambda e: e.activation(out=eexc[:], in_=eexc[:], func=AF.Exp), reads=[eexc_t], writes=[eexc_t])
            P.op("vector", lambda e, pt=pt: e.tensor_tensor(out=eend[:], in0=pt[:], in1=cum[:], op=ALU.subtract), reads=[pt, cum_t], writes=[eend_t])
            P.op("scalar", lambda e: e.activation(out=eend[:], in_=eend[:], func=AF.Exp), reads=[eend_t], writes=[eend_t])
            if STAGE == 4:
                finals.append(P.dma('sync', y[t0:t0 + 128, :], xs[:, 0:512], reads=[xs_t])); return
            rt, rt_t = W_["rt"]; at_, at_t = W_["at"]; kt, kt_t = W_["kt"]; bt, bt_t = W_["bt"]; kh, kh_t = W_["kh"]; bh, bh_t = W_["bh"]
            P.op("vector", lambda e: e.tensor_tensor(out=rt[:], in0=r_, in1=epos[:], op=ALU.mult), reads=[xs_t, epos_t], writes=[rt_t])
            P.op("vector", lambda e: e.scalar_tensor_tensor(out=at_[:], in0=kk[:], scalar=-1.0, in1=eexc[:], op0=ALU.mult, op1=ALU.mult), reads=[kk_t, eexc_t], writes=[at_t])
            P.op("vector", lambda e: e.tensor_tensor(out=kt[:], in0=kmod[:], in1=eneg[:], op=ALU.mult), reads=[kmod_t, eneg_t], writes=[kt_t])
            P.op("gpsimd", lambda e: e.tensor_tensor(out=bt[:], in0=bvec[:], in1=eneg[:], op=ALU.mult), reads=[bvec_t, eneg_t], writes=[bt_t])
            P.op("vector", lambda e: e.tensor_tensor(out=kh[:], in0=kmod[:], in1=eend[:], op=ALU.mult), reads=[kmod_t, eend_t], writes=[kh_t])
            P.op("gpsimd", lambda e: e.tensor_tensor(out=bh[:], in0=bvec[:], in1=eend[:], op=ALU.mult), reads=[bvec_t, eend_t], writes=[bh_t])

            cm, cm_t = CM
            srcs = [(bt, bt_t), (kt, kt_t), (at_, at_t), (rt, rt_t)]
            for h in range(NH):
                pb = nb()
                for q, (src, src_t) in enumerate(srcs):
                    P.op("tensor", lambda e, pb=pb, q=q, src=src, h=h: e.transpose(pb[0:64, q * 128:(q + 1) * 128], src[:, h * 64:(h + 1) * 64], ident[:]),
                         reads=[src_t, ident_t], writes=[pb])
                outap = cm[:, h, :, :, :].rearrange("p c q j -> p q c j")
                inap = pb[0:64, :].rearrange("p (q c j) -> p q c j", q=4, c=2)
                if h % 2 == 0:
                    P.op("vector", lambda e, outap=outap, inap=inap: e.tensor_copy(out=outap, in_=inap), reads=[pb], writes=[cm_t])
                else:
                    P.op("scalar", lambda e, outap=outap, inap=inap: e.copy(out=outap, in_=inap), reads=[pb], writes=[cm_t])
            if STAGE == 5:
                finals.append(P.dma('sync', y[t0:t0 + 128, :], xs[:, 0:512], reads=[xs_t])); return
            shs = []
            for qi, (src_ap, src_t) in enumerate(((xs[64:128, 1024:1536], xs_t), (at_[64:128, :], at_t), (bh[64:128, :], bh_t), (kh[64:128, :], kh_t))):
                d_, d_t = SH[qi]
                P.dma("sync", d_[:], src_ap, reads=[src_t], writes=[d_t])
                shs.append((d_, d_t))
            Y, Y_t = W_["Y"]
            gens = [pre_chunk(c2, xs, xs_t, at_, at_t, bh, bh_t, kh, kh_t, shs, cm, cm_t) for c2 in range(2)]
            alive = [True, True]
            while any(alive):
                for gi in range(2):
                    if alive[gi]:
                        try:
                            next(gens[gi])
                        except StopIteration:
                            alive[gi] = False
            for c2 in range(2):
                seq_chunk(c2, xs, xs_t, at_, at_t, bh, bh_t, kh, kh_t, shs, cm, cm_t)

            mean, mean_t = st8["mean"]; var, var_t = st8["var"]; s8, s8_t = st8["s8"]
            yn, yn_t = W_["yn"]; bon, bon_t = W_["bon"]
            P.op("vector", lambda e: e.tensor_reduce(out=mean[:], in_=v3(Y[:]), axis=AX.X, op=ALU.add), reads=[Y_t], writes=[mean_t])
            P.op("vector", lambda e: e.tensor_scalar(out=mean[:], in0=mean[:], scalar1=1.0 / 64, scalar2=None, op0=ALU.mult), reads=[mean_t], writes=[mean_t])
            P.op("vector", lambda e: e.tensor_tensor(out=v3(yn[:]), in0=v3(Y[:]), in1=bc8(mean[:]), op=ALU.subtract), reads=[Y_t, mean_t], writes=[yn_t])
            P.op("gpsimd", lambda e: e.tensor_tensor(out=tmp[:], in0=yn[:], in1=yn[:], op=ALU.mult), reads=[yn_t], writes=[tmp_t])
            P.op("vector", lambda e: e.tensor_reduce(out=var[:], in_=v3(tmp[:]), axis=AX.X, op=ALU.add), reads=[tmp_t], writes=[var_t])
            P.op("scalar", lambda e: e.activation(out=var[:], in_=var[:], func=AF.Sqrt, bias=GN_EPS, scale=1.0 / 64), reads=[var_t], writes=[var_t])
            P.op("vector", lambda e: e.reciprocal(out=var[:], in_=var[:]), reads=[var_t], writes=[var_t])
            P.op("vector", lambda e: e.tensor_tensor(out=v3(yn[:]), in0=v3(yn[:]), in1=bc8(var[:]), op=ALU.mult), reads=[var_t], writes=[yn_t])
            P.op("gpsimd", lambda e: e.tensor_tensor(out=yn[:], in0=yn[:], in1=vb["gn_g"][0][:], op=ALU.mult), reads=[vb["gn_g"][1]], writes=[yn_t])
            P.op("vector", lambda e: e.tensor_tensor(out=yn[:], in0=yn[:], in1=vb["gn_b"][0][:], op=ALU.add), reads=[vb["gn_b"][1]], writes=[yn_t])
            P.op("gpsimd", lambda e: e.tensor_tensor(out=tmp2[:], in0=r_, in1=kmod[:], op=ALU.mult), reads=[xs_t, kmod_t], writes=[tmp2_t])
            P.op("gpsimd", lambda e: e.tensor_tensor(out=tmp2[:], in0=tmp2[:], in1=vb["r_k"][0][:], op=ALU.mult), reads=[vb["r_k"][1]], writes=[tmp2_t])
            P.op("vector", lambda e: e.tensor_reduce(out=s8[:], in_=v3(tmp2[:]), axis=AX.X, op=ALU.add), reads=[tmp2_t], writes=[s8_t])
            P.op("vector", lambda e: e.tensor_tensor(out=v3(bon[:]), in0=v3(v_), in1=bc8(s8[:]), op=ALU.mult), reads=[xs_t, s8_t], writes=[bon_t])
            P.op("vector", lambda e: e.tensor_tensor(out=yn[:], in0=yn[:], in1=bon[:], op=ALU.add), reads=[bon_t], writes=[yn_t])
            P.op("vector", lambda e: e.tensor_tensor(out=yn[:], in0=yn[:], in1=g_[:], op=ALU.mult), reads=[g_t], writes=[yn_t])
            finals.append(P.dma("sync", y[t0:t0 + 128, :], yn[:], reads=[yn_t]))
            if DBG:
                Hc, Hc_t = Hs[hcur[0]]
                finals.append(P.dma("sync", dbg[ti], Hc[:].rearrange("p a i -> p (a i)"), reads=[Hc_t]))
        for ti in range(ntiles):
            do_tile(ti)


S = 4096
NSLOT = 16
NQ = NSLOT * 128
AH = 8
IH = 16
TOPK = 256
NEG = -30000.0


def np_rel_bucket(dist):
    max_exact = 16
    d_f = np.maximum(dist, 1).astype(np.float32)
    large = max_exact + (np.log(d_f / max_exact) / np.float32(math.log(128 / max_exact)) * (32 - max_exact)).astype(np.int32)
    large = np.minimum(large, 31)
    return np.where(dist < max_exact, dist, large)


def dsa_consts(e, rel_bias):
    q = np.arange(128)
    tri = (q[None, :] <= q[:, None]).astype(np.float32)
    cm = np.zeros((128, 256), np.float32)
    if e == 0:
        cm[:, 0:128] = tri
    else:
        cm[:, 0:128] = 1.0
        cm[:, 128:256] = tri
    nbig = ((cm - 1.0) * 1e30).astype(np.float32)
    NB = np.zeros((128, 3, AH, 128), np.float32)
    for r in range(3):
        delta = e + 1 - r
        dist = np.maximum(delta * 128 + q[None, :] - q[:, None], 0)
        b = np_rel_bucket(dist.astype(np.int32))
        NB[:, r, :, :] = np.transpose(rel_bias[b], (0, 2, 1))
    cfar = np.broadcast_to(rel_bias[31][None, :], (128, AH)).astype(np.float32).copy()
    return {"cm": cm, "nbig": nbig, "NB": NB, "cfar": cfar, "ident": np.eye(128, dtype=np.float32)}


def dsa_phase(P, banks, G, lng_d, lnb_d, cst, msel_d, y, nslot=NSLOT):
    cm_d, nbig_d, NB_d, cfar_d, ident_d = cst["cm"], cst["nbig"], cst["NB"], cst["cfar"], cst["ident"]
    if True:
        rr = [0]

        def nb():
            b = banks[rr[0]]
            rr[0] = (rr[0] + 1) % 4
            return b
        acc = [banks[6], banks[7]]
        pscb = [banks[4], banks[5]]
        pscn = [0]

        def mk(name, shape, dt=F32):
            t = P.sb(name, shape, dt)
            return t, T(t[:])

        msel, msel_t = mk("msel3", [128, 2]); P.dma("sync", msel[:], msel_d, writes=[msel_t])
        ident, ident_t = mk("ident_s", [128, 128]); P.dma("sync", ident[:], ident_d, writes=[ident_t])
        identb, identb_t = mk("identb", [128, 128], BF16)
        P.op("vector", lambda e: e.tensor_copy(out=identb[:], in_=ident[:]), reads=[ident_t], writes=[identb_t])
        cm, cm_t = mk("cm_s", [128, 256]); P.dma("sync", cm[:], cm_d, writes=[cm_t])
        nbig, nbig_t = mk("nbig_s", [128, 256]); P.dma("sync", nbig[:], nbig_d, writes=[nbig_t])
        cfar, cfar_t = mk("cfar_s", [128, AH]); P.dma("sync", cfar[:], cfar_d, writes=[cfar_t])
        NBb, NBb_t = mk("NBb", [128, 3, AH, 128], BF16)
        kT, kT_t = mk("kT_s", [128, 4, S], BF16)
        V1, V1_t = mk("V1", [128, 32, AH, 65], BF16)
        kiT, kiT_t = mk("kiT", [64, S], BF16)
        P.push_scope()
        NBf, NBf_t = mk("NBf", [128, 3, AH, 128]); P.dma("sync", NBf[:], NB_d, writes=[NBf_t])
        for r in range(3):
            for h in range(AH):
                P.op("vector", lambda e, r=r, h=h: e.tensor_scalar(out=NBb[:, r, h, :], in0=NBf[:, r, h, :], scalar1=cfar[:, h:h + 1], scalar2=None, op0=ALU.subtract),
                     reads=[NBf_t, cfar_t], writes=[NBb_t])
        lng, lng_t = mk("lng_bc", [128, 64]); P.dma("sync", lng[:], lng_d.partition_broadcast(128), writes=[lng_t])
        lnb, lnb_t = mk("lnb_bc", [128, 64]); P.dma("sync", lnb[:], lnb_d.partition_broadcast(128), writes=[lnb_t])

        for r_ in range(2):
            for hp in range(4):
                P.dma("gpsimd", kT[:, hp, r_ * 2048:(r_ + 1) * 2048], G["kT"].rows(r_, hp * 128, 128), writes=[kT_t])
        P.op("vector", lambda e: e.memset(V1[:, :, :, 64:65], 1.0), writes=[V1_t])
        for kb in range(32):
            P.dma("gpsimd", V1[:, kb, :, 0:64], G["V"].rows(kb // 16, (kb % 16) * 128, 128).rearrange("p (h d) -> p h d", h=AH), writes=[V1_t])
        ki, ki_t = mk("ki", [128, 32, 64])
        for r_ in range(2):
            P.dma("sync", ki[:, r_ * 16:(r_ + 1) * 16, :], G["kw"].rows(r_, 0, 2048)[:, 0:64].rearrange("(kb p) d -> p kb d", p=128), writes=[ki_t])
        st, st_t = mk("kist", [128, 32]); ksq, ksq_t = mk("kisq", [128, 32, 64])
        bc32 = lambda ap: ap.unsqueeze(2).to_broadcast([128, 32, 64])
        P.op("vector", lambda e: e.tensor_reduce(out=st[:], in_=ki[:], axis=AX.X, op=ALU.add), reads=[ki_t], writes=[st_t])
        P.op("vector", lambda e: e.tensor_scalar(out=st[:], in0=st[:], scalar1=1.0 / 64, scalar2=None, op0=ALU.mult), reads=[st_t], writes=[st_t])
        P.op("vector", lambda e: e.tensor_tensor(out=ki[:], in0=ki[:], in1=bc32(st[:]), op=ALU.subtract), reads=[st_t], writes=[ki_t])
        P.op("vector", lambda e: e.tensor_tensor(out=ksq[:], in0=ki[:], in1=ki[:], op=ALU.mult), reads=[ki_t], writes=[ksq_t])
        P.op("vector", lambda e: e.tensor_reduce(out=st[:], in_=ksq[:], axis=AX.X, op=ALU.add), reads=[ksq_t], writes=[st_t])
        P.op("scalar", lambda e: e.activation(out=st[:], in_=st[:], func=AF.Sqrt, bias=1e-5, scale=1.0 / 64), reads=[st_t], writes=[st_t])
        P.op("vector", lambda e: e.reciprocal(out=st[:], in_=st[:]), reads=[st_t], writes=[st_t])
        P.op("vector", lambda e: e.tensor_tensor(out=ki[:], in0=ki[:], in1=bc32(st[:]), op=ALU.mult), reads=[st_t], writes=[ki_t])
        P.op("vector", lambda e: e.tensor_tensor(out=ki[:], in0=ki[:], in1=lng[:].unsqueeze(1).to_broadcast([128, 32, 64]), op=ALU.mult), reads=[lng_t], writes=[ki_t])
        P.op("vector", lambda e: e.tensor_tensor(out=ki[:], in0=ki[:], in1=lnb[:].unsqueeze(1).to_broadcast([128, 32, 64]), op=ALU.add), reads=[lnb_t], writes=[ki_t])
        for g in range(8):
            pb = nb()
            for j in range(4):
                kb = g * 4 + j
                P.op("tensor", lambda e, pb=pb, j=j, kb=kb: e.transpose(pb[0:64, j * 128:(j + 1) * 128], ki[:, kb, :], ident[:]), reads=[ki_t, ident_t], writes=[pb])
            P.op("scalar", lambda e, pb=pb, g=g: e.copy(out=kiT[:, g * 512:(g + 1) * 512], in_=pb[0:64, :]), reads=[pb], writes=[kiT_t])

        P.barrier()
        P.pop_scope()
        qf, qf_t = mk("qf", [128, 4, 128])
        qfb, qfb_t = mk("qfb", [128, 4, 128])
        qf2, qf2_t = mk("qf2", [128, 4, 256])
        qi2 = [mk(f"qi2h{i}", [64, IH, 128]) for i in range(1)]
        qib, qib_t = mk("qib", [64, IH, 128])
        qia, qia_t = mk("qia", [64, IH, 128])
        wq2, wq2_t = mk("wq2", [128, 2, IH])
        wqb, wqb_t = mk("wqb", [128, IH])
        qTz, qTz_t = mk("qTz", [128, AH, 128], BF16)
        P.op("vector", lambda e: e.memset(qTz[:], 0.0), writes=[qTz_t])
        qiT, qiT_t = mk("qiT", [64, IH, 128], BF16)
        wq, wq_t = mk("wq", [128, IH])
        diagall, diagall_t = mk("diagall", [128, IH, 128], BF16)
        stage = [mk(f"stage{i}", [128, 512], BF16) for i in range(2)]
        score2 = [mk(f"score{i}", [128, S]) for i in range(2)]
        work, work_t = mk("work", [128, S])
        madd2 = [mk(f"madd{i}", [128, S], BF16) for i in range(2)]
        mx, mx_t = mk("mx", [128, 8])
        thr, thr_t = mk("thr", [128, 1])
        PT = [mk(f"PT{i}", [128, 4, 128], BF16) for i in range(2)]
        rec, rec_t = mk("rec", [128, AH])
        yo = [mk(f"yo{i}", [128, 512]) for i in range(1)]
        cnt = {"stage": 0, "PT": 0}

        def blend2(c0, c0_t, c1, c1_t, tb, tb_t, out, out_t, np_):
            P.op("scalar", lambda e: e.activation(out=out, in_=c0, func=AF.Identity, scale=msel[0:np_, 0:1]), reads=[c0_t, msel_t], writes=[out_t])
            P.op("scalar", lambda e: e.activation(out=tb, in_=c1, func=AF.Identity, scale=msel[0:np_, 1:2]), reads=[c1_t, msel_t], writes=[tb_t])
            P.op("gpsimd", lambda e: e.tensor_tensor(out=out, in0=out, in1=tb, op=ALU.add), reads=[tb_t], writes=[out_t])

        def st_idx(i):
            nkb = 2 * i + 2
            nkeys = nkb * 128
            r_ = i // 8
            lt0 = (i - 8 * r_) * 256
            score, score_t = score2[i % 2]
            qh, qh_t = qi2[0]
            for cand in range(2):
                for hq in range(4):
                    P.dma("sync", qh[:, hq * 4:(hq + 1) * 4, :], G["qiT"].rows(r_, hq * 256, 256)[:, lt0 + cand * 128:lt0 + (cand + 1) * 128].rearrange("(h d) q -> d h q", d=64), writes=[qh_t])
                if cand == 0:
                    P.op("scalar", lambda e: e.activation(out=qia[:], in_=qh[:], func=AF.Identity, scale=msel[0:64, 0:1]), reads=[qh_t, msel_t], writes=[qia_t])
                else:
                    P.op("scalar", lambda e: e.activation(out=qib[:], in_=qh[:], func=AF.Identity, scale=msel[0:64, 1:2]), reads=[qh_t, msel_t], writes=[qib_t])
                    P.op("gpsimd", lambda e: e.tensor_tensor(out=qiT[:], in0=qia[:], in1=qib[:], op=ALU.add), reads=[qia_t, qib_t], writes=[qiT_t])
            P.dma("sync", wq2[:], G["kw"].rows(r_, lt0, 256)[:, 64:80].rearrange("(c p) n -> p c n", p=128), writes=[wq2_t])
            blend2(wq2[:, 0, :], wq2_t, wq2[:, 1, :], wq2_t, wqb[:], wqb_t, wq[:], wq_t, 128)
            for h in range(IH):
                P.op("scalar", lambda e, h=h: e.activation(out=diagall[:, h, :], in_=ident[:], func=AF.Identity, scale=wq[:, h:h + 1]), reads=[ident_t, wq_t], writes=[diagall_t])
            for g0 in range(0, nkeys, 512):
                n = min(512, nkeys - g0)
                psc = pscb[pscn[0] % 2]; pscn[0] += 1
                for h in range(IH):
                    pd = nb()
                    P.op("tensor", lambda e, pd=pd, h=h, g0=g0, n=n: e.matmul(pd[:, 0:n], qiT[:, h, :], kiT[:, g0:g0 + n], start=True, stop=True), reads=[qiT_t, kiT_t], writes=[pd])
                    sg, sg_t = stage[cnt["stage"] % 2]; cnt["stage"] += 1
                    P.op("scalar", lambda e, pd=pd, sg=sg, n=n: e.activation(out=sg[:, 0:n], in_=pd[:, 0:n], func=AF.Relu), reads=[pd], writes=[sg_t])
                    P.op("tensor", lambda e, psc=psc, sg=sg, h=h, n=n: e.matmul(psc[:, 0:n], diagall[:, h, :], sg[:, 0:n], start=(h == 0), stop=(h == IH - 1)), reads=[diagall_t, sg_t], writes=[psc])
                P.op("scalar", lambda e, psc=psc, g0=g0, n=n, score=score: e.copy(out=score[:, g0:g0 + n], in_=psc[:, 0:n]), reads=[psc], writes=[score_t])
            l0 = nkeys - 256
            P.op("gpsimd", lambda e, score=score: e.tensor_tensor(out=score[:, l0:nkeys], in0=score[:, l0:nkeys], in1=cm[:], op=ALU.mult), reads=[cm_t], writes=[score_t])
            P.op("gpsimd", lambda e, score=score: e.tensor_tensor(out=score[:, l0:nkeys], in0=score[:, l0:nkeys], in1=nbig[:], op=ALU.add), reads=[nbig_t], writes=[score_t])

        def st_topk(i):
            nkeys = (2 * i + 2) * 128
            score, score_t = score2[i % 2]
            madd, madd_t = madd2[i % 2]
            if i == 0:
                P.op("vector", lambda e: e.memset(thr[:], -1e29), writes=[thr_t])
            else:
                P.op("gpsimd", lambda e, score=score: e.tensor_copy(out=work[:, 0:nkeys], in_=score[:, 0:nkeys]), reads=[score_t], writes=[work_t])
                for rnd in range(TOPK // 8):
                    P.op("vector", lambda e: e.max(out=mx[:], in_=work[:, 0:nkeys]), reads=[work_t], writes=[mx_t])
                    if rnd < TOPK // 8 - 1:
                        P.op("vector", lambda e: e.match_replace(out=work[:, 0:nkeys], in_to_replace=mx[:], in_values=work[:, 0:nkeys], imm_value=-1e30), reads=[mx_t], writes=[work_t])
                P.op("vector", lambda e: e.tensor_copy(out=thr[:], in_=mx[:, 7:8]), reads=[mx_t], writes=[thr_t])
            P.op("vector", lambda e, score=score, madd=madd: e.tensor_scalar(out=madd[:, 0:nkeys], in0=score[:, 0:nkeys], scalar1=thr[:, 0:1], scalar2=NEG, op0=ALU.is_lt, op1=ALU.mult),
                 reads=[score_t, thr_t], writes=[madd_t])

        def st_attn(i):
            q0 = i * 128
            nkb = 2 * i + 2
            r_ = i // 8
            lt0 = (i - 8 * r_) * 256
            madd, madd_t = madd2[i % 2]
            for hp in range(4):
                P.dma("sync", qf2[:, hp, :], G["qT"].rows(r_, hp * 128, 128)[:, lt0:lt0 + 256], writes=[qf2_t])
            blend2(qf2[:, :, 0:128], qf2_t, qf2[:, :, 128:256], qf2_t, qfb[:], qfb_t, qf[:], qf_t, 128)
            qz = qTz[:].rearrange("p (hp e) q -> p hp e q", e=2)
            P.op("scalar", lambda e: e.mul(out=qz[0:64, :, 0, :], in_=qf[0:64, :, :], mul=0.125), reads=[qf_t], writes=[qTz_t])
            P.op("scalar", lambda e: e.mul(out=qz[64:128, :, 1, :], in_=qf[64:128, :, :], mul=0.125), reads=[qf_t], writes=[qTz_t])
            for kb in range(nkb):
                r = kb - (nkb - 3)
                ks = slice(kb * 128, (kb + 1) * 128)
                for half in range(2):
                    pl = nb()
                    fns = []
                    for hh in range(4):
                        h = half * 4 + hh
                        hp = h // 2
                        cs = slice(hh * 128, (hh + 1) * 128)
                        near = (r >= 0)
                        fns.append(lambda e, pl=pl, cs=cs, hp=hp, h=h, ks=ks: e.matmul(pl[:, cs], kT[:, hp, ks], qTz[:, h, :], start=True, stop=False))
                        fns.append(lambda e, pl=pl, cs=cs, ks=ks, near=near, madd=madd: e.matmul(pl[:, cs], madd[:, ks], identb[:], start=False, stop=(not near)))
                        if near:
                            fns.append(lambda e, pl=pl, cs=cs, r=r, h=h: e.matmul(pl[:, cs], identb[:], NBb[:, r, h, :], start=False, stop=True))
                    P.group("tensor", fns, reads=[kT_t, qTz_t, madd_t, identb_t, NBb_t], writes=[pl])
                    pt_, pt_t = PT[cnt["PT"] % 2]; cnt["PT"] += 1
                    P.op("scalar", lambda e, pl=pl, pt_=pt_: e.activation(out=pt_[:].rearrange("p h q -> p (h q)"), in_=pl[:], func=AF.Exp), reads=[pl], writes=[pt_t])
                    fns = []
                    for hh in range(4):
                        h = half * 4 + hh
                        fns.append(lambda e, hh=hh, h=h, pt_=pt_, kb=kb, half=half: e.matmul(acc[half][:, hh * 65:(hh + 1) * 65], pt_[:, hh, :], V1[:, kb, h, :], start=(kb == 0 and hh == 0), stop=(kb == nkb - 1 and hh == 3), skip_group_check=True))
                    P.group("tensor", fns, reads=[pt_t, V1_t], writes=[acc[half]])
            yo_, yo_t = yo[0]
            for half in range(2):
                a3 = acc[half][:, 0:260].rearrange("p (h d) -> p h d", h=4)
                P.op("vector", lambda e, a3=a3, half=half: e.reciprocal(out=rec[:, half * 4:(half + 1) * 4], in_=a3[:, :, 64]), reads=[acc[half]], writes=[rec_t])
                P.op("vector", lambda e, a3=a3, half=half, yo_=yo_: e.tensor_tensor(out=yo_[:, half * 256:(half + 1) * 256].rearrange("p (h d) -> p h d", h=4), in0=a3[:, :, 0:64],
                                                                                in1=rec[:, half * 4:(half + 1) * 4].unsqueeze(2).to_broadcast([128, 4, 64]), op=ALU.mult),
                     reads=[acc[half], rec_t], writes=[yo_t])
            P.dma("sync", y[q0:q0 + 128, :], yo_[:], reads=[yo_t])

        for s_ in range(-1, nslot + 1):
            if 0 <= s_ + 1 < nslot:
                st_idx(s_ + 1)
            if 0 <= s_ < nslot:
                st_topk(s_)
            if 0 <= s_ - 1 < nslot:
                st_attn(s_ - 1)


RW = 1024
SGD = 512
ATD = 512
SG_OFF = 0
GATE_OFF = 1024


def sg_consts():
    j = np.arange(128)
    return {"sgmask": (j[:, None] <= j[None, :]).astype(np.float32),
            "identf": np.eye(128, dtype=np.float32)}


def wslot(P, C):
    s = C.cnt["w1"] % 2
    C.cnt["w1"] += 1
    return C.w1[s], C.w1_t[s]


def mixer(P, C, M, t0, load_y_branches=None):
    win_v = M.w_in.rearrange("(c p) n -> p c n", p=128)
    wbr_v = M.w_branch
    hb = C.h
    merged = lambda c: hb[:, c, :]
    yrw = lambda c: hb[:, 16 + c, :]
    yat = lambda c: hb[:, 24 + c, :]
    uT = lambda c: hb[:, 28 + c, :]
    ysg = lambda c: hb[:, 32 + c, :]
    ht = C.h_t
    def load_cast(src, nchunks, base):
        sv = src.rearrange("(c p) t -> p c t", p=128)
        for c in range(nchunks):
            s = C.cnt["sq"] % 2
            C.cnt["sq"] += 1
            sq, sqt = C.sq[s], C.sq_t[s]
            P.dma("sync", sq[:], sv[:, c, t0:t0 + TT], writes=[sqt])
            P.op("gpsimd", lambda e, sq=sq, c=c: e.tensor_copy(out=hb[:, base + c, :], in_=sq[:]), reads=[sqt], writes=[ht[base + c]])
    if getattr(M, "fused", False):
        load_y_branches(P, C, M, t0)
        sg_off, gate_off = 3360, 7024
    else:
        load_cast(M.yrwT, 8, 16)
        load_cast(M.yatT, 4, 24)
        sg_off, gate_off = SG_OFF, GATE_OFF
    wt, wtt = wslot(P, C)
    wload(P, C, wt, wtt, [(wt[:], win_v[:, :, sg_off:sg_off + 512])], f"{C.ph}_sgu", KC * 512)
    for c in range(4):
        pu = next_bank(C)
        fns = [(lambda e, k=k, pu=pu, wt=wt, c=c: e.matmul(pu[:], wt[:, k, c * 128:(c + 1) * 128], C.xb[:, k, :], start=(k == 0), stop=(k == KC - 1))) for k in range(KC)]
        P.group("tensor", fns, reads=[wtt] + C.xb_t, writes=[pu])
        P.op("scalar", lambda e, pu=pu, c=c: e.activation(out=uT(c), in_=pu[:], func=AF.Gelu), reads=[pu], writes=[ht[28 + c]])
    wt, wtt = wslot(P, C)
    wload(P, C, wt, wtt, [(wt[:], win_v[:, :, sg_off + 512:sg_off + 1024])], f"{C.ph}_sgv", KC * 512)
    for ch in range(TT // 128):
        ts = slice(ch * 128, (ch + 1) * 128)
        pv = next_bank(C)
        fns = [(lambda e, k=k, pv=pv, wt=wt, ts=ts: e.matmul(pv[:], C.xb[:, k, ts], wt[:, k, :], start=(k == 0), stop=(k == KC - 1))) for k in range(KC)]
        P.group("tensor", fns, reads=[wtt] + C.xb_t, writes=[pv])
        vg, vgt = M.vg, M.vg_t
        P.op("scalar", lambda e, pv=pv: e.activation(out=vg[:], in_=pv[:], func=AF.Gelu), reads=[pv], writes=[vgt])
        st, stt = M.st, M.st_t
        vsq, vsqt = M.vsq, M.vsq_t
        P.op("vector", lambda e: e.tensor_reduce(out=st[:, 0:1], in_=vg[:], axis=AX.X, op=ALU.add), reads=[vgt], writes=[stt])
        P.op("vector", lambda e: e.tensor_scalar(out=st[:, 0:1], in0=st[:, 0:1], scalar1=1.0 / SGD, scalar2=None, op0=ALU.mult), reads=[stt], writes=[stt])
        P.op("vector", lambda e: e.tensor_scalar(out=vg[:], in0=vg[:], scalar1=st[:, 0:1], scalar2=None, op0=ALU.subtract), reads=[stt], writes=[vgt])
        P.op("gpsimd", lambda e: e.tensor_tensor(out=vsq[:], in0=vg[:], in1=vg[:], op=ALU.mult), reads=[vgt], writes=[vsqt])
        P.op("vector", lambda e: e.tensor_reduce(out=st[:, 1:2], in_=vsq[:], axis=AX.X, op=ALU.add), reads=[vsqt], writes=[stt])
        P.op("scalar", lambda e: e.activation(out=st[:, 1:2], in_=st[:, 1:2], func=AF.Sqrt, bias=LN_EPS, scale=1.0 / SGD), reads=[stt], writes=[stt])
        P.op("vector", lambda e: e.reciprocal(out=st[:, 1:2], in_=st[:, 1:2]), reads=[stt], writes=[stt])
        P.op("vector", lambda e: e.scalar_tensor_tensor(out=vg[:], in0=vg[:], scalar=st[:, 1:2], in1=M.sglg[:], op0=ALU.mult, op1=ALU.mult), reads=[stt, M.sglg_t], writes=[vgt])
        vb, vbt = M.vb, M.vb_t
        P.op("vector", lambda e: e.tensor_tensor(out=vb[:], in0=vg[:], in1=M.sglb[:], op=ALU.add), reads=[vgt, M.sglb_t], writes=[vbt])
        for g in range(4):
            pm = next_bank(C)
            P.op("tensor", lambda e, pm=pm, g=g: e.matmul(pm[:, 0:128], vb[:, g * 128:(g + 1) * 128], M.swT[:, g, :], start=True, stop=True), reads=[vbt, M.swT_t], writes=[pm])
            mx_, mxt = M.mx, M.mx_t
            P.op("vector", lambda e, pm=pm, g=g: e.tensor_tensor(out=mx_[:], in0=pm[:, 0:128], in1=M.sgbb[:, g, :], op=ALU.add), reads=[pm, M.sgbb_t], writes=[mxt])
            P.op("vector", lambda e, g=g, ts=ts: e.tensor_tensor(out=ysg(g)[:, ts], in0=uT(g)[:, ts], in1=mx_[:], op=ALU.mult), reads=[mxt, ht[28 + g]], writes=[ht[32 + g]])
    for c in range(KC):
        cs = slice(c * 128, (c + 1) * 128)
        wt, wtt = wslot(P, C)
        parts = [(wt[:, :, b * 128:(b + 1) * 128], win_v[:, :, gate_off + b * D + c * 128:gate_off + b * D + (c + 1) * 128]) for b in range(3)]
        parts.append((wt[:, :, 384:512], wbr_v.rearrange("(c p) n -> p c n", p=128)[:, :, cs]))
        wload(P, C, wt, wtt, parts, f"{C.ph}_g{c}", KC * 512)
        gts = []
        for b in range(3):
            pg = next_bank(C)
            fns = [(lambda e, k=k, pg=pg, wt=wt, b=b: e.matmul(pg[:], wt[:, k, b * 128:(b + 1) * 128], C.xb[:, k, :], start=(k == 0), stop=(k == KC - 1))) for k in range(KC)]
            P.group("tensor", fns, reads=[wtt] + C.xb_t, writes=[pg])
            gt, gtt = M.gate[b], M.gate_t[b]
            P.op("scalar", lambda e, pg=pg, gt=gt, b=b, c=c: e.activation(out=gt[:], in_=pg[:], func=AF.Sigmoid, bias=M.bg[:, b * KC + c:b * KC + c + 1]), reads=[pg, M.bg_t], writes=[gtt])
            gts.append((gt, gtt))
        srcs = [([yrw(k) for k in range(8)], [ht[16 + k] for k in range(8)], 0),
                ([ysg(k) for k in range(4)], [ht[32 + k] for k in range(4)], 8),
                ([yat(k) for k in range(4)], [ht[24 + k] for k in range(4)], 12)]
        for b, (ops, ops_t, kb0) in enumerate(srcs):
            pz = next_bank(C)
            n = len(ops)
            fns = [(lambda e, k=k, pz=pz, wt=wt, ops=ops, kb0=kb0, n=n: e.matmul(pz[:], wt[:, kb0 + k, 384:512], ops[k], start=(k == 0), stop=(k == n - 1))) for k in range(n)]
            P.group("tensor", fns, reads=[wtt] + ops_t, writes=[pz])
            gt, gtt = gts[b]
            if b == 0:
                P.op("vector", lambda e, pz=pz, gt=gt, c=c: e.tensor_tensor(out=M.macc[:], in0=gt[:], in1=pz[:], op=ALU.mult), reads=[gtt, pz], writes=[M.macc_t])
            else:
                P.op("vector", lambda e, pz=pz, gt=gt: e.tensor_tensor(out=gt[:], in0=gt[:], in1=pz[:], op=ALU.mult), reads=[pz], writes=[gtt])
                if b == 1:
                    P.op("gpsimd", lambda e, gt=gt: e.tensor_tensor(out=M.macc[:], in0=M.macc[:], in1=gt[:], op=ALU.add), reads=[gtt], writes=[M.macc_t])
                else:
                    P.op("gpsimd", lambda e, gt=gt, c=c: e.tensor_tensor(out=merged(c), in0=M.macc[:], in1=gt[:], op=ALU.add), reads=[gtt, M.macc_t], writes=[ht[c]])
    wo_v = M.w_o.rearrange("(c p) n -> p c n", p=128)
    for c4 in range(KC // 4):
        wt, wtt = wslot(P, C)
        wload(P, C, wt, wtt, [(wt[:], wo_v[:, :, c4 * 512:(c4 + 1) * 512])], f"{C.ph}_wo{c4}", KC * 512)
        for j in range(4):
            c = c4 * 4 + j
            po = next_bank(C)
            fns = [(lambda e, k=k, po=po, wt=wt, j=j: e.matmul(po[:], wt[:, k, j * 128:(j + 1) * 128], merged(k), start=(k == 0), stop=(k == KC - 1))) for k in range(KC)]
            P.group("tensor", fns, reads=[wtt] + [ht[k] for k in range(KC)], writes=[po])
            P.op("vector", lambda e, c=c, po=po: e.scalar_tensor_tensor(out=C.xf[:, c, :], in0=C.xf[:, c, :], scalar=ALPHA, in1=po[:], op0=ALU.mult, op1=ALU.add),
                 reads=[po], writes=[C.xf_t[c]])
    layer_norm(P, C, M.ln2g, M.ln2g_t, M.ln2b, M.ln2b_t)


def build_p4(Ttot):
    nc = bass.Bass("TRN2", target_bir_lowering=False)
    din = lambda n, sh: nc.dram_tensor(n, sh, F32, kind="ExternalInput").ap()
    x1T = din("x1T", [D, Ttot]); yrwT = din("yrwT", [RW, Ttot]); yatT = din("yatT", [ATD, Ttot])
    w_in = din("w_in", [D, 7168]); b_gate = din("b_gate", [3 * D])
    sg_ln_g = din("sg_ln_g", [SGD]); sg_ln_b = din("sg_ln_b", [SGD]); sgwT = din("sgwT", [4, 128, 128]); sg_b = din("sg_b", [4, 128])
    w_branch = din("w_branch", [D, D]); w_o = din("w_o", [D, D])
    ln2g_d = din("ln2g", [D]); ln2b_d = din("ln2b", [D])
    w1 = din("w1", [D, 2 * DFF]); w2 = din("w2", [DFF, D]); ln3g_d = din("ln3g", [D]); ln3b_d = din("ln3b", [D])
    sgmask_d = din("sgmask", [128, 128])
    x3T = nc.dram_tensor("x3T", [D, Ttot], F32, kind="ExternalOutput").ap()
    import os
    DBG = "DBG" in os.environ
    if DBG:
        x2T = nc.dram_tensor("x2T", [D, Ttot], F32, kind="ExternalOutput").ap()
    with ExitStack() as es:
        P = Prog(nc, es)
        C = setup_common(P, nc)
        M = Ctx()
        M.w_in, M.w_branch, M.w_o, M.yrwT, M.yatT = w_in, w_branch, w_o, yrwT, yatT
        M.ln2g, M.ln2g_t = load_vec_cols(P, "ln2g_s", ln2g_d, KC)
        M.ln2b, M.ln2b_t = load_vec_cols(P, "ln2b_s", ln2b_d, KC)
        ln3g, ln3g_t = load_vec_cols(P, "ln3g_s", ln3g_d, KC)
        ln3b, ln3b_t = load_vec_cols(P, "ln3b_s", ln3b_d, KC)
        M.bg, M.bg_t = load_vec_cols(P, "bg_s", b_gate, 3 * KC)

        def mk(name, shape, dt=F32):
            t = P.sb(name, shape, dt)
            return t, T(t[:])
        M.sglg, M.sglg_t = mk("sglg", [128, SGD]); P.dma("sync", M.sglg[:], sg_ln_g.partition_broadcast(128), writes=[M.sglg_t])
        M.sglb, M.sglb_t = mk("sglb", [128, SGD]); P.dma("sync", M.sglb[:], sg_ln_b.partition_broadcast(128), writes=[M.sglb_t])
        M.sgbb, M.sgbb_t = mk("sgbb", [128, 4, 128])
        for g in range(4):
            P.dma("sync", M.sgbb[:, g, :], sg_b[g].partition_broadcast(128), writes=[M.sgbb_t])
        swf, swf_t = mk("swf", [128, 4, 128]); P.dma("sync", swf[:], sgwT.rearrange("g j i -> j g i"), writes=[swf_t])
        smk, smk_t = mk("smk", [128, 128]); P.dma("sync", smk[:], sgmask_d, writes=[smk_t])
        M.swT, M.swT_t = mk("swT", [128, 4, 128], BF16)
        P.op("vector", lambda e: e.tensor_tensor(out=M.swT[:], in0=swf[:], in1=smk[:].unsqueeze(1).to_broadcast([128, 4, 128]), op=ALU.mult), reads=[swf_t, smk_t], writes=[M.swT_t])
        M.vg, M.vg_t = mk("vg", [128, SGD]); M.vsq, M.vsq_t = mk("vsq", [128, SGD]); M.vb, M.vb_t = mk("vb", [128, SGD], BF16)
        M.st, M.st_t = mk("sgst", [128, 2]); M.mx, M.mx_t = mk("sgmx", [128, 128])
        M.gate = []; M.gate_t = []
        for b in range(3):
            g_, g_t = mk(f"gate{b}", [128, TT]); M.gate.append(g_); M.gate_t.append(g_t)
        M.macc, M.macc_t = mk("macc", [128, TT])
        finals = []
        for ti in range(Ttot // TT):
            t0 = ti * TT
            load_x(P, C, x1T, t0)
            mixer(P, C, M, t0)
            if DBG:
                finals += store_x(P, C, x2T, t0)
            ffn_ln(P, C, w1, w2, ln3g, ln3g_t, ln3b, ln3b_t)
            finals += store_x(P, C, x3T, t0)
        P.finish(finals)
    return nc


NL = 4
RW_OFF = 0
ATT_OFF = 4384
SGC_OFF = 3360
GATEC_OFF = 7024


class Exch:
    def __init__(self, nc, name, rows, cols, cr):
        self.S = nc.dram_tensor("S_" + name, [rows, cols], F32).ap()
        self.cr = cr
        self.nch = rows // cr
        self.G = [nc.dram_tensor(f"G_{name}{k}", [2 * cr, cols], F32).ap() for k in range(self.nch)]

    def pairs(self):
        return [(self.S[k * self.cr:(k + 1) * self.cr, :], self.G[k]) for k in range(self.nch)]

    def rows(self, rank, r0, n):
        k, off = r0 // self.cr, r0 % self.cr
        assert off + n <= self.cr
        return self.G[k][rank * self.cr + off:rank * self.cr + off + n, :]


def project_tok(P, C, w_ap, c0, ncol, out_ap, oc0, t0, stg):
    wv = w_ap.rearrange("(c p) n -> p c n", p=128)
    wt, wtt = wslot(P, C)
    if ncol == 512:
        wload(P, C, wt, wtt, [(wt[:, :, 0:ncol], wv[:, :, c0:c0 + ncol])], f"{C.ph}_pt{c0}", KC * 512)
    else:
        P.dma("gpsimd", wt[:, :, 0:ncol], wv[:, :, c0:c0 + ncol], writes=[wtt])
    for ch in range(TT // 128):
        ts = slice(ch * 128, (ch + 1) * 128)
        pp = next_bank(C)
        fns = [(lambda e, k=k, pp=pp, wt=wt, ts=ts: e.matmul(pp[:, 0:ncol], C.xb[:, k, ts], wt[:, k, 0:ncol], start=(k == 0), stop=(k == KC - 1))) for k in range(KC)]
        P.group("tensor", fns, reads=[wtt] + C.xb_t, writes=[pp])
        j = stg["cnt"] % len(stg["o"])
        stg["cnt"] += 1
        ot, ott = stg["o"][j], stg["ot"][j]
        if j % 2 == 0:
            P.op("scalar", lambda e, ot=ot, pp=pp: e.copy(out=ot[:, 0:ncol], in_=pp[:, 0:ncol]), reads=[pp], writes=[ott])
        else:
            P.op("vector", lambda e, ot=ot, pp=pp: e.tensor_copy(out=ot[:, 0:ncol], in_=pp[:, 0:ncol]), reads=[pp], writes=[ott])
        P.dma("sync", out_ap[t0 + ch * 128:t0 + (ch + 1) * 128, oc0:oc0 + ncol], ot[:, 0:ncol], reads=[ott])


def project_feat(P, C, w_ap, c0, nrows, out_ap, r0, t0, stg):
    wv = w_ap.rearrange("(c p) n -> p c n", p=128)
    for g0 in range(0, nrows, 512):
        n = min(512, nrows - g0)
        wt, wtt = wslot(P, C)
        wload(P, C, wt, wtt, [(wt[:, :, 0:n], wv[:, :, c0 + g0:c0 + g0 + n])], f"{C.ph}_pf{c0 + g0}", KC * 512)
        for j0 in range(0, n, 128):
            pp = next_bank(C)
            fns = [(lambda e, k=k, pp=pp, wt=wt, j0=j0: e.matmul(pp[:], wt[:, k, j0:j0 + 128], C.xb[:, k, :], start=(k == 0), stop=(k == KC - 1))) for k in range(KC)]
            P.group("tensor", fns, reads=[wtt] + C.xb_t, writes=[pp])
            j = stg["cnt"] % len(stg["o"])
            stg["cnt"] += 1
            ot, ott = stg["o"][j], stg["ot"][j]
            if j % 2 == 0:
                P.op("scalar", lambda e, ot=ot, pp=pp: e.copy(out=ot[:], in_=pp[:]), reads=[pp], writes=[ott])
            else:
                P.op("vector", lambda e, ot=ot, pp=pp: e.tensor_copy(out=ot[:], in_=pp[:]), reads=[pp], writes=[ott])
            P.dma("sync", out_ap[r0 + g0 + j0:r0 + g0 + j0 + 128, t0:t0 + TT], ot[:], reads=[ott])


def load_y_branches(P, C, M, t0):
    hb, ht = C.h, C.h_t
    for ch in range(TT // 128):
        tok = t0 + ch * 128
        j = (t0 // 128 + ch)
        e_ = j % 2
        cands = []
        for hf in range(2):
            yc, yct = M.ycand[hf]
            P.dma("sync", yc[:, 0:512], M.G_yrw.rows(0, hf * 2048 + tok, 128), writes=[yct])
            P.dma("sync", yc[:, 512:1024], M.G_yrw.rows(1, hf * 2048 + tok, 128), writes=[yct])
            slot = hf * 8 + j // 2
            P.dma("sync", yc[:, 1024:1536], M.G_yat.rows(e_, slot * 128, 128), writes=[yct])
            cands.append((yc, yct))
        ys, yst = M.ysel
        P.op("vector", lambda e, ys=ys, c0=cands[0][0]: e.tensor_scalar(out=ys[:], in0=c0[:], scalar1=M.msel[:, 0:1], scalar2=None, op0=ALU.mult), reads=[cands[0][1], M.msel_t], writes=[yst])
        P.op("vector", lambda e, ys=ys, c1=cands[1][0]: e.scalar_tensor_tensor(out=ys[:], in0=c1[:], scalar=M.msel[:, 1:2], in1=ys[:], op0=ALU.mult, op1=ALU.add), reads=[cands[1][1], M.msel_t], writes=[yst])
        for g in range(3):
            pb = next_bank(C)
            for q in range(4):
                c = g * 4 + q
                P.op("tensor", lambda e, pb=pb, q=q, c=c, ys=ys: e.transpose(pb[:, q * 128:(q + 1) * 128], ys[:, c * 128:(c + 1) * 128], M.identf[:]), reads=[yst, M.identf_t], writes=[pb])
            outap = hb[:, 16 + g * 4:16 + g * 4 + 4, ch * 128:(ch + 1) * 128]
            inap = pb[:].rearrange("p (q t) -> p q t", q=4)
            if g % 2 == 0:
                P.op("scalar", lambda e, outap=outap, inap=inap: e.copy(out=outap, in_=inap), reads=[pb], writes=[ht[16 + g * 4 + q] for q in range(4)])
            else:
                P.op("vector", lambda e, outap=outap, inap=inap: e.tensor_copy(out=outap, in_=inap), reads=[pb], writes=[ht[16 + g * 4 + q] for q in range(4)])


def build_fused(nlayers=NL, L=NL):
    nc = bass.Bass("TRN2", target_bir_lowering=False)
    din = lambda n, sh: nc.dram_tensor(n, sh, F32, kind="ExternalInput").ap()
    I = {}
    I["xT"] = din("xT", [D, 2048])
    for n, sh in (("ffn1_w_in", [L, D, 2 * DFF]), ("ffn1_w_out", [L, DFF, D]), ("ln1_g", [L, D]), ("ln1_b", [L, D]),
                  ("w_in", [L, D, 13168]), ("b_gate", [L, 3 * D]),
                  ("mu_c", [L, 1824]), ("w0_c", [L, 512]), ("a0_c", [L, 512]), ("k_k_c", [L, 512]), ("k_a_c", [L, 512]), ("r_k_c", [L, 512]),
                  ("gn_g_c", [L, 512]), ("gn_b_c", [L, 512]), ("w2_c", [L, 64, 512]), ("a2_c", [L, 64, 512]), ("g2_c", [L, 160, 512]),
                  ("sg_ln_g", [L, 512]), ("sg_ln_b", [L, 512]), ("sgwT", [L, 4, 128, 128]), ("sg_b", [L, 4, 128]),
                  ("idx_ln_g", [L, 64]), ("idx_ln_b", [L, 64]), ("w_branch", [L, D, D]), ("w_o", [L, D, D]),
                  ("ln2_g", [L, D]), ("ln2_b", [L, D]), ("ffn2_w_in", [L, D, 2 * DFF]), ("ffn2_w_out", [L, DFF, D]), ("ln3_g", [L, D]), ("ln3_b", [L, D]),
                  ("msel", [128, 2]), ("sgmask", [128, 128])):
        I[n] = din(n, sh)
    rc = {n: din("rc_" + n, list(v.shape)) for n, v in rwkv_consts().items()}
    dc = {n: din("dc_" + n, list(v.shape)) for n, v in dsa_consts(0, np.zeros((32, 8), np.float32)).items()}
    xoT = nc.dram_tensor("xoT", [D, 2048], F32, kind="ExternalOutput").ap()
    dt_ = lambda n, sh: nc.dram_tensor(n, sh, F32).ap()
    x1s = dt_("x1s", [D, 2048]); xcur = dt_("xcur", [D, 2048])
    X_prw = Exch(nc, "prw", 2048, 3360, 128); X_qT = Exch(nc, "qT", 512, 2048, 256); X_kT = Exch(nc, "kT", 512, 2048, 256)
    X_qiT = Exch(nc, "qiT", 1024, 2048, 256); X_V = Exch(nc, "V", 2048, 512, 1024); X_kw = Exch(nc, "kw", 2048, 80, 2048)
    X_yrw = Exch(nc, "yrw", 4096, 512, 1024); X_yat = Exch(nc, "yat", 2048, 512, 1024)
    S_prw, S_qT, S_kT, S_qiT, S_V, S_kw, S_yrw, S_yat = X_prw.S, X_qT.S, X_kT.S, X_qiT.S, X_V.S, X_kw.S, X_yrw.S, X_yat.S
    groups = [[0, 1], [2, 3], [4, 5], [6, 7]]
    with ExitStack() as es:
        P = Prog(nc, es)
        banks = [T(P.ps(f"bank{i}", [128, 512], F32)) for i in range(8)]

        def mk(name, shape, dt=F32):
            t = P.sb(name, shape, dt)
            return t, T(t[:])

        def exchange(pairs):
            P.barrier()
            for (s_, g_) in pairs:
                P.coll("AllGather", groups, s_, g_)
            P.barrier()

        WC = {"t": {}, "ev": {}}
        for l in range(nlayers):
            P.push_scope()
            C = setup_common(P, nc, banks)
            C.wc = WC; C.ph = "p1"
            lng, lng_t = load_vec_cols(P, "ln1g", I["ln1_g"][l], KC)
            lnb, lnb_t = load_vec_cols(P, "ln1b", I["ln1_b"][l], KC)
            stg = {"o": [], "ot": [], "cnt": 0}
            for i in range(3):
                o_, ot_ = mk(f"stg{i}", [128, TT]); stg["o"].append(o_); stg["ot"].append(ot_)
            xsrc = I["xT"] if l == 0 else xcur
            w_in_l = I["w_in"][l]
            for ti in range(2048 // TT):
                t0 = ti * TT
                C.ti = ti
                load_x(P, C, xsrc, t0)
                ffn_ln(P, C, I["ffn1_w_in"][l], I["ffn1_w_out"][l], lng, lng_t, lnb, lnb_t)
                store_x(P, C, x1s, t0)
                for c0 in range(0, 3360, 512):
                    project_tok(P, C, w_in_l, RW_OFF + c0, min(512, 3360 - c0), S_prw, c0, t0, stg)
                project_feat(P, C, w_in_l, ATT_OFF + 0, 512, S_qT, 0, t0, stg)
                project_feat(P, C, w_in_l, ATT_OFF + 512, 512, S_kT, 0, t0, stg)
                project_tok(P, C, w_in_l, ATT_OFF + 1024, 512, S_V, 0, t0, stg)
                project_feat(P, C, w_in_l, ATT_OFF + 1536, 1024, S_qiT, 0, t0, stg)
                project_tok(P, C, w_in_l, ATT_OFF + 2560, 80, S_kw, 0, t0, stg)
            exchange(X_prw.pairs() + X_qT.pairs() + X_kT.pairs() + X_qiT.pairs() + X_V.pairs() + X_kw.pairs())
            P.pop_scope()
            P.push_scope()
            prm = {"mu": I["mu_c"][l], "w0": I["w0_c"][l], "a0": I["a0_c"][l], "k_k": I["k_k_c"][l], "k_a": I["k_a_c"][l], "r_k": I["r_k_c"][l],
                   "gn_g": I["gn_g_c"][l], "gn_b": I["gn_b_c"][l], "w2": I["w2_c"][l], "a2": I["a2_c"][l], "g2": I["g2_c"][l]}
            rwkv_phase(P, banks, X_prw, prm, rc, I["msel"], S_yrw)
            P.barrier()
            P.pop_scope()
            P.push_scope()
            dsa_phase(P, banks, {"kT": X_kT, "V": X_V, "kw": X_kw, "qT": X_qT, "qiT": X_qiT}, I["idx_ln_g"][l], I["idx_ln_b"][l], dc, I["msel"], S_yat)
            exchange(X_yrw.pairs() + X_yat.pairs())
            P.pop_scope()
            P.push_scope()
            C = setup_common(P, nc, banks)
            C.wc = WC; C.ph = "p4"
            M = Ctx()
            M.w_in, M.w_branch, M.w_o = I["w_in"][l], I["w_branch"][l], I["w_o"][l]
            M.G_yrw, M.G_yat = X_yrw, X_yat
            M.ln2g, M.ln2g_t = load_vec_cols(P, "ln2g", I["ln2_g"][l], KC)
            M.ln2b, M.ln2b_t = load_vec_cols(P, "ln2b", I["ln2_b"][l], KC)
            ln3g, ln3g_t = load_vec_cols(P, "ln3g", I["ln3_g"][l], KC)
            ln3b, ln3b_t = load_vec_cols(P, "ln3b", I["ln3_b"][l], KC)
            M.bg, M.bg_t = load_vec_cols(P, "bg", I["b_gate"][l], 3 * KC)
            M.msel, M.msel_t = mk("msel4", [128, 2]); P.dma("sync", M.msel[:], I["msel"], writes=[M.msel_t])
            M.identf, M.identf_t = mk("identf", [128, 128]); P.dma("sync", M.identf[:], rc["ident"], writes=[M.identf_t])
            M.sglg, M.sglg_t = mk("sglg", [128, SGD]); P.dma("sync", M.sglg[:], I["sg_ln_g"][l].partition_broadcast(128), writes=[M.sglg_t])
            M.sglb, M.sglb_t = mk("sglb", [128, SGD]); P.dma("sync", M.sglb[:], I["sg_ln_b"][l].partition_broadcast(128), writes=[M.sglb_t])
            M.sgbb, M.sgbb_t = mk("sgbb", [128, 4, 128])
            for g in range(4):
                P.dma("sync", M.sgbb[:, g, :], I["sg_b"][l][g].partition_broadcast(128), writes=[M.sgbb_t])
            swf, swf_t = mk("swf", [128, 4, 128]); P.dma("sync", swf[:], I["sgwT"][l].rearrange("g j i -> j g i"), writes=[swf_t])
            smk, smk_t = mk("smk", [128, 128]); P.dma("sync", smk[:], I["sgmask"], writes=[smk_t])
            M.swT, M.swT_t = mk("swT", [128, 4, 128], BF16)
            P.op("vector", lambda e, M=M, swf=swf, smk=smk: e.tensor_tensor(out=M.swT[:], in0=swf[:], in1=smk[:].unsqueeze(1).to_broadcast([128, 4, 128]), op=ALU.mult), reads=[swf_t, smk_t], writes=[M.swT_t])
            M.vg, M.vg_t = mk("vg", [128, SGD]); M.vsq, M.vsq_t = mk("vsq", [128, SGD]); M.vb, M.vb_t = mk("vb", [128, SGD], BF16)
            M.st, M.st_t = mk("sgst", [128, 2]); M.mx, M.mx_t = mk("sgmx", [128, 128])
            M.gate = []; M.gate_t = []
            for b in range(3):
                g_, g_t = mk(f"gate{b}", [128, TT]); M.gate.append(g_); M.gate_t.append(g_t)
            M.macc, M.macc_t = mk("macc", [128, TT])
            M.ycand = [mk(f"ycand{i}", [128, 1536]) for i in range(2)]
            M.ysel = mk("ysel", [128, 1536])
            M.fused = True
            for ti in range(2048 // TT):
                t0 = ti * TT
                C.ti = ti
                load_x(P, C, x1s, t0)
                mixer(P, C, M, t0, load_y_branches)
                ffn_ln(P, C, I["ffn2_w_in"][l], I["ffn2_w_out"][l], ln3g, ln3g_t, ln3b, ln3b_t)
                fin = store_x(P, C, xoT if l == nlayers - 1 else xcur, t0)
            P.barrier()
            P.pop_scope()
        P.finish(P.all_events())
    return nc


_FUSED = {}


def _c(a):
    return np.ascontiguousarray(a, dtype=np.float32)


def make_in_maps(inp):
    x = np.asarray(inp["x"], dtype=np.float32)
    B, S_, D_ = x.shape
    xf = x.reshape(B * S_, D_)
    NCORE = 8
    TSH = (B * S_) // NCORE
    shared = {}
    for k in ("ffn1_w_in", "ffn1_w_out", "ln1_g", "ln1_b", "w_in", "b_gate", "sg_ln_g", "sg_ln_b", "sg_b", "idx_ln_g", "idx_ln_b",
              "w_branch", "w_o", "ln2_g", "ln2_b", "ffn2_w_in", "ffn2_w_out", "ln3_g", "ln3_b"):
        shared[k] = _c(inp[k])
    shared["sgwT"] = _c(np.transpose(np.asarray(inp["sg_w"], dtype=np.float32), (0, 1, 3, 2)))
    shared["sgmask"] = _c(sg_consts()["sgmask"])
    for n, v in rwkv_consts().items():
        shared["rc_" + n] = _c(v)
    rel_bias = np.asarray(inp["rel_bias"], dtype=np.float32)
    L = shared["w_in"].shape[0]
    per_e = []
    for e in range(2):
        m = {}
        hs = slice(e * 512, (e + 1) * 512)
        cols = np.r_[e * 512:(e + 1) * 512, 1024 + e * 512:1024 + (e + 1) * 512, 2048 + e * 512:2048 + (e + 1) * 512, 3072:3360]
        m["mu_c"] = _c(np.asarray(inp["rwkv_mu"])[:, cols])
        for n, src in (("w0_c", "rwkv_w0"), ("a0_c", "rwkv_a0"), ("k_k_c", "rwkv_k_k"), ("k_a_c", "rwkv_k_a"), ("gn_g_c", "rwkv_gn_g"), ("gn_b_c", "rwkv_gn_b")):
            m[n] = _c(np.asarray(inp[src])[:, hs])
        m["r_k_c"] = _c(np.asarray(inp["rwkv_r_k"]).reshape(L, -1)[:, hs])
        for n, src in (("w2_c", "rwkv_w2"), ("a2_c", "rwkv_a2"), ("g2_c", "rwkv_g2")):
            m[n] = _c(np.asarray(inp[src])[:, :, hs])
        for n, v in dsa_consts(e, rel_bias).items():
            m["dc_" + n] = _c(v)
        ms = np.zeros((128, 2), np.float32)
        ms[:, e] = 1.0
        m["msel"] = ms
        per_e.append(m)
    in_maps = []
    for c in range(NCORE):
        m = dict(shared)
        m.update(per_e[c % 2])
        m["xT"] = _c(xf[c * TSH:(c + 1) * TSH].T)
        in_maps.append(m)
    return in_maps, (B, S_, D_)


def kernel(**inp):
    if "nc" not in _FUSED:
        _FUSED["nc"] = build_fused()
    in_maps, (B, S_, D_) = make_in_maps(inp)
    cores = list(range(8))
    res = run_bass_kernel_spmd(_FUSED["nc"], in_maps, core_ids=cores)
    out = np.concatenate([np.ascontiguousarray(res.results[c]["xoT"].T) for c in cores], axis=0).reshape(B, S_, D_)
    return out.astype(np.float32)
```
